# Optimizing a Trainium2 kernel written in Bass

```python
import math
import jax, jax.numpy as jnp
from jax import lax
import numpy as np

D_MODEL = 2048
BATCH = 2
SEQ = 8192
DEPTH = 2
DEC_BATCH = 4
DEC_SEQ = 2048
PAST_LEN = 128

GRID_W = 64
N_BRANCH = 4
BRANCH_W = D_MODEL // N_BRANCH
EPS = 1e-6
A_HEADS = 4
A_DH = BRANCH_W // A_HEADS
A_GATES = 4 * A_HEADS
CHUNK = 128
M_INIT = -1e30
B_HEADS = 4
B_DH = BRANCH_W // B_HEADS
WIN_ROWS = 8
WIN_COLS = 16
QCOL_BLK = 16
KCOL_BLK = 32
C_HEADS = 8
C_KV_HEADS = 2
C_DH = BRANCH_W // C_HEADS
C_WINDOW = 128
C_BLOCK = 128
N_BUCKETS = 32
MAX_DIST = 128
CONV_W = 31
IN_SIZES = (BRANCH_W, BRANCH_W, BRANCH_W, BRANCH_W, BRANCH_W, A_GATES,
            BRANCH_W, BRANCH_W, BRANCH_W, BRANCH_W,
            BRANCH_W, C_KV_HEADS * C_DH, C_KV_HEADS * C_DH, BRANCH_W,
            2 * BRANCH_W, BRANCH_W,
            N_BRANCH * D_MODEL)
IN_COLS = sum(IN_SIZES)

kernel_name = 'hybrid_bidir_encoder_parallel_gated'


def rms_norm(x, w):
    xf = x.astype(jnp.float32)
    y = xf * lax.rsqrt(jnp.mean(xf * xf, -1, keepdims=True) + EPS) * w.astype(jnp.float32)
    return y.astype(x.dtype)


def head_rms(t, w):
    tf = t.astype(jnp.float32)
    y = tf * lax.rsqrt(jnp.mean(tf * tf, -1, keepdims=True) + EPS) * w.astype(jnp.float32)
    return y.astype(t.dtype)


def mlstm_chunkwise(q, k, v, ig, fg):
    bs, nh, t, d = q.shape
    nc = t // CHUNK
    q = q.reshape(bs, nh, nc, CHUNK, d)
    k = k.reshape(bs, nh, nc, CHUNK, d)
    v = v.reshape(bs, nh, nc, CHUNK, d)
    ig = ig.reshape(bs, nh, nc, CHUNK)
    b = jnp.cumsum(jax.nn.log_sigmoid(fg).reshape(bs, nh, nc, CHUNK), axis=-1)
    g = b[..., -1]
    w_end = g[..., None] - b + ig
    m_loc = jnp.max(w_end, axis=-1)
    a_end = jnp.exp(w_end - m_loc[..., None])
    c_loc = jnp.einsum('bhcs,bhcsd,bhcse->bhcde', a_end, k, v)
    n_loc = jnp.einsum('bhcs,bhcsd->bhcd', a_end, k)

    def step(carry, inp):
        c_st, n_st, m_st = carry
        g_c, m_l, c_l, n_l = inp
        m_new = jnp.maximum(g_c + m_st, m_l)
        a_old = jnp.exp(g_c + m_st - m_new)
        a_new = jnp.exp(m_l - m_new)
        c_new = a_old[..., None, None] * c_st + a_new[..., None, None] * c_l
        n_new = a_old[..., None] * n_st + a_new[..., None] * n_l
        return (c_new, n_new, m_new), (c_st, n_st, m_st)

    init = (jnp.zeros((bs, nh, d, d), jnp.float32), jnp.zeros((bs, nh, d), jnp.float32),
            jnp.full((bs, nh), M_INIT, jnp.float32))
    mv = lambda u: jnp.moveaxis(u, 2, 0)
    _, (c_in, n_in, m_in) = lax.scan(step, init, (mv(g), mv(m_loc), mv(c_loc), mv(n_loc)))
    c_in = jnp.moveaxis(c_in, 0, 2)
    n_in = jnp.moveaxis(n_in, 0, 2)
    m_in = jnp.moveaxis(m_in, 0, 2)
    dmat = b[..., :, None] - b[..., None, :] + ig[..., None, :]
    causal = jnp.tril(jnp.ones((CHUNK, CHUNK), bool))
    dmat = jnp.where(causal, dmat, -jnp.inf)
    inter = b + m_in[..., None]
    m_t = jnp.maximum(jnp.max(dmat, axis=-1), inter)
    s = jnp.einsum('bhctd,bhcsd->bhcts', q, k) * jnp.exp(dmat - m_t[..., None])
    a_in = jnp.exp(inter - m_t)
    num = jnp.einsum('bhcts,bhcse->bhcte', s, v) + a_in[..., None] * jnp.einsum('bhctd,bhcde->bhcte', q, c_in)
    den = jnp.sum(s, axis=-1) + a_in * jnp.einsum('bhctd,bhcd->bhct', q, n_in)
    h = num / jnp.maximum(jnp.abs(den), jnp.exp(-m_t))[..., None]
    return h.reshape(bs, nh, t, d)


def mlstm_branch(aq, ak, av, ao, agates, b_gate, norm_w):
    bs, t, _ = aq.shape
    f32 = jnp.float32
    heads = lambda u: u.astype(f32).reshape(bs, t, A_HEADS, A_DH).transpose(0, 2, 1, 3)
    q, k, v = heads(aq), heads(ak) * (A_DH ** -0.5), heads(av)
    gt = (agates.astype(f32) + b_gate.astype(f32)).transpose(0, 2, 1)
    i_fw, i_bw = gt[:, 0:A_HEADS], gt[:, A_HEADS:2 * A_HEADS]
    f_fw, f_bw = gt[:, 2 * A_HEADS:3 * A_HEADS], gt[:, 3 * A_HEADS:]
    h_fw = mlstm_chunkwise(q, k, v, i_fw, f_fw)
    rev = lambda u: jnp.flip(u, axis=2)
    h_bw = rev(mlstm_chunkwise(rev(q), rev(k), rev(v), rev(i_bw), rev(f_bw)))
    h = (h_fw + h_bw).transpose(0, 2, 1, 3)
    h = h * lax.rsqrt(jnp.mean(h * h, -1, keepdims=True) + EPS) * norm_w.astype(f32).reshape(A_HEADS, A_DH)
    return (h.reshape(bs, t, BRANCH_W) * jax.nn.sigmoid(ao.astype(f32))).astype(aq.dtype)


def neighbourhood_attention(bq, bk, bv, q_norm_w, k_norm_w, rpb):
    bs, t, _ = bq.shape
    rows = t // GRID_W
    kh = min(WIN_ROWS, rows)
    ncb = GRID_W // QCOL_BLK
    grid = lambda u: u.reshape(bs, rows, GRID_W, B_HEADS, B_DH)
    q = (head_rms(grid(bq), q_norm_w) * (B_DH ** -0.5)).reshape(bs, rows, ncb, QCOL_BLK, B_HEADS, B_DH)
    k = head_rms(grid(bk), k_norm_w)
    v = grid(bv)
    r = jnp.arange(rows)
    row_idx = jnp.clip(r - kh // 2, 0, rows - kh)[:, None] + jnp.arange(kh)[None, :]
    cb = jnp.arange(ncb)
    col_idx = jnp.clip(cb * QCOL_BLK - WIN_COLS // 2, 0, GRID_W - KCOL_BLK)[:, None] + jnp.arange(KCOL_BLK)[None, :]
    ri, ci = row_idx[:, None, :, None], col_idx[None, :, None, :]
    kg = k[:, ri, ci]
    vg = v[:, ri, ci]
    s = jnp.einsum('brcqhd,brcjkhd->bhrcqjk', q, kg).astype(jnp.float32)
    qc = cb[:, None] * QCOL_BLK + jnp.arange(QCOL_BLK)[None, :]
    qs = jnp.clip(qc - WIN_COLS // 2, 0, GRID_W - WIN_COLS)
    kc = col_idx[:, None, :]
    col_ok = (kc >= qs[..., None]) & (kc < qs[..., None] + WIN_COLS)
    dr = row_idx - r[:, None] + WIN_ROWS - 1
    dc = jnp.clip(kc - qc[..., None] + WIN_COLS - 1, 0, 2 * WIN_COLS - 2)
    bias = rpb.astype(jnp.float32)[:, dr[:, None, None, :, None], dc[None, :, :, None, :]]
    s = jnp.where(col_ok[:, :, None, :], s + bias[None], -jnp.inf)
    p = jax.nn.softmax(s.reshape(s.shape[:-2] + (kh * KCOL_BLK,)), axis=-1).reshape(s.shape)
    o = jnp.einsum('bhrcqjk,brcjkhd->brcqhd', p.astype(vg.dtype), vg)
    return o.reshape(bs, t, BRANCH_W)


def t5_bucket(rel):
    half = N_BUCKETS // 2
    max_exact = half // 2
    n = jnp.abs(rel)
    nf = jnp.maximum(n, 1).astype(jnp.float32)
    large = max_exact + (jnp.log(nf / max_exact) / math.log(MAX_DIST / max_exact) * (half - max_exact)).astype(jnp.int32)
    large = jnp.minimum(large, half - 1)
    return jnp.where(rel > 0, half, 0) + jnp.where(n < max_exact, n, large)


def window_gqa(cq, ck, cv, q_norm_w, k_norm_w, sink, rel_bias):
    bs, t, _ = cq.shape
    nb = t // C_BLOCK
    grp = C_HEADS // C_KV_HEADS
    q = head_rms(cq.reshape(bs, t, C_HEADS, C_DH), q_norm_w) * (C_DH ** -0.5)
    q = q.reshape(bs, nb, C_BLOCK, C_KV_HEADS, grp, C_DH)
    k = head_rms(ck.reshape(bs, t, C_KV_HEADS, C_DH), k_norm_w)
    v = cv.reshape(bs, t, C_KV_HEADS, C_DH)

    def band(u):
        up = jnp.pad(u, ((0, 0), (C_BLOCK, C_BLOCK), (0, 0), (0, 0))).reshape(bs, nb + 2, C_BLOCK, C_KV_HEADS, C_DH)
        return jnp.concatenate([up[:, :-2], up[:, 1:-1], up[:, 2:]], axis=2)

    kb, vb = band(k), band(v)
    s = jnp.einsum('bnqhgd,bnkhd->bhgnqk', q, kb).astype(jnp.float32)
    qi = jnp.arange(C_BLOCK)
    ki = jnp.arange(3 * C_BLOCK)
    rel = ki[None, :] - C_BLOCK - qi[:, None]
    bias = rel_bias.astype(jnp.float32)[t5_bucket(rel)].transpose(2, 0, 1).reshape(C_KV_HEADS, grp, C_BLOCK, 3 * C_BLOCK)
    kpos = jnp.arange(nb)[:, None, None] * C_BLOCK - C_BLOCK + ki[None, None, :]
    ok = (jnp.abs(rel)[None] <= C_WINDOW) & (kpos >= 0) & (kpos < t)
    s = jnp.where(ok, s + bias[None, :, :, None], -jnp.inf)
    sk = sink.astype(jnp.float32).reshape(1, C_KV_HEADS, grp, 1, 1, 1)
    m = jnp.maximum(jnp.max(s, axis=-1, keepdims=True), sk)
    p = jnp.exp(s - m)
    p = p / (jnp.sum(p, axis=-1, keepdims=True) + jnp.exp(sk - m))
    o = jnp.einsum('bhgnqk,bnkhd->bnqhgd', p.astype(vb.dtype), vb)
    return o.reshape(bs, t, BRANCH_W)


def conformer_conv(dglu, conv_w, conv_b, ln_w, ln_b):
    a, g = jnp.split(dglu, 2, axis=-1)
    u = a * jax.nn.sigmoid(g)
    u = lax.conv_general_dilated(u, conv_w[:, None, :].astype(u.dtype), window_strides=(1,),
                                 padding=[(CONV_W // 2, CONV_W // 2)],
                                 dimension_numbers=('NWC', 'WIO', 'NWC'),
                                 feature_group_count=BRANCH_W) + conv_b
    uf = u.astype(jnp.float32)
    mu = jnp.mean(uf, -1, keepdims=True)
    var = jnp.mean(jnp.square(uf - mu), -1, keepdims=True)
    uf = (uf - mu) * lax.rsqrt(var + EPS) * ln_w.astype(jnp.float32) + ln_b.astype(jnp.float32)
    return jax.nn.silu(uf).astype(dglu.dtype)


def mixer_layer(x, c, rel_bias, norm_w, w_ada, b_ada, w_in, b_gate, mlstm_norm_w, na_q_norm, na_k_norm,
                na_rpb, swa_q_norm, swa_k_norm, swa_sink, conv_w, conv_b, conv_ln_w, conv_ln_b, w_branch, w_out):
    mod = jax.nn.silu(c) @ w_ada + b_ada
    shift, scale, gate = jnp.split(mod, 3, axis=-1)
    h = rms_norm(x, norm_w) * (1 + scale[:, None, :]) + shift[:, None, :]
    pts = []
    acc = 0
    for sz in IN_SIZES[:-1]:
        acc += sz
        pts.append(acc)
    ws = jnp.split(w_in, pts, axis=1)
    (aq, ak, av, ao, az, ag, bq, bk, bv, bz, cq, ck, cv, cz, dglu, dz) = [h @ w for w in ws[:-1]]
    w_merge = ws[-1].reshape(D_MODEL, N_BRANCH, D_MODEL)
    y_a = mlstm_branch(aq, ak, av, ao, ag, b_gate, mlstm_norm_w) * jax.nn.silu(az)
    y_b = neighbourhood_attention(bq, bk, bv, na_q_norm, na_k_norm, na_rpb) * jax.nn.silu(bz)
    y_c = window_gqa(cq, ck, cv, swa_q_norm, swa_k_norm, swa_sink, rel_bias) * jax.nn.silu(cz)
    y_d = conformer_conv(dglu, conv_w, conv_b, conv_ln_w, conv_ln_b) * jax.nn.silu(dz)
    merged = jax.nn.sigmoid(h @ w_merge[:, 0]) * (y_a @ w_branch[0])
    merged = merged + jax.nn.sigmoid(h @ w_merge[:, 1]) * (y_b @ w_branch[1])
    merged = merged + jax.nn.sigmoid(h @ w_merge[:, 2]) * (y_c @ w_branch[2])
    merged = merged + jax.nn.sigmoid(h @ w_merge[:, 3]) * (y_d @ w_branch[3])
    return x + gate[:, None, :] * (merged @ w_out)


def trunk(x, c, rel_bias, norm_w, w_ada, b_ada, w_in, b_gate, mlstm_norm_w, na_q_norm, na_k_norm, na_rpb,
          swa_q_norm, swa_k_norm, swa_sink, conv_w, conv_b, conv_ln_w, conv_ln_b, w_branch, w_out):
    for l in range(DEPTH):
        x = mixer_layer(x, c, rel_bias, norm_w[l], w_ada[l], b_ada[l], w_in[l], b_gate[l], mlstm_norm_w[l],
                        na_q_norm[l], na_k_norm[l], na_rpb[l], swa_q_norm[l], swa_k_norm[l], swa_sink[l],
                        conv_w[l], conv_b[l], conv_ln_w[l], conv_ln_b[l], w_branch[l], w_out[l])
    return x


def setup_inputs(seed: int = 0) -> dict:
    key = jax.random.key(seed)
    ks = jax.random.split(key, 24)
    f32 = jnp.float32
    nrm = lambda k, shape, s: s * jax.random.normal(k, shape, f32)
    d = D_MODEL
    b_gate = jnp.concatenate([jnp.zeros((DEPTH, 2 * A_HEADS), f32),
                              jnp.tile(jnp.linspace(3.0, 6.0, A_HEADS, dtype=f32), (DEPTH, 2))], axis=1) \
        + nrm(ks[9], (DEPTH, A_GATES), 0.1)
    return {
        'x_prompt': nrm(ks[0], (BATCH, SEQ, d), 1.0),
        'x_sample': nrm(ks[1], (DEC_BATCH, DEC_SEQ, d), 1.0),
        'c_prompt': nrm(ks[2], (BATCH, d), 1.0),
        'c_sample': nrm(ks[3], (DEC_BATCH, d), 1.0),
        'rel_bias': nrm(ks[4], (N_BUCKETS, C_HEADS), 0.1),
        'norm_w': 1.0 + nrm(ks[5], (DEPTH, d), 0.05),
        'w_ada': nrm(ks[6], (DEPTH, d, 3 * d), 0.5 * d ** -0.5),
        'b_ada': nrm(ks[7], (DEPTH, 3 * d), 0.01),
        'w_in': nrm(ks[8], (DEPTH, d, IN_COLS), d ** -0.5),
        'b_gate': b_gate,
        'mlstm_norm_w': 1.0 + nrm(ks[10], (DEPTH, BRANCH_W), 0.05),
        'na_q_norm': 1.0 + nrm(ks[11], (DEPTH, B_DH), 0.05),
        'na_k_norm': 1.0 + nrm(ks[12], (DEPTH, B_DH), 0.05),
        'na_rpb': nrm(ks[13], (DEPTH, B_HEADS, 2 * WIN_ROWS - 1, 2 * WIN_COLS - 1), 0.1),
        'swa_q_norm': 1.0 + nrm(ks[14], (DEPTH, C_DH), 0.05),
        'swa_k_norm': 1.0 + nrm(ks[15], (DEPTH, C_DH), 0.05),
        'swa_sink': nrm(ks[16], (DEPTH, C_HEADS), 0.5),
        'conv_w': nrm(ks[17], (DEPTH, CONV_W, BRANCH_W), CONV_W ** -0.5),
        'conv_b': nrm(ks[18], (DEPTH, BRANCH_W), 0.01),
        'conv_ln_w': 1.0 + nrm(ks[19], (DEPTH, BRANCH_W), 0.05),
        'conv_ln_b': nrm(ks[20], (DEPTH, BRANCH_W), 0.01),
        'w_branch': nrm(ks[21], (DEPTH, N_BRANCH, BRANCH_W, d), BRANCH_W ** -0.5),
        'w_out': nrm(ks[22], (DEPTH, d, d), d ** -0.5),
    }


def reference(x_prompt, x_sample, c_prompt, c_sample, rel_bias, norm_w, w_ada, b_ada, w_in, b_gate,
              mlstm_norm_w, na_q_norm, na_k_norm, na_rpb, swa_q_norm, swa_k_norm, swa_sink, conv_w, conv_b,
              conv_ln_w, conv_ln_b, w_branch, w_out):
    y_prompt = trunk(x_prompt, c_prompt, rel_bias, norm_w, w_ada, b_ada, w_in, b_gate, mlstm_norm_w,
                     na_q_norm, na_k_norm, na_rpb, swa_q_norm, swa_k_norm, swa_sink, conv_w, conv_b,
                     conv_ln_w, conv_ln_b, w_branch, w_out)
    y_sample = trunk(x_sample, c_sample, rel_bias, norm_w, w_ada, b_ada, w_in, b_gate, mlstm_norm_w,
                     na_q_norm, na_k_norm, na_rpb, swa_q_norm, swa_k_norm, swa_sink, conv_w, conv_b,
                     conv_ln_w, conv_ln_b, w_branch, w_out)
    return (y_prompt, y_sample)
```

```python
import contextlib
import math
import numpy as np
import concourse.bass as bass
import concourse.mybir as mybir
from concourse.bass_utils import run_bass_kernel_spmd

F32 = mybir.dt.float32
BF16 = mybir.dt.bfloat16
ALU = mybir.AluOpType
AF = mybir.ActivationFunctionType

D = 2048
KC = 16
IN_COLS = 15632
EPS = 1e-6
C_AQ, C_AK, C_AV, C_AO, C_AZ, C_AG = 0, 512, 1024, 1536, 2048, 2560
C_BQ, C_BK, C_BV, C_BZ = 2576, 3088, 3600, 4112
C_CQ, C_CK, C_CV, C_CZ = 4624, 5136, 5264, 5392
C_DA, C_DG, C_DZ, C_MG = 5904, 6416, 6928, 7440
NFM = 7440

ENGS = ("tensor", "vector", "scalar", "gpsimd", "sync")
NDMA = 28
SEM_EPOCH = 30000


class Buf:
    __slots__ = ("w", "r")

    def __init__(self):
        self.w = None
        self.r = []


class KB:
    def __init__(self, nc):
        self.nc = nc
        self.ops = {e: [] for e in ENGS}
        self.cnt = {}
        self.known = {e: {} for e in ENGS}
        self.dma_rr = 0
        self.dma_last = {}
        self.pending = {e: [] for e in ENGS}
        self.tot = {}

    def barrier(self):
        for e in ENGS:
            for k, v in list(self.cnt.items()) + list(self.dma_last.items()):
                self._need(e, (k, v), self.pending[e])

    def _need(self, eng, dep, waits):
        if dep is None:
            return
        k, v = dep
        if eng == "tensor" and k.startswith("tensor#"):
            return
        if self.known[eng].get(k, 0) >= v:
            return
        self.known[eng][k] = v
        waits.append((k, v))

    def op(self, eng, fn, reads=(), writes=(), dma=False):
        waits = self.pending[eng]
        self.pending[eng] = []
        for b in reads:
            self._need(eng, b.w, waits)
        for b in writes:
            self._need(eng, b.w, waits)
            for d in b.r:
                self._need(eng, d, waits)
        if dma:
            k = f"dma{self.dma_rr % NDMA}"
            self.dma_rr += 1
            last = self.dma_last.get(k, 0)
            if last:
                self._need(eng, (k, last), waits)
            v = last + 16
            self.dma_last[k] = v
            inc = 16
        else:
            tot = self.tot.get(eng, 0) + 1
            self.tot[eng] = tot
            k = f"{eng}#{(tot - 1) // SEM_EPOCH}"
            v = (tot - 1) % SEM_EPOCH + 1
            self.cnt[k] = v
            inc = 1
        tag = (k, v)
        for b in reads:
            b.r.append(tag)
            if len(b.r) > 64:
                b.r = b.r[-64:]
        for b in writes:
            b.w = tag
            b.r = []
        self.ops[eng].append((waits, fn, k, inc))

    def emit(self):
        nc = self.nc
        keys = set()
        for e in ENGS:
            for waits, fn, k, inc in self.ops[e]:
                keys.add(k)
        with contextlib.ExitStack() as st:
            sems = {k: st.enter_context(nc.semaphore(f"s_{k}")) for k in sorted(keys)}
            block = st.enter_context(nc.Block())

            def mk(e):
                def body(eng):
                    for waits, fn, k, inc in self.ops[e]:
                        for (wk, wv) in waits:
                            eng.wait_ge(sems[wk], wv)
                        fn(eng).then_inc(sems[k], inc)
                    if e == "sync":
                        for k2 in sorted(keys):
                            tot = self.dma_last.get(k2) if k2.startswith("dma") else self.cnt.get(k2)
                            if tot:
                                eng.wait_ge(sems[k2], tot)
                return body

            for e in ENGS:
                if self.ops[e] or e == "sync":
                    getattr(block, e)(mk(e))


class TL:
    def __init__(self, ap):
        self.ap = ap
        self.b = Buf()

    def __getitem__(self, k):
        return self.ap[k]


def t5_bucket_np(rel):
    half, max_exact = 16, 8
    n = np.abs(rel)
    nf = np.maximum(n, 1).astype(np.float32)
    large = max_exact + (np.log(nf / np.float32(max_exact)) / np.float32(math.log(128 / max_exact)) * (half - max_exact)).astype(np.int32)
    large = np.minimum(large, half - 1)
    return np.where(rel > 0, half, 0) + np.where(n < max_exact, n, large)


def make_consts():
    c = {}
    c["ident"] = np.eye(128, dtype=np.float32)
    s = np.arange(128)
    c["tri_fw"] = (s[:, None] <= s[None, :]).astype(np.float32)
    c["tri_bw"] = (s[:, None] >= s[None, :]).astype(np.float32)
    bo = np.zeros((128, 128), np.float32)
    bo[:64, :64] = 1
    bo[64:, 64:] = 1
    c["blk64"] = bo
    k = np.arange(128)[:, None]
    q = np.arange(128)[None, :]
    gm = np.zeros((3, 32, 128, 128), np.float32)
    for o in range(3):
        rel = k + 128 * (o - 1) - q
        bk = t5_bucket_np(rel)
        ok = np.abs(rel) <= 128
        for b in range(32):
            gm[o, b] = ((bk == b) & ok)
    c["gqa_m"] = gm
    kc = np.arange(64)[:, None]
    qc = np.arange(64)[None, :]
    qs = np.clip(qc - 8, 0, 48)
    ok = (kc >= qs) & (kc < qs + 16)
    dc = np.clip(kc - qc + 15, 0, 30)
    nm = np.zeros((31, 128, 64), np.float32)
    for d in range(31):
        m = ((dc == d) & ok).astype(np.float32)
        nm[d, :64] = m
        nm[d, 64:] = m
    c["na_m"] = nm
    return c


def build(seqs, depth, dbg=()):
    nc = bass.Bass("TRN2", target_bir_lowering=False)
    kb = KB(nc)
    st = contextlib.ExitStack()

    def din(name, shape, dt=F32):
        return TL(nc.dram_tensor(name, list(shape), dt, kind="ExternalInput").ap())

    def dscr(name, shape, dt=BF16, kind="Internal"):
        return TL(nc.dram_tensor(name, list(shape), dt, kind=kind).ap())

    AW = 53200
    arena = st.enter_context(nc.sbuf_tensor("arena", [128, AW], F32))
    aoff = [0]

    def sb(name, shape, dt=F32):
        n = 1
        for d_ in shape[1:]:
            n *= d_
        words = (n * (2 if dt == BF16 else 4) + 3) // 4
        assert aoff[0] + words <= AW, (name, aoff[0], words)
        v = arena[0:shape[0], aoff[0]:aoff[0] + words]
        aoff[0] += words
        if dt != F32:
            v = v.bitcast(dt)
        if len(shape) == 3:
            v = v.rearrange("p (a b) -> p a b", a=shape[1])
        elif len(shape) == 4:
            v = v.rearrange("p (a b c) -> p a b c", a=shape[1], b=shape[2])
        return TL(v)

    def ps(name, shape, dt=F32):
        return TL(st.enter_context(nc.psum_tensor(name, list(shape), dt))[:])

    X = {n: din(f"x_{n}", [T, D]) for n, T in seqs}
    Cc = {n: din(f"c_{n}", [1, D]) for n, T in seqs}
    Yout = {n: dscr(f"y_{n}", [T, D], F32, kind="ExternalOutput") for n, T in seqs}
    rel_bias = din("rel_bias", [32, 8])
    norm_w = din("norm_w", [depth, D])
    w_ada = din("w_ada", [depth, D, 3 * D])
    b_ada = din("b_ada", [depth, 3 * D])
    w_in = din("w_in", [depth, D, IN_COLS])
    b_gate = din("b_gate", [depth, 16])
    mlstm_norm_w = din("mlstm_norm_w", [depth, 512])
    na_q_norm = din("na_q_norm", [depth, 128])
    na_k_norm = din("na_k_norm", [depth, 128])
    na_rpb = din("na_rpb", [depth, 4, 15, 31])
    swa_q_norm = din("swa_q_norm", [depth, 64])
    swa_k_norm = din("swa_k_norm", [depth, 64])
    swa_sink = din("swa_sink", [depth, 8])
    conv_w = din("conv_w", [depth, 31, 512])
    conv_b = din("conv_b", [depth, 512])
    conv_ln_w = din("conv_ln_w", [depth, 512])
    conv_ln_b = din("conv_ln_b", [depth, 512])
    w_branch = din("w_branch", [depth, 4, 512, D])
    w_out = din("w_out", [depth, D, D])
    k_ident = din("k_ident", [128, 128])
    k_tri_fw = din("k_tri_fw", [128, 128])
    k_tri_bw = din("k_tri_bw", [128, 128])
    k_blk64 = din("k_blk64", [128, 128])
    k_gqa_m = din("k_gqa_m", [3, 32, 128, 128])
    k_na_m = din("k_na_m", [31, 128, 64])

    FM = {n: dscr(f"fm_{n}", [NFM, T]) for n, T in seqs}
    TMv = {n: dscr(f"tm_{n}", [T, 1664]) for n, T in seqs}
    TMg = {n: dscr(f"tg_{n}", [T, 16], F32) for n, T in seqs}
    HT = {n: dscr(f"ht_{n}", [D, T]) for n, T in seqs}
    YM = {n: dscr(f"ym_{n}", [D, T], BF16, kind=("ExternalOutput" if DEBUG_YM else "Internal")) for n, T in seqs}
    X1 = {n: dscr(f"x1_{n}", [T, D], F32) for n, T in seqs}
    modrow = dscr("modrow", [depth * len(seqs), D], F32)
    WT = {}
    WSRC = {}
    DBG = {}

    ident_f = sb("ident_f", [128, 128])
    ident_b = sb("ident_b", [128, 128], BF16)
    ones_b = sb("ones_b", [128, 128], BF16)
    ones_f = sb("ones_f", [128, 128])
    blk64_b = sb("blk64_b", [128, 128], BF16)
    tri_f = {"fw": sb("tri_fw", [128, 128]), "bw": sb("tri_bw", [128, 128])}
    kb.op("sync", lambda e: e.dma_start(out=ident_f.ap, in_=k_ident.ap), writes=[ident_f.b], dma=True)
    kb.op("vector", lambda e: e.tensor_copy(out=ident_b.ap, in_=ident_f.ap), reads=[ident_f.b], writes=[ident_b.b])
    kb.op("vector", lambda e: e.memset(ones_b.ap, 1.0), writes=[ones_b.b])
    kb.op("vector", lambda e: e.memset(ones_f.ap, 1.0), writes=[ones_f.b])
    kb.op("sync", lambda e: e.dma_start(out=tri_f["fw"].ap, in_=k_tri_fw.ap), writes=[tri_f["fw"].b], dma=True)
    kb.op("sync", lambda e: e.dma_start(out=tri_f["bw"].ap, in_=k_tri_bw.ap), writes=[tri_f["bw"].b], dma=True)
    eps_c = sb("eps_c", [128, 1])
    kb.op("vector", lambda e: e.memset(eps_c.ap, EPS), writes=[eps_c.b])
    tmpc = sb("tmpc", [128, 128])
    kb.op("sync", lambda e: e.dma_start(out=tmpc.ap, in_=k_blk64.ap), writes=[tmpc.b], dma=True)
    kb.op("vector", lambda e: e.tensor_copy(out=blk64_b.ap, in_=tmpc.ap), reads=[tmpc.b], writes=[blk64_b.b])

    PS = [ps(f"ps{i}", [128, 512]) for i in range(6)]
    PSB = [ps(f"psb{i}", [128, 1024], BF16) for i in range(2)]
    psrr = [0]

    def nps():
        psrr[0] += 1
        return PS[psrr[0] % 6]

    psbr = [0]

    def npsb():
        psbr[0] += 1
        return PSB[psbr[0] % 2]

    class Pool:
        def __init__(self, name, shape, dt, n):
            self.t = [sb(f"{name}{i}", shape, dt) for i in range(n)]
            self.i = 0

        def get(self):
            self.i += 1
            return self.t[self.i % len(self.t)]

    evac_rr = [0]

    def evac_eng():
        evac_rr[0] += 1
        return "vector" if evac_rr[0] % 2 else "scalar"

    def copy_op(eng, out, in_, reads, writes):
        if eng == "scalar":
            kb.op("scalar", lambda e: e.activation(out=out, in_=in_, func=AF.Copy), reads=reads, writes=writes)
        else:
            kb.op(eng, lambda e: e.tensor_copy(out=out, in_=in_), reads=reads, writes=writes)

    nseq = len(seqs)
    modA = sb("modA", [128, depth, nseq, KC])
    modB = sb("modB", [128, depth, nseq, KC])
    cs = sb("cs", [128, KC, nseq])
    modfm = sb("modfm", [128, 48, nseq])
    badafm = sb("badafm", [128, 48])
    nwfm = sb("nwfm", [128, KC])
    bg_bc = sb("bg_bc", [128, 16])
    EB = sb("EB", [128, 3, 8, 128])
    PERSIST = aoff[0]

    def adaln():
        w32 = Pool("w32", [128, KC, 128], F32, 3)
        for si, (n, T) in enumerate(seqs):
            kb.op("sync", lambda e, si=si, n=n: e.dma_start(out=cs.ap[:, :, si], in_=Cc[n].ap.rearrange("o (k p) -> p (o k)", p=128), allow_slow_non_contiguous=True),
                  writes=[cs.b], dma=True)
        kb.op("scalar", lambda e: e.activation(out=cs.ap, in_=cs.ap, func=AF.Silu), reads=[cs.b], writes=[cs.b])
        for l in range(depth):
            kb.op("sync", lambda e, l=l: e.dma_start(out=badafm.ap, in_=b_ada.ap[l:l + 1, :].rearrange("o (k p) -> p (o k)", p=128), allow_slow_non_contiguous=True),
                  writes=[badafm.b], dma=True)
            kb.op("sync", lambda e, l=l: e.dma_start(out=nwfm.ap, in_=norm_w.ap[l:l + 1, :].rearrange("o (k p) -> p (o k)", p=128), allow_slow_non_contiguous=True),
                  writes=[nwfm.b], dma=True)
            pm = nps()
            for f in range(48):
                wt = w32.get()
                kb.op("sync", lambda e, l=l, f=f, wt=wt: e.dma_start(out=wt.ap, in_=w_ada.ap[l, :, f * 128:(f + 1) * 128].rearrange("(k p) n -> p k n", p=128)),
                      writes=[wt.b], dma=True)
                for k in range(KC):
                    kb.op("tensor", lambda e, k=k, f=f, pm=pm, wt=wt: e.matmul(pm.ap[:, f * nseq:(f + 1) * nseq], lhsT=wt.ap[:, k, :],
                                                                          rhs=cs.ap[:, k, :], start=(k == 0), stop=(k == KC - 1)),
                          reads=[wt.b, cs.b], writes=[pm.b])
            for si in range(nseq):
                kb.op("vector", lambda e, pm=pm, si=si: e.tensor_tensor(out=modfm.ap[:, :, si], in0=pm.ap[:, 0:48 * nseq].rearrange("p (f s) -> p f s", s=nseq)[:, :, si],
                                                                in1=badafm.ap, op=ALU.add),
                      reads=[pm.b, badafm.b], writes=[modfm.b])
            for si, (n, T) in enumerate(seqs):
                kb.op("vector", lambda e, l=l, si=si: e.scalar_tensor_tensor(out=modA.ap[:, l, si, :], in0=modfm.ap[:, 16:32, si], scalar=1.0, in1=nwfm.ap,
                                                                           op0=ALU.add, op1=ALU.mult),
                      reads=[modfm.b, nwfm.b], writes=[modA.b])
                kb.op("vector", lambda e, l=l, si=si: e.tensor_copy(out=modB.ap[:, l, si, :], in_=modfm.ap[:, 0:16, si]), reads=[modfm.b], writes=[modB.b])
                kb.op("sync", lambda e, l=l, si=si: e.dma_start(out=modrow.ap[l * nseq + si:l * nseq + si + 1, :].rearrange("o (k p) -> p (o k)", p=128),
                                                               in_=modfm.ap[:, 32:48, si], allow_slow_non_contiguous=True),
                      reads=[modfm.b], writes=[modrow.b], dma=True)
        kb.barrier()
        aoff[0] = PERSIST

    FM_RANGES = [(0, 1024), (1536, 2560), (2576, 3600), (4112, 5264), (5392, 7440)]
    TM_RANGES = [(512, 1536, 0), (3600, 4112, 1024), (5264, 5392, 1536)]

    def fm_func(col):
        if C_AO <= col < C_AZ:
            return AF.Sigmoid
        if C_AZ <= col < C_AG or C_BZ <= col < C_CQ or C_CZ <= col < C_DA or C_DZ <= col < C_MG:
            return AF.Silu
        return None

    def phase1(l, si, n, T, xsrc):
        xt_pool = Pool("xt", [128, D], F32, 2)
        xn_t = sb("xn", [128, 4, D], BF16)
        junk = sb("junk", [128, D], BF16)
        ssq = sb("ssq", [128, 8])
        hT = sb("hT", [128, KC, 1024], BF16)
        ofm = Pool("ofm", [128, 1024], BF16, 3)
        otm = Pool("otm", [128, 512], BF16, 3)
        otg = Pool("otg", [128, 16], F32, 2)
        wtile = Pool("wtile", [128, KC, 512], BF16, 4)

        wjobs = []
        for g_ in range(T // 1024):
            for (c0_, c1_) in FM_RANGES:
                for w0_ in range(c0_, c1_, 512):
                    wjobs.append((w0_, min(512, c1_ - w0_)))
            for (c0_, c1_, _d) in TM_RANGES:
                for w0_ in range(c0_, c1_, 512):
                    wjobs.append((w0_, min(512, c1_ - w0_)))
            wjobs.append((C_AG, 16))

        def mk_loader(c0, ncols):
            def ld():
                wt = wtile.get()
                wload(wt.ap, wt.b, l, "in", 0, c0, ncols)
                return wt
            return ld
        pf = Prefetch([mk_loader(c0, nco) for (c0, nco) in wjobs], ahead=2)
        wji = [0]

        def load_w(c0, ncols):
            i = wji[0]
            assert wjobs[i] == (c0, ncols), (wjobs[i], c0, ncols)
            wji[0] += 1
            return pf.get(i)

        kb.op("sync", lambda e: e.dma_start(out=bg_bc.ap, in_=b_gate.ap[l:l + 1, :].broadcast_to([128, 16])), writes=[bg_bc.b], dma=True)
        def do_group1(g):
            for half in range(2):
                for j in range(4):
                    t0 = g * 1024 + half * 512 + j * 128
                    xt = xt_pool.get()
                    kb.op("sync", lambda e, xt=xt, t0=t0: e.dma_start(out=xt.ap, in_=xsrc.ap[t0:t0 + 128, :]), reads=[xsrc.b], writes=[xt.b], dma=True)
                    kb.op("scalar", lambda e, xt=xt, j=j: e.activation(out=junk.ap, in_=xt.ap, func=AF.Square, accum_out=ssq.ap[:, j:j + 1]),
                          reads=[xt.b], writes=[junk.b, ssq.b])
                    kb.op("vector", lambda e, j=j: e.tensor_scalar(out=ssq.ap[:, 4 + j:5 + j], in0=ssq.ap[:, j:j + 1], scalar1=1.0 / D, scalar2=EPS,
                                                                    op0=ALU.mult, op1=ALU.add), reads=[ssq.b], writes=[ssq.b])
                    kb.op("scalar", lambda e, j=j: e.activation(out=ssq.ap[:, 4 + j:5 + j], in_=ssq.ap[:, 4 + j:5 + j], func=AF.Sqrt), reads=[ssq.b], writes=[ssq.b])
                    kb.op("vector", lambda e, j=j: e.reciprocal(out=ssq.ap[:, 4 + j:5 + j], in_=ssq.ap[:, 4 + j:5 + j]), reads=[ssq.b], writes=[ssq.b])
                    kb.op("vector", lambda e, xt=xt, j=j: e.tensor_scalar(out=xn_t.ap[:, j, :], in0=xt.ap, scalar1=ssq.ap[:, 4 + j:5 + j], scalar2=None,
                                                                           op0=ALU.mult), reads=[xt.b, ssq.b], writes=[xn_t.b])
                for k in range(KC):
                    pb = npsb()
                    for j in range(4):
                        kb.op("tensor", lambda e, pb=pb, j=j, k=k: e.transpose(pb.ap[:, j * 128:(j + 1) * 128], xn_t.ap[:, j, k * 128:(k + 1) * 128], ident_b.ap),
                              reads=[xn_t.b, ident_b.b], writes=[pb.b])
                    dst = hT.ap[:, k, half * 512:(half + 1) * 512]
                    if k % 2 == 0:
                        kb.op("scalar", lambda e, pb=pb, k=k, dst=dst: e.activation(out=dst, in_=pb.ap[:, 0:512], func=AF.Identity,
                                                                                     bias=modB.ap[:, l, si, k:k + 1], scale=modA.ap[:, l, si, k:k + 1]),
                              reads=[pb.b, modA.b, modB.b], writes=[hT.b])
                    else:
                        kb.op("vector", lambda e, pb=pb, k=k, dst=dst: e.tensor_scalar(out=dst, in0=pb.ap[:, 0:512], scalar1=modA.ap[:, l, si, k:k + 1],
                                                                                        scalar2=modB.ap[:, l, si, k:k + 1], op0=ALU.mult, op1=ALU.add),
                              reads=[pb.b, modA.b, modB.b], writes=[hT.b])
            kb.op("sync", lambda e, g=g: e.dma_start(out=HT[n].ap[:, g * 1024:(g + 1) * 1024].rearrange("(k p) t -> p k t", p=128), in_=hT.ap),
                  reads=[hT.b], writes=[HT[n].b], dma=True)
            for (c0, c1) in FM_RANGES:
                for w0 in range(c0, c1, 512):
                    ncols = min(512, c1 - w0)
                    wt = load_w(w0, ncols)
                    for mc in range(ncols // 128):
                        col = w0 + mc * 128
                        o = ofm.get()
                        fn = fm_func(col)
                        for tt in range(2):
                            p = nps()
                            for k in range(KC):
                                kb.op("tensor", lambda e, p=p, wt=wt, mc=mc, k=k, tt=tt: e.matmul(p.ap, lhsT=wt.ap[:, k, mc * 128:(mc + 1) * 128],
                                                                                            rhs=hT.ap[:, k, tt * 512:(tt + 1) * 512],
                                                                                            start=(k == 0), stop=(k == KC - 1)),
                                      reads=[wt.b, hT.b], writes=[p.b])
                            dst = o.ap[:, tt * 512:(tt + 1) * 512]
                            if fn is None:
                                copy_op(evac_eng(), dst, p.ap, [p.b], [o.b])
                            else:
                                kb.op("scalar", lambda e, p=p, dst=dst, fn=fn: e.activation(out=dst, in_=p.ap, func=fn), reads=[p.b], writes=[o.b])
                        kb.op("sync", lambda e, o=o, col=col, g=g: e.dma_start(out=FM[n].ap[col:col + 128, g * 1024:(g + 1) * 1024], in_=o.ap),
                              reads=[o.b], writes=[FM[n].b], dma=True)
            for (c0, c1, dcol) in TM_RANGES:
                for w0 in range(c0, c1, 512):
                    ncols = min(512, c1 - w0)
                    wt = load_w(w0, ncols)
                    for sub in range(8):
                        p = nps()
                        for k in range(KC):
                            kb.op("tensor", lambda e, p=p, wt=wt, k=k, sub=sub, ncols=ncols: e.matmul(p.ap[:, 0:ncols], lhsT=hT.ap[:, k, sub * 128:(sub + 1) * 128],
                                                                                                 rhs=wt.ap[:, k, 0:ncols], start=(k == 0), stop=(k == KC - 1)),
                                  reads=[wt.b, hT.b], writes=[p.b])
                        o = otm.get()
                        copy_op(evac_eng(), o.ap[:, 0:ncols], p.ap[:, 0:ncols], [p.b], [o.b])
                        t0 = g * 1024 + sub * 128
                        dc = dcol + (w0 - c0)
                        kb.op("sync", lambda e, o=o, t0=t0, dc=dc, ncols=ncols: e.dma_start(out=TMv[n].ap[t0:t0 + 128, dc:dc + ncols], in_=o.ap[:, 0:ncols]),
                              reads=[o.b], writes=[TMv[n].b], dma=True)
            wt = load_w(C_AG, 16)
            for sub in range(8):
                p = nps()
                for k in range(KC):
                    kb.op("tensor", lambda e, p=p, wt=wt, k=k, sub=sub: e.matmul(p.ap[:, 0:16], lhsT=hT.ap[:, k, sub * 128:(sub + 1) * 128],
                                                                            rhs=wt.ap[:, k, 0:16], start=(k == 0), stop=(k == KC - 1)),
                          reads=[wt.b, hT.b], writes=[p.b])
                o = otg.get()
                kb.op("vector", lambda e, o=o, p=p: e.tensor_tensor(out=o.ap, in0=p.ap[:, 0:16], in1=bg_bc.ap, op=ALU.add), reads=[p.b, bg_bc.b], writes=[o.b])
                t0 = g * 1024 + sub * 128
                kb.op("sync", lambda e, o=o, t0=t0: e.dma_start(out=TMg[n].ap[t0:t0 + 128, :], in_=o.ap), reads=[o.b], writes=[TMg[n].b], dma=True)
        for g in range(T // 1024):
            do_group1(g)
        kb.barrier()
        aoff[0] = PERSIST

    def phase3(l, si, n, T, xsrc, xdst):
        hT = sb("hT3", [128, KC, 1024], BF16)
        yT_t = sb("yT", [128, KC, 1024], BF16)
        mT_t = sb("mT", [128, KC, 1024], BF16)
        wm_pool = Pool("wm", [128, KC, 256], BF16, 4)
        wbr_pool = Pool("wbr", [128, 4, 256], BF16, 4)

        def mk_merge(mg, i):
            def ld():
                wt = wm_pool.get()
                wload(wt.ap, wt.b, l, "in", 0, C_MG + i * 2048 + mg * 256, 256)
                wb_ = wbr_pool.get()
                wload(wb_.ap, wb_.b, l, "br", i, mg * 256, 256)
                return (wt, wb_)
            return ld

        def mk_out(nn):
            def ld():
                wt = wm_pool.get()
                wload(wt.ap, wt.b, l, "out", 0, nn * 256, 256)
                return wt
            return ld
        loaders3 = []
        for g_ in range(T // 1024):
            for mg_ in range(8):
                for i_ in range(4):
                    loaders3.append(mk_merge(mg_, i_))
            for nn_ in range(8):
                loaders3.append(mk_out(nn_))
        pf3 = Prefetch(loaders3, ahead=2)
        pfi = [0]
        ytmp = Pool("ytmp", [128, 1024], BF16, 3)
        uld = [sb(f"uld{i}", [128, 1024], BF16) for i in range(4)]
        sg_pool = Pool("sg", [128, 512], F32, 2)
        acc_pool = Pool("acc", [128, 512], F32, 8)
        tmp_pool = Pool("tmp3", [128, 512], F32, 2)
        xo_pool = Pool("xo", [128, 512], F32, 2)
        usq = Pool("usq", [128, 512], BF16, 2)
        stat = Pool("stat", [128, 512], F32, 3)
        gsl = sb("gsl", [128, 512])
        lnw = sb("lnw", [128, 4])
        lnb = sb("lnb", [128, 4])
        kb.op("sync", lambda e: e.dma_start(out=lnw.ap, in_=conv_ln_w.ap[l:l + 1, :].rearrange("o (k p) -> p (o k)", p=128), allow_slow_non_contiguous=True), writes=[lnw.b], dma=True)
        kb.op("sync", lambda e: e.dma_start(out=lnb.ap, in_=conv_ln_b.ap[l:l + 1, :].rearrange("o (k p) -> p (o k)", p=128), allow_slow_non_contiguous=True), writes=[lnb.b], dma=True)
        def do_group(g):
            tsl = slice(g * 1024, (g + 1) * 1024)
            kb.op("sync", lambda e: e.dma_start(out=hT.ap, in_=HT[n].ap[:, tsl].rearrange("(k p) t -> p k t", p=128)), reads=[HT[n].b], writes=[hT.b], dma=True)
            for br in range(3):
                zc = (C_AZ, C_BZ, C_CZ)[br]
                for c4 in range(4):
                    a = ytmp.get()
                    kb.op("sync", lambda e, a=a, br=br, c4=c4: e.dma_start(out=a.ap, in_=YM[n].ap[br * 512 + c4 * 128: br * 512 + c4 * 128 + 128, tsl]),
                          reads=[YM[n].b], writes=[a.b], dma=True)
                    z = ytmp.get()
                    kb.op("sync", lambda e, z=z, zc=zc, c4=c4: e.dma_start(out=z.ap, in_=FM[n].ap[zc + c4 * 128: zc + c4 * 128 + 128, tsl]),
                          reads=[FM[n].b], writes=[z.b], dma=True)
                    dst = yT_t.ap[:, br * 4 + c4, :]
                    if br == 0:
                        s_ = ytmp.get()
                        kb.op("sync", lambda e, s_=s_, c4=c4: e.dma_start(out=s_.ap, in_=FM[n].ap[C_AO + c4 * 128: C_AO + c4 * 128 + 128, tsl]),
                              reads=[FM[n].b], writes=[s_.b], dma=True)
                        kb.op("gpsimd", lambda e, z=z, s_=s_: e.tensor_tensor(out=z.ap, in0=z.ap, in1=s_.ap, op=ALU.mult), reads=[z.b, s_.b], writes=[z.b])
                    kb.op("vector", lambda e, a=a, z=z, dst=dst: e.tensor_tensor(out=dst, in0=a.ap, in1=z.ap, op=ALU.mult), reads=[a.b, z.b], writes=[yT_t.b])
            ul = uld
            for c4 in range(4):
                kb.op("sync", lambda e, c4=c4: e.dma_start(out=ul[c4].ap, in_=YM[n].ap[1536 + c4 * 128: 1536 + c4 * 128 + 128, tsl]),
                      reads=[YM[n].b], writes=[ul[c4].b], dma=True)
            for tt in range(2):
                cs_ = slice(tt * 512, (tt + 1) * 512)
                p1 = nps()
                p2 = nps()
                for c4 in range(4):
                    q2 = usq.get()
                    kb.op("gpsimd", lambda e, q2=q2, c4=c4, cs_=cs_: e.tensor_tensor(out=q2.ap, in0=ul[c4].ap[:, cs_], in1=ul[c4].ap[:, cs_], op=ALU.mult),
                          reads=[ul[c4].b], writes=[q2.b])
                    kb.op("tensor", lambda e, p1=p1, c4=c4, cs_=cs_: e.matmul(p1.ap, lhsT=ones_b.ap, rhs=ul[c4].ap[:, cs_], start=(c4 == 0), stop=(c4 == 3)),
                          reads=[ones_b.b, ul[c4].b], writes=[p1.b])
                    kb.op("tensor", lambda e, p2=p2, q2=q2, c4=c4: e.matmul(p2.ap, lhsT=ones_b.ap, rhs=q2.ap, start=(c4 == 0), stop=(c4 == 3)),
                          reads=[ones_b.b, q2.b], writes=[p2.b])
                mean = stat.get()
                rstd = stat.get()
                m2 = stat.get()
                kb.op("scalar", lambda e, mean=mean, p1=p1: e.activation(out=mean.ap, in_=p1.ap, func=AF.Copy, scale=1.0 / 512), reads=[p1.b], writes=[mean.b])
                kb.op("vector", lambda e, mean=mean, m2=m2: e.tensor_tensor(out=m2.ap, in0=mean.ap, in1=mean.ap, op=ALU.mult), reads=[mean.b], writes=[m2.b])
                kb.op("vector", lambda e, rstd=rstd, p2=p2, m2=m2: e.scalar_tensor_tensor(out=rstd.ap, in0=p2.ap, scalar=1.0 / 512, in1=m2.ap, op0=ALU.mult, op1=ALU.subtract),
                      reads=[p2.b, m2.b], writes=[rstd.b])
                kb.op("scalar", lambda e, rstd=rstd: e.activation(out=rstd.ap, in_=rstd.ap, func=AF.Sqrt, bias=eps_c.ap[:, 0:1]), reads=[rstd.b, eps_c.b], writes=[rstd.b])
                kb.op("vector", lambda e, rstd=rstd: e.reciprocal(out=rstd.ap, in_=rstd.ap), reads=[rstd.b], writes=[rstd.b])
                for c4 in range(4):
                    t1 = tmp_pool.get()
                    kb.op("vector", lambda e, t1=t1, c4=c4, mean=mean, cs_=cs_: e.tensor_tensor(out=t1.ap, in0=ul[c4].ap[:, cs_], in1=mean.ap, op=ALU.subtract),
                          reads=[ul[c4].b, mean.b], writes=[t1.b])
                    kb.op("gpsimd", lambda e, t1=t1, rstd=rstd: e.tensor_tensor(out=t1.ap, in0=t1.ap, in1=rstd.ap, op=ALU.mult), reads=[t1.b, rstd.b], writes=[t1.b])
                    kb.op("scalar", lambda e, t1=t1, c4=c4: e.activation(out=t1.ap, in_=t1.ap, func=AF.Silu, bias=lnb.ap[:, c4:c4 + 1], scale=lnw.ap[:, c4:c4 + 1]),
                          reads=[t1.b, lnw.b, lnb.b], writes=[t1.b])
                    z = usq.get()
                    kb.op("sync", lambda e, z=z, c4=c4, tt=tt: e.dma_start(out=z.ap, in_=FM[n].ap[C_DZ + c4 * 128: C_DZ + c4 * 128 + 128, g * 1024 + tt * 512: g * 1024 + tt * 512 + 512]),
                          reads=[FM[n].b], writes=[z.b], dma=True)
                    kb.op("vector", lambda e, t1=t1, z=z, c4=c4, cs_=cs_: e.tensor_tensor(out=yT_t.ap[:, 12 + c4, cs_], in0=t1.ap, in1=z.ap, op=ALU.mult),
                          reads=[t1.b, z.b], writes=[yT_t.b])
            for mg in range(8):
                accs = [acc_pool.get() for _ in range(4)]
                for i in range(4):
                    wt, wb_ = pf3.get(pfi[0])
                    pfi[0] += 1
                    for mc in range(2):
                        m = mg * 2 + mc
                        for tt in range(2):
                            cs_ = slice(tt * 512, (tt + 1) * 512)
                            acc = accs[mc * 2 + tt]
                            pa = nps()
                            for k in range(KC):
                                kb.op("tensor", lambda e, pa=pa, k=k, mc=mc, cs_=cs_, wt=wt: e.matmul(pa.ap, lhsT=wt.ap[:, k, mc * 128:(mc + 1) * 128], rhs=hT.ap[:, k, cs_],
                                                                                                 start=(k == 0), stop=(k == KC - 1)),
                                      reads=[wt.b, hT.b], writes=[pa.b])
                            pb_ = nps()
                            for k in range(4):
                                kb.op("tensor", lambda e, pb_=pb_, i=i, k=k, mc=mc, cs_=cs_, wb_=wb_: e.matmul(pb_.ap, lhsT=wb_.ap[:, k, mc * 128:(mc + 1) * 128], rhs=yT_t.ap[:, i * 4 + k, cs_],
                                                                                                          start=(k == 0), stop=(k == 3)),
                                      reads=[wb_.b, yT_t.b], writes=[pb_.b])
                            sg = sg_pool.get()
                            kb.op("scalar", lambda e, sg=sg, pa=pa: e.activation(out=sg.ap, in_=pa.ap, func=AF.Sigmoid), reads=[pa.b], writes=[sg.b])
                            if i == 0:
                                kb.op("vector", lambda e, acc=acc, sg=sg, pb_=pb_: e.tensor_tensor(out=acc.ap, in0=pb_.ap, in1=sg.ap, op=ALU.mult),
                                      reads=[pb_.b, sg.b], writes=[acc.b])
                            else:
                                kb.op("vector", lambda e, sg=sg, pb_=pb_: e.tensor_tensor(out=sg.ap, in0=pb_.ap, in1=sg.ap, op=ALU.mult),
                                      reads=[pb_.b, sg.b], writes=[sg.b])
                                if i < 3:
                                    kb.op("gpsimd", lambda e, acc=acc, sg=sg: e.tensor_tensor(out=acc.ap, in0=acc.ap, in1=sg.ap, op=ALU.add),
                                          reads=[acc.b, sg.b], writes=[acc.b])
                                else:
                                    kb.op("gpsimd", lambda e, acc=acc, sg=sg, m=m, cs_=cs_: e.tensor_tensor(out=mT_t.ap[:, m, cs_], in0=acc.ap, in1=sg.ap, op=ALU.add),
                                          reads=[acc.b, sg.b], writes=[mT_t.b])
            for nn in range(8):
                wt = pf3.get(pfi[0])
                pfi[0] += 1
                kb.op("sync", lambda e, nn=nn: e.dma_start(out=gsl.ap[:, 0:256], in_=modrow.ap[l * nseq + si:l * nseq + si + 1, nn * 256:(nn + 1) * 256].broadcast_to([128, 256])),
                      reads=[modrow.b], writes=[gsl.b], dma=True)
                for sub in range(8):
                    t0 = g * 1024 + sub * 128
                    p = nps()
                    for k in range(KC):
                        kb.op("tensor", lambda e, p=p, wt=wt, k=k, sub=sub: e.matmul(p.ap[:, 0:256], lhsT=mT_t.ap[:, k, sub * 128:(sub + 1) * 128], rhs=wt.ap[:, k, :],
                                                                               start=(k == 0), stop=(k == KC - 1)),
                              reads=[wt.b, mT_t.b], writes=[p.b])
                    xo = xo_pool.get()
                    kb.op("sync", lambda e, xo=xo, t0=t0, nn=nn: e.dma_start(out=xo.ap[:, 0:256], in_=xsrc.ap[t0:t0 + 128, nn * 256:(nn + 1) * 256]), reads=[xsrc.b], writes=[xo.b], dma=True)
                    t1 = tmp_pool.get()
                    kb.op("vector", lambda e, t1=t1, p=p: e.tensor_tensor(out=t1.ap[:, 0:256], in0=p.ap[:, 0:256], in1=gsl.ap[:, 0:256], op=ALU.mult),
                          reads=[p.b, gsl.b], writes=[t1.b])
                    kb.op("gpsimd", lambda e, t1=t1, xo=xo: e.tensor_tensor(out=xo.ap[:, 0:256], in0=xo.ap[:, 0:256], in1=t1.ap[:, 0:256], op=ALU.add), reads=[xo.b, t1.b], writes=[xo.b])
                    kb.op("sync", lambda e, xo=xo, t0=t0, nn=nn: e.dma_start(out=xdst.ap[t0:t0 + 128, nn * 256:(nn + 1) * 256], in_=xo.ap[:, 0:256]), reads=[xo.b], writes=[xdst.b], dma=True)
        for g in range(T // 1024):
            do_group(g)
        kb.barrier()
        aoff[0] = PERSIST

    def wt_get(l, kind, idx, c0, ncols):
        key = (l, kind, idx, c0, ncols)
        if key in WT:
            return WT[key]
        nk = 4 if kind == "br" else KC
        t = dscr(f"wt_{l}_{kind}_{idx}_{c0}_{ncols}", [128, nk * ncols])
        if kind == "in":
            src = w_in.ap[l, :, c0:c0 + ncols]
        elif kind == "br":
            src = w_branch.ap[l, idx, :, c0:c0 + ncols]
        else:
            src = w_out.ap[l, :, c0:c0 + ncols]
        WSRC[key] = src
        WT[key] = (t, nk)
        return WT[key]

    P1_TILES = []
    for (c0_, c1_) in [(0, 1024), (1536, 2560), (2576, 3600), (4112, 5264), (5392, 7440)]:
        for w0_ in range(c0_, c1_, 512):
            P1_TILES.append((w0_, min(512, c1_ - w0_)))
    for (c0_, c1_) in [(512, 1536), (3600, 4112), (5264, 5392)]:
        for w0_ in range(c0_, c1_, 512):
            P1_TILES.append((w0_, min(512, c1_ - w0_)))
    P1_TILES.append((C_AG, 16))

    def convert_tiles(keys):
        st32 = Pool("cv32", [128, KC, 512], F32, 2)
        st16 = Pool("cv16", [128, KC, 512], BF16, 2)
        for ci, key in enumerate(keys):
            (l, kind, idx, c0, ncols) = key
            t, nk = wt_get(l, kind, idx, c0, ncols)
            src = WSRC[key]
            a = st32.get()
            b = st16.get()
            kb.op("sync", lambda e, a=a, src=src, nk=nk, ncols=ncols: e.dma_start(out=a.ap[:, 0:nk, 0:ncols], in_=src.rearrange("(k p) n -> p k n", p=128)),
                  writes=[a.b], dma=True)
            eng = "gpsimd" if ci % 3 != 2 else "vector"
            kb.op(eng, lambda e, a=a, b=b, nk=nk, ncols=ncols: e.tensor_copy(out=b.ap[:, 0:nk, 0:ncols], in_=a.ap[:, 0:nk, 0:ncols]), reads=[a.b], writes=[b.b])
            kb.op("scalar", lambda e, b=b, t=t, nk=nk, ncols=ncols: e.dma_start(out=t.ap.rearrange("p (k n) -> p k n", k=nk), in_=b.ap[:, 0:nk, 0:ncols]),
                  reads=[b.b], writes=[t.b], dma=True)
        kb.barrier()
        aoff[0] = PERSIST

    def p1_keys(l):
        return [(l, "in", 0, c0, nco) for (c0, nco) in P1_TILES]

    def p3_keys(l):
        ks = []
        for mg in range(8):
            for i in range(4):
                ks.append((l, "in", 0, C_MG + i * 2048 + mg * 256, 256))
                ks.append((l, "br", i, mg * 256, 256))
        for nn in range(8):
            ks.append((l, "out", 0, nn * 256, 256))
        return ks

    WQ = "scalar"

    def wload(dst, dst_b, l, kind, idx, c0, ncols):
        t, nk = wt_get(l, kind, idx, c0, ncols)
        kb.op(WQ, lambda e: e.dma_start(out=dst[:, 0:nk, 0:ncols], in_=t.ap.rearrange("p (k n) -> p k n", k=nk)),
              reads=[t.b], writes=[dst_b], dma=True)

    class Prefetch:
        def __init__(self, loaders, ahead=2):
            self.loaders, self.ahead, self.tiles, self.issued = loaders, ahead, {}, 0

        def get(self, i):
            while self.issued <= min(i + self.ahead, len(self.loaders) - 1):
                self.tiles[self.issued] = self.loaders[self.issued]()
                self.issued += 1
            return self.tiles.pop(i)

    env = dict(locals())
    MIX = build_mixers(env)

    convert_tiles(p1_keys(0))
    adaln()
    MIX.gqa_tables()
    kb.barrier()
    aoff[0] = PERSIST
    for l in range(depth):
        for si, (n, T) in enumerate(seqs):
            xsrc = X[n] if l == 0 else X1[n]
            phase1(l, si, n, T, xsrc)
        convert_tiles(p3_keys(l) + (p1_keys(l + 1) if l + 1 < depth else []))
        for si, (n, T) in enumerate(seqs):
            MIX(l, si, n, T)
            kb.barrier()
            aoff[0] = PERSIST
        for si, (n, T) in enumerate(seqs):
            xsrc = X[n] if l == 0 else X1[n]
            xdst = Yout[n] if l == depth - 1 else X1[n]
            phase3(l, si, n, T, xsrc, xdst)
    kb.emit()
    st.close()
    return nc


def build_mixers(env):
    g_ = env
    kb, sb, nps, npsb, Pool, copy_op = g_["kb"], g_["sb"], g_["nps"], g_["npsb"], g_["Pool"], g_["copy_op"]
    FM, TMv, TMg, YM = g_["FM"], g_["TMv"], g_["TMg"], g_["YM"]
    ident_b, ident_f, ones_b, ones_f, blk64_b, tri_f, eps_c = (g_[k] for k in ("ident_b", "ident_f", "ones_b", "ones_f", "blk64_b", "tri_f", "eps_c"))
    consts = make_consts()

    def dma(eng, out, in_, reads, writes, slow=False):
        if slow:
            kb.op(eng, lambda e: e.dma_start(out=out, in_=in_, allow_slow_non_contiguous=True), reads=reads, writes=writes, dma=True)
        else:
            kb.op(eng, lambda e: e.dma_start(out=out, in_=in_), reads=reads, writes=writes, dma=True)

    def V(fn, reads, writes, eng="vector"):
        kb.op(eng, fn, reads=reads, writes=writes)

    def MM(out, lhsT, rhs, start, stop, reads, writes):
        kb.op("tensor", lambda e: e.matmul(out, lhsT=lhsT, rhs=rhs, start=start, stop=stop), reads=reads, writes=writes)

    def rsqrt_tile(dst, src_ps, scale, n, reads):
        V(lambda e: e.tensor_scalar(out=dst.ap[:, 0:n], in0=src_ps, scalar1=scale, scalar2=EPS, op0=ALU.mult, op1=ALU.add), reads, [dst.b])
        kb.op("scalar", lambda e: e.activation(out=dst.ap[:, 0:n], in_=dst.ap[:, 0:n], func=AF.Sqrt), reads=[dst.b], writes=[dst.b])
        V(lambda e: e.reciprocal(out=dst.ap[:, 0:n], in_=dst.ap[:, 0:n]), [dst.b], [dst.b])

    def headnorm_fm(dst, src, T, wcol, lhs_ones, scale_div, extra_scale):
        sq = Pool("hn_sq", [128, 512], BF16, 2)
        rs = Pool("hn_rs", [128, 512], F32, 2)
        for t0 in range(0, T, 512):
            q2 = sq.get()
            V(lambda e, q2=q2, t0=t0: e.tensor_tensor(out=q2.ap, in0=src.ap[:, t0:t0 + 512], in1=src.ap[:, t0:t0 + 512], op=ALU.mult), [src.b], [q2.b], eng="gpsimd")
            p = nps()
            MM(p.ap, lhs_ones.ap, q2.ap, True, True, [lhs_ones.b, q2.b], [p.b])
            r = rs.get()
            rsqrt_tile(r, p.ap, 1.0 / scale_div, 512, [p.b])
            V(lambda e, r=r, t0=t0: e.scalar_tensor_tensor(out=dst.ap[:, t0:t0 + 512], in0=src.ap[:, t0:t0 + 512], scalar=wcol, in1=r.ap, op0=ALU.mult, op1=ALU.mult),
              [src.b, r.b], [dst.b])

    def conv(l, n, T):
        conv_w, conv_b = g_["conv_w"], g_["conv_b"]
        a_t = sb("cv_a", [128, T], BF16)
        g_t = sb("cv_g", [128, T], BF16)
        up = sb("cv_u", [128, T + 32], BF16)
        dg = sb("cv_dg", [128, 31, 128], BF16)
        cw = sb("cv_w", [128, 31])
        cb = sb("cv_b", [128, 1])
        osb = Pool("cv_o", [128, 512], BF16, 3)
        for c4 in range(4):
            dma("sync", cw.ap, conv_w.ap[l, :, c4 * 128:(c4 + 1) * 128].rearrange("w p -> p w"), [], [cw.b], slow=True)
            dma("sync", cb.ap, conv_b.ap[l:l + 1, c4 * 128:(c4 + 1) * 128].rearrange("o p -> p o"), [], [cb.b], slow=True)
            for w in range(31):
                V(lambda e, w=w: e.tensor_scalar(out=dg.ap[:, w, :], in0=ident_f.ap, scalar1=cw.ap[:, w:w + 1], scalar2=None, op0=ALU.mult), [ident_f.b, cw.b], [dg.b],
                  eng=("vector" if w % 2 else "gpsimd"))
            for t0 in range(0, T, 1024):
                dma("sync", a_t.ap[:, t0:t0 + 1024], FM[n].ap[C_DA + c4 * 128:C_DA + c4 * 128 + 128, t0:t0 + 1024], [FM[n].b], [a_t.b])
                dma("sync", g_t.ap[:, t0:t0 + 1024], FM[n].ap[C_DG + c4 * 128:C_DG + c4 * 128 + 128, t0:t0 + 1024], [FM[n].b], [g_t.b])
            V(lambda e: e.memset(up.ap[:, 0:16], 0.0), [], [up.b])
            V(lambda e: e.memset(up.ap[:, T + 15:T + 32], 0.0), [], [up.b])
            kb.op("scalar", lambda e: e.activation(out=g_t.ap, in_=g_t.ap, func=AF.Sigmoid), reads=[g_t.b], writes=[g_t.b])
            V(lambda e: e.tensor_tensor(out=up.ap[:, 15:15 + T], in0=a_t.ap, in1=g_t.ap, op=ALU.mult), [a_t.b, g_t.b], [up.b])
            for t0 in range(0, T, 512):
                p = nps()
                for w in range(31):
                    MM(p.ap, dg.ap[:, w, :], up.ap[:, t0 + w:t0 + w + 512], w == 0, w == 30, [dg.b, up.b], [p.b])
                o = osb.get()
                kb.op("scalar", lambda e, o=o, p=p: e.activation(out=o.ap, in_=p.ap, func=AF.Identity, bias=cb.ap[:, 0:1]), reads=[p.b, cb.b], writes=[o.b])
                dma("sync", YM[n].ap[1536 + c4 * 128:1536 + c4 * 128 + 128, t0:t0 + 512], o.ap, [o.b], [YM[n].b])

    gq_state = {}

    def gqa_tables():
        rel_bias, k_gqa_m = g_["rel_bias"], g_["k_gqa_m"]
        EB = g_["EB"]
        rbb = sb("gq_rbb", [128, 256])
        val = sb("gq_val", [128, 3, 128])
        mk = Pool("gq_mk", [128, 128], F32, 3)
        dma("sync", rbb.ap, rel_bias.ap.rearrange("b h -> (b h)").unsqueeze(0).broadcast_to([128, 256]), [], [rbb.b])
        V(lambda e: e.memset(EB.ap, 0.0), [], [EB.b])
        V(lambda e: e.memset(val.ap, 0.0), [], [val.b])
        gm = consts["gqa_m"]
        for o in range(3):
            for b in range(32):
                if not gm[o, b].any():
                    continue
                m = mk.get()
                dma("sync", m.ap, k_gqa_m.ap[o, b], [], [m.b])
                V(lambda e, m=m, o=o: e.tensor_tensor(out=val.ap[:, o, :], in0=val.ap[:, o, :], in1=m.ap, op=ALU.add), [m.b, val.b], [val.b], eng="gpsimd")
                for h in range(8):
                    V(lambda e, m=m, o=o, b=b, h=h: e.scalar_tensor_tensor(out=EB.ap[:, o, h, :], in0=m.ap, scalar=rbb.ap[:, b * 8 + h:b * 8 + h + 1], in1=EB.ap[:, o, h, :],
                                                                         op0=ALU.mult, op1=ALU.add), [m.b, rbb.b, EB.b], [EB.b])
        kb.op("scalar", lambda e: e.activation(out=EB.ap, in_=EB.ap, func=AF.Exp), reads=[EB.b], writes=[EB.b])
        for o in range(3):
            for h in range(8):
                V(lambda e, o=o, h=h: e.tensor_tensor(out=EB.ap[:, o, h, :], in0=EB.ap[:, o, h, :], in1=val.ap[:, o, :], op=ALU.mult), [EB.b, val.b], [EB.b])

    def gqa(l, n, T):
        swa_q_norm, swa_k_norm, swa_sink = g_["swa_q_norm"], g_["swa_k_norm"], g_["swa_sink"]
        EB = g_["EB"]
        nb = T // 128
        kraw = sb("gq_kraw", [128, T], BF16)
        kn = sb("gq_kn", [128, T], BF16)
        qraw = sb("gq_qraw", [128, T], BF16)
        qn = sb("gq_qn", [128, T], BF16)
        vp = [sb(f"gq_vp{i}", [128, nb, 128], BF16) for i in range(2)]
        vraw = sb("gq_vraw", [128, nb, 64], BF16)
        on = [sb(f"gq_on{i}", [128, 128], BF16) for i in range(2)]
        wq = sb("gq_wq", [128, 1])
        wk = sb("gq_wk", [128, 1])
        sk = sb("gq_sk", [128, 4])
        Pf = Pool("gq_pf", [128, 2, 384], F32, 2)
        Pb = Pool("gq_pb", [128, 2, 384], BF16, 2)
        rc = Pool("gq_rc", [128, 128], F32, 2)
        ost = Pool("gq_ost", [128, 1024], BF16, 2)
        for i in range(2):
            V(lambda e, i=i: e.memset(on[i].ap, 0.0), [], [on[i].b])
            V(lambda e, i=i: e.memset(on[i].ap[:, 64 * i:64 * i + 64], 1.0), [], [on[i].b])
            V(lambda e, i=i: e.memset(vp[i].ap, 0.0), [], [vp[i].b], eng="gpsimd")
        for half in range(2):
            dma("sync", wq.ap[64 * half:64 * half + 64, :], swa_q_norm.ap[l:l + 1, :].rearrange("o p -> p o"), [], [wq.b], slow=True)
            dma("sync", wk.ap[64 * half:64 * half + 64, :], swa_k_norm.ap[l:l + 1, :].rearrange("o p -> p o"), [], [wk.b], slow=True)
        V(lambda e: e.tensor_scalar(out=wq.ap, in0=wq.ap, scalar1=0.125, scalar2=None, op0=ALU.mult), [wq.b], [wq.b])
        for kvh in range(2):
            for half in range(2):
                for t0 in range(0, T, 2048):
                    tw = min(2048, T - t0)
                    dma("sync", kraw.ap[64 * half:64 * half + 64, t0:t0 + tw], FM[n].ap[C_CK + 64 * kvh:C_CK + 64 * kvh + 64, t0:t0 + tw], [FM[n].b], [kraw.b])
            headnorm_fm(kn, kraw, T, wk.ap[:, 0:1], blk64_b, 64.0, 1.0)
            dma("sync", vraw.ap, TMv[n].ap[:, 1536 + 64 * kvh:1536 + 64 * kvh + 64].rearrange("(b p) c -> p b c", p=128), [TMv[n].b], [vraw.b])
            for i in range(2):
                V(lambda e, i=i: e.tensor_copy(out=vp[i].ap[:, :, 64 * i:64 * i + 64], in_=vraw.ap), [vraw.b], [vp[i].b])
            for hp in range(2):
                h0 = kvh * 4 + hp * 2
                for half in range(2):
                    dma("sync", sk.ap[64 * half:64 * half + 64, 0:1], swa_sink.ap[l:l + 1, h0 + half:h0 + half + 1].broadcast_to([64, 1]), [], [sk.b])
                kb.op("scalar", lambda e: e.activation(out=sk.ap[:, 1:2], in_=sk.ap[:, 0:1], func=AF.Exp), reads=[sk.b], writes=[sk.b])
                for t0 in range(0, T, 2048):
                    tw = min(2048, T - t0)
                    dma("sync", qraw.ap[:, t0:t0 + tw], FM[n].ap[C_CQ + 64 * h0:C_CQ + 64 * h0 + 128, t0:t0 + tw], [FM[n].b], [qraw.b])
                headnorm_fm(qn, qraw, T, wq.ap[:, 0:1], blk64_b, 64.0, 1.0)
                o_t = None
                for qb in range(nb):
                    if qb % 8 == 0:
                        o_t = ost.get()
                    os_ = [o for o in range(3) if 0 <= qb + o - 1 < nb]
                    o0, o1 = os_[0], os_[-1] + 1
                    pS = [nps(), nps()]
                    for hh in range(2):
                        for o in os_:
                            kbk = qb + o - 1
                            MM(pS[hh].ap[:, o * 128:(o + 1) * 128], kn.ap[64 * hh:64 * hh + 64, kbk * 128:(kbk + 1) * 128], qn.ap[64 * hh:64 * hh + 64, qb * 128:(qb + 1) * 128],
                               True, True, [kn.b, qn.b], [pS[hh].b])
                    pf = Pf.get()
                    pb = Pb.get()
                    for hh in range(2):
                        kb.op("scalar", lambda e, hh=hh, pf=pf, pS=pS, o0=o0, o1=o1: e.activation(out=pf.ap[:, hh, o0 * 128:o1 * 128], in_=pS[hh].ap[:, o0 * 128:o1 * 128], func=AF.Exp),
                              reads=[pS[hh].b], writes=[pf.b])
                        V(lambda e, hh=hh, pf=pf, pb=pb, o0=o0, o1=o1, h0=h0: e.tensor_tensor(out=pb.ap[:, hh, o0 * 128:o1 * 128].rearrange("p (o q) -> p o q", q=128),
                                                                                              in0=pf.ap[:, hh, o0 * 128:o1 * 128].rearrange("p (o q) -> p o q", q=128),
                                                                                              in1=EB.ap[:, o0:o1, h0 + hh, :], op=ALU.mult),
                          [pf.b, EB.b], [pb.b], eng=("vector" if hh else "gpsimd"))
                    pO = nps()
                    pD = nps()
                    cnt = 0
                    tot = 2 * len(os_)
                    for hh in range(2):
                        for o in os_:
                            kbk = qb + o - 1
                            MM(pO.ap[:, 0:128], vp[hh].ap[:, kbk, :], pb.ap[:, hh, o * 128:(o + 1) * 128], cnt == 0, cnt == tot - 1, [vp[hh].b, pb.b], [pO.b])
                            MM(pD.ap[:, 0:128], on[hh].ap, pb.ap[:, hh, o * 128:(o + 1) * 128], cnt == 0, cnt == tot - 1, [on[hh].b, pb.b], [pD.b])
                            cnt += 1
                    r = rc.get()
                    V(lambda e, r=r, pD=pD: e.tensor_scalar(out=r.ap, in0=pD.ap[:, 0:128], scalar1=sk.ap[:, 1:2], scalar2=None, op0=ALU.add), [pD.b, sk.b], [r.b])
                    V(lambda e, r=r: e.reciprocal(out=r.ap, in_=r.ap), [r.b], [r.b])
                    V(lambda e, r=r, pO=pO, o_t=o_t, qb=qb: e.tensor_tensor(out=o_t.ap[:, (qb % 8) * 128:(qb % 8) * 128 + 128], in0=pO.ap[:, 0:128], in1=r.ap, op=ALU.mult),
                      [pO.b, r.b], [o_t.b])
                    if qb % 8 == 7:
                        row = 1024 + 128 * (kvh * 2 + hp)
                        dma("sync", YM[n].ap[row:row + 128, (qb - 7) * 128:(qb + 1) * 128], o_t.ap, [o_t.b], [YM[n].b])

    def na(l, n, T):
        na_q_norm, na_k_norm, na_rpb, k_na_m = g_["na_q_norm"], g_["na_k_norm"], g_["na_rpb"], g_["k_na_m"]
        rows = T // 64
        nb = T // 128
        Mc = sb("na_mc", [128, 31, 64])
        RP = sb("na_rp", [128, 14, 31])
        ET = sb("na_et", [128, 14, 64])
        okm = sb("na_ok", [128, 64])
        tmpE = sb("na_te", [128, 14, 64])
        qraw = sb("na_qraw", [128, T], BF16)
        kraw = sb("na_kraw", [128, T], BF16)
        qn = sb("na_qn", [128, T], BF16)
        kn = sb("na_kn", [128, T], BF16)
        vA = sb("na_vA", [128, nb, 128], BF16)
        vB = sb("na_vB", [128, nb, 128], BF16)
        wq = sb("na_wq", [128, 1])
        wk = sb("na_wk", [128, 1])
        Pf = Pool("na_pf", [128, 4, 64], F32, 2)
        Pb = Pool("na_pb", [128, 4, 64], BF16, 2)
        rc = Pool("na_rc", [128, 64], F32, 2)
        ost = Pool("na_ost", [128, 1024], BF16, 2)
        dma("sync", Mc.ap, k_na_m.ap.rearrange("d p q -> p d q"), [], [Mc.b])
        dma("sync", wq.ap, na_q_norm.ap[l:l + 1, :].rearrange("o p -> p o"), [], [wq.b], slow=True)
        dma("sync", wk.ap, na_k_norm.ap[l:l + 1, :].rearrange("o p -> p o"), [], [wk.b], slow=True)
        V(lambda e: e.tensor_scalar(out=wq.ap, in0=wq.ap, scalar1=float(128 ** -0.5), scalar2=None, op0=ALU.mult), [wq.b], [wq.b])
        V(lambda e: e.memset(okm.ap, 0.0), [], [okm.b])
        for dc in range(31):
            V(lambda e, dc=dc: e.tensor_tensor(out=okm.ap, in0=okm.ap, in1=Mc.ap[:, dc, :], op=ALU.add), [Mc.b, okm.b], [okm.b])
        for h in range(4):
            for half in range(2):
                dma("sync", RP.ap[64 * half:64 * half + 64, :, :], na_rpb.ap[l, h:h + 1, half:half + 14, :].broadcast_to([64, 14, 31]), [], [RP.b])
            V(lambda e: e.memset(ET.ap, 0.0), [], [ET.b])
            for dc in range(31):
                V(lambda e, dc=dc: e.tensor_tensor(out=tmpE.ap, in0=Mc.ap[:, dc, :].unsqueeze(1).broadcast_to([128, 14, 64]),
                                                    in1=RP.ap[:, :, dc].unsqueeze(2).broadcast_to([128, 14, 64]), op=ALU.mult), [Mc.b, RP.b], [tmpE.b])
                V(lambda e: e.tensor_tensor(out=ET.ap, in0=ET.ap, in1=tmpE.ap, op=ALU.add), [tmpE.b, ET.b], [ET.b], eng="gpsimd")
            kb.op("scalar", lambda e: e.activation(out=ET.ap, in_=ET.ap, func=AF.Exp), reads=[ET.b], writes=[ET.b])
            V(lambda e: e.tensor_tensor(out=ET.ap, in0=ET.ap, in1=okm.ap.unsqueeze(1).broadcast_to([128, 14, 64]), op=ALU.mult), [ET.b, okm.b], [ET.b])
            for t0 in range(0, T, 2048):
                tw = min(2048, T - t0)
                dma("sync", qraw.ap[:, t0:t0 + tw], FM[n].ap[C_BQ + 128 * h:C_BQ + 128 * h + 128, t0:t0 + tw], [FM[n].b], [qraw.b])
                dma("sync", kraw.ap[:, t0:t0 + tw], FM[n].ap[C_BK + 128 * h:C_BK + 128 * h + 128, t0:t0 + tw], [FM[n].b], [kraw.b])
            headnorm_fm(qn, qraw, T, wq.ap[:, 0:1], ones_b, 128.0, 1.0)
            headnorm_fm(kn, kraw, T, wk.ap[:, 0:1], ones_b, 128.0, 1.0)
            dma("sync", vA.ap, TMv[n].ap[:, 1024 + 128 * h:1024 + 128 * h + 128].rearrange("(b p) c -> p b c", p=128), [TMv[n].b], [vA.b])
            dma("sync", vB.ap[:, 0:nb - 1, :], TMv[n].ap[64:T - 64, 1024 + 128 * h:1024 + 128 * h + 128].rearrange("(b p) c -> p b c", p=128), [TMv[n].b], [vB.b])
            o_t = None
            for r in range(rows):
                if r % 16 == 0:
                    o_t = ost.get()
                start = min(max(r - 4, 0), rows - 8)
                off = start - r + 7
                pS = nps()
                for j2 in range(4):
                    k0 = 64 * (start + 2 * j2)
                    MM(pS.ap[:, j2 * 64:(j2 + 1) * 64], kn.ap[:, k0:k0 + 128], qn.ap[:, 64 * r:64 * r + 64], True, True, [kn.b, qn.b], [pS.b])
                pf = Pf.get()
                pb = Pb.get()
                kb.op("scalar", lambda e, pf=pf, pS=pS: e.activation(out=pf.ap, in_=pS.ap[:, 0:256].rearrange("p (j q) -> p j q", q=64), func=AF.Exp), reads=[pS.b], writes=[pf.b])
                V(lambda e, pf=pf, pb=pb, off=off: e.tensor_tensor(out=pb.ap, in0=pf.ap, in1=ET.ap[:, off:off + 7:2, :], op=ALU.mult), [pf.b, ET.b], [pb.b],
                  eng=("vector" if r % 2 else "gpsimd"))
                pO = nps()
                pD = nps()
                for j2 in range(4):
                    rr = start + 2 * j2
                    vt = vA.ap[:, rr // 2, :] if rr % 2 == 0 else vB.ap[:, (rr - 1) // 2, :]
                    vb_ = vA.b if rr % 2 == 0 else vB.b
                    MM(pO.ap[:, 0:64], vt, pb.ap[:, j2, :], j2 == 0, j2 == 3, [vb_, pb.b], [pO.b])
                    MM(pD.ap[:, 0:64], ones_b.ap, pb.ap[:, j2, :], j2 == 0, j2 == 3, [ones_b.b, pb.b], [pD.b])
                rcp = rc.get()
                V(lambda e, rcp=rcp, pD=pD: e.reciprocal(out=rcp.ap, in_=pD.ap[:, 0:64]), [pD.b], [rcp.b])
                V(lambda e, rcp=rcp, pO=pO, o_t=o_t, r=r: e.tensor_tensor(out=o_t.ap[:, (r % 16) * 64:(r % 16) * 64 + 64], in0=pO.ap[:, 0:64], in1=rcp.ap, op=ALU.mult),
                  [pO.b, rcp.b], [o_t.b])
                if r % 16 == 15:
                    dma("sync", YM[n].ap[512 + 128 * h:512 + 128 * h + 128, (r - 15) * 64:(r + 1) * 64], o_t.ap, [o_t.b], [YM[n].b])

    def mlstm(l, n, T):
        mlstm_norm_w = g_["mlstm_norm_w"]
        nb = T // 128
        qT = sb("ml_qT", [128, T], BF16)
        kT = sb("ml_kT", [128, T], BF16)
        ktm = sb("ml_ktm", [128, nb, 128], BF16)
        vaug = sb("ml_va", [128, nb, 132], BF16)
        gt = sb("ml_gt", [128, nb, 16])
        hfw = sb("ml_hfw", [128, nb, 128])
        nwb = sb("ml_nw", [128, 128])
        lf = sb("ml_lf", [128, 2, nb])
        bcs = sb("ml_b", [128, 2, nb])
        gb = sb("ml_g", [128, 2, nb])
        beta = sb("ml_beta", [128, 2, nb])
        gam = sb("ml_gam", [128, 2, nb])
        emb = sb("ml_emb", [128, 2, nb])
        eg = sb("ml_eg", [128, 2, nb])
        Cst = sb("ml_C", [128, 132])
        Cb = sb("ml_Cb", [128, 132], BF16)
        STp = Pool("ml_st", [128, 128], BF16, 3)
        kgp = Pool("ml_kg", [128, 128], BF16, 3)
        sm = Pool("ml_sm", [128, 4], F32, 4)
        hs = Pool("ml_hs", [128, 128], F32, 3)
        hb = Pool("ml_hb", [128, 128], BF16, 3)
        jk = sb("ml_jk", [128, 128], BF16)
        ost = Pool("ml_ost", [128, 1024], BF16, 2)
        V(lambda e: e.memset(vaug.ap[:, :, 128:129], 1.0), [], [vaug.b])
        for h in range(4):
            for t0 in range(0, T, 2048):
                tw = min(2048, T - t0)
                dma("sync", qT.ap[:, t0:t0 + tw], FM[n].ap[C_AQ + 128 * h:C_AQ + 128 * h + 128, t0:t0 + tw], [FM[n].b], [qT.b])
                dma("sync", kT.ap[:, t0:t0 + tw], FM[n].ap[C_AK + 128 * h:C_AK + 128 * h + 128, t0:t0 + tw], [FM[n].b], [kT.b])
            dma("sync", ktm.ap, TMv[n].ap[:, 128 * h:128 * h + 128].rearrange("(b p) c -> p b c", p=128), [TMv[n].b], [ktm.b])
            dma("sync", vaug.ap[:, :, 0:128], TMv[n].ap[:, 512 + 128 * h:512 + 128 * h + 128].rearrange("(b p) c -> p b c", p=128), [TMv[n].b], [vaug.b])
            dma("sync", gt.ap, TMg[n].ap.rearrange("(b p) c -> p b c", p=128), [TMg[n].b], [gt.b])
            dma("sync", nwb.ap, mlstm_norm_w.ap[l:l + 1, 128 * h:128 * h + 128].broadcast_to([128, 128]), [], [nwb.b])
            for d, dn in enumerate(("fw", "bw")):
                icol, fcol = 4 * d + h, 8 + 4 * d + h
                kb.op("scalar", lambda e, d=d, fcol=fcol: e.activation(out=lf.ap[:, d, :], in_=gt.ap[:, :, fcol], func=AF.Exp, scale=-1.0), reads=[gt.b], writes=[lf.b])
                V(lambda e, d=d: e.tensor_scalar(out=lf.ap[:, d, :], in0=lf.ap[:, d, :], scalar1=1.0, scalar2=None, op0=ALU.add), [lf.b], [lf.b])
                kb.op("scalar", lambda e, d=d: e.activation(out=lf.ap[:, d, :], in_=lf.ap[:, d, :], func=AF.Ln), reads=[lf.b], writes=[lf.b])
                V(lambda e, d=d: e.tensor_scalar(out=lf.ap[:, d, :], in0=lf.ap[:, d, :], scalar1=-1.0, scalar2=None, op0=ALU.mult), [lf.b], [lf.b])
                p = nps()
                MM(p.ap[:, 0:nb], tri_f[dn].ap, lf.ap[:, d, :], True, True, [tri_f[dn].b, lf.b], [p.b])
                V(lambda e, d=d, p=p: e.tensor_copy(out=bcs.ap[:, d, :], in_=p.ap[:, 0:nb]), [p.b], [bcs.b])
                p2 = nps()
                MM(p2.ap[:, 0:nb], ones_f.ap, lf.ap[:, d, :], True, True, [ones_f.b, lf.b], [p2.b])
                V(lambda e, d=d, p2=p2: e.tensor_copy(out=gb.ap[:, d, :], in_=p2.ap[:, 0:nb]), [p2.b], [gb.b])
                kb.op("scalar", lambda e, d=d: e.activation(out=eg.ap[:, d, :], in_=gb.ap[:, d, :], func=AF.Exp), reads=[gb.b], writes=[eg.b])
                kb.op("scalar", lambda e, d=d: e.activation(out=emb.ap[:, d, :], in_=bcs.ap[:, d, :], func=AF.Exp, scale=-1.0), reads=[bcs.b], writes=[emb.b])
                V(lambda e, d=d, icol=icol: e.tensor_tensor(out=beta.ap[:, d, :], in0=gt.ap[:, :, icol], in1=bcs.ap[:, d, :], op=ALU.subtract), [gt.b, bcs.b], [beta.b])
                kb.op("scalar", lambda e, d=d: e.activation(out=beta.ap[:, d, :], in_=beta.ap[:, d, :], func=AF.Exp), reads=[beta.b], writes=[beta.b])
                V(lambda e, d=d: e.tensor_scalar(out=beta.ap[:, d, :], in0=beta.ap[:, d, :], scalar1=float(128 ** -0.5), scalar2=None, op0=ALU.mult), [beta.b], [beta.b])
                V(lambda e, d=d: e.tensor_tensor(out=gam.ap[:, d, :], in0=beta.ap[:, d, :], in1=eg.ap[:, d, :], op=ALU.mult), [beta.b, eg.b], [gam.b])
            o_t = None
            for d, dn in enumerate(("fw", "bw")):
                order = list(range(nb)) if d == 0 else list(range(nb - 1, -1, -1))
                for ci, c in enumerate(order):
                    csl = slice(c * 128, (c + 1) * 128)
                    pR = nps()
                    MM(pR.ap[:, 0:128], kT.ap[:, csl], qT.ap[:, csl], True, True, [kT.b, qT.b], [pR.b])
                    stt = STp.get()
                    V(lambda e, stt=stt, pR=pR, d=d, c=c, dn=dn: e.scalar_tensor_tensor(out=stt.ap, in0=pR.ap[:, 0:128], scalar=beta.ap[:, d, c:c + 1], in1=tri_f[dn].ap,
                                                                                          op0=ALU.mult, op1=ALU.mult), [pR.b, beta.b, tri_f[dn].b], [stt.b])
                    pX = nps()
                    MM(pX.ap[:, 0:129], stt.ap, vaug.ap[:, c, 0:129], True, ci == 0, [stt.b, vaug.b], [pX.b])
                    if ci > 0:
                        MM(pX.ap[:, 0:129], qT.ap[:, csl], Cb.ap[:, 0:129], False, True, [qT.b, Cb.b], [pX.b])
                    s4 = sm.get()
                    kb.op("scalar", lambda e, s4=s4, pX=pX: e.activation(out=s4.ap[:, 0:1], in_=pX.ap[:, 128:129], func=AF.Abs), reads=[pX.b], writes=[s4.b])
                    V(lambda e, s4=s4, d=d, c=c: e.tensor_tensor(out=s4.ap[:, 0:1], in0=s4.ap[:, 0:1], in1=emb.ap[:, d, c:c + 1], op=ALU.max), [s4.b, emb.b], [s4.b])
                    V(lambda e, s4=s4: e.reciprocal(out=s4.ap[:, 1:2], in_=s4.ap[:, 0:1]), [s4.b], [s4.b])
                    if d == 0:
                        V(lambda e, s4=s4, pX=pX, c=c: e.tensor_scalar(out=hfw.ap[:, c, :], in0=pX.ap[:, 0:128], scalar1=s4.ap[:, 1:2], scalar2=None, op0=ALU.mult),
                          [pX.b, s4.b], [hfw.b])
                    else:
                        if ci % 8 == 0:
                            o_t = ost.get()
                        hsum = hs.get()
                        V(lambda e, s4=s4, pX=pX, c=c, hsum=hsum: e.scalar_tensor_tensor(out=hsum.ap, in0=pX.ap[:, 0:128], scalar=s4.ap[:, 1:2], in1=hfw.ap[:, c, :],
                                                                                          op0=ALU.mult, op1=ALU.add), [pX.b, s4.b, hfw.b], [hsum.b])
                        kb.op("scalar", lambda e, hsum=hsum, s4=s4: e.activation(out=jk.ap, in_=hsum.ap, func=AF.Square, accum_out=s4.ap[:, 2:3]),
                              reads=[hsum.b], writes=[jk.b, s4.b])
                        V(lambda e, s4=s4: e.tensor_scalar(out=s4.ap[:, 3:4], in0=s4.ap[:, 2:3], scalar1=1.0 / 128, scalar2=EPS, op0=ALU.mult, op1=ALU.add), [s4.b], [s4.b])
                        kb.op("scalar", lambda e, s4=s4: e.activation(out=s4.ap[:, 3:4], in_=s4.ap[:, 3:4], func=AF.Sqrt), reads=[s4.b], writes=[s4.b])
                        V(lambda e, s4=s4: e.reciprocal(out=s4.ap[:, 3:4], in_=s4.ap[:, 3:4]), [s4.b], [s4.b])
                        hbt = hb.get()
                        V(lambda e, hsum=hsum, s4=s4, hbt=hbt: e.scalar_tensor_tensor(out=hbt.ap, in0=hsum.ap, scalar=s4.ap[:, 3:4], in1=nwb.ap, op0=ALU.mult, op1=ALU.mult),
                          [hsum.b, s4.b, nwb.b], [hbt.b])
                        pT = npsb()
                        kb.op("tensor", lambda e, pT=pT, hbt=hbt: e.transpose(pT.ap[:, 0:128], hbt.ap, ident_b.ap), reads=[hbt.b, ident_b.b], writes=[pT.b])
                        copy_op("scalar", o_t.ap[:, (c % 8) * 128:(c % 8) * 128 + 128], pT.ap[:, 0:128], [pT.b], [o_t.b])
                        if c % 8 == 0:
                            dma("sync", YM[n].ap[128 * h:128 * h + 128, c * 128:(c + 8) * 128], o_t.ap, [o_t.b], [YM[n].b])
                    if ci < nb - 1:
                        kg = kgp.get()
                        V(lambda e, kg=kg, d=d, c=c: e.tensor_scalar(out=kg.ap, in0=ktm.ap[:, c, :], scalar1=gam.ap[:, d, c:c + 1], scalar2=None, op0=ALU.mult),
                          [ktm.b, gam.b], [kg.b], eng="gpsimd")
                        pC = nps()
                        MM(pC.ap[:, 0:129], kg.ap, vaug.ap[:, c, 0:129], True, True, [kg.b, vaug.b], [pC.b])
                        if ci == 0:
                            V(lambda e, pC=pC: e.tensor_copy(out=Cst.ap[:, 0:129], in_=pC.ap[:, 0:129]), [pC.b], [Cst.b])
                        else:
                            V(lambda e, pC=pC, d=d, c=c: e.scalar_tensor_tensor(out=Cst.ap[:, 0:129], in0=Cst.ap[:, 0:129], scalar=eg.ap[:, d, c:c + 1], in1=pC.ap[:, 0:129],
                                                                                  op0=ALU.mult, op1=ALU.add), [Cst.b, eg.b, pC.b], [Cst.b])
                        copy_op("scalar", Cb.ap[:, 0:129], Cst.ap[:, 0:129], [Cst.b], [Cb.b])

    PERSIST = g_["PERSIST"]
    aoff = g_["aoff"]

    def MIX(l, si, n, T):
        for fn in (conv, gqa, na, mlstm):
            if fn.__name__ in SKIP_MIX:
                continue
            fn(l, n, T)
            kb.barrier()
            aoff[0] = PERSIST
    MIX.gqa_tables = gqa_tables
    return MIX


SKIP_MIX = set()
DEBUG_YM = False


_W_KEYS = ["rel_bias", "norm_w", "w_ada", "b_ada", "w_in", "b_gate", "mlstm_norm_w", "na_q_norm", "na_k_norm", "na_rpb",
           "swa_q_norm", "swa_k_norm", "swa_sink", "conv_w", "conv_b", "conv_ln_w", "conv_ln_b", "w_branch", "w_out"]


def kernel(**inputs):
    xp = np.ascontiguousarray(np.asarray(inputs["x_prompt"], dtype=np.float32))
    xs = np.ascontiguousarray(np.asarray(inputs["x_sample"], dtype=np.float32))
    cp = np.asarray(inputs["c_prompt"], dtype=np.float32)
    cs = np.asarray(inputs["c_sample"], dtype=np.float32)
    TP, TS = xp.shape[1], xs.shape[1]
    depth = np.asarray(inputs["norm_w"]).shape[0]
    nc = build([("P", TP), ("S", TS)], depth)
    consts = make_consts()
    shared = {k: np.ascontiguousarray(np.asarray(inputs[k], dtype=np.float32)) for k in _W_KEYS}
    shared.update({"k_ident": consts["ident"], "k_tri_fw": consts["tri_fw"], "k_tri_bw": consts["tri_bw"], "k_blk64": consts["blk64"],
                   "k_gqa_m": consts["gqa_m"], "k_na_m": consts["na_m"]})
    in_maps = []
    for i in range(8):
        m = dict(shared)
        m["x_P"] = xp[i // 4]
        m["c_P"] = cp[i // 4][None]
        m["x_S"] = xs[i // 2]
        m["c_S"] = cs[i // 2][None]
        in_maps.append(m)
    res = run_bass_kernel_spmd(nc, in_maps, core_ids=list(range(8)))
    yp = np.empty_like(xp)
    ys = np.empty_like(xs)
    qp, qs = TP // 4, TS // 2
    for i in range(8):
        r = res.results[i]
        a = i % 4
        yp[i // 4, a * qp:(a + 1) * qp] = np.asarray(r["y_P"])[a * qp:(a + 1) * qp]
        b = i % 2
        ys[i // 2, b * qs:(b + 1) * qs] = np.asarray(r["y_S"])[b * qs:(b + 1) * qs]
    return (yp, ys)
```

```python
import contextlib
import math
import numpy as np
import concourse.bass as bass
import concourse.mybir as mybir
from concourse.bass_utils import run_bass_kernel_spmd

F32 = mybir.dt.float32
BF16 = mybir.dt.bfloat16
ALU = mybir.AluOpType
AF = mybir.ActivationFunctionType

D = 2048
KC = 16
IN_COLS = 15632
EPS = 1e-6
C_AQ, C_AK, C_AV, C_AO, C_AZ, C_AG = 0, 512, 1024, 1536, 2048, 2560
C_BQ, C_BK, C_BV, C_BZ = 2576, 3088, 3600, 4112
C_CQ, C_CK, C_CV, C_CZ = 4624, 5136, 5264, 5392
C_DA, C_DG, C_DZ, C_MG = 5904, 6416, 6928, 7440
NFM = 7440

ENGS = ("tensor", "vector", "scalar", "gpsimd", "sync")
NDMA = 28
SEM_EPOCH = 30000


class Buf:
    __slots__ = ("w", "r")

    def __init__(self):
        self.w = None
        self.r = []


class KB:
    def __init__(self, nc):
        self.nc = nc
        self.ops = {e: [] for e in ENGS}
        self.cnt = {}
        self.known = {e: {} for e in ENGS}
        self.dma_rr = 0
        self.dma_last = {}
        self.pending = {e: [] for e in ENGS}
        self.tot = {}

    def barrier(self):
        for e in ENGS:
            for k, v in list(self.cnt.items()) + list(self.dma_last.items()):
                self._need(e, (k, v), self.pending[e])

    def _need(self, eng, dep, waits):
        if dep is None:
            return
        k, v = dep
        if eng == "tensor" and k.startswith("tensor#"):
            return
        if self.known[eng].get(k, 0) >= v:
            return
        self.known[eng][k] = v
        waits.append((k, v))

    def op(self, eng, fn, reads=(), writes=(), dma=False):
        waits = self.pending[eng]
        self.pending[eng] = []
        for b in reads:
            self._need(eng, b.w, waits)
        for b in writes:
            self._need(eng, b.w, waits)
            for d in b.r:
                self._need(eng, d, waits)
        if dma:
            k = f"dma{self.dma_rr % NDMA}"
            self.dma_rr += 1
            last = self.dma_last.get(k, 0)
            if last:
                self._need(eng, (k, last), waits)
            v = last + 16
            self.dma_last[k] = v
            inc = 16
        else:
            tot = self.tot.get(eng, 0) + 1
            self.tot[eng] = tot
            k = f"{eng}#{(tot - 1) // SEM_EPOCH}"
            v = (tot - 1) % SEM_EPOCH + 1
            self.cnt[k] = v
            inc = 1
        tag = (k, v)
        for b in reads:
            b.r.append(tag)
            if len(b.r) > 64:
                b.r = b.r[-64:]
        for b in writes:
            b.w = tag
            b.r = []
        self.ops[eng].append((waits, fn, k, inc))

    def emit(self):
        nc = self.nc
        keys = set()
        for e in ENGS:
            for waits, fn, k, inc in self.ops[e]:
                keys.add(k)
        with contextlib.ExitStack() as st:
            sems = {k: st.enter_context(nc.semaphore(f"s_{k}")) for k in sorted(keys)}
            block = st.enter_context(nc.Block())

            def mk(e):
                def body(eng):
                    for waits, fn, k, inc in self.ops[e]:
                        for (wk, wv) in waits:
                            eng.wait_ge(sems[wk], wv)
                        fn(eng).then_inc(sems[k], inc)
                    if e == "sync":
                        for k2 in sorted(keys):
                            tot = self.dma_last.get(k2) if k2.startswith("dma") else self.cnt.get(k2)
                            if tot:
                                eng.wait_ge(sems[k2], tot)
                return body

            for e in ENGS:
                if self.ops[e] or e == "sync":
                    getattr(block, e)(mk(e))


class TL:
    def __init__(self, ap):
        self.ap = ap
        self.b = Buf()

    def __getitem__(self, k):
        return self.ap[k]


def t5_bucket_np(rel):
    half, max_exact = 16, 8
    n = np.abs(rel)
    nf = np.maximum(n, 1).astype(np.float32)
    large = max_exact + (np.log(nf / np.float32(max_exact)) / np.float32(math.log(128 / max_exact)) * (half - max_exact)).astype(np.int32)
    large = np.minimum(large, half - 1)
    return np.where(rel > 0, half, 0) + np.where(n < max_exact, n, large)


def make_consts():
    c = {}
    c["ident"] = np.eye(128, dtype=np.float32)
    s = np.arange(128)
    c["tri_fw"] = (s[:, None] <= s[None, :]).astype(np.float32)
    c["tri_bw"] = (s[:, None] >= s[None, :]).astype(np.float32)
    bo = np.zeros((128, 128), np.float32)
    bo[:64, :64] = 1
    bo[64:, 64:] = 1
    c["blk64"] = bo
    k = np.arange(128)[:, None]
    q = np.arange(128)[None, :]
    gm = np.zeros((3, 32, 128, 128), np.float32)
    for o in range(3):
        rel = k + 128 * (o - 1) - q
        bk = t5_bucket_np(rel)
        ok = np.abs(rel) <= 128
        for b in range(32):
            gm[o, b] = ((bk == b) & ok)
    c["gqa_m"] = gm
    kc = np.arange(64)[:, None]
    qc = np.arange(64)[None, :]
    qs = np.clip(qc - 8, 0, 48)
    ok = (kc >= qs) & (kc < qs + 16)
    dc = np.clip(kc - qc + 15, 0, 30)
    nm = np.zeros((31, 128, 64), np.float32)
    for d in range(31):
        m = ((dc == d) & ok).astype(np.float32)
        nm[d, :64] = m
        nm[d, 64:] = m
    c["na_m"] = nm
    return c


def build(seqs, depth, dbg=()):
    nc = bass.Bass("TRN2", target_bir_lowering=False)
    kb = KB(nc)
    st = contextlib.ExitStack()

    def din(name, shape, dt=F32):
        return TL(nc.dram_tensor(name, list(shape), dt, kind="ExternalInput").ap())

    def dscr(name, shape, dt=BF16, kind="Internal"):
        return TL(nc.dram_tensor(name, list(shape), dt, kind=kind).ap())

    AW = 53200
    arena = st.enter_context(nc.sbuf_tensor("arena", [128, AW], F32))
    aoff = [0]

    def sb(name, shape, dt=F32):
        n = 1
        for d_ in shape[1:]:
            n *= d_
        words = (n * (2 if dt == BF16 else 4) + 3) // 4
        assert aoff[0] + words <= AW, (name, aoff[0], words)
        v = arena[0:shape[0], aoff[0]:aoff[0] + words]
        aoff[0] += words
        if dt != F32:
            v = v.bitcast(dt)
        if len(shape) == 3:
            v = v.rearrange("p (a b) -> p a b", a=shape[1])
        elif len(shape) == 4:
            v = v.rearrange("p (a b c) -> p a b c", a=shape[1], b=shape[2])
        return TL(v)

    def ps(name, shape, dt=F32):
        return TL(st.enter_context(nc.psum_tensor(name, list(shape), dt))[:])

    X = {n: din(f"x_{n}", [T, D]) for n, T in seqs}
    Cc = {n: din(f"c_{n}", [1, D]) for n, T in seqs}
    Yout = {n: dscr(f"y_{n}", [T, D], F32, kind="ExternalOutput") for n, T in seqs}
    rel_bias = din("rel_bias", [32, 8])
    norm_w = din("norm_w", [depth, D])
    w_ada = din("w_ada", [depth, D, 3 * D])
    b_ada = din("b_ada", [depth, 3 * D])
    w_in = din("w_in", [depth, D, IN_COLS])
    b_gate = din("b_gate", [depth, 16])
    mlstm_norm_w = din("mlstm_norm_w", [depth, 512])
    na_q_norm = din("na_q_norm", [depth, 128])
    na_k_norm = din("na_k_norm", [depth, 128])
    na_rpb = din("na_rpb", [depth, 4, 15, 31])
    swa_q_norm = din("swa_q_norm", [depth, 64])
    swa_k_norm = din("swa_k_norm", [depth, 64])
    swa_sink = din("swa_sink", [depth, 8])
    conv_w = din("conv_w", [depth, 31, 512])
    conv_b = din("conv_b", [depth, 512])
    conv_ln_w = din("conv_ln_w", [depth, 512])
    conv_ln_b = din("conv_ln_b", [depth, 512])
    w_branch = din("w_branch", [depth, 4, 512, D])
    w_out = din("w_out", [depth, D, D])
    k_ident = din("k_ident", [128, 128])
    k_tri_fw = din("k_tri_fw", [128, 128])
    k_tri_bw = din("k_tri_bw", [128, 128])
    k_blk64 = din("k_blk64", [128, 128])
    k_gqa_m = din("k_gqa_m", [3, 32, 128, 128])
    k_na_m = din("k_na_m", [31, 128, 64])

    FM = {n: dscr(f"fm_{n}", [NFM, T]) for n, T in seqs}
    TMv = {n: dscr(f"tm_{n}", [T, 1664]) for n, T in seqs}
    TMg = {n: dscr(f"tg_{n}", [T, 16], F32) for n, T in seqs}
    HT = {n: dscr(f"ht_{n}", [D, T]) for n, T in seqs}
    YM = {n: dscr(f"ym_{n}", [D, T], BF16, kind=("ExternalOutput" if DEBUG_YM else "Internal")) for n, T in seqs}
    X1 = {n: dscr(f"x1_{n}", [T, D], F32) for n, T in seqs}
    modrow = dscr("modrow", [depth * len(seqs), D], F32)
    WT = {}
    WSRC = {}
    DBG = {}

    ident_f = sb("ident_f", [128, 128])
    ident_b = sb("ident_b", [128, 128], BF16)
    ones_b = sb("ones_b", [128, 128], BF16)
    ones_f = sb("ones_f", [128, 128])
    blk64_b = sb("blk64_b", [128, 128], BF16)
    tri_f = {"fw": sb("tri_fw", [128, 128]), "bw": sb("tri_bw", [128, 128])}
    kb.op("sync", lambda e: e.dma_start(out=ident_f.ap, in_=k_ident.ap), writes=[ident_f.b], dma=True)
    kb.op("vector", lambda e: e.tensor_copy(out=ident_b.ap, in_=ident_f.ap), reads=[ident_f.b], writes=[ident_b.b])
    kb.op("vector", lambda e: e.memset(ones_b.ap, 1.0), writes=[ones_b.b])
    kb.op("vector", lambda e: e.memset(ones_f.ap, 1.0), writes=[ones_f.b])
    kb.op("sync", lambda e: e.dma_start(out=tri_f["fw"].ap, in_=k_tri_fw.ap), writes=[tri_f["fw"].b], dma=True)
    kb.op("sync", lambda e: e.dma_start(out=tri_f["bw"].ap, in_=k_tri_bw.ap), writes=[tri_f["bw"].b], dma=True)
    eps_c = sb("eps_c", [128, 1])
    kb.op("vector", lambda e: e.memset(eps_c.ap, EPS), writes=[eps_c.b])
    tmpc = sb("tmpc", [128, 128])
    kb.op("sync", lambda e: e.dma_start(out=tmpc.ap, in_=k_blk64.ap), writes=[tmpc.b], dma=True)
    kb.op("vector", lambda e: e.tensor_copy(out=blk64_b.ap, in_=tmpc.ap), reads=[tmpc.b], writes=[blk64_b.b])

    PS = [ps(f"ps{i}", [128, 512]) for i in range(6)]
    PSB = [ps(f"psb{i}", [128, 1024], BF16) for i in range(2)]
    psrr = [0]

    psmode = [6]

    def nps():
        psrr[0] += 1
        return PS[psrr[0] % psmode[0]]

    PSH = [TL(PS[4].ap[:, 0:256]), TL(PS[4].ap[:, 256:512]), TL(PS[5].ap[:, 0:256]), TL(PS[5].ap[:, 256:512])]
    pshr = [0]

    def npsh():
        pshr[0] += 1
        return PSH[pshr[0] % 4]

    psbr = [0]

    def npsb():
        psbr[0] += 1
        return PSB[psbr[0] % 2]

    class Pool:
        def __init__(self, name, shape, dt, n):
            self.t = [sb(f"{name}{i}", shape, dt) for i in range(n)]
            self.i = 0

        def get(self):
            self.i += 1
            return self.t[self.i % len(self.t)]

    evac_rr = [0]

    def evac_eng():
        evac_rr[0] += 1
        return "vector" if evac_rr[0] % 2 else "scalar"

    def copy_op(eng, out, in_, reads, writes):
        if eng == "scalar":
            kb.op("scalar", lambda e: e.activation(out=out, in_=in_, func=AF.Copy), reads=reads, writes=writes)
        else:
            kb.op(eng, lambda e: e.tensor_copy(out=out, in_=in_), reads=reads, writes=writes)

    nseq = len(seqs)
    modA = sb("modA", [128, depth, nseq, KC])
    modB = sb("modB", [128, depth, nseq, KC])
    cs = sb("cs", [128, KC, nseq])
    modfm = sb("modfm", [128, 48, nseq])
    badafm = sb("badafm", [128, 48])
    nwfm = sb("nwfm", [128, KC])
    bg_bc = sb("bg_bc", [128, 16])
    EB = sb("EB", [128, 3, 8, 128])
    PERSIST = aoff[0]

    def adaln():
        w32 = Pool("w32", [128, KC, 128], F32, 3)
        for si, (n, T) in enumerate(seqs):
            kb.op("sync", lambda e, si=si, n=n: e.dma_start(out=cs.ap[:, :, si], in_=Cc[n].ap.rearrange("o (k p) -> p (o k)", p=128), allow_slow_non_contiguous=True),
                  writes=[cs.b], dma=True)
        kb.op("scalar", lambda e: e.activation(out=cs.ap, in_=cs.ap, func=AF.Silu), reads=[cs.b], writes=[cs.b])
        for l in range(depth):
            kb.op("sync", lambda e, l=l: e.dma_start(out=badafm.ap, in_=b_ada.ap[l:l + 1, :].rearrange("o (k p) -> p (o k)", p=128), allow_slow_non_contiguous=True),
                  writes=[badafm.b], dma=True)
            kb.op("sync", lambda e, l=l: e.dma_start(out=nwfm.ap, in_=norm_w.ap[l:l + 1, :].rearrange("o (k p) -> p (o k)", p=128), allow_slow_non_contiguous=True),
                  writes=[nwfm.b], dma=True)
            pm = nps()
            for f in range(48):
                wt = w32.get()
                kb.op("sync", lambda e, l=l, f=f, wt=wt: e.dma_start(out=wt.ap, in_=w_ada.ap[l, :, f * 128:(f + 1) * 128].rearrange("(k p) n -> p k n", p=128)),
                      writes=[wt.b], dma=True)
                for k in range(KC):
                    kb.op("tensor", lambda e, k=k, f=f, pm=pm, wt=wt: e.matmul(pm.ap[:, f * nseq:(f + 1) * nseq], lhsT=wt.ap[:, k, :],
                                                                          rhs=cs.ap[:, k, :], start=(k == 0), stop=(k == KC - 1)),
                          reads=[wt.b, cs.b], writes=[pm.b])
            for si in range(nseq):
                kb.op("vector", lambda e, pm=pm, si=si: e.tensor_tensor(out=modfm.ap[:, :, si], in0=pm.ap[:, 0:48 * nseq].rearrange("p (f s) -> p f s", s=nseq)[:, :, si],
                                                                in1=badafm.ap, op=ALU.add),
                      reads=[pm.b, badafm.b], writes=[modfm.b])
            for si, (n, T) in enumerate(seqs):
                kb.op("vector", lambda e, l=l, si=si: e.scalar_tensor_tensor(out=modA.ap[:, l, si, :], in0=modfm.ap[:, 16:32, si], scalar=1.0, in1=nwfm.ap,
                                                                           op0=ALU.add, op1=ALU.mult),
                      reads=[modfm.b, nwfm.b], writes=[modA.b])
                kb.op("vector", lambda e, l=l, si=si: e.tensor_copy(out=modB.ap[:, l, si, :], in_=modfm.ap[:, 0:16, si]), reads=[modfm.b], writes=[modB.b])
                kb.op("sync", lambda e, l=l, si=si: e.dma_start(out=modrow.ap[l * nseq + si:l * nseq + si + 1, :].rearrange("o (k p) -> p (o k)", p=128),
                                                               in_=modfm.ap[:, 32:48, si], allow_slow_non_contiguous=True),
                      reads=[modfm.b], writes=[modrow.b], dma=True)
        kb.barrier()
        aoff[0] = PERSIST

    FM_RANGES = [(0, 1024), (1536, 2560), (2576, 3600), (4112, 5264), (5392, 7440)]
    TM_RANGES = [(512, 1536, 0), (3600, 4112, 1024), (5264, 5392, 1536)]

    def fm_func(col):
        if C_AO <= col < C_AZ:
            return AF.Sigmoid
        if C_AZ <= col < C_AG or C_BZ <= col < C_CQ or C_CZ <= col < C_DA or C_DZ <= col < C_MG:
            return AF.Silu
        return None

    def phase1(l, si, n, T, xsrc):
        xt_pool = Pool("xt", [128, D], F32, 2)
        xn_t = sb("xn", [128, 4, D], BF16)
        junk = sb("junk", [128, D], BF16)
        ssq = sb("ssq", [128, 8])
        hT = sb("hT", [128, KC, 1024], BF16)
        ofm = Pool("ofm", [128, 1024], BF16, 3)
        otm = Pool("otm", [128, 512], BF16, 3)
        otg = Pool("otg", [128, 16], F32, 2)
        wtile = Pool("wtile", [128, KC, 512], BF16, 4)

        wjobs = []
        for g_ in range(T // 1024):
            for (c0_, c1_) in FM_RANGES:
                for w0_ in range(c0_, c1_, 512):
                    wjobs.append((w0_, min(512, c1_ - w0_)))
            for (c0_, c1_, _d) in TM_RANGES:
                for w0_ in range(c0_, c1_, 512):
                    wjobs.append((w0_, min(512, c1_ - w0_)))
            wjobs.append((C_AG, 16))

        def mk_loader(c0, ncols):
            def ld():
                wt = wtile.get()
                wload(wt.ap, wt.b, l, "in", 0, c0, ncols)
                return wt
            return ld
        pf = Prefetch([mk_loader(c0, nco) for (c0, nco) in wjobs], ahead=2)
        wji = [0]

        def load_w(c0, ncols):
            i = wji[0]
            assert wjobs[i] == (c0, ncols), (wjobs[i], c0, ncols)
            wji[0] += 1
            return pf.get(i)

        kb.op("sync", lambda e: e.dma_start(out=bg_bc.ap, in_=b_gate.ap[l:l + 1, :].broadcast_to([128, 16])), writes=[bg_bc.b], dma=True)
        def do_group1(g):
            for half in range(2):
                for j in range(4):
                    t0 = g * 1024 + half * 512 + j * 128
                    xt = xt_pool.get()
                    kb.op("sync", lambda e, xt=xt, t0=t0: e.dma_start(out=xt.ap, in_=xsrc.ap[t0:t0 + 128, :]), reads=[xsrc.b], writes=[xt.b], dma=True)
                    kb.op("scalar", lambda e, xt=xt, j=j: e.activation(out=junk.ap, in_=xt.ap, func=AF.Square, accum_out=ssq.ap[:, j:j + 1]),
                          reads=[xt.b], writes=[junk.b, ssq.b])
                    kb.op("vector", lambda e, j=j: e.tensor_scalar(out=ssq.ap[:, 4 + j:5 + j], in0=ssq.ap[:, j:j + 1], scalar1=1.0 / D, scalar2=EPS,
                                                                    op0=ALU.mult, op1=ALU.add), reads=[ssq.b], writes=[ssq.b])
                    kb.op("scalar", lambda e, j=j: e.activation(out=ssq.ap[:, 4 + j:5 + j], in_=ssq.ap[:, 4 + j:5 + j], func=AF.Sqrt), reads=[ssq.b], writes=[ssq.b])
                    kb.op("vector", lambda e, j=j: e.reciprocal(out=ssq.ap[:, 4 + j:5 + j], in_=ssq.ap[:, 4 + j:5 + j]), reads=[ssq.b], writes=[ssq.b])
                    kb.op("vector", lambda e, xt=xt, j=j: e.tensor_scalar(out=xn_t.ap[:, j, :], in0=xt.ap, scalar1=ssq.ap[:, 4 + j:5 + j], scalar2=None,
                                                                           op0=ALU.mult), reads=[xt.b, ssq.b], writes=[xn_t.b])
                for k in range(KC):
                    pb = npsb()
                    for j in range(4):
                        kb.op("tensor", lambda e, pb=pb, j=j, k=k: e.transpose(pb.ap[:, j * 128:(j + 1) * 128], xn_t.ap[:, j, k * 128:(k + 1) * 128], ident_b.ap),
                              reads=[xn_t.b, ident_b.b], writes=[pb.b])
                    dst = hT.ap[:, k, half * 512:(half + 1) * 512]
                    if k % 2 == 0:
                        kb.op("scalar", lambda e, pb=pb, k=k, dst=dst: e.activation(out=dst, in_=pb.ap[:, 0:512], func=AF.Identity,
                                                                                     bias=modB.ap[:, l, si, k:k + 1], scale=modA.ap[:, l, si, k:k + 1]),
                              reads=[pb.b, modA.b, modB.b], writes=[hT.b])
                    else:
                        kb.op("vector", lambda e, pb=pb, k=k, dst=dst: e.tensor_scalar(out=dst, in0=pb.ap[:, 0:512], scalar1=modA.ap[:, l, si, k:k + 1],
                                                                                        scalar2=modB.ap[:, l, si, k:k + 1], op0=ALU.mult, op1=ALU.add),
                              reads=[pb.b, modA.b, modB.b], writes=[hT.b])
            kb.op("sync", lambda e, g=g: e.dma_start(out=HT[n].ap[:, g * 1024:(g + 1) * 1024].rearrange("(k p) t -> p k t", p=128), in_=hT.ap),
                  reads=[hT.b], writes=[HT[n].b], dma=True)
            for (c0, c1) in FM_RANGES:
                for w0 in range(c0, c1, 512):
                    ncols = min(512, c1 - w0)
                    wt = load_w(w0, ncols)
                    for mc in range(ncols // 128):
                        col = w0 + mc * 128
                        o = ofm.get()
                        fn = fm_func(col)
                        for tt in range(2):
                            p = nps()
                            for k in range(KC):
                                kb.op("tensor", lambda e, p=p, wt=wt, mc=mc, k=k, tt=tt: e.matmul(p.ap, lhsT=wt.ap[:, k, mc * 128:(mc + 1) * 128],
                                                                                            rhs=hT.ap[:, k, tt * 512:(tt + 1) * 512],
                                                                                            start=(k == 0), stop=(k == KC - 1)),
                                      reads=[wt.b, hT.b], writes=[p.b])
                            dst = o.ap[:, tt * 512:(tt + 1) * 512]
                            if fn is None:
                                copy_op(evac_eng(), dst, p.ap, [p.b], [o.b])
                            else:
                                kb.op("scalar", lambda e, p=p, dst=dst, fn=fn: e.activation(out=dst, in_=p.ap, func=fn), reads=[p.b], writes=[o.b])
                        kb.op("sync", lambda e, o=o, col=col, g=g: e.dma_start(out=FM[n].ap[col:col + 128, g * 1024:(g + 1) * 1024], in_=o.ap),
                              reads=[o.b], writes=[FM[n].b], dma=True)
            for (c0, c1, dcol) in TM_RANGES:
                for w0 in range(c0, c1, 512):
                    ncols = min(512, c1 - w0)
                    wt = load_w(w0, ncols)
                    for sub in range(8):
                        p = nps()
                        for k in range(KC):
                            kb.op("tensor", lambda e, p=p, wt=wt, k=k, sub=sub, ncols=ncols: e.matmul(p.ap[:, 0:ncols], lhsT=hT.ap[:, k, sub * 128:(sub + 1) * 128],
                                                                                                 rhs=wt.ap[:, k, 0:ncols], start=(k == 0), stop=(k == KC - 1)),
                                  reads=[wt.b, hT.b], writes=[p.b])
                        o = otm.get()
                        copy_op(evac_eng(), o.ap[:, 0:ncols], p.ap[:, 0:ncols], [p.b], [o.b])
                        t0 = g * 1024 + sub * 128
                        dc = dcol + (w0 - c0)
                        kb.op("sync", lambda e, o=o, t0=t0, dc=dc, ncols=ncols: e.dma_start(out=TMv[n].ap[t0:t0 + 128, dc:dc + ncols], in_=o.ap[:, 0:ncols]),
                              reads=[o.b], writes=[TMv[n].b], dma=True)
            wt = load_w(C_AG, 16)
            for sub in range(8):
                p = nps()
                for k in range(KC):
                    kb.op("tensor", lambda e, p=p, wt=wt, k=k, sub=sub: e.matmul(p.ap[:, 0:16], lhsT=hT.ap[:, k, sub * 128:(sub + 1) * 128],
                                                                            rhs=wt.ap[:, k, 0:16], start=(k == 0), stop=(k == KC - 1)),
                          reads=[wt.b, hT.b], writes=[p.b])
                o = otg.get()
                kb.op("vector", lambda e, o=o, p=p: e.tensor_tensor(out=o.ap, in0=p.ap[:, 0:16], in1=bg_bc.ap, op=ALU.add), reads=[p.b, bg_bc.b], writes=[o.b])
                t0 = g * 1024 + sub * 128
                kb.op("sync", lambda e, o=o, t0=t0: e.dma_start(out=TMg[n].ap[t0:t0 + 128, :], in_=o.ap), reads=[o.b], writes=[TMg[n].b], dma=True)
        for g in range(T // 1024):
            do_group1(g)
        kb.barrier()
        aoff[0] = PERSIST

    def phase3(l, si, n, T, xsrc, xdst):
        hT = sb("hT3", [128, KC, 1024], BF16)
        yT_t = sb("yT", [128, KC, 1024], BF16)
        mT_t = sb("mT", [128, KC, 1024], BF16)
        wm_pool = Pool("wm", [128, KC, 256], BF16, 4)
        wbr_pool = Pool("wbr", [128, 4, 256], BF16, 4)

        def mk_merge(mg, i):
            def ld():
                wt = wm_pool.get()
                wload(wt.ap, wt.b, l, "in", 0, C_MG + i * 2048 + mg * 256, 256)
                wb_ = wbr_pool.get()
                wload(wb_.ap, wb_.b, l, "br", i, mg * 256, 256)
                return (wt, wb_)
            return ld

        def mk_out(nn):
            def ld():
                wt = wm_pool.get()
                wload(wt.ap, wt.b, l, "out", 0, nn * 256, 256)
                return wt
            return ld
        loaders3 = []
        for g_ in range(T // 1024):
            for mg_ in range(8):
                for i_ in range(4):
                    loaders3.append(mk_merge(mg_, i_))
            for nn_ in range(8):
                loaders3.append(mk_out(nn_))
        pf3 = Prefetch(loaders3, ahead=2)
        pfi = [0]
        ytmp = Pool("ytmp", [128, 1024], BF16, 3)
        uld = [sb(f"uld{i}", [128, 1024], BF16) for i in range(4)]
        sg_pool = Pool("sg", [128, 512], F32, 2)
        acc_pool = Pool("acc", [128, 512], F32, 8)
        tmp_pool = Pool("tmp3", [128, 512], F32, 2)
        xo_pool = Pool("xo", [128, 512], F32, 2)
        usq = Pool("usq", [128, 512], BF16, 2)
        stat = Pool("stat", [128, 512], F32, 3)
        gsl = sb("gsl", [128, 512])
        lnw = sb("lnw", [128, 4])
        lnb = sb("lnb", [128, 4])
        kb.op("sync", lambda e: e.dma_start(out=lnw.ap, in_=conv_ln_w.ap[l:l + 1, :].rearrange("o (k p) -> p (o k)", p=128), allow_slow_non_contiguous=True), writes=[lnw.b], dma=True)
        kb.op("sync", lambda e: e.dma_start(out=lnb.ap, in_=conv_ln_b.ap[l:l + 1, :].rearrange("o (k p) -> p (o k)", p=128), allow_slow_non_contiguous=True), writes=[lnb.b], dma=True)
        def do_group(g):
            tsl = slice(g * 1024, (g + 1) * 1024)
            kb.op("sync", lambda e: e.dma_start(out=hT.ap, in_=HT[n].ap[:, tsl].rearrange("(k p) t -> p k t", p=128)), reads=[HT[n].b], writes=[hT.b], dma=True)
            for br in range(3):
                zc = (C_AZ, C_BZ, C_CZ)[br]
                for c4 in range(4):
                    a = ytmp.get()
                    kb.op("sync", lambda e, a=a, br=br, c4=c4: e.dma_start(out=a.ap, in_=YM[n].ap[br * 512 + c4 * 128: br * 512 + c4 * 128 + 128, tsl]),
                          reads=[YM[n].b], writes=[a.b], dma=True)
                    z = ytmp.get()
                    kb.op("sync", lambda e, z=z, zc=zc, c4=c4: e.dma_start(out=z.ap, in_=FM[n].ap[zc + c4 * 128: zc + c4 * 128 + 128, tsl]),
                          reads=[FM[n].b], writes=[z.b], dma=True)
                    dst = yT_t.ap[:, br * 4 + c4, :]
                    if br == 0:
                        s_ = ytmp.get()
                        kb.op("sync", lambda e, s_=s_, c4=c4: e.dma_start(out=s_.ap, in_=FM[n].ap[C_AO + c4 * 128: C_AO + c4 * 128 + 128, tsl]),
                              reads=[FM[n].b], writes=[s_.b], dma=True)
                        kb.op("gpsimd", lambda e, z=z, s_=s_: e.tensor_tensor(out=z.ap, in0=z.ap, in1=s_.ap, op=ALU.mult), reads=[z.b, s_.b], writes=[z.b])
                    kb.op("vector", lambda e, a=a, z=z, dst=dst: e.tensor_tensor(out=dst, in0=a.ap, in1=z.ap, op=ALU.mult), reads=[a.b, z.b], writes=[yT_t.b])
            ul = uld
            for c4 in range(4):
                kb.op("sync", lambda e, c4=c4: e.dma_start(out=ul[c4].ap, in_=YM[n].ap[1536 + c4 * 128: 1536 + c4 * 128 + 128, tsl]),
                      reads=[YM[n].b], writes=[ul[c4].b], dma=True)
            for tt in range(2):
                cs_ = slice(tt * 512, (tt + 1) * 512)
                p1 = nps()
                p2 = nps()
                for c4 in range(4):
                    q2 = usq.get()
                    kb.op("gpsimd", lambda e, q2=q2, c4=c4, cs_=cs_: e.tensor_tensor(out=q2.ap, in0=ul[c4].ap[:, cs_], in1=ul[c4].ap[:, cs_], op=ALU.mult),
                          reads=[ul[c4].b], writes=[q2.b])
                    kb.op("tensor", lambda e, p1=p1, c4=c4, cs_=cs_: e.matmul(p1.ap, lhsT=ones_b.ap, rhs=ul[c4].ap[:, cs_], start=(c4 == 0), stop=(c4 == 3)),
                          reads=[ones_b.b, ul[c4].b], writes=[p1.b])
                    kb.op("tensor", lambda e, p2=p2, q2=q2, c4=c4: e.matmul(p2.ap, lhsT=ones_b.ap, rhs=q2.ap, start=(c4 == 0), stop=(c4 == 3)),
                          reads=[ones_b.b, q2.b], writes=[p2.b])
                mean = stat.get()
                rstd = stat.get()
                m2 = stat.get()
                kb.op("scalar", lambda e, mean=mean, p1=p1: e.activation(out=mean.ap, in_=p1.ap, func=AF.Copy, scale=1.0 / 512), reads=[p1.b], writes=[mean.b])
                kb.op("vector", lambda e, mean=mean, m2=m2: e.tensor_tensor(out=m2.ap, in0=mean.ap, in1=mean.ap, op=ALU.mult), reads=[mean.b], writes=[m2.b])
                kb.op("vector", lambda e, rstd=rstd, p2=p2, m2=m2: e.scalar_tensor_tensor(out=rstd.ap, in0=p2.ap, scalar=1.0 / 512, in1=m2.ap, op0=ALU.mult, op1=ALU.subtract),
                      reads=[p2.b, m2.b], writes=[rstd.b])
                kb.op("scalar", lambda e, rstd=rstd: e.activation(out=rstd.ap, in_=rstd.ap, func=AF.Sqrt, bias=eps_c.ap[:, 0:1]), reads=[rstd.b, eps_c.b], writes=[rstd.b])
                kb.op("vector", lambda e, rstd=rstd: e.reciprocal(out=rstd.ap, in_=rstd.ap), reads=[rstd.b], writes=[rstd.b])
                for c4 in range(4):
                    t1 = tmp_pool.get()
                    kb.op("vector", lambda e, t1=t1, c4=c4, mean=mean, cs_=cs_: e.tensor_tensor(out=t1.ap, in0=ul[c4].ap[:, cs_], in1=mean.ap, op=ALU.subtract),
                          reads=[ul[c4].b, mean.b], writes=[t1.b])
                    kb.op("gpsimd", lambda e, t1=t1, rstd=rstd: e.tensor_tensor(out=t1.ap, in0=t1.ap, in1=rstd.ap, op=ALU.mult), reads=[t1.b, rstd.b], writes=[t1.b])
                    kb.op("scalar", lambda e, t1=t1, c4=c4: e.activation(out=t1.ap, in_=t1.ap, func=AF.Silu, bias=lnb.ap[:, c4:c4 + 1], scale=lnw.ap[:, c4:c4 + 1]),
                          reads=[t1.b, lnw.b, lnb.b], writes=[t1.b])
                    z = usq.get()
                    kb.op("sync", lambda e, z=z, c4=c4, tt=tt: e.dma_start(out=z.ap, in_=FM[n].ap[C_DZ + c4 * 128: C_DZ + c4 * 128 + 128, g * 1024 + tt * 512: g * 1024 + tt * 512 + 512]),
                          reads=[FM[n].b], writes=[z.b], dma=True)
                    kb.op("vector", lambda e, t1=t1, z=z, c4=c4, cs_=cs_: e.tensor_tensor(out=yT_t.ap[:, 12 + c4, cs_], in0=t1.ap, in1=z.ap, op=ALU.mult),
                          reads=[t1.b, z.b], writes=[yT_t.b])
            for mg in range(8):
                accs = [acc_pool.get() for _ in range(4)]
                for i in range(4):
                    wt, wb_ = pf3.get(pfi[0])
                    pfi[0] += 1
                    for mc in range(2):
                        m = mg * 2 + mc
                        for tt in range(2):
                            cs_ = slice(tt * 512, (tt + 1) * 512)
                            acc = accs[mc * 2 + tt]
                            pa = nps()
                            for k in range(KC):
                                kb.op("tensor", lambda e, pa=pa, k=k, mc=mc, cs_=cs_, wt=wt: e.matmul(pa.ap, lhsT=wt.ap[:, k, mc * 128:(mc + 1) * 128], rhs=hT.ap[:, k, cs_],
                                                                                                 start=(k == 0), stop=(k == KC - 1)),
                                      reads=[wt.b, hT.b], writes=[pa.b])
                            pb_ = nps()
                            for k in range(4):
                                kb.op("tensor", lambda e, pb_=pb_, i=i, k=k, mc=mc, cs_=cs_, wb_=wb_: e.matmul(pb_.ap, lhsT=wb_.ap[:, k, mc * 128:(mc + 1) * 128], rhs=yT_t.ap[:, i * 4 + k, cs_],
                                                                                                          start=(k == 0), stop=(k == 3)),
                                      reads=[wb_.b, yT_t.b], writes=[pb_.b])
                            sg = sg_pool.get()
                            kb.op("scalar", lambda e, sg=sg, pa=pa: e.activation(out=sg.ap, in_=pa.ap, func=AF.Sigmoid), reads=[pa.b], writes=[sg.b])
                            if i == 0:
                                kb.op("vector", lambda e, acc=acc, sg=sg, pb_=pb_: e.tensor_tensor(out=acc.ap, in0=pb_.ap, in1=sg.ap, op=ALU.mult),
                                      reads=[pb_.b, sg.b], writes=[acc.b])
                            else:
                                kb.op("vector", lambda e, sg=sg, pb_=pb_: e.tensor_tensor(out=sg.ap, in0=pb_.ap, in1=sg.ap, op=ALU.mult),
                                      reads=[pb_.b, sg.b], writes=[sg.b])
                                if i < 3:
                                    kb.op("gpsimd", lambda e, acc=acc, sg=sg: e.tensor_tensor(out=acc.ap, in0=acc.ap, in1=sg.ap, op=ALU.add),
                                          reads=[acc.b, sg.b], writes=[acc.b])
                                else:
                                    kb.op("gpsimd", lambda e, acc=acc, sg=sg, m=m, cs_=cs_: e.tensor_tensor(out=mT_t.ap[:, m, cs_], in0=acc.ap, in1=sg.ap, op=ALU.add),
                                          reads=[acc.b, sg.b], writes=[mT_t.b])
            for nn in range(8):
                wt = pf3.get(pfi[0])
                pfi[0] += 1
                kb.op("sync", lambda e, nn=nn: e.dma_start(out=gsl.ap[:, 0:256], in_=modrow.ap[l * nseq + si:l * nseq + si + 1, nn * 256:(nn + 1) * 256].broadcast_to([128, 256])),
                      reads=[modrow.b], writes=[gsl.b], dma=True)
                for sub in range(8):
                    t0 = g * 1024 + sub * 128
                    p = nps()
                    for k in range(KC):
                        kb.op("tensor", lambda e, p=p, wt=wt, k=k, sub=sub: e.matmul(p.ap[:, 0:256], lhsT=mT_t.ap[:, k, sub * 128:(sub + 1) * 128], rhs=wt.ap[:, k, :],
                                                                               start=(k == 0), stop=(k == KC - 1)),
                              reads=[wt.b, mT_t.b], writes=[p.b])
                    xo = xo_pool.get()
                    kb.op("sync", lambda e, xo=xo, t0=t0, nn=nn: e.dma_start(out=xo.ap[:, 0:256], in_=xsrc.ap[t0:t0 + 128, nn * 256:(nn + 1) * 256]), reads=[xsrc.b], writes=[xo.b], dma=True)
                    t1 = tmp_pool.get()
                    kb.op("vector", lambda e, t1=t1, p=p: e.tensor_tensor(out=t1.ap[:, 0:256], in0=p.ap[:, 0:256], in1=gsl.ap[:, 0:256], op=ALU.mult),
                          reads=[p.b, gsl.b], writes=[t1.b])
                    kb.op("gpsimd", lambda e, t1=t1, xo=xo: e.tensor_tensor(out=xo.ap[:, 0:256], in0=xo.ap[:, 0:256], in1=t1.ap[:, 0:256], op=ALU.add), reads=[xo.b, t1.b], writes=[xo.b])
                    kb.op("sync", lambda e, xo=xo, t0=t0, nn=nn: e.dma_start(out=xdst.ap[t0:t0 + 128, nn * 256:(nn + 1) * 256], in_=xo.ap[:, 0:256]), reads=[xo.b], writes=[xdst.b], dma=True)
        for g in range(T // 1024):
            do_group(g)
        kb.barrier()
        aoff[0] = PERSIST

    def wt_get(l, kind, idx, c0, ncols):
        key = (l, kind, idx, c0, ncols)
        if key in WT:
            return WT[key]
        nk = 4 if kind == "br" else KC
        t = dscr(f"wt_{l}_{kind}_{idx}_{c0}_{ncols}", [128, nk * ncols])
        if kind == "in":
            src = w_in.ap[l, :, c0:c0 + ncols]
        elif kind == "br":
            src = w_branch.ap[l, idx, :, c0:c0 + ncols]
        else:
            src = w_out.ap[l, :, c0:c0 + ncols]
        WSRC[key] = src
        WT[key] = (t, nk)
        return WT[key]

    P1_TILES = []
    for (c0_, c1_) in [(0, 1024), (1536, 2560), (2576, 3600), (4112, 5264), (5392, 7440)]:
        for w0_ in range(c0_, c1_, 512):
            P1_TILES.append((w0_, min(512, c1_ - w0_)))
    for (c0_, c1_) in [(512, 1536), (3600, 4112), (5264, 5392)]:
        for w0_ in range(c0_, c1_, 512):
            P1_TILES.append((w0_, min(512, c1_ - w0_)))
    P1_TILES.append((C_AG, 16))

    def convert_tiles(keys):
        st32 = Pool("cv32", [128, KC, 512], F32, 2)
        st16 = Pool("cv16", [128, KC, 512], BF16, 2)
        for ci, key in enumerate(keys):
            (l, kind, idx, c0, ncols) = key
            t, nk = wt_get(l, kind, idx, c0, ncols)
            src = WSRC[key]
            a = st32.get()
            b = st16.get()
            kb.op("sync", lambda e, a=a, src=src, nk=nk, ncols=ncols: e.dma_start(out=a.ap[:, 0:nk, 0:ncols], in_=src.rearrange("(k p) n -> p k n", p=128)),
                  writes=[a.b], dma=True)
            eng = "gpsimd" if ci % 3 != 2 else "vector"
            kb.op(eng, lambda e, a=a, b=b, nk=nk, ncols=ncols: e.tensor_copy(out=b.ap[:, 0:nk, 0:ncols], in_=a.ap[:, 0:nk, 0:ncols]), reads=[a.b], writes=[b.b])
            kb.op("scalar", lambda e, b=b, t=t, nk=nk, ncols=ncols: e.dma_start(out=t.ap.rearrange("p (k n) -> p k n", k=nk), in_=b.ap[:, 0:nk, 0:ncols]),
                  reads=[b.b], writes=[t.b], dma=True)
        kb.barrier()
        aoff[0] = PERSIST

    def p1_keys(l):
        return [(l, "in", 0, c0, nco) for (c0, nco) in P1_TILES]

    def p3_keys(l):
        ks = []
        for mg in range(8):
            for i in range(4):
                ks.append((l, "in", 0, C_MG + i * 2048 + mg * 256, 256))
                ks.append((l, "br", i, mg * 256, 256))
        for nn in range(8):
            ks.append((l, "out", 0, nn * 256, 256))
        return ks

    WQ = "scalar"

    def wload(dst, dst_b, l, kind, idx, c0, ncols):
        t, nk = wt_get(l, kind, idx, c0, ncols)
        kb.op(WQ, lambda e: e.dma_start(out=dst[:, 0:nk, 0:ncols], in_=t.ap.rearrange("p (k n) -> p k n", k=nk)),
              reads=[t.b], writes=[dst_b], dma=True)

    class Prefetch:
        def __init__(self, loaders, ahead=2):
            self.loaders, self.ahead, self.tiles, self.issued = loaders, ahead, {}, 0

        def get(self, i):
            while self.issued <= min(i + self.ahead, len(self.loaders) - 1):
                self.tiles[self.issued] = self.loaders[self.issued]()
                self.issued += 1
            return self.tiles.pop(i)

    env = dict(locals())
    MIX = build_mixers(env)

    convert_tiles(p1_keys(0))
    adaln()
    MIX.gqa_tables()
    kb.barrier()
    aoff[0] = PERSIST
    for l in range(depth):
        for si, (n, T) in enumerate(seqs):
            xsrc = X[n] if l == 0 else X1[n]
            phase1(l, si, n, T, xsrc)
        convert_tiles(p3_keys(l) + (p1_keys(l + 1) if l + 1 < depth else []))
        for si, (n, T) in enumerate(seqs):
            MIX(l, si, n, T)
            kb.barrier()
            aoff[0] = PERSIST
        for si, (n, T) in enumerate(seqs):
            xsrc = X[n] if l == 0 else X1[n]
            xdst = Yout[n] if l == depth - 1 else X1[n]
            phase3(l, si, n, T, xsrc, xdst)
    kb.emit()
    st.close()
    return nc


def build_mixers(env):
    g_ = env
    kb, sb, nps, npsb, Pool, copy_op = g_["kb"], g_["sb"], g_["nps"], g_["npsb"], g_["Pool"], g_["copy_op"]
    FM, TMv, TMg, YM = g_["FM"], g_["TMv"], g_["TMg"], g_["YM"]
    ident_b, ident_f, ones_b, ones_f, blk64_b, tri_f, eps_c = (g_[k] for k in ("ident_b", "ident_f", "ones_b", "ones_f", "blk64_b", "tri_f", "eps_c"))
    consts = make_consts()
    npsh = g_["npsh"]

    def interleave(gens):
        gens = list(gens)
        while gens:
            nxt = []
            for g in gens:
                try:
                    next(g)
                    nxt.append(g)
                except StopIteration:
                    pass
            gens = nxt

    def dma(eng, out, in_, reads, writes, slow=False):
        if slow:
            kb.op(eng, lambda e: e.dma_start(out=out, in_=in_, allow_slow_non_contiguous=True), reads=reads, writes=writes, dma=True)
        else:
            kb.op(eng, lambda e: e.dma_start(out=out, in_=in_), reads=reads, writes=writes, dma=True)

    def V(fn, reads, writes, eng="vector"):
        kb.op(eng, fn, reads=reads, writes=writes)

    def MM(out, lhsT, rhs, start, stop, reads, writes):
        kb.op("tensor", lambda e: e.matmul(out, lhsT=lhsT, rhs=rhs, start=start, stop=stop), reads=reads, writes=writes)

    def rsqrt_tile(dst, src_ps, scale, n, reads):
        V(lambda e: e.tensor_scalar(out=dst.ap[:, 0:n], in0=src_ps, scalar1=scale, scalar2=EPS, op0=ALU.mult, op1=ALU.add), reads, [dst.b])
        kb.op("scalar", lambda e: e.activation(out=dst.ap[:, 0:n], in_=dst.ap[:, 0:n], func=AF.Sqrt), reads=[dst.b], writes=[dst.b])
        V(lambda e: e.reciprocal(out=dst.ap[:, 0:n], in_=dst.ap[:, 0:n]), [dst.b], [dst.b])

    def headnorm_fm(dst, src, T, wcol, lhs_ones, scale_div, extra_scale):
        sq = Pool("hn_sq", [128, 512], BF16, 2)
        rs = Pool("hn_rs", [128, 512], F32, 2)
        for t0 in range(0, T, 512):
            q2 = sq.get()
            V(lambda e, q2=q2, t0=t0: e.tensor_tensor(out=q2.ap, in0=src.ap[:, t0:t0 + 512], in1=src.ap[:, t0:t0 + 512], op=ALU.mult), [src.b], [q2.b], eng="gpsimd")
            p = nps()
            MM(p.ap, lhs_ones.ap, q2.ap, True, True, [lhs_ones.b, q2.b], [p.b])
            r = rs.get()
            rsqrt_tile(r, p.ap, 1.0 / scale_div, 512, [p.b])
            V(lambda e, r=r, t0=t0: e.scalar_tensor_tensor(out=dst.ap[:, t0:t0 + 512], in0=src.ap[:, t0:t0 + 512], scalar=wcol, in1=r.ap, op0=ALU.mult, op1=ALU.mult),
              [src.b, r.b], [dst.b])

    def conv(l, n, T):
        conv_w, conv_b = g_["conv_w"], g_["conv_b"]
        a_t = sb("cv_a", [128, T], BF16)
        g_t = sb("cv_g", [128, T], BF16)
        up = sb("cv_u", [128, T + 32], BF16)
        dg = sb("cv_dg", [128, 31, 128], BF16)
        cw = sb("cv_w", [128, 31])
        cb = sb("cv_b", [128, 1])
        osb = Pool("cv_o", [128, 512], BF16, 3)
        for c4 in range(4):
            dma("sync", cw.ap, conv_w.ap[l, :, c4 * 128:(c4 + 1) * 128].rearrange("w p -> p w"), [], [cw.b], slow=True)
            dma("sync", cb.ap, conv_b.ap[l:l + 1, c4 * 128:(c4 + 1) * 128].rearrange("o p -> p o"), [], [cb.b], slow=True)
            for w in range(31):
                V(lambda e, w=w: e.tensor_scalar(out=dg.ap[:, w, :], in0=ident_f.ap, scalar1=cw.ap[:, w:w + 1], scalar2=None, op0=ALU.mult), [ident_f.b, cw.b], [dg.b],
                  eng=("vector" if w % 2 else "gpsimd"))
            for t0 in range(0, T, 1024):
                dma("sync", a_t.ap[:, t0:t0 + 1024], FM[n].ap[C_DA + c4 * 128:C_DA + c4 * 128 + 128, t0:t0 + 1024], [FM[n].b], [a_t.b])
                dma("sync", g_t.ap[:, t0:t0 + 1024], FM[n].ap[C_DG + c4 * 128:C_DG + c4 * 128 + 128, t0:t0 + 1024], [FM[n].b], [g_t.b])
            V(lambda e: e.memset(up.ap[:, 0:16], 0.0), [], [up.b])
            V(lambda e: e.memset(up.ap[:, T + 15:T + 32], 0.0), [], [up.b])
            kb.op("scalar", lambda e: e.activation(out=g_t.ap, in_=g_t.ap, func=AF.Sigmoid), reads=[g_t.b], writes=[g_t.b])
            V(lambda e: e.tensor_tensor(out=up.ap[:, 15:15 + T], in0=a_t.ap, in1=g_t.ap, op=ALU.mult), [a_t.b, g_t.b], [up.b])
            for t0 in range(0, T, 512):
                p = nps()
                for w in range(31):
                    MM(p.ap, dg.ap[:, w, :], up.ap[:, t0 + w:t0 + w + 512], w == 0, w == 30, [dg.b, up.b], [p.b])
                o = osb.get()
                kb.op("scalar", lambda e, o=o, p=p: e.activation(out=o.ap, in_=p.ap, func=AF.Identity, bias=cb.ap[:, 0:1]), reads=[p.b, cb.b], writes=[o.b])
                dma("sync", YM[n].ap[1536 + c4 * 128:1536 + c4 * 128 + 128, t0:t0 + 512], o.ap, [o.b], [YM[n].b])

    gq_state = {}

    def gqa_tables():
        rel_bias, k_gqa_m = g_["rel_bias"], g_["k_gqa_m"]
        EB = g_["EB"]
        rbb = sb("gq_rbb", [128, 256])
        val = sb("gq_val", [128, 3, 128])
        mk = Pool("gq_mk", [128, 128], F32, 3)
        dma("sync", rbb.ap, rel_bias.ap.rearrange("b h -> (b h)").unsqueeze(0).broadcast_to([128, 256]), [], [rbb.b])
        V(lambda e: e.memset(EB.ap, 0.0), [], [EB.b])
        V(lambda e: e.memset(val.ap, 0.0), [], [val.b])
        gm = consts["gqa_m"]
        for o in range(3):
            for b in range(32):
                if not gm[o, b].any():
                    continue
                m = mk.get()
                dma("sync", m.ap, k_gqa_m.ap[o, b], [], [m.b])
                V(lambda e, m=m, o=o: e.tensor_tensor(out=val.ap[:, o, :], in0=val.ap[:, o, :], in1=m.ap, op=ALU.add), [m.b, val.b], [val.b], eng="gpsimd")
                for h in range(8):
                    V(lambda e, m=m, o=o, b=b, h=h: e.scalar_tensor_tensor(out=EB.ap[:, o, h, :], in0=m.ap, scalar=rbb.ap[:, b * 8 + h:b * 8 + h + 1], in1=EB.ap[:, o, h, :],
                                                                         op0=ALU.mult, op1=ALU.add), [m.b, rbb.b, EB.b], [EB.b])
        kb.op("scalar", lambda e: e.activation(out=EB.ap, in_=EB.ap, func=AF.Exp), reads=[EB.b], writes=[EB.b])
        for o in range(3):
            for h in range(8):
                V(lambda e, o=o, h=h: e.tensor_tensor(out=EB.ap[:, o, h, :], in0=EB.ap[:, o, h, :], in1=val.ap[:, o, :], op=ALU.mult), [EB.b, val.b], [EB.b])

    def gqa(l, n, T):
        swa_q_norm, swa_k_norm, swa_sink = g_["swa_q_norm"], g_["swa_k_norm"], g_["swa_sink"]
        EB = g_["EB"]
        nb = T // 128
        kraw = sb("gq_kraw", [128, T], BF16)
        kn = sb("gq_kn", [128, T], BF16)
        qraw = sb("gq_qraw", [128, T], BF16)
        qn = sb("gq_qn", [128, T], BF16)
        vp = [sb(f"gq_vp{i}", [128, nb, 128], BF16) for i in range(2)]
        vraw = sb("gq_vraw", [128, nb, 64], BF16)
        on = [sb(f"gq_on{i}", [128, 128], BF16) for i in range(2)]
        wq = sb("gq_wq", [128, 1])
        wk = sb("gq_wk", [128, 1])
        sk = sb("gq_sk", [128, 4])
        Pf = Pool("gq_pf", [128, 2, 384], F32, 4)
        Pb = Pool("gq_pb", [128, 2, 384], BF16, 4)
        rc = Pool("gq_rc", [128, 128], F32, 4)
        ost = Pool("gq_ost", [128, 1024], BF16, 2)
        for i in range(2):
            V(lambda e, i=i: e.memset(on[i].ap, 0.0), [], [on[i].b])
            V(lambda e, i=i: e.memset(on[i].ap[:, 64 * i:64 * i + 64], 1.0), [], [on[i].b])
            V(lambda e, i=i: e.memset(vp[i].ap, 0.0), [], [vp[i].b], eng="gpsimd")
        for half in range(2):
            dma("sync", wq.ap[64 * half:64 * half + 64, :], swa_q_norm.ap[l:l + 1, :].rearrange("o p -> p o"), [], [wq.b], slow=True)
            dma("sync", wk.ap[64 * half:64 * half + 64, :], swa_k_norm.ap[l:l + 1, :].rearrange("o p -> p o"), [], [wk.b], slow=True)
        V(lambda e: e.tensor_scalar(out=wq.ap, in0=wq.ap, scalar1=0.125, scalar2=None, op0=ALU.mult), [wq.b], [wq.b])
        for kvh in range(2):
            for half in range(2):
                for t0 in range(0, T, 2048):
                    tw = min(2048, T - t0)
                    dma("sync", kraw.ap[64 * half:64 * half + 64, t0:t0 + tw], FM[n].ap[C_CK + 64 * kvh:C_CK + 64 * kvh + 64, t0:t0 + tw], [FM[n].b], [kraw.b])
            headnorm_fm(kn, kraw, T, wk.ap[:, 0:1], blk64_b, 64.0, 1.0)
            dma("sync", vraw.ap, TMv[n].ap[:, 1536 + 64 * kvh:1536 + 64 * kvh + 64].rearrange("(b p) c -> p b c", p=128), [TMv[n].b], [vraw.b])
            for i in range(2):
                V(lambda e, i=i: e.tensor_copy(out=vp[i].ap[:, :, 64 * i:64 * i + 64], in_=vraw.ap), [vraw.b], [vp[i].b])
            for hp in range(2):
                h0 = kvh * 4 + hp * 2
                for half in range(2):
                    dma("sync", sk.ap[64 * half:64 * half + 64, 0:1], swa_sink.ap[l:l + 1, h0 + half:h0 + half + 1].broadcast_to([64, 1]), [], [sk.b])
                kb.op("scalar", lambda e: e.activation(out=sk.ap[:, 1:2], in_=sk.ap[:, 0:1], func=AF.Exp), reads=[sk.b], writes=[sk.b])
                for t0 in range(0, T, 2048):
                    tw = min(2048, T - t0)
                    dma("sync", qraw.ap[:, t0:t0 + tw], FM[n].ap[C_CQ + 64 * h0:C_CQ + 64 * h0 + 128, t0:t0 + tw], [FM[n].b], [qraw.b])
                headnorm_fm(qn, qraw, T, wq.ap[:, 0:1], blk64_b, 64.0, 1.0)
                ostm = {}

                def qb_gen(qb, h0=h0, kvh=kvh, hp=hp, ostm=ostm):
                    if qb // 8 not in ostm:
                        ostm[qb // 8] = ost.get()
                    o_t = ostm[qb // 8]
                    os_ = [o for o in range(3) if 0 <= qb + o - 1 < nb]
                    o0, o1 = os_[0], os_[-1] + 1
                    pS = [nps(), nps()]
                    for hh in range(2):
                        for o in os_:
                            kbk = qb + o - 1
                            MM(pS[hh].ap[:, o * 128:(o + 1) * 128], kn.ap[64 * hh:64 * hh + 64, kbk * 128:(kbk + 1) * 128], qn.ap[64 * hh:64 * hh + 64, qb * 128:(qb + 1) * 128],
                               True, True, [kn.b, qn.b], [pS[hh].b])
                    yield
                    pf = Pf.get()
                    pb = Pb.get()
                    for hh in range(2):
                        kb.op("scalar", lambda e, hh=hh, pf=pf, pS=pS, o0=o0, o1=o1: e.activation(out=pf.ap[:, hh, o0 * 128:o1 * 128], in_=pS[hh].ap[:, o0 * 128:o1 * 128], func=AF.Exp),
                              reads=[pS[hh].b], writes=[pf.b])
                    yield
                    for hh in range(2):
                        V(lambda e, hh=hh, pf=pf, pb=pb, o0=o0, o1=o1, h0=h0: e.tensor_tensor(out=pb.ap[:, hh, o0 * 128:o1 * 128].rearrange("p (o q) -> p o q", q=128),
                                                                                              in0=pf.ap[:, hh, o0 * 128:o1 * 128].rearrange("p (o q) -> p o q", q=128),
                                                                                              in1=EB.ap[:, o0:o1, h0 + hh, :], op=ALU.mult),
                          [pf.b, EB.b], [pb.b], eng=("vector" if hh else "gpsimd"))
                    yield
                    pOD = nps()
                    tot = 2 * len(os_)
                    cnt = 0
                    for hh in range(2):
                        for o in os_:
                            kbk = qb + o - 1
                            MM(pOD.ap[:, 0:128], vp[hh].ap[:, kbk, :], pb.ap[:, hh, o * 128:(o + 1) * 128], cnt == 0, cnt == tot - 1, [vp[hh].b, pb.b], [pOD.b])
                            cnt += 1
                    cnt = 0
                    for hh in range(2):
                        for o in os_:
                            MM(pOD.ap[:, 128:256], on[hh].ap, pb.ap[:, hh, o * 128:(o + 1) * 128], cnt == 0, cnt == tot - 1, [on[hh].b, pb.b], [pOD.b])
                            cnt += 1
                    yield
                    r = rc.get()
                    V(lambda e, r=r, pOD=pOD: e.tensor_scalar(out=r.ap, in0=pOD.ap[:, 128:256], scalar1=sk.ap[:, 1:2], scalar2=None, op0=ALU.add), [pOD.b, sk.b], [r.b])
                    V(lambda e, r=r: e.reciprocal(out=r.ap, in_=r.ap), [r.b], [r.b])
                    yield
                    V(lambda e, r=r, pOD=pOD, o_t=o_t, qb=qb: e.tensor_tensor(out=o_t.ap[:, (qb % 8) * 128:(qb % 8) * 128 + 128], in0=pOD.ap[:, 0:128], in1=r.ap, op=ALU.mult),
                      [pOD.b, r.b], [o_t.b])
                    if qb % 8 == 7:
                        row = 1024 + 128 * (kvh * 2 + hp)
                        dma("sync", YM[n].ap[row:row + 128, (qb - 7) * 128:(qb + 1) * 128], o_t.ap, [o_t.b], [YM[n].b])

                for q0 in range(0, nb, GQ_W):
                    interleave([qb_gen(q) for q in range(q0, min(q0 + GQ_W, nb))])

    def na(l, n, T):
        na_q_norm, na_k_norm, na_rpb, k_na_m = g_["na_q_norm"], g_["na_k_norm"], g_["na_rpb"], g_["k_na_m"]
        rows = T // 64
        nb = T // 128
        Mc = sb("na_mc", [128, 31, 64])
        RP = sb("na_rp", [128, 14, 31])
        ET = sb("na_et", [128, 14, 64])
        okm = sb("na_ok", [128, 64])
        tmpE = sb("na_te", [128, 14, 64])
        qraw = sb("na_qraw", [128, T], BF16)
        kraw = sb("na_kraw", [128, T], BF16)
        qn = sb("na_qn", [128, T], BF16)
        kn = sb("na_kn", [128, T], BF16)
        vA = sb("na_vA", [128, nb, 128], BF16)
        vB = sb("na_vB", [128, nb, 128], BF16)
        wq = sb("na_wq", [128, 1])
        wk = sb("na_wk", [128, 1])
        Pf = Pool("na_pf", [128, 4, 64], F32, 6)
        Pb = Pool("na_pb", [128, 4, 64], BF16, 6)
        rc = Pool("na_rc", [128, 64], F32, 6)
        ost = Pool("na_ost", [128, 1024], BF16, 2)
        dma("sync", Mc.ap, k_na_m.ap.rearrange("d p q -> p d q"), [], [Mc.b])
        dma("sync", wq.ap, na_q_norm.ap[l:l + 1, :].rearrange("o p -> p o"), [], [wq.b], slow=True)
        dma("sync", wk.ap, na_k_norm.ap[l:l + 1, :].rearrange("o p -> p o"), [], [wk.b], slow=True)
        V(lambda e: e.tensor_scalar(out=wq.ap, in0=wq.ap, scalar1=float(128 ** -0.5), scalar2=None, op0=ALU.mult), [wq.b], [wq.b])
        V(lambda e: e.memset(okm.ap, 0.0), [], [okm.b])
        for dc in range(31):
            V(lambda e, dc=dc: e.tensor_tensor(out=okm.ap, in0=okm.ap, in1=Mc.ap[:, dc, :], op=ALU.add), [Mc.b, okm.b], [okm.b])
        for h in range(4):
            for half in range(2):
                dma("sync", RP.ap[64 * half:64 * half + 64, :, :], na_rpb.ap[l, h:h + 1, half:half + 14, :].broadcast_to([64, 14, 31]), [], [RP.b])
            V(lambda e: e.memset(ET.ap, 0.0), [], [ET.b])
            for dc in range(31):
                V(lambda e, dc=dc: e.tensor_tensor(out=tmpE.ap, in0=Mc.ap[:, dc, :].unsqueeze(1).broadcast_to([128, 14, 64]),
                                                    in1=RP.ap[:, :, dc].unsqueeze(2).broadcast_to([128, 14, 64]), op=ALU.mult), [Mc.b, RP.b], [tmpE.b])
                V(lambda e: e.tensor_tensor(out=ET.ap, in0=ET.ap, in1=tmpE.ap, op=ALU.add), [tmpE.b, ET.b], [ET.b], eng="gpsimd")
            kb.op("scalar", lambda e: e.activation(out=ET.ap, in_=ET.ap, func=AF.Exp), reads=[ET.b], writes=[ET.b])
            V(lambda e: e.tensor_tensor(out=ET.ap, in0=ET.ap, in1=okm.ap.unsqueeze(1).broadcast_to([128, 14, 64]), op=ALU.mult), [ET.b, okm.b], [ET.b])
            for t0 in range(0, T, 2048):
                tw = min(2048, T - t0)
                dma("sync", qraw.ap[:, t0:t0 + tw], FM[n].ap[C_BQ + 128 * h:C_BQ + 128 * h + 128, t0:t0 + tw], [FM[n].b], [qraw.b])
                dma("sync", kraw.ap[:, t0:t0 + tw], FM[n].ap[C_BK + 128 * h:C_BK + 128 * h + 128, t0:t0 + tw], [FM[n].b], [kraw.b])
            headnorm_fm(qn, qraw, T, wq.ap[:, 0:1], ones_b, 128.0, 1.0)
            headnorm_fm(kn, kraw, T, wk.ap[:, 0:1], ones_b, 128.0, 1.0)
            dma("sync", vA.ap, TMv[n].ap[:, 1024 + 128 * h:1024 + 128 * h + 128].rearrange("(b p) c -> p b c", p=128), [TMv[n].b], [vA.b])
            dma("sync", vB.ap[:, 0:nb - 1, :], TMv[n].ap[64:T - 64, 1024 + 128 * h:1024 + 128 * h + 128].rearrange("(b p) c -> p b c", p=128), [TMv[n].b], [vB.b])
            ostm = {}

            def row_gen(r):
                if r // 16 not in ostm:
                    ostm[r // 16] = ost.get()
                o_t = ostm[r // 16]
                start = min(max(r - 4, 0), rows - 8)
                off = start - r + 7
                pS = nps()
                for j2 in range(4):
                    k0 = 64 * (start + 2 * j2)
                    MM(pS.ap[:, j2 * 64:(j2 + 1) * 64], kn.ap[:, k0:k0 + 128], qn.ap[:, 64 * r:64 * r + 64], True, True, [kn.b, qn.b], [pS.b])
                yield
                pf = Pf.get()
                pb = Pb.get()
                kb.op("scalar", lambda e, pf=pf, pS=pS: e.activation(out=pf.ap, in_=pS.ap[:, 0:256].rearrange("p (j q) -> p j q", q=64), func=AF.Exp), reads=[pS.b], writes=[pf.b])
                yield
                V(lambda e, pf=pf, pb=pb, off=off: e.tensor_tensor(out=pb.ap, in0=pf.ap, in1=ET.ap[:, off:off + 7:2, :], op=ALU.mult), [pf.b, ET.b], [pb.b],
                  eng=("vector" if r % 2 else "gpsimd"))
                yield
                pOD = nps()
                for j2 in range(4):
                    rr = start + 2 * j2
                    vt = vA.ap[:, rr // 2, :] if rr % 2 == 0 else vB.ap[:, (rr - 1) // 2, :]
                    vb_ = vA.b if rr % 2 == 0 else vB.b
                    MM(pOD.ap[:, 0:64], vt, pb.ap[:, j2, :], j2 == 0, j2 == 3, [vb_, pb.b], [pOD.b])
                for j2 in range(4):
                    MM(pOD.ap[:, 64:128], ones_b.ap, pb.ap[:, j2, :], j2 == 0, j2 == 3, [ones_b.b, pb.b], [pOD.b])
                yield
                rcp = rc.get()
                V(lambda e, rcp=rcp, pOD=pOD: e.reciprocal(out=rcp.ap, in_=pOD.ap[:, 64:128]), [pOD.b], [rcp.b])
                yield
                V(lambda e, rcp=rcp, pOD=pOD, o_t=o_t, r=r: e.tensor_tensor(out=o_t.ap[:, (r % 16) * 64:(r % 16) * 64 + 64], in0=pOD.ap[:, 0:64], in1=rcp.ap, op=ALU.mult),
                  [pOD.b, rcp.b], [o_t.b])
                if r % 16 == 15:
                    dma("sync", YM[n].ap[512 + 128 * h:512 + 128 * h + 128, (r - 15) * 64:(r + 1) * 64], o_t.ap, [o_t.b], [YM[n].b])

            for r0 in range(0, rows, NA_W):
                interleave([row_gen(r) for r in range(r0, min(r0 + NA_W, rows))])

    def mlstm(l, n, T):
        mlstm_norm_w = g_["mlstm_norm_w"]
        nb = T // 128
        qT = sb("ml_qT", [128, T], BF16)
        kT = sb("ml_kT", [128, T], BF16)
        ktm = sb("ml_ktm", [128, nb, 128], BF16)
        vaug = sb("ml_va", [128, nb, 132], BF16)
        gt = sb("ml_gt", [128, nb, 16])
        hfw = sb("ml_hfw", [128, nb, 128])
        nwb = sb("ml_nw", [128, 128])
        lf = sb("ml_lf", [128, 2, nb])
        bcs = sb("ml_b", [128, 2, nb])
        gb = sb("ml_g", [128, 2, nb])
        beta = sb("ml_beta", [128, 2, nb])
        gam = sb("ml_gam", [128, 2, nb])
        emb = sb("ml_emb", [128, 2, nb])
        eg = sb("ml_eg", [128, 2, nb])
        Cst = sb("ml_C", [128, 132])
        Cb = sb("ml_Cb", [128, 132], BF16)
        STp = Pool("ml_st", [128, 128], BF16, 3)
        kgp = Pool("ml_kg", [128, 128], BF16, 3)
        sm = Pool("ml_sm", [128, 4], F32, 4)
        hs = Pool("ml_hs", [128, 128], F32, 3)
        hb = Pool("ml_hb", [128, 128], BF16, 3)
        jk = sb("ml_jk", [128, 128], BF16)
        ost = Pool("ml_ost", [128, 1024], BF16, 2)
        V(lambda e: e.memset(vaug.ap[:, :, 128:129], 1.0), [], [vaug.b])
        for h in range(4):
            for t0 in range(0, T, 2048):
                tw = min(2048, T - t0)
                dma("sync", qT.ap[:, t0:t0 + tw], FM[n].ap[C_AQ + 128 * h:C_AQ + 128 * h + 128, t0:t0 + tw], [FM[n].b], [qT.b])
                dma("sync", kT.ap[:, t0:t0 + tw], FM[n].ap[C_AK + 128 * h:C_AK + 128 * h + 128, t0:t0 + tw], [FM[n].b], [kT.b])
            dma("sync", ktm.ap, TMv[n].ap[:, 128 * h:128 * h + 128].rearrange("(b p) c -> p b c", p=128), [TMv[n].b], [ktm.b])
            dma("sync", vaug.ap[:, :, 0:128], TMv[n].ap[:, 512 + 128 * h:512 + 128 * h + 128].rearrange("(b p) c -> p b c", p=128), [TMv[n].b], [vaug.b])
            dma("sync", gt.ap, TMg[n].ap.rearrange("(b p) c -> p b c", p=128), [TMg[n].b], [gt.b])
            dma("sync", nwb.ap, mlstm_norm_w.ap[l:l + 1, 128 * h:128 * h + 128].broadcast_to([128, 128]), [], [nwb.b])
            for d, dn in enumerate(("fw", "bw")):
                icol, fcol = 4 * d + h, 8 + 4 * d + h
                kb.op("scalar", lambda e, d=d, fcol=fcol: e.activation(out=lf.ap[:, d, :], in_=gt.ap[:, :, fcol], func=AF.Exp, scale=-1.0), reads=[gt.b], writes=[lf.b])
                V(lambda e, d=d: e.tensor_scalar(out=lf.ap[:, d, :], in0=lf.ap[:, d, :], scalar1=1.0, scalar2=None, op0=ALU.add), [lf.b], [lf.b])
                kb.op("scalar", lambda e, d=d: e.activation(out=lf.ap[:, d, :], in_=lf.ap[:, d, :], func=AF.Ln), reads=[lf.b], writes=[lf.b])
                V(lambda e, d=d: e.tensor_scalar(out=lf.ap[:, d, :], in0=lf.ap[:, d, :], scalar1=-1.0, scalar2=None, op0=ALU.mult), [lf.b], [lf.b])
                p = nps()
                MM(p.ap[:, 0:nb], tri_f[dn].ap, lf.ap[:, d, :], True, True, [tri_f[dn].b, lf.b], [p.b])
                V(lambda e, d=d, p=p: e.tensor_copy(out=bcs.ap[:, d, :], in_=p.ap[:, 0:nb]), [p.b], [bcs.b])
                p2 = nps()
                MM(p2.ap[:, 0:nb], ones_f.ap, lf.ap[:, d, :], True, True, [ones_f.b, lf.b], [p2.b])
                V(lambda e, d=d, p2=p2: e.tensor_copy(out=gb.ap[:, d, :], in_=p2.ap[:, 0:nb]), [p2.b], [gb.b])
                kb.op("scalar", lambda e, d=d: e.activation(out=eg.ap[:, d, :], in_=gb.ap[:, d, :], func=AF.Exp), reads=[gb.b], writes=[eg.b])
                kb.op("scalar", lambda e, d=d: e.activation(out=emb.ap[:, d, :], in_=bcs.ap[:, d, :], func=AF.Exp, scale=-1.0), reads=[bcs.b], writes=[emb.b])
                V(lambda e, d=d, icol=icol: e.tensor_tensor(out=beta.ap[:, d, :], in0=gt.ap[:, :, icol], in1=bcs.ap[:, d, :], op=ALU.subtract), [gt.b, bcs.b], [beta.b])
                kb.op("scalar", lambda e, d=d: e.activation(out=beta.ap[:, d, :], in_=beta.ap[:, d, :], func=AF.Exp), reads=[beta.b], writes=[beta.b])
                V(lambda e, d=d: e.tensor_scalar(out=beta.ap[:, d, :], in0=beta.ap[:, d, :], scalar1=float(128 ** -0.5), scalar2=None, op0=ALU.mult), [beta.b], [beta.b])
                V(lambda e, d=d: e.tensor_tensor(out=gam.ap[:, d, :], in0=beta.ap[:, d, :], in1=eg.ap[:, d, :], op=ALU.mult), [beta.b, eg.b], [gam.b])
            o_t = None
            for d, dn in enumerate(("fw", "bw")):
                order = list(range(nb)) if d == 0 else list(range(nb - 1, -1, -1))
                for ci, c in enumerate(order):
                    csl = slice(c * 128, (c + 1) * 128)
                    pR = nps()
                    MM(pR.ap[:, 0:128], kT.ap[:, csl], qT.ap[:, csl], True, True, [kT.b, qT.b], [pR.b])
                    stt = STp.get()
                    V(lambda e, stt=stt, pR=pR, d=d, c=c, dn=dn: e.scalar_tensor_tensor(out=stt.ap, in0=pR.ap[:, 0:128], scalar=beta.ap[:, d, c:c + 1], in1=tri_f[dn].ap,
                                                                                          op0=ALU.mult, op1=ALU.mult), [pR.b, beta.b, tri_f[dn].b], [stt.b])
                    pX = nps()
                    MM(pX.ap[:, 0:129], stt.ap, vaug.ap[:, c, 0:129], True, ci == 0, [stt.b, vaug.b], [pX.b])
                    if ci > 0:
                        MM(pX.ap[:, 0:129], qT.ap[:, csl], Cb.ap[:, 0:129], False, True, [qT.b, Cb.b], [pX.b])
                    s4 = sm.get()
                    kb.op("scalar", lambda e, s4=s4, pX=pX: e.activation(out=s4.ap[:, 0:1], in_=pX.ap[:, 128:129], func=AF.Abs), reads=[pX.b], writes=[s4.b])
                    V(lambda e, s4=s4, d=d, c=c: e.tensor_tensor(out=s4.ap[:, 0:1], in0=s4.ap[:, 0:1], in1=emb.ap[:, d, c:c + 1], op=ALU.max), [s4.b, emb.b], [s4.b])
                    V(lambda e, s4=s4: e.reciprocal(out=s4.ap[:, 1:2], in_=s4.ap[:, 0:1]), [s4.b], [s4.b])
                    if d == 0:
                        V(lambda e, s4=s4, pX=pX, c=c: e.tensor_scalar(out=hfw.ap[:, c, :], in0=pX.ap[:, 0:128], scalar1=s4.ap[:, 1:2], scalar2=None, op0=ALU.mult),
                          [pX.b, s4.b], [hfw.b])
                    else:
                        if ci % 8 == 0:
                            o_t = ost.get()
                        hsum = hs.get()
                        V(lambda e, s4=s4, pX=pX, c=c, hsum=hsum: e.scalar_tensor_tensor(out=hsum.ap, in0=pX.ap[:, 0:128], scalar=s4.ap[:, 1:2], in1=hfw.ap[:, c, :],
                                                                                          op0=ALU.mult, op1=ALU.add), [pX.b, s4.b, hfw.b], [hsum.b])
                        kb.op("scalar", lambda e, hsum=hsum, s4=s4: e.activation(out=jk.ap, in_=hsum.ap, func=AF.Square, accum_out=s4.ap[:, 2:3]),
                              reads=[hsum.b], writes=[jk.b, s4.b])
                        V(lambda e, s4=s4: e.tensor_scalar(out=s4.ap[:, 3:4], in0=s4.ap[:, 2:3], scalar1=1.0 / 128, scalar2=EPS, op0=ALU.mult, op1=ALU.add), [s4.b], [s4.b])
                        kb.op("scalar", lambda e, s4=s4: e.activation(out=s4.ap[:, 3:4], in_=s4.ap[:, 3:4], func=AF.Sqrt), reads=[s4.b], writes=[s4.b])
                        V(lambda e, s4=s4: e.reciprocal(out=s4.ap[:, 3:4], in_=s4.ap[:, 3:4]), [s4.b], [s4.b])
                        hbt = hb.get()
                        V(lambda e, hsum=hsum, s4=s4, hbt=hbt: e.scalar_tensor_tensor(out=hbt.ap, in0=hsum.ap, scalar=s4.ap[:, 3:4], in1=nwb.ap, op0=ALU.mult, op1=ALU.mult),
                          [hsum.b, s4.b, nwb.b], [hbt.b])
                        pT = npsb()
                        kb.op("tensor", lambda e, pT=pT, hbt=hbt: e.transpose(pT.ap[:, 0:128], hbt.ap, ident_b.ap), reads=[hbt.b, ident_b.b], writes=[pT.b])
                        copy_op("scalar", o_t.ap[:, (c % 8) * 128:(c % 8) * 128 + 128], pT.ap[:, 0:128], [pT.b], [o_t.b])
                        if c % 8 == 0:
                            dma("sync", YM[n].ap[128 * h:128 * h + 128, c * 128:(c + 8) * 128], o_t.ap, [o_t.b], [YM[n].b])
                    if ci < nb - 1:
                        kg = kgp.get()
                        V(lambda e, kg=kg, d=d, c=c: e.tensor_scalar(out=kg.ap, in0=ktm.ap[:, c, :], scalar1=gam.ap[:, d, c:c + 1], scalar2=None, op0=ALU.mult),
                          [ktm.b, gam.b], [kg.b], eng="gpsimd")
                        pC = nps()
                        MM(pC.ap[:, 0:129], kg.ap, vaug.ap[:, c, 0:129], True, True, [kg.b, vaug.b], [pC.b])
                        if ci == 0:
                            V(lambda e, pC=pC: e.tensor_copy(out=Cst.ap[:, 0:129], in_=pC.ap[:, 0:129]), [pC.b], [Cst.b])
                        else:
                            V(lambda e, pC=pC, d=d, c=c: e.scalar_tensor_tensor(out=Cst.ap[:, 0:129], in0=Cst.ap[:, 0:129], scalar=eg.ap[:, d, c:c + 1], in1=pC.ap[:, 0:129],
                                                                                  op0=ALU.mult, op1=ALU.add), [Cst.b, eg.b, pC.b], [Cst.b])
                        copy_op("scalar", Cb.ap[:, 0:129], Cst.ap[:, 0:129], [Cst.b], [Cb.b])

    PERSIST = g_["PERSIST"]
    aoff = g_["aoff"]

    def MIX(l, si, n, T):
        g_["psmode"][0] = 6
        for fn in (conv, gqa, na, mlstm):
            if fn.__name__ in SKIP_MIX:
                continue
            fn(l, n, T)
            kb.barrier()
            aoff[0] = PERSIST
        g_["psmode"][0] = 6
    MIX.gqa_tables = gqa_tables
    return MIX


SKIP_MIX = set()
DEBUG_YM = False


_W_KEYS = ["rel_bias", "norm_w", "w_ada", "b_ada", "w_in", "b_gate", "mlstm_norm_w", "na_q_norm", "na_k_norm", "na_rpb",
           "swa_q_norm", "swa_k_norm", "swa_sink", "conv_w", "conv_b", "conv_ln_w", "conv_ln_b", "w_branch", "w_out"]


def kernel(**inputs):
    xp = np.ascontiguousarray(np.asarray(inputs["x_prompt"], dtype=np.float32))
    xs = np.ascontiguousarray(np.asarray(inputs["x_sample"], dtype=np.float32))
    cp = np.asarray(inputs["c_prompt"], dtype=np.float32)
    cs = np.asarray(inputs["c_sample"], dtype=np.float32)
    TP, TS = xp.shape[1], xs.shape[1]
    depth = np.asarray(inputs["norm_w"]).shape[0]
    nc = build([("P", TP), ("S", TS)], depth)
    consts = make_consts()
    shared = {k: np.ascontiguousarray(np.asarray(inputs[k], dtype=np.float32)) for k in _W_KEYS}
    shared.update({"k_ident": consts["ident"], "k_tri_fw": consts["tri_fw"], "k_tri_bw": consts["tri_bw"], "k_blk64": consts["blk64"],
                   "k_gqa_m": consts["gqa_m"], "k_na_m": consts["na_m"]})
    in_maps = []
    for i in range(8):
        m = dict(shared)
        m["x_P"] = xp[i // 4]
        m["c_P"] = cp[i // 4][None]
        m["x_S"] = xs[i // 2]
        m["c_S"] = cs[i // 2][None]
        in_maps.append(m)
    res = run_bass_kernel_spmd(nc, in_maps, core_ids=list(range(8)))
    yp = np.empty_like(xp)
    ys = np.empty_like(xs)
    qp, qs = TP // 4, TS // 2
    for i in range(8):
        r = res.results[i]
        a = i % 4
        yp[i // 4, a * qp:(a + 1) * qp] = np.asarray(r["y_P"])[a * qp:(a + 1) * qp]
        b = i % 2
        ys[i // 2, b * qs:(b + 1) * qs] = np.asarray(r["y_S"])[b * qs:(b + 1) * qs]
    return (yp, ys)
NA_W = 3
GQ_W = 2
```

```python
import contextlib
import math
import numpy as np
import concourse.bass as bass
import concourse.mybir as mybir
from concourse.bass_utils import run_bass_kernel_spmd

F32 = mybir.dt.float32
BF16 = mybir.dt.bfloat16
ALU = mybir.AluOpType
AF = mybir.ActivationFunctionType

D = 2048
KC = 16
IN_COLS = 15632
EPS = 1e-6
C_AQ, C_AK, C_AV, C_AO, C_AZ, C_AG = 0, 512, 1024, 1536, 2048, 2560
C_BQ, C_BK, C_BV, C_BZ = 2576, 3088, 3600, 4112
C_CQ, C_CK, C_CV, C_CZ = 4624, 5136, 5264, 5392
C_DA, C_DG, C_DZ, C_MG = 5904, 6416, 6928, 7440
NFM = 7440

ENGS = ("tensor", "vector", "scalar", "gpsimd", "sync")
NDMA = 28
SEM_EPOCH = 30000


class Buf:
    __slots__ = ("w", "r")

    def __init__(self):
        self.w = None
        self.r = []


class KB:
    def __init__(self, nc):
        self.nc = nc
        self.ops = {e: [] for e in ENGS}
        self.cnt = {}
        self.known = {e: {} for e in ENGS}
        self.dma_rr = 0
        self.dma_last = {}
        self.pending = {e: [] for e in ENGS}
        self.tot = {}

    def barrier(self):
        for e in ENGS:
            for k, v in list(self.cnt.items()) + list(self.dma_last.items()):
                self._need(e, (k, v), self.pending[e])

    def _need(self, eng, dep, waits):
        if dep is None:
            return
        k, v = dep
        if eng == "tensor" and k.startswith("tensor#"):
            return
        if self.known[eng].get(k, 0) >= v:
            return
        self.known[eng][k] = v
        waits.append((k, v))

    def op(self, eng, fn, reads=(), writes=(), dma=False):
        waits = self.pending[eng]
        self.pending[eng] = []
        for b in reads:
            self._need(eng, b.w, waits)
        for b in writes:
            self._need(eng, b.w, waits)
            for d in b.r:
                self._need(eng, d, waits)
        if dma:
            k = f"dma{self.dma_rr % NDMA}"
            self.dma_rr += 1
            last = self.dma_last.get(k, 0)
            if last:
                self._need(eng, (k, last), waits)
            v = last + 16
            self.dma_last[k] = v
            inc = 16
        else:
            tot = self.tot.get(eng, 0) + 1
            self.tot[eng] = tot
            k = f"{eng}#{(tot - 1) // SEM_EPOCH}"
            v = (tot - 1) % SEM_EPOCH + 1
            self.cnt[k] = v
            inc = 1
        tag = (k, v)
        for b in reads:
            b.r.append(tag)
            if len(b.r) > 64:
                b.r = b.r[-64:]
        for b in writes:
            b.w = tag
            b.r = []
        self.ops[eng].append((waits, fn, k, inc))

    def emit(self):
        nc = self.nc
        keys = set()
        for e in ENGS:
            for waits, fn, k, inc in self.ops[e]:
                keys.add(k)
        with contextlib.ExitStack() as st:
            sems = {k: st.enter_context(nc.semaphore(f"s_{k}")) for k in sorted(keys)}
            block = st.enter_context(nc.Block())

            def mk(e):
                def body(eng):
                    if e == "sync" and getattr(self, "pre_sync", None) is not None:
                        self.pre_sync(eng)
                    for waits, fn, k, inc in self.ops[e]:
                        for (wk, wv) in waits:
                            eng.wait_ge(sems[wk], wv)
                        fn(eng).then_inc(sems[k], inc)
                    if e == "sync":
                        for k2 in sorted(keys):
                            tot = self.dma_last.get(k2) if k2.startswith("dma") else self.cnt.get(k2)
                            if tot:
                                eng.wait_ge(sems[k2], tot)
                return body

            for e in ENGS:
                if self.ops[e] or e == "sync":
                    getattr(block, e)(mk(e))


class TL:
    def __init__(self, ap):
        self.ap = ap
        self.b = Buf()

    def __getitem__(self, k):
        return self.ap[k]


def t5_bucket_np(rel):
    half, max_exact = 16, 8
    n = np.abs(rel)
    nf = np.maximum(n, 1).astype(np.float32)
    large = max_exact + (np.log(nf / np.float32(max_exact)) / np.float32(math.log(128 / max_exact)) * (half - max_exact)).astype(np.int32)
    large = np.minimum(large, half - 1)
    return np.where(rel > 0, half, 0) + np.where(n < max_exact, n, large)


def make_consts():
    c = {}
    c["ident"] = np.eye(128, dtype=np.float32)
    s = np.arange(128)
    c["tri_fw"] = (s[:, None] <= s[None, :]).astype(np.float32)
    c["tri_bw"] = (s[:, None] >= s[None, :]).astype(np.float32)
    bo = np.zeros((128, 128), np.float32)
    bo[:64, :64] = 1
    bo[64:, 64:] = 1
    c["blk64"] = bo
    k = np.arange(128)[:, None]
    q = np.arange(128)[None, :]
    gm = np.zeros((3, 32, 128, 128), np.float32)
    for o in range(3):
        rel = k + 128 * (o - 1) - q
        bk = t5_bucket_np(rel)
        ok = np.abs(rel) <= 128
        for b in range(32):
            gm[o, b] = ((bk == b) & ok)
    c["gqa_m"] = gm
    kc = np.arange(64)[:, None]
    qc = np.arange(64)[None, :]
    qs = np.clip(qc - 8, 0, 48)
    ok = (kc >= qs) & (kc < qs + 16)
    dc = np.clip(kc - qc + 15, 0, 30)
    nm = np.zeros((31, 128, 64), np.float32)
    for d in range(31):
        m = ((dc == d) & ok).astype(np.float32)
        nm[d, :64] = m
        nm[d, 64:] = m
    c["na_m"] = nm
    return c


def build(seqs, depth, dbg=()):
    nc = bass.Bass("TRN2", target_bir_lowering=False)
    kb = KB(nc)
    st = contextlib.ExitStack()

    def din(name, shape, dt=F32):
        return TL(nc.dram_tensor(name, list(shape), dt, kind="ExternalInput").ap())

    def dscr(name, shape, dt=BF16, kind="Internal"):
        return TL(nc.dram_tensor(name, list(shape), dt, kind=kind).ap())

    AW = 53200
    arena = st.enter_context(nc.sbuf_tensor("arena", [128, AW], F32))
    aoff = [0]

    def sb(name, shape, dt=F32):
        n = 1
        for d_ in shape[1:]:
            n *= d_
        words = (n * (2 if dt == BF16 else 4) + 3) // 4
        assert aoff[0] + words <= AW, (name, aoff[0], words)
        v = arena[0:shape[0], aoff[0]:aoff[0] + words]
        aoff[0] += words
        if dt != F32:
            v = v.bitcast(dt)
        if len(shape) == 3:
            v = v.rearrange("p (a b) -> p a b", a=shape[1])
        elif len(shape) == 4:
            v = v.rearrange("p (a b c) -> p a b c", a=shape[1], b=shape[2])
        return TL(v)

    def ps(name, shape, dt=F32):
        return TL(st.enter_context(nc.psum_tensor(name, list(shape), dt))[:])

    X = {n: din(f"x_{n}", [T, D]) for n, T in seqs}
    Cc = {n: din(f"c_{n}", [1, D]) for n, T in seqs}
    Yout = {n: dscr(f"y_{n}", [T, D], F32, kind="ExternalOutput") for n, T in seqs}
    rel_bias = din("rel_bias", [32, 8])
    norm_w = din("norm_w", [depth, D])
    w_ada = din("w_ada", [depth, D, 3 * D])
    b_ada = din("b_ada", [depth, 3 * D])
    w_in = din("w_in", [depth, D, IN_COLS])
    b_gate = din("b_gate", [depth, 16])
    mlstm_norm_w = din("mlstm_norm_w", [depth, 512])
    na_q_norm = din("na_q_norm", [depth, 128])
    na_k_norm = din("na_k_norm", [depth, 128])
    na_rpb = din("na_rpb", [depth, 4, 15, 31])
    swa_q_norm = din("swa_q_norm", [depth, 64])
    swa_k_norm = din("swa_k_norm", [depth, 64])
    swa_sink = din("swa_sink", [depth, 8])
    conv_w = din("conv_w", [depth, 31, 512])
    conv_b = din("conv_b", [depth, 512])
    conv_ln_w = din("conv_ln_w", [depth, 512])
    conv_ln_b = din("conv_ln_b", [depth, 512])
    w_branch = din("w_branch", [depth, 4, 512, D])
    w_out = din("w_out", [depth, D, D])
    k_ident = din("k_ident", [128, 128])
    k_tri_fw = din("k_tri_fw", [128, 128])
    k_tri_bw = din("k_tri_bw", [128, 128])
    k_blk64 = din("k_blk64", [128, 128])
    k_gqa_m = din("k_gqa_m", [3, 32, 128, 128])
    k_na_m = din("k_na_m", [31, 128, 64])

    FM = {n: dscr(f"fm_{n}", [NFM, T]) for n, T in seqs}
    TMv = {n: dscr(f"tm_{n}", [T, 1664]) for n, T in seqs}
    TMg = {n: dscr(f"tg_{n}", [T, 16], F32) for n, T in seqs}
    HT = {n: dscr(f"ht_{n}", [D, T]) for n, T in seqs}
    YM = {n: dscr(f"ym_{n}", [D, T], BF16, kind=("ExternalOutput" if DEBUG_YM else "Internal")) for n, T in seqs}
    X1 = {n: dscr(f"x1_{n}", [T, D], F32) for n, T in seqs}
    modrow = dscr("modrow", [depth * len(seqs), D], F32)
    WT = {}
    WSRC = {}
    DBG = {}

    ident_f = sb("ident_f", [128, 128])
    ident_b = sb("ident_b", [128, 128], BF16)
    ones_b = sb("ones_b", [128, 128], BF16)
    ones_f = sb("ones_f", [128, 128])
    blk64_b = sb("blk64_b", [128, 128], BF16)
    tri_f = {"fw": sb("tri_fw", [128, 128]), "bw": sb("tri_bw", [128, 128])}
    kb.op("sync", lambda e: e.dma_start(out=ident_f.ap, in_=k_ident.ap), writes=[ident_f.b], dma=True)
    kb.op("vector", lambda e: e.tensor_copy(out=ident_b.ap, in_=ident_f.ap), reads=[ident_f.b], writes=[ident_b.b])
    kb.op("vector", lambda e: e.memset(ones_b.ap, 1.0), writes=[ones_b.b])
    kb.op("vector", lambda e: e.memset(ones_f.ap, 1.0), writes=[ones_f.b])
    kb.op("sync", lambda e: e.dma_start(out=tri_f["fw"].ap, in_=k_tri_fw.ap), writes=[tri_f["fw"].b], dma=True)
    kb.op("sync", lambda e: e.dma_start(out=tri_f["bw"].ap, in_=k_tri_bw.ap), writes=[tri_f["bw"].b], dma=True)
    eps_c = sb("eps_c", [128, 1])
    kb.op("vector", lambda e: e.memset(eps_c.ap, EPS), writes=[eps_c.b])
    tmpc = sb("tmpc", [128, 128])
    kb.op("sync", lambda e: e.dma_start(out=tmpc.ap, in_=k_blk64.ap), writes=[tmpc.b], dma=True)
    kb.op("vector", lambda e: e.tensor_copy(out=blk64_b.ap, in_=tmpc.ap), reads=[tmpc.b], writes=[blk64_b.b])

    PS = [ps(f"ps{i}", [128, 512]) for i in range(6)]
    PSB = [ps(f"psb{i}", [128, 1024], BF16) for i in range(2)]
    psrr = [0]

    psmode = [6]

    def nps():
        psrr[0] += 1
        return PS[psrr[0] % psmode[0]]

    PSH = [TL(PS[4].ap[:, 0:256]), TL(PS[4].ap[:, 256:512]), TL(PS[5].ap[:, 0:256]), TL(PS[5].ap[:, 256:512])]
    pshr = [0]

    def npsh():
        pshr[0] += 1
        return PSH[pshr[0] % 4]

    psbr = [0]

    def npsb():
        psbr[0] += 1
        return PSB[psbr[0] % 2]

    class Pool:
        def __init__(self, name, shape, dt, n):
            self.t = [sb(f"{name}{i}", shape, dt) for i in range(n)]
            self.i = 0

        def get(self):
            self.i += 1
            return self.t[self.i % len(self.t)]

    evac_rr = [0]

    def evac_eng():
        evac_rr[0] += 1
        return "vector" if evac_rr[0] % 2 else "scalar"

    def copy_op(eng, out, in_, reads, writes):
        if eng == "scalar":
            kb.op("scalar", lambda e: e.activation(out=out, in_=in_, func=AF.Copy), reads=reads, writes=writes)
        else:
            kb.op(eng, lambda e: e.tensor_copy(out=out, in_=in_), reads=reads, writes=writes)

    nseq = len(seqs)
    modA = sb("modA", [128, depth, nseq, KC])
    modB = sb("modB", [128, depth, nseq, KC])
    cs = sb("cs", [128, KC, nseq])
    modfm = sb("modfm", [128, 48, nseq])
    badafm = sb("badafm", [128, 48])
    nwfm = sb("nwfm", [128, KC])
    bg_bc = sb("bg_bc", [128, 16])
    EB = sb("EB", [128, 3, 8, 128])
    PERSIST = aoff[0]

    def adaln():
        w32 = Pool("w32", [128, KC, 128], F32, 3)
        for si, (n, T) in enumerate(seqs):
            kb.op("sync", lambda e, si=si, n=n: e.dma_start(out=cs.ap[:, :, si], in_=Cc[n].ap.rearrange("o (k p) -> p (o k)", p=128), allow_slow_non_contiguous=True),
                  writes=[cs.b], dma=True)
        kb.op("scalar", lambda e: e.activation(out=cs.ap, in_=cs.ap, func=AF.Silu), reads=[cs.b], writes=[cs.b])
        for l in range(depth):
            kb.op("sync", lambda e, l=l: e.dma_start(out=badafm.ap, in_=b_ada.ap[l:l + 1, :].rearrange("o (k p) -> p (o k)", p=128), allow_slow_non_contiguous=True),
                  writes=[badafm.b], dma=True)
            kb.op("sync", lambda e, l=l: e.dma_start(out=nwfm.ap, in_=norm_w.ap[l:l + 1, :].rearrange("o (k p) -> p (o k)", p=128), allow_slow_non_contiguous=True),
                  writes=[nwfm.b], dma=True)
            pm = nps()
            for f in range(48):
                wt = w32.get()
                kb.op("sync", lambda e, l=l, f=f, wt=wt: e.dma_start(out=wt.ap, in_=w_ada.ap[l, :, f * 128:(f + 1) * 128].rearrange("(k p) n -> p k n", p=128)),
                      writes=[wt.b], dma=True)
                for k in range(KC):
                    kb.op("tensor", lambda e, k=k, f=f, pm=pm, wt=wt: e.matmul(pm.ap[:, f * nseq:(f + 1) * nseq], lhsT=wt.ap[:, k, :],
                                                                          rhs=cs.ap[:, k, :], start=(k == 0), stop=(k == KC - 1)),
                          reads=[wt.b, cs.b], writes=[pm.b])
            for si in range(nseq):
                kb.op("vector", lambda e, pm=pm, si=si: e.tensor_tensor(out=modfm.ap[:, :, si], in0=pm.ap[:, 0:48 * nseq].rearrange("p (f s) -> p f s", s=nseq)[:, :, si],
                                                                in1=badafm.ap, op=ALU.add),
                      reads=[pm.b, badafm.b], writes=[modfm.b])
            for si, (n, T) in enumerate(seqs):
                kb.op("vector", lambda e, l=l, si=si: e.scalar_tensor_tensor(out=modA.ap[:, l, si, :], in0=modfm.ap[:, 16:32, si], scalar=1.0, in1=nwfm.ap,
                                                                           op0=ALU.add, op1=ALU.mult),
                      reads=[modfm.b, nwfm.b], writes=[modA.b])
                kb.op("vector", lambda e, l=l, si=si: e.tensor_copy(out=modB.ap[:, l, si, :], in_=modfm.ap[:, 0:16, si]), reads=[modfm.b], writes=[modB.b])
                kb.op("sync", lambda e, l=l, si=si: e.dma_start(out=modrow.ap[l * nseq + si:l * nseq + si + 1, :].rearrange("o (k p) -> p (o k)", p=128),
                                                               in_=modfm.ap[:, 32:48, si], allow_slow_non_contiguous=True),
                      reads=[modfm.b], writes=[modrow.b], dma=True)
        kb.barrier()
        aoff[0] = PERSIST

    FM_RANGES = [(0, 1024), (1536, 2560), (2576, 3600), (4112, 5264), (5392, 7440)]
    TM_RANGES = [(512, 1536, 0), (3600, 4112, 1024), (5264, 5392, 1536)]

    def fm_func(col):
        if C_AO <= col < C_AZ:
            return AF.Sigmoid
        if C_AZ <= col < C_AG or C_BZ <= col < C_CQ or C_CZ <= col < C_DA or C_DZ <= col < C_MG:
            return AF.Silu
        return None

    def phase1(l, si, n, T, xsrc):
        xt_pool = Pool("xt", [128, D], F32, 2)
        xn_t = sb("xn", [128, 4, D], BF16)
        junk = sb("junk", [128, D], BF16)
        ssq = sb("ssq", [128, 8])
        hT = sb("hT", [128, KC, 1024], BF16)
        ofm = Pool("ofm", [128, 1024], BF16, 3)
        otm = Pool("otm", [128, 512], BF16, 3)
        otg = Pool("otg", [128, 16], F32, 2)
        wtile = Pool("wtile", [128, KC, 512], BF16, 4)

        wjobs = []
        for g_ in range(T // 1024):
            for (c0_, c1_) in FM_RANGES:
                for w0_ in range(c0_, c1_, 512):
                    wjobs.append((w0_, min(512, c1_ - w0_)))
            for (c0_, c1_, _d) in TM_RANGES:
                for w0_ in range(c0_, c1_, 512):
                    wjobs.append((w0_, min(512, c1_ - w0_)))
            wjobs.append((C_AG, 16))

        def mk_loader(c0, ncols):
            def ld():
                wt = wtile.get()
                wload(wt.ap, wt.b, l, "in", 0, c0, ncols)
                return wt
            return ld
        pf = Prefetch([mk_loader(c0, nco) for (c0, nco) in wjobs], ahead=2)
        wji = [0]

        def load_w(c0, ncols):
            i = wji[0]
            assert wjobs[i] == (c0, ncols), (wjobs[i], c0, ncols)
            wji[0] += 1
            return pf.get(i)

        kb.op("sync", lambda e: e.dma_start(out=bg_bc.ap, in_=b_gate.ap[l:l + 1, :].broadcast_to([128, 16])), writes=[bg_bc.b], dma=True)
        hTs = [hT, sb("hTb", [128, KC, 1024], BF16)]

        def prep_gen(g):
            hT = hTs[g % 2]
            for half in range(2):
                for j in range(4):
                    t0 = g * 1024 + half * 512 + j * 128
                    xt = xt_pool.get()
                    kb.op("sync", lambda e, xt=xt, t0=t0: e.dma_start(out=xt.ap, in_=xsrc.ap[t0:t0 + 128, :]), reads=[xsrc.b], writes=[xt.b], dma=True)
                    kb.op("scalar", lambda e, xt=xt, j=j: e.activation(out=junk.ap, in_=xt.ap, func=AF.Square, accum_out=ssq.ap[:, j:j + 1]),
                          reads=[xt.b], writes=[junk.b, ssq.b])
                    kb.op("vector", lambda e, j=j: e.tensor_scalar(out=ssq.ap[:, 4 + j:5 + j], in0=ssq.ap[:, j:j + 1], scalar1=1.0 / D, scalar2=EPS,
                                                                    op0=ALU.mult, op1=ALU.add), reads=[ssq.b], writes=[ssq.b])
                    kb.op("scalar", lambda e, j=j: e.activation(out=ssq.ap[:, 4 + j:5 + j], in_=ssq.ap[:, 4 + j:5 + j], func=AF.Sqrt), reads=[ssq.b], writes=[ssq.b])
                    kb.op("vector", lambda e, j=j: e.reciprocal(out=ssq.ap[:, 4 + j:5 + j], in_=ssq.ap[:, 4 + j:5 + j]), reads=[ssq.b], writes=[ssq.b])
                    kb.op("vector", lambda e, xt=xt, j=j: e.tensor_scalar(out=xn_t.ap[:, j, :], in0=xt.ap, scalar1=ssq.ap[:, 4 + j:5 + j], scalar2=None,
                                                                           op0=ALU.mult), reads=[xt.b, ssq.b], writes=[xn_t.b])
                    yield
                for k in range(KC):
                    pb = npsb()
                    for j in range(4):
                        kb.op("tensor", lambda e, pb=pb, j=j, k=k: e.transpose(pb.ap[:, j * 128:(j + 1) * 128], xn_t.ap[:, j, k * 128:(k + 1) * 128], ident_b.ap),
                              reads=[xn_t.b, ident_b.b], writes=[pb.b])
                    dst = hT.ap[:, k, half * 512:(half + 1) * 512]
                    if k % 2 == 0:
                        kb.op("scalar", lambda e, pb=pb, k=k, dst=dst: e.activation(out=dst, in_=pb.ap[:, 0:512], func=AF.Identity,
                                                                                     bias=modB.ap[:, l, si, k:k + 1], scale=modA.ap[:, l, si, k:k + 1]),
                              reads=[pb.b, modA.b, modB.b], writes=[hT.b])
                    else:
                        kb.op("vector", lambda e, pb=pb, k=k, dst=dst: e.tensor_scalar(out=dst, in0=pb.ap[:, 0:512], scalar1=modA.ap[:, l, si, k:k + 1],
                                                                                        scalar2=modB.ap[:, l, si, k:k + 1], op0=ALU.mult, op1=ALU.add),
                              reads=[pb.b, modA.b, modB.b], writes=[hT.b])
                    if k % 4 == 3:
                        yield
            kb.op("sync", lambda e, g=g, hT=hT: e.dma_start(out=HT[n].ap[:, g * 1024:(g + 1) * 1024].rearrange("(k p) t -> p k t", p=128), in_=hT.ap),
                  reads=[hT.b], writes=[HT[n].b], dma=True)

        preps = {}

        def pump(g, nsteps=1):
            if g >= T // 1024:
                return
            if g not in preps:
                preps[g] = prep_gen(g)
            for _ in range(nsteps):
                try:
                    next(preps[g])
                except StopIteration:
                    break

        def do_group1(g):
            pump(g, 1000)
            hT = hTs[g % 2]
            for (c0, c1) in FM_RANGES:
                for w0 in range(c0, c1, 512):
                    ncols = min(512, c1 - w0)
                    wt = load_w(w0, ncols)
                    for mc in range(ncols // 128):
                        col = w0 + mc * 128
                        o = ofm.get()
                        fn = fm_func(col)
                        for tt in range(2):
                            p = nps()
                            for k in range(KC):
                                kb.op("tensor", lambda e, p=p, wt=wt, mc=mc, k=k, tt=tt: e.matmul(p.ap, lhsT=wt.ap[:, k, mc * 128:(mc + 1) * 128],
                                                                                            rhs=hT.ap[:, k, tt * 512:(tt + 1) * 512],
                                                                                            start=(k == 0), stop=(k == KC - 1)),
                                      reads=[wt.b, hT.b], writes=[p.b])
                            dst = o.ap[:, tt * 512:(tt + 1) * 512]
                            if fn is None:
                                copy_op(evac_eng(), dst, p.ap, [p.b], [o.b])
                            else:
                                kb.op("scalar", lambda e, p=p, dst=dst, fn=fn: e.activation(out=dst, in_=p.ap, func=fn), reads=[p.b], writes=[o.b])
                        kb.op("sync", lambda e, o=o, col=col, g=g: e.dma_start(out=FM[n].ap[col:col + 128, g * 1024:(g + 1) * 1024], in_=o.ap),
                              reads=[o.b], writes=[FM[n].b], dma=True)
                        pump(g + 1, 1)
            for (c0, c1, dcol) in TM_RANGES:
                for w0 in range(c0, c1, 512):
                    ncols = min(512, c1 - w0)
                    wt = load_w(w0, ncols)
                    for sub in range(8):
                        p = nps()
                        for k in range(KC):
                            kb.op("tensor", lambda e, p=p, wt=wt, k=k, sub=sub, ncols=ncols: e.matmul(p.ap[:, 0:ncols], lhsT=hT.ap[:, k, sub * 128:(sub + 1) * 128],
                                                                                                 rhs=wt.ap[:, k, 0:ncols], start=(k == 0), stop=(k == KC - 1)),
                                  reads=[wt.b, hT.b], writes=[p.b])
                        o = otm.get()
                        copy_op(evac_eng(), o.ap[:, 0:ncols], p.ap[:, 0:ncols], [p.b], [o.b])
                        t0 = g * 1024 + sub * 128
                        dc = dcol + (w0 - c0)
                        kb.op("sync", lambda e, o=o, t0=t0, dc=dc, ncols=ncols: e.dma_start(out=TMv[n].ap[t0:t0 + 128, dc:dc + ncols], in_=o.ap[:, 0:ncols]),
                              reads=[o.b], writes=[TMv[n].b], dma=True)
            wt = load_w(C_AG, 16)
            for sub in range(8):
                p = nps()
                for k in range(KC):
                    kb.op("tensor", lambda e, p=p, wt=wt, k=k, sub=sub: e.matmul(p.ap[:, 0:16], lhsT=hT.ap[:, k, sub * 128:(sub + 1) * 128],
                                                                            rhs=wt.ap[:, k, 0:16], start=(k == 0), stop=(k == KC - 1)),
                          reads=[wt.b, hT.b], writes=[p.b])
                o = otg.get()
                kb.op("vector", lambda e, o=o, p=p: e.tensor_tensor(out=o.ap, in0=p.ap[:, 0:16], in1=bg_bc.ap, op=ALU.add), reads=[p.b, bg_bc.b], writes=[o.b])
                t0 = g * 1024 + sub * 128
                kb.op("sync", lambda e, o=o, t0=t0: e.dma_start(out=TMg[n].ap[t0:t0 + 128, :], in_=o.ap), reads=[o.b], writes=[TMg[n].b], dma=True)
        for g in range(T // 1024):
            do_group1(g)
        kb.barrier()
        aoff[0] = PERSIST

    def phase3(l, si, n, T, xsrc, xdst):
        hT = sb("hT3", [128, KC, 1024], BF16)
        yT_t = sb("yT", [128, KC, 1024], BF16)
        yTb = [Buf() for _ in range(KC)]
        mT_t = sb("mT", [128, KC, 1024], BF16)
        wm_pool = Pool("wm", [128, KC, 256], BF16, 4)
        wbr_pool = Pool("wbr", [128, 4, 256], BF16, 4)

        def mk_merge(mg, i):
            def ld():
                wt = wm_pool.get()
                wload(wt.ap, wt.b, l, "in", 0, C_MG + i * 2048 + mg * 256, 256)
                wb_ = wbr_pool.get()
                wload(wb_.ap, wb_.b, l, "br", i, mg * 256, 256)
                return (wt, wb_)
            return ld

        def mk_out(nn):
            def ld():
                wt = wm_pool.get()
                wload(wt.ap, wt.b, l, "out", 0, nn * 256, 256)
                return wt
            return ld
        loaders3 = []
        for g_ in range(T // 1024):
            for mg_ in range(8):
                for i_ in range(4):
                    loaders3.append(mk_merge(mg_, i_))
            for nn_ in range(8):
                loaders3.append(mk_out(nn_))
        pf3 = Prefetch(loaders3, ahead=2)
        pfi = [0]
        ytmp = Pool("ytmp", [128, 1024], BF16, 3)
        uld = [sb(f"uld{i}", [128, 1024], BF16) for i in range(4)]
        sg_pool = Pool("sg", [128, 512], F32, 2)
        acc_pool = Pool("acc", [128, 512], F32, 8)
        tmp_pool = Pool("tmp3", [128, 512], F32, 2)
        xo_pool = Pool("xo", [128, 512], F32, 2)
        usq = Pool("usq", [128, 512], BF16, 2)
        stat = Pool("stat", [128, 512], F32, 3)
        gsl = sb("gsl", [128, 512])
        lnw = sb("lnw", [128, 4])
        lnb = sb("lnb", [128, 4])
        kb.op("sync", lambda e: e.dma_start(out=lnw.ap, in_=conv_ln_w.ap[l:l + 1, :].rearrange("o (k p) -> p (o k)", p=128), allow_slow_non_contiguous=True), writes=[lnw.b], dma=True)
        kb.op("sync", lambda e: e.dma_start(out=lnb.ap, in_=conv_ln_b.ap[l:l + 1, :].rearrange("o (k p) -> p (o k)", p=128), allow_slow_non_contiguous=True), writes=[lnb.b], dma=True)
        def do_group(g):
            tsl = slice(g * 1024, (g + 1) * 1024)
            kb.op("sync", lambda e: e.dma_start(out=hT.ap, in_=HT[n].ap[:, tsl].rearrange("(k p) t -> p k t", p=128)), reads=[HT[n].b], writes=[hT.b], dma=True)
            for br in range(3):
                zc = (C_AZ, C_BZ, C_CZ)[br]
                for c4 in range(4):
                    a = ytmp.get()
                    kb.op("sync", lambda e, a=a, br=br, c4=c4: e.dma_start(out=a.ap, in_=YM[n].ap[br * 512 + c4 * 128: br * 512 + c4 * 128 + 128, tsl]),
                          reads=[YM[n].b], writes=[a.b], dma=True)
                    z = ytmp.get()
                    kb.op("sync", lambda e, z=z, zc=zc, c4=c4: e.dma_start(out=z.ap, in_=FM[n].ap[zc + c4 * 128: zc + c4 * 128 + 128, tsl]),
                          reads=[FM[n].b], writes=[z.b], dma=True)
                    dst = yT_t.ap[:, br * 4 + c4, :]
                    if br == 0:
                        s_ = ytmp.get()
                        kb.op("sync", lambda e, s_=s_, c4=c4: e.dma_start(out=s_.ap, in_=FM[n].ap[C_AO + c4 * 128: C_AO + c4 * 128 + 128, tsl]),
                              reads=[FM[n].b], writes=[s_.b], dma=True)
                        kb.op("gpsimd", lambda e, z=z, s_=s_: e.tensor_tensor(out=z.ap, in0=z.ap, in1=s_.ap, op=ALU.mult), reads=[z.b, s_.b], writes=[z.b])
                    kb.op("vector", lambda e, a=a, z=z, dst=dst: e.tensor_tensor(out=dst, in0=a.ap, in1=z.ap, op=ALU.mult), reads=[a.b, z.b], writes=[yTb[br * 4 + c4]])
            ul = uld
            for c4 in range(4):
                kb.op("sync", lambda e, c4=c4: e.dma_start(out=ul[c4].ap, in_=YM[n].ap[1536 + c4 * 128: 1536 + c4 * 128 + 128, tsl]),
                      reads=[YM[n].b], writes=[ul[c4].b], dma=True)
            for tt in range(2):
                cs_ = slice(tt * 512, (tt + 1) * 512)
                p1 = nps()
                p2 = nps()
                for c4 in range(4):
                    q2 = usq.get()
                    kb.op("gpsimd", lambda e, q2=q2, c4=c4, cs_=cs_: e.tensor_tensor(out=q2.ap, in0=ul[c4].ap[:, cs_], in1=ul[c4].ap[:, cs_], op=ALU.mult),
                          reads=[ul[c4].b], writes=[q2.b])
                    kb.op("tensor", lambda e, p1=p1, c4=c4, cs_=cs_: e.matmul(p1.ap, lhsT=ones_b.ap, rhs=ul[c4].ap[:, cs_], start=(c4 == 0), stop=(c4 == 3)),
                          reads=[ones_b.b, ul[c4].b], writes=[p1.b])
                    kb.op("tensor", lambda e, p2=p2, q2=q2, c4=c4: e.matmul(p2.ap, lhsT=ones_b.ap, rhs=q2.ap, start=(c4 == 0), stop=(c4 == 3)),
                          reads=[ones_b.b, q2.b], writes=[p2.b])
                mean = stat.get()
                rstd = stat.get()
                m2 = stat.get()
                kb.op("scalar", lambda e, mean=mean, p1=p1: e.activation(out=mean.ap, in_=p1.ap, func=AF.Copy, scale=1.0 / 512), reads=[p1.b], writes=[mean.b])
                kb.op("vector", lambda e, mean=mean, m2=m2: e.tensor_tensor(out=m2.ap, in0=mean.ap, in1=mean.ap, op=ALU.mult), reads=[mean.b], writes=[m2.b])
                kb.op("vector", lambda e, rstd=rstd, p2=p2, m2=m2: e.scalar_tensor_tensor(out=rstd.ap, in0=p2.ap, scalar=1.0 / 512, in1=m2.ap, op0=ALU.mult, op1=ALU.subtract),
                      reads=[p2.b, m2.b], writes=[rstd.b])
                kb.op("scalar", lambda e, rstd=rstd: e.activation(out=rstd.ap, in_=rstd.ap, func=AF.Sqrt, bias=eps_c.ap[:, 0:1]), reads=[rstd.b, eps_c.b], writes=[rstd.b])
                kb.op("vector", lambda e, rstd=rstd: e.reciprocal(out=rstd.ap, in_=rstd.ap), reads=[rstd.b], writes=[rstd.b])
                for c4 in range(4):
                    t1 = tmp_pool.get()
                    kb.op("vector", lambda e, t1=t1, c4=c4, mean=mean, cs_=cs_: e.tensor_tensor(out=t1.ap, in0=ul[c4].ap[:, cs_], in1=mean.ap, op=ALU.subtract),
                          reads=[ul[c4].b, mean.b], writes=[t1.b])
                    kb.op("gpsimd", lambda e, t1=t1, rstd=rstd: e.tensor_tensor(out=t1.ap, in0=t1.ap, in1=rstd.ap, op=ALU.mult), reads=[t1.b, rstd.b], writes=[t1.b])
                    kb.op("scalar", lambda e, t1=t1, c4=c4: e.activation(out=t1.ap, in_=t1.ap, func=AF.Silu, bias=lnb.ap[:, c4:c4 + 1], scale=lnw.ap[:, c4:c4 + 1]),
                          reads=[t1.b, lnw.b, lnb.b], writes=[t1.b])
                    z = usq.get()
                    kb.op("sync", lambda e, z=z, c4=c4, tt=tt: e.dma_start(out=z.ap, in_=FM[n].ap[C_DZ + c4 * 128: C_DZ + c4 * 128 + 128, g * 1024 + tt * 512: g * 1024 + tt * 512 + 512]),
                          reads=[FM[n].b], writes=[z.b], dma=True)
                    kb.op("vector", lambda e, t1=t1, z=z, c4=c4, cs_=cs_: e.tensor_tensor(out=yT_t.ap[:, 12 + c4, cs_], in0=t1.ap, in1=z.ap, op=ALU.mult),
                          reads=[t1.b, z.b], writes=[yTb[12 + c4]])
            for mg in range(8):
                accs = [acc_pool.get() for _ in range(4)]
                for i in range(4):
                    wt, wb_ = pf3.get(pfi[0])
                    pfi[0] += 1
                    for mc in range(2):
                        m = mg * 2 + mc
                        for tt in range(2):
                            cs_ = slice(tt * 512, (tt + 1) * 512)
                            acc = accs[mc * 2 + tt]
                            pa = nps()
                            for k in range(KC):
                                kb.op("tensor", lambda e, pa=pa, k=k, mc=mc, cs_=cs_, wt=wt: e.matmul(pa.ap, lhsT=wt.ap[:, k, mc * 128:(mc + 1) * 128], rhs=hT.ap[:, k, cs_],
                                                                                                 start=(k == 0), stop=(k == KC - 1)),
                                      reads=[wt.b, hT.b], writes=[pa.b])
                            pb_ = nps()
                            for k in range(4):
                                kb.op("tensor", lambda e, pb_=pb_, i=i, k=k, mc=mc, cs_=cs_, wb_=wb_: e.matmul(pb_.ap, lhsT=wb_.ap[:, k, mc * 128:(mc + 1) * 128], rhs=yT_t.ap[:, i * 4 + k, cs_],
                                                                                                          start=(k == 0), stop=(k == 3)),
                                      reads=[wb_.b, yTb[i * 4 + k]], writes=[pb_.b])
                            sg = sg_pool.get()
                            kb.op("scalar", lambda e, sg=sg, pa=pa: e.activation(out=sg.ap, in_=pa.ap, func=AF.Sigmoid), reads=[pa.b], writes=[sg.b])
                            if i == 0:
                                kb.op("vector", lambda e, acc=acc, sg=sg, pb_=pb_: e.tensor_tensor(out=acc.ap, in0=pb_.ap, in1=sg.ap, op=ALU.mult),
                                      reads=[pb_.b, sg.b], writes=[acc.b])
                            else:
                                kb.op("vector", lambda e, sg=sg, pb_=pb_: e.tensor_tensor(out=sg.ap, in0=pb_.ap, in1=sg.ap, op=ALU.mult),
                                      reads=[pb_.b, sg.b], writes=[sg.b])
                                if i < 3:
                                    kb.op("gpsimd", lambda e, acc=acc, sg=sg: e.tensor_tensor(out=acc.ap, in0=acc.ap, in1=sg.ap, op=ALU.add),
                                          reads=[acc.b, sg.b], writes=[acc.b])
                                else:
                                    kb.op("gpsimd", lambda e, acc=acc, sg=sg, m=m, cs_=cs_: e.tensor_tensor(out=mT_t.ap[:, m, cs_], in0=acc.ap, in1=sg.ap, op=ALU.add),
                                          reads=[acc.b, sg.b], writes=[mT_t.b])
            for nn in range(8):
                wt = pf3.get(pfi[0])
                pfi[0] += 1
                kb.op("sync", lambda e, nn=nn: e.dma_start(out=gsl.ap[:, 0:256], in_=modrow.ap[l * nseq + si:l * nseq + si + 1, nn * 256:(nn + 1) * 256].broadcast_to([128, 256])),
                      reads=[modrow.b], writes=[gsl.b], dma=True)
                for sub in range(8):
                    t0 = g * 1024 + sub * 128
                    p = nps()
                    for k in range(KC):
                        kb.op("tensor", lambda e, p=p, wt=wt, k=k, sub=sub: e.matmul(p.ap[:, 0:256], lhsT=mT_t.ap[:, k, sub * 128:(sub + 1) * 128], rhs=wt.ap[:, k, :],
                                                                               start=(k == 0), stop=(k == KC - 1)),
                              reads=[wt.b, mT_t.b], writes=[p.b])
                    xo = xo_pool.get()
                    kb.op("sync", lambda e, xo=xo, t0=t0, nn=nn: e.dma_start(out=xo.ap[:, 0:256], in_=xsrc.ap[t0:t0 + 128, nn * 256:(nn + 1) * 256]), reads=[xsrc.b], writes=[xo.b], dma=True)
                    t1 = tmp_pool.get()
                    kb.op("vector", lambda e, t1=t1, p=p: e.tensor_tensor(out=t1.ap[:, 0:256], in0=p.ap[:, 0:256], in1=gsl.ap[:, 0:256], op=ALU.mult),
                          reads=[p.b, gsl.b], writes=[t1.b])
                    kb.op("gpsimd", lambda e, t1=t1, xo=xo: e.tensor_tensor(out=xo.ap[:, 0:256], in0=xo.ap[:, 0:256], in1=t1.ap[:, 0:256], op=ALU.add), reads=[xo.b, t1.b], writes=[xo.b])
                    kb.op("sync", lambda e, xo=xo, t0=t0, nn=nn: e.dma_start(out=xdst.ap[t0:t0 + 128, nn * 256:(nn + 1) * 256], in_=xo.ap[:, 0:256]), reads=[xo.b], writes=[xdst.b], dma=True)
        for g in range(T // 1024):
            do_group(g)
        kb.barrier()
        aoff[0] = PERSIST

    def wt_get(l, kind, idx, c0, ncols):
        key = (l, kind, idx, c0, ncols)
        if key in WT:
            return WT[key]
        nk = 4 if kind == "br" else KC
        t = dscr(f"wt_{l}_{kind}_{idx}_{c0}_{ncols}", [128, nk * ncols])
        if kind == "in":
            src = w_in.ap[l, :, c0:c0 + ncols]
        elif kind == "br":
            src = w_branch.ap[l, idx, :, c0:c0 + ncols]
        else:
            src = w_out.ap[l, :, c0:c0 + ncols]
        WSRC[key] = src
        WT[key] = (t, nk)
        return WT[key]

    P1_TILES = []
    for (c0_, c1_) in [(0, 1024), (1536, 2560), (2576, 3600), (4112, 5264), (5392, 7440)]:
        for w0_ in range(c0_, c1_, 512):
            P1_TILES.append((w0_, min(512, c1_ - w0_)))
    for (c0_, c1_) in [(512, 1536), (3600, 4112), (5264, 5392)]:
        for w0_ in range(c0_, c1_, 512):
            P1_TILES.append((w0_, min(512, c1_ - w0_)))
    P1_TILES.append((C_AG, 16))

    def convert_tiles(keys):
        st32 = Pool("cv32", [128, KC, 512], F32, 2)
        st16 = Pool("cv16", [128, KC, 512], BF16, 2)
        for ci, key in enumerate(keys):
            (l, kind, idx, c0, ncols) = key
            t, nk = wt_get(l, kind, idx, c0, ncols)
            src = WSRC[key]
            a = st32.get()
            b = st16.get()
            kb.op("sync", lambda e, a=a, src=src, nk=nk, ncols=ncols: e.dma_start(out=a.ap[:, 0:nk, 0:ncols], in_=src.rearrange("(k p) n -> p k n", p=128)),
                  writes=[a.b], dma=True)
            eng = "gpsimd" if ci % 3 != 2 else "vector"
            kb.op(eng, lambda e, a=a, b=b, nk=nk, ncols=ncols: e.tensor_copy(out=b.ap[:, 0:nk, 0:ncols], in_=a.ap[:, 0:nk, 0:ncols]), reads=[a.b], writes=[b.b])
            kb.op("scalar", lambda e, b=b, t=t, nk=nk, ncols=ncols: e.dma_start(out=t.ap.rearrange("p (k n) -> p k n", k=nk), in_=b.ap[:, 0:nk, 0:ncols]),
                  reads=[b.b], writes=[t.b], dma=True)
        kb.barrier()
        aoff[0] = PERSIST

    def p1_keys(l):
        return [(l, "in", 0, c0, nco) for (c0, nco) in P1_TILES]

    def p3_keys(l):
        ks = []
        for mg in range(8):
            for i in range(4):
                ks.append((l, "in", 0, C_MG + i * 2048 + mg * 256, 256))
                ks.append((l, "br", i, mg * 256, 256))
        for nn in range(8):
            ks.append((l, "out", 0, nn * 256, 256))
        return ks

    WQ = "scalar"

    def wload(dst, dst_b, l, kind, idx, c0, ncols):
        t, nk = wt_get(l, kind, idx, c0, ncols)
        kb.op(WQ, lambda e: e.dma_start(out=dst[:, 0:nk, 0:ncols], in_=t.ap.rearrange("p (k n) -> p k n", k=nk)),
              reads=[t.b], writes=[dst_b], dma=True)

    class Prefetch:
        def __init__(self, loaders, ahead=2):
            self.loaders, self.ahead, self.tiles, self.issued = loaders, ahead, {}, 0

        def get(self, i):
            while self.issued <= min(i + self.ahead, len(self.loaders) - 1):
                self.tiles[self.issued] = self.loaders[self.issued]()
                self.issued += 1
            return self.tiles.pop(i)

    env = dict(locals())
    MIX = build_mixers(env)

    convert_tiles(p1_keys(0))
    adaln()
    MIX.gqa_tables()
    kb.barrier()
    aoff[0] = PERSIST
    for l in range(depth):
        for si, (n, T) in enumerate(seqs):
            xsrc = X[n] if l == 0 else X1[n]
            phase1(l, si, n, T, xsrc)
        convert_tiles(p3_keys(l) + (p1_keys(l + 1) if l + 1 < depth else []))
        for si, (n, T) in enumerate(seqs):
            MIX(l, si, n, T)
            kb.barrier()
            aoff[0] = PERSIST
        for si, (n, T) in enumerate(seqs):
            xsrc = X[n] if l == 0 else X1[n]
            xdst = Yout[n] if l == depth - 1 else X1[n]
            phase3(l, si, n, T, xsrc, xdst)
    kb.emit()
    st.close()
    return nc


def build_mixers(env):
    g_ = env
    kb, sb, nps, npsb, Pool, copy_op = g_["kb"], g_["sb"], g_["nps"], g_["npsb"], g_["Pool"], g_["copy_op"]
    FM, TMv, TMg, YM = g_["FM"], g_["TMv"], g_["TMg"], g_["YM"]
    ident_b, ident_f, ones_b, ones_f, blk64_b, tri_f, eps_c = (g_[k] for k in ("ident_b", "ident_f", "ones_b", "ones_f", "blk64_b", "tri_f", "eps_c"))
    consts = make_consts()
    npsh = g_["npsh"]

    def interleave(gens):
        gens = list(gens)
        while gens:
            nxt = []
            for g in gens:
                try:
                    next(g)
                    nxt.append(g)
                except StopIteration:
                    pass
            gens = nxt

    def dma(eng, out, in_, reads, writes, slow=False):
        if slow:
            kb.op(eng, lambda e: e.dma_start(out=out, in_=in_, allow_slow_non_contiguous=True), reads=reads, writes=writes, dma=True)
        else:
            kb.op(eng, lambda e: e.dma_start(out=out, in_=in_), reads=reads, writes=writes, dma=True)

    def V(fn, reads, writes, eng="vector"):
        kb.op(eng, fn, reads=reads, writes=writes)

    def MM(out, lhsT, rhs, start, stop, reads, writes):
        kb.op("tensor", lambda e: e.matmul(out, lhsT=lhsT, rhs=rhs, start=start, stop=stop), reads=reads, writes=writes)

    def rsqrt_tile(dst, src_ps, scale, n, reads):
        V(lambda e: e.tensor_scalar(out=dst.ap[:, 0:n], in0=src_ps, scalar1=scale, scalar2=EPS, op0=ALU.mult, op1=ALU.add), reads, [dst.b])
        kb.op("scalar", lambda e: e.activation(out=dst.ap[:, 0:n], in_=dst.ap[:, 0:n], func=AF.Sqrt), reads=[dst.b], writes=[dst.b])
        V(lambda e: e.reciprocal(out=dst.ap[:, 0:n], in_=dst.ap[:, 0:n]), [dst.b], [dst.b])

    def headnorm_fm(dst, src, T, wcol, lhs_ones, scale_div, extra_scale):
        sq = Pool("hn_sq", [128, 512], BF16, 2)
        rs = Pool("hn_rs", [128, 512], F32, 2)
        for t0 in range(0, T, 512):
            q2 = sq.get()
            V(lambda e, q2=q2, t0=t0: e.tensor_tensor(out=q2.ap, in0=src.ap[:, t0:t0 + 512], in1=src.ap[:, t0:t0 + 512], op=ALU.mult), [src.b], [q2.b], eng="gpsimd")
            p = nps()
            MM(p.ap, lhs_ones.ap, q2.ap, True, True, [lhs_ones.b, q2.b], [p.b])
            r = rs.get()
            rsqrt_tile(r, p.ap, 1.0 / scale_div, 512, [p.b])
            V(lambda e, r=r, t0=t0: e.scalar_tensor_tensor(out=dst.ap[:, t0:t0 + 512], in0=src.ap[:, t0:t0 + 512], scalar=wcol, in1=r.ap, op0=ALU.mult, op1=ALU.mult),
              [src.b, r.b], [dst.b])

    def conv(l, n, T):
        conv_w, conv_b = g_["conv_w"], g_["conv_b"]
        a_t = sb("cv_a", [128, T], BF16)
        g_t = sb("cv_g", [128, T], BF16)
        up = sb("cv_u", [128, T + 32], BF16)
        dg = sb("cv_dg", [128, 31, 128], BF16)
        cw = sb("cv_w", [128, 31])
        cb = sb("cv_b", [128, 1])
        osb = Pool("cv_o", [128, 512], BF16, 3)
        for c4 in range(4):
            dma("sync", cw.ap, conv_w.ap[l, :, c4 * 128:(c4 + 1) * 128].rearrange("w p -> p w"), [], [cw.b], slow=True)
            dma("sync", cb.ap, conv_b.ap[l:l + 1, c4 * 128:(c4 + 1) * 128].rearrange("o p -> p o"), [], [cb.b], slow=True)
            for w in range(31):
                V(lambda e, w=w: e.tensor_scalar(out=dg.ap[:, w, :], in0=ident_f.ap, scalar1=cw.ap[:, w:w + 1], scalar2=None, op0=ALU.mult), [ident_f.b, cw.b], [dg.b],
                  eng=("vector" if w % 2 else "gpsimd"))
            for t0 in range(0, T, 1024):
                dma("sync", a_t.ap[:, t0:t0 + 1024], FM[n].ap[C_DA + c4 * 128:C_DA + c4 * 128 + 128, t0:t0 + 1024], [FM[n].b], [a_t.b])
                dma("sync", g_t.ap[:, t0:t0 + 1024], FM[n].ap[C_DG + c4 * 128:C_DG + c4 * 128 + 128, t0:t0 + 1024], [FM[n].b], [g_t.b])
            V(lambda e: e.memset(up.ap[:, 0:16], 0.0), [], [up.b])
            V(lambda e: e.memset(up.ap[:, T + 15:T + 32], 0.0), [], [up.b])
            kb.op("scalar", lambda e: e.activation(out=g_t.ap, in_=g_t.ap, func=AF.Sigmoid), reads=[g_t.b], writes=[g_t.b])
            V(lambda e: e.tensor_tensor(out=up.ap[:, 15:15 + T], in0=a_t.ap, in1=g_t.ap, op=ALU.mult), [a_t.b, g_t.b], [up.b])
            for t0 in range(0, T, 512):
                p = nps()
                for w in range(31):
                    MM(p.ap, dg.ap[:, w, :], up.ap[:, t0 + w:t0 + w + 512], w == 0, w == 30, [dg.b, up.b], [p.b])
                o = osb.get()
                kb.op("scalar", lambda e, o=o, p=p: e.activation(out=o.ap, in_=p.ap, func=AF.Identity, bias=cb.ap[:, 0:1]), reads=[p.b, cb.b], writes=[o.b])
                dma("sync", YM[n].ap[1536 + c4 * 128:1536 + c4 * 128 + 128, t0:t0 + 512], o.ap, [o.b], [YM[n].b])

    gq_state = {}

    def gqa_tables():
        rel_bias, k_gqa_m = g_["rel_bias"], g_["k_gqa_m"]
        EB = g_["EB"]
        rbb = sb("gq_rbb", [128, 256])
        val = sb("gq_val", [128, 3, 128])
        mk = Pool("gq_mk", [128, 128], F32, 3)
        dma("sync", rbb.ap, rel_bias.ap.rearrange("b h -> (b h)").unsqueeze(0).broadcast_to([128, 256]), [], [rbb.b])
        V(lambda e: e.memset(EB.ap, 0.0), [], [EB.b])
        V(lambda e: e.memset(val.ap, 0.0), [], [val.b])
        gm = consts["gqa_m"]
        for o in range(3):
            for b in range(32):
                if not gm[o, b].any():
                    continue
                m = mk.get()
                dma("sync", m.ap, k_gqa_m.ap[o, b], [], [m.b])
                V(lambda e, m=m, o=o: e.tensor_tensor(out=val.ap[:, o, :], in0=val.ap[:, o, :], in1=m.ap, op=ALU.add), [m.b, val.b], [val.b], eng="gpsimd")
                for h in range(8):
                    V(lambda e, m=m, o=o, b=b, h=h: e.scalar_tensor_tensor(out=EB.ap[:, o, h, :], in0=m.ap, scalar=rbb.ap[:, b * 8 + h:b * 8 + h + 1], in1=EB.ap[:, o, h, :],
                                                                         op0=ALU.mult, op1=ALU.add), [m.b, rbb.b, EB.b], [EB.b])
        kb.op("scalar", lambda e: e.activation(out=EB.ap, in_=EB.ap, func=AF.Exp), reads=[EB.b], writes=[EB.b])
        for o in range(3):
            for h in range(8):
                V(lambda e, o=o, h=h: e.tensor_tensor(out=EB.ap[:, o, h, :], in0=EB.ap[:, o, h, :], in1=val.ap[:, o, :], op=ALU.mult), [EB.b, val.b], [EB.b])

    def gqa(l, n, T):
        swa_q_norm, swa_k_norm, swa_sink = g_["swa_q_norm"], g_["swa_k_norm"], g_["swa_sink"]
        EB = g_["EB"]
        nb = T // 128
        kraw = sb("gq_kraw", [128, T], BF16)
        kn = sb("gq_kn", [128, T], BF16)
        qraw = sb("gq_qraw", [128, T], BF16)
        qn = sb("gq_qn", [128, T], BF16)
        vp = [sb(f"gq_vp{i}", [128, nb, 128], BF16) for i in range(2)]
        vraw = sb("gq_vraw", [128, nb, 64], BF16)
        on = [sb(f"gq_on{i}", [128, 128], BF16) for i in range(2)]
        wq = sb("gq_wq", [128, 1])
        wk = sb("gq_wk", [128, 1])
        sk = sb("gq_sk", [128, 4])
        Pf = Pool("gq_pf", [128, 2, 384], F32, 4)
        Pb = Pool("gq_pb", [128, 2, 384], BF16, 4)
        rc = Pool("gq_rc", [128, 128], F32, 4)
        ost = Pool("gq_ost", [128, 1024], BF16, 2)
        for i in range(2):
            V(lambda e, i=i: e.memset(on[i].ap, 0.0), [], [on[i].b])
            V(lambda e, i=i: e.memset(on[i].ap[:, 64 * i:64 * i + 64], 1.0), [], [on[i].b])
            V(lambda e, i=i: e.memset(vp[i].ap, 0.0), [], [vp[i].b], eng="gpsimd")
        for half in range(2):
            dma("sync", wq.ap[64 * half:64 * half + 64, :], swa_q_norm.ap[l:l + 1, :].rearrange("o p -> p o"), [], [wq.b], slow=True)
            dma("sync", wk.ap[64 * half:64 * half + 64, :], swa_k_norm.ap[l:l + 1, :].rearrange("o p -> p o"), [], [wk.b], slow=True)
        V(lambda e: e.tensor_scalar(out=wq.ap, in0=wq.ap, scalar1=0.125, scalar2=None, op0=ALU.mult), [wq.b], [wq.b])
        for kvh in range(2):
            for half in range(2):
                for t0 in range(0, T, 2048):
                    tw = min(2048, T - t0)
                    dma("sync", kraw.ap[64 * half:64 * half + 64, t0:t0 + tw], FM[n].ap[C_CK + 64 * kvh:C_CK + 64 * kvh + 64, t0:t0 + tw], [FM[n].b], [kraw.b])
            headnorm_fm(kn, kraw, T, wk.ap[:, 0:1], blk64_b, 64.0, 1.0)
            dma("sync", vraw.ap, TMv[n].ap[:, 1536 + 64 * kvh:1536 + 64 * kvh + 64].rearrange("(b p) c -> p b c", p=128), [TMv[n].b], [vraw.b])
            for i in range(2):
                V(lambda e, i=i: e.tensor_copy(out=vp[i].ap[:, :, 64 * i:64 * i + 64], in_=vraw.ap), [vraw.b], [vp[i].b])
            for hp in range(2):
                h0 = kvh * 4 + hp * 2
                for half in range(2):
                    dma("sync", sk.ap[64 * half:64 * half + 64, 0:1], swa_sink.ap[l:l + 1, h0 + half:h0 + half + 1].broadcast_to([64, 1]), [], [sk.b])
                kb.op("scalar", lambda e: e.activation(out=sk.ap[:, 1:2], in_=sk.ap[:, 0:1], func=AF.Exp), reads=[sk.b], writes=[sk.b])
                for t0 in range(0, T, 2048):
                    tw = min(2048, T - t0)
                    dma("sync", qraw.ap[:, t0:t0 + tw], FM[n].ap[C_CQ + 64 * h0:C_CQ + 64 * h0 + 128, t0:t0 + tw], [FM[n].b], [qraw.b])
                headnorm_fm(qn, qraw, T, wq.ap[:, 0:1], blk64_b, 64.0, 1.0)
                ostm = {}

                def qb_gen(qb, h0=h0, kvh=kvh, hp=hp, ostm=ostm):
                    if qb // 8 not in ostm:
                        ostm[qb // 8] = ost.get()
                    o_t = ostm[qb // 8]
                    os_ = [o for o in range(3) if 0 <= qb + o - 1 < nb]
                    o0, o1 = os_[0], os_[-1] + 1
                    pS = [nps(), nps()]
                    for hh in range(2):
                        for o in os_:
                            kbk = qb + o - 1
                            MM(pS[hh].ap[:, o * 128:(o + 1) * 128], kn.ap[64 * hh:64 * hh + 64, kbk * 128:(kbk + 1) * 128], qn.ap[64 * hh:64 * hh + 64, qb * 128:(qb + 1) * 128],
                               True, True, [kn.b, qn.b], [pS[hh].b])
                    yield
                    pf = Pf.get()
                    pb = Pb.get()
                    for hh in range(2):
                        kb.op("scalar", lambda e, hh=hh, pf=pf, pS=pS, o0=o0, o1=o1: e.activation(out=pf.ap[:, hh, o0 * 128:o1 * 128], in_=pS[hh].ap[:, o0 * 128:o1 * 128], func=AF.Exp),
                              reads=[pS[hh].b], writes=[pf.b])
                    yield
                    for hh in range(2):
                        V(lambda e, hh=hh, pf=pf, pb=pb, o0=o0, o1=o1, h0=h0: e.tensor_tensor(out=pb.ap[:, hh, o0 * 128:o1 * 128].rearrange("p (o q) -> p o q", q=128),
                                                                                              in0=pf.ap[:, hh, o0 * 128:o1 * 128].rearrange("p (o q) -> p o q", q=128),
                                                                                              in1=EB.ap[:, o0:o1, h0 + hh, :], op=ALU.mult),
                          [pf.b, EB.b], [pb.b], eng=("vector" if hh else "gpsimd"))
                    yield
                    pOD = nps()
                    tot = 2 * len(os_)
                    cnt = 0
                    for hh in range(2):
                        for o in os_:
                            kbk = qb + o - 1
                            MM(pOD.ap[:, 0:128], vp[hh].ap[:, kbk, :], pb.ap[:, hh, o * 128:(o + 1) * 128], cnt == 0, cnt == tot - 1, [vp[hh].b, pb.b], [pOD.b])
                            cnt += 1
                    cnt = 0
                    for hh in range(2):
                        for o in os_:
                            MM(pOD.ap[:, 128:256], on[hh].ap, pb.ap[:, hh, o * 128:(o + 1) * 128], cnt == 0, cnt == tot - 1, [on[hh].b, pb.b], [pOD.b])
                            cnt += 1
                    yield
                    r = rc.get()
                    V(lambda e, r=r, pOD=pOD: e.tensor_scalar(out=r.ap, in0=pOD.ap[:, 128:256], scalar1=sk.ap[:, 1:2], scalar2=None, op0=ALU.add), [pOD.b, sk.b], [r.b])
                    V(lambda e, r=r: e.reciprocal(out=r.ap, in_=r.ap), [r.b], [r.b])
                    yield
                    V(lambda e, r=r, pOD=pOD, o_t=o_t, qb=qb: e.tensor_tensor(out=o_t.ap[:, (qb % 8) * 128:(qb % 8) * 128 + 128], in0=pOD.ap[:, 0:128], in1=r.ap, op=ALU.mult),
                      [pOD.b, r.b], [o_t.b])
                    if qb % 8 == 7:
                        row = 1024 + 128 * (kvh * 2 + hp)
                        dma("sync", YM[n].ap[row:row + 128, (qb - 7) * 128:(qb + 1) * 128], o_t.ap, [o_t.b], [YM[n].b])

                for q0 in range(0, nb, GQ_W):
                    interleave([qb_gen(q) for q in range(q0, min(q0 + GQ_W, nb))])

    def na(l, n, T):
        na_q_norm, na_k_norm, na_rpb, k_na_m = g_["na_q_norm"], g_["na_k_norm"], g_["na_rpb"], g_["k_na_m"]
        rows = T // 64
        nb = T // 128
        Mc = sb("na_mc", [128, 31, 64])
        RP = sb("na_rp", [128, 14, 31])
        ET = sb("na_et", [128, 14, 64])
        okm = sb("na_ok", [128, 64])
        tmpE = sb("na_te", [128, 14, 64])
        qraw = sb("na_qraw", [128, T], BF16)
        kraw = sb("na_kraw", [128, T], BF16)
        qn = sb("na_qn", [128, T], BF16)
        kn = sb("na_kn", [128, T], BF16)
        vA = sb("na_vA", [128, nb, 128], BF16)
        vB = sb("na_vB", [128, nb, 128], BF16)
        wq = sb("na_wq", [128, 1])
        wk = sb("na_wk", [128, 1])
        Pf = Pool("na_pf", [128, 4, 64], F32, 6)
        Pb = Pool("na_pb", [128, 4, 64], BF16, 6)
        rc = Pool("na_rc", [128, 64], F32, 6)
        ost = Pool("na_ost", [128, 1024], BF16, 2)
        dma("sync", Mc.ap, k_na_m.ap.rearrange("d p q -> p d q"), [], [Mc.b])
        dma("sync", wq.ap, na_q_norm.ap[l:l + 1, :].rearrange("o p -> p o"), [], [wq.b], slow=True)
        dma("sync", wk.ap, na_k_norm.ap[l:l + 1, :].rearrange("o p -> p o"), [], [wk.b], slow=True)
        V(lambda e: e.tensor_scalar(out=wq.ap, in0=wq.ap, scalar1=float(128 ** -0.5), scalar2=None, op0=ALU.mult), [wq.b], [wq.b])
        V(lambda e: e.memset(okm.ap, 0.0), [], [okm.b])
        for dc in range(31):
            V(lambda e, dc=dc: e.tensor_tensor(out=okm.ap, in0=okm.ap, in1=Mc.ap[:, dc, :], op=ALU.add), [Mc.b, okm.b], [okm.b])
        for h in range(4):
            for half in range(2):
                dma("sync", RP.ap[64 * half:64 * half + 64, :, :], na_rpb.ap[l, h:h + 1, half:half + 14, :].broadcast_to([64, 14, 31]), [], [RP.b])
            V(lambda e: e.memset(ET.ap, 0.0), [], [ET.b])
            for dc in range(31):
                V(lambda e, dc=dc: e.tensor_tensor(out=tmpE.ap, in0=Mc.ap[:, dc, :].unsqueeze(1).broadcast_to([128, 14, 64]),
                                                    in1=RP.ap[:, :, dc].unsqueeze(2).broadcast_to([128, 14, 64]), op=ALU.mult), [Mc.b, RP.b], [tmpE.b])
                V(lambda e: e.tensor_tensor(out=ET.ap, in0=ET.ap, in1=tmpE.ap, op=ALU.add), [tmpE.b, ET.b], [ET.b], eng="gpsimd")
            kb.op("scalar", lambda e: e.activation(out=ET.ap, in_=ET.ap, func=AF.Exp), reads=[ET.b], writes=[ET.b])
            V(lambda e: e.tensor_tensor(out=ET.ap, in0=ET.ap, in1=okm.ap.unsqueeze(1).broadcast_to([128, 14, 64]), op=ALU.mult), [ET.b, okm.b], [ET.b])
            for t0 in range(0, T, 2048):
                tw = min(2048, T - t0)
                dma("sync", qraw.ap[:, t0:t0 + tw], FM[n].ap[C_BQ + 128 * h:C_BQ + 128 * h + 128, t0:t0 + tw], [FM[n].b], [qraw.b])
                dma("sync", kraw.ap[:, t0:t0 + tw], FM[n].ap[C_BK + 128 * h:C_BK + 128 * h + 128, t0:t0 + tw], [FM[n].b], [kraw.b])
            headnorm_fm(qn, qraw, T, wq.ap[:, 0:1], ones_b, 128.0, 1.0)
            headnorm_fm(kn, kraw, T, wk.ap[:, 0:1], ones_b, 128.0, 1.0)
            dma("sync", vA.ap, TMv[n].ap[:, 1024 + 128 * h:1024 + 128 * h + 128].rearrange("(b p) c -> p b c", p=128), [TMv[n].b], [vA.b])
            dma("sync", vB.ap[:, 0:nb - 1, :], TMv[n].ap[64:T - 64, 1024 + 128 * h:1024 + 128 * h + 128].rearrange("(b p) c -> p b c", p=128), [TMv[n].b], [vB.b])
            ostm = {}

            def row_gen(r):
                if r // 16 not in ostm:
                    ostm[r // 16] = ost.get()
                o_t = ostm[r // 16]
                start = min(max(r - 4, 0), rows - 8)
                off = start - r + 7
                pS = nps()
                for j2 in range(4):
                    k0 = 64 * (start + 2 * j2)
                    MM(pS.ap[:, j2 * 64:(j2 + 1) * 64], kn.ap[:, k0:k0 + 128], qn.ap[:, 64 * r:64 * r + 64], True, True, [kn.b, qn.b], [pS.b])
                yield
                pf = Pf.get()
                pb = Pb.get()
                kb.op("scalar", lambda e, pf=pf, pS=pS: e.activation(out=pf.ap, in_=pS.ap[:, 0:256].rearrange("p (j q) -> p j q", q=64), func=AF.Exp), reads=[pS.b], writes=[pf.b])
                yield
                V(lambda e, pf=pf, pb=pb, off=off: e.tensor_tensor(out=pb.ap, in0=pf.ap, in1=ET.ap[:, off:off + 7:2, :], op=ALU.mult), [pf.b, ET.b], [pb.b],
                  eng=("vector" if r % 2 else "gpsimd"))
                yield
                pOD = nps()
                for j2 in range(4):
                    rr = start + 2 * j2
                    vt = vA.ap[:, rr // 2, :] if rr % 2 == 0 else vB.ap[:, (rr - 1) // 2, :]
                    vb_ = vA.b if rr % 2 == 0 else vB.b
                    MM(pOD.ap[:, 0:64], vt, pb.ap[:, j2, :], j2 == 0, j2 == 3, [vb_, pb.b], [pOD.b])
                for j2 in range(4):
                    MM(pOD.ap[:, 64:128], ones_b.ap, pb.ap[:, j2, :], j2 == 0, j2 == 3, [ones_b.b, pb.b], [pOD.b])
                yield
                rcp = rc.get()
                V(lambda e, rcp=rcp, pOD=pOD: e.reciprocal(out=rcp.ap, in_=pOD.ap[:, 64:128]), [pOD.b], [rcp.b])
                yield
                V(lambda e, rcp=rcp, pOD=pOD, o_t=o_t, r=r: e.tensor_tensor(out=o_t.ap[:, (r % 16) * 64:(r % 16) * 64 + 64], in0=pOD.ap[:, 0:64], in1=rcp.ap, op=ALU.mult),
                  [pOD.b, rcp.b], [o_t.b])
                if r % 16 == 15:
                    dma("sync", YM[n].ap[512 + 128 * h:512 + 128 * h + 128, (r - 15) * 64:(r + 1) * 64], o_t.ap, [o_t.b], [YM[n].b])

            for r0 in range(0, rows, NA_W):
                interleave([row_gen(r) for r in range(r0, min(r0 + NA_W, rows))])

    def mlstm(l, n, T):
        mlstm_norm_w = g_["mlstm_norm_w"]
        nb = T // 128
        qT = sb("ml_qT", [128, T], BF16)
        kT = sb("ml_kT", [128, T], BF16)
        ktm = sb("ml_ktm", [128, nb, 128], BF16)
        vaug = sb("ml_va", [128, nb, 132], BF16)
        gt = sb("ml_gt", [128, nb, 16])
        hfw = sb("ml_hfw", [128, nb, 128])
        nwb = sb("ml_nw", [128, 128])
        lf = sb("ml_lf", [128, 2, nb])
        bcs = sb("ml_b", [128, 2, nb])
        gb = sb("ml_g", [128, 2, nb])
        beta = sb("ml_beta", [128, 2, nb])
        gam = sb("ml_gam", [128, 2, nb])
        emb = sb("ml_emb", [128, 2, nb])
        eg = sb("ml_eg", [128, 2, nb])
        Cst = sb("ml_C", [128, 132])
        Cb = sb("ml_Cb", [128, 132], BF16)
        STp = Pool("ml_st", [128, 128], BF16, 3)
        kgp = Pool("ml_kg", [128, 128], BF16, 3)
        sm = Pool("ml_sm", [128, 4], F32, 4)
        hs = Pool("ml_hs", [128, 128], F32, 3)
        hb = Pool("ml_hb", [128, 128], BF16, 3)
        jk = sb("ml_jk", [128, 128], BF16)
        ost = Pool("ml_ost", [128, 1024], BF16, 2)
        V(lambda e: e.memset(vaug.ap[:, :, 128:129], 1.0), [], [vaug.b])
        for h in range(4):
            for t0 in range(0, T, 2048):
                tw = min(2048, T - t0)
                dma("sync", qT.ap[:, t0:t0 + tw], FM[n].ap[C_AQ + 128 * h:C_AQ + 128 * h + 128, t0:t0 + tw], [FM[n].b], [qT.b])
                dma("sync", kT.ap[:, t0:t0 + tw], FM[n].ap[C_AK + 128 * h:C_AK + 128 * h + 128, t0:t0 + tw], [FM[n].b], [kT.b])
            dma("sync", ktm.ap, TMv[n].ap[:, 128 * h:128 * h + 128].rearrange("(b p) c -> p b c", p=128), [TMv[n].b], [ktm.b])
            dma("sync", vaug.ap[:, :, 0:128], TMv[n].ap[:, 512 + 128 * h:512 + 128 * h + 128].rearrange("(b p) c -> p b c", p=128), [TMv[n].b], [vaug.b])
            dma("sync", gt.ap, TMg[n].ap.rearrange("(b p) c -> p b c", p=128), [TMg[n].b], [gt.b])
            dma("sync", nwb.ap, mlstm_norm_w.ap[l:l + 1, 128 * h:128 * h + 128].broadcast_to([128, 128]), [], [nwb.b])
            for d, dn in enumerate(("fw", "bw")):
                icol, fcol = 4 * d + h, 8 + 4 * d + h
                kb.op("scalar", lambda e, d=d, fcol=fcol: e.activation(out=lf.ap[:, d, :], in_=gt.ap[:, :, fcol], func=AF.Exp, scale=-1.0), reads=[gt.b], writes=[lf.b])
                V(lambda e, d=d: e.tensor_scalar(out=lf.ap[:, d, :], in0=lf.ap[:, d, :], scalar1=1.0, scalar2=None, op0=ALU.add), [lf.b], [lf.b])
                kb.op("scalar", lambda e, d=d: e.activation(out=lf.ap[:, d, :], in_=lf.ap[:, d, :], func=AF.Ln), reads=[lf.b], writes=[lf.b])
                V(lambda e, d=d: e.tensor_scalar(out=lf.ap[:, d, :], in0=lf.ap[:, d, :], scalar1=-1.0, scalar2=None, op0=ALU.mult), [lf.b], [lf.b])
                p = nps()
                MM(p.ap[:, 0:nb], tri_f[dn].ap, lf.ap[:, d, :], True, True, [tri_f[dn].b, lf.b], [p.b])
                V(lambda e, d=d, p=p: e.tensor_copy(out=bcs.ap[:, d, :], in_=p.ap[:, 0:nb]), [p.b], [bcs.b])
                p2 = nps()
                MM(p2.ap[:, 0:nb], ones_f.ap, lf.ap[:, d, :], True, True, [ones_f.b, lf.b], [p2.b])
                V(lambda e, d=d, p2=p2: e.tensor_copy(out=gb.ap[:, d, :], in_=p2.ap[:, 0:nb]), [p2.b], [gb.b])
                kb.op("scalar", lambda e, d=d: e.activation(out=eg.ap[:, d, :], in_=gb.ap[:, d, :], func=AF.Exp), reads=[gb.b], writes=[eg.b])
                kb.op("scalar", lambda e, d=d: e.activation(out=emb.ap[:, d, :], in_=bcs.ap[:, d, :], func=AF.Exp, scale=-1.0), reads=[bcs.b], writes=[emb.b])
                V(lambda e, d=d, icol=icol: e.tensor_tensor(out=beta.ap[:, d, :], in0=gt.ap[:, :, icol], in1=bcs.ap[:, d, :], op=ALU.subtract), [gt.b, bcs.b], [beta.b])
                kb.op("scalar", lambda e, d=d: e.activation(out=beta.ap[:, d, :], in_=beta.ap[:, d, :], func=AF.Exp), reads=[beta.b], writes=[beta.b])
                V(lambda e, d=d: e.tensor_scalar(out=beta.ap[:, d, :], in0=beta.ap[:, d, :], scalar1=float(128 ** -0.5), scalar2=None, op0=ALU.mult), [beta.b], [beta.b])
                V(lambda e, d=d: e.tensor_tensor(out=gam.ap[:, d, :], in0=beta.ap[:, d, :], in1=eg.ap[:, d, :], op=ALU.mult), [beta.b, eg.b], [gam.b])
            o_t = None
            for d, dn in enumerate(("fw", "bw")):
                order = list(range(nb)) if d == 0 else list(range(nb - 1, -1, -1))
                for ci, c in enumerate(order):
                    csl = slice(c * 128, (c + 1) * 128)
                    pR = nps()
                    MM(pR.ap[:, 0:128], kT.ap[:, csl], qT.ap[:, csl], True, True, [kT.b, qT.b], [pR.b])
                    stt = STp.get()
                    V(lambda e, stt=stt, pR=pR, d=d, c=c, dn=dn: e.scalar_tensor_tensor(out=stt.ap, in0=pR.ap[:, 0:128], scalar=beta.ap[:, d, c:c + 1], in1=tri_f[dn].ap,
                                                                                          op0=ALU.mult, op1=ALU.mult), [pR.b, beta.b, tri_f[dn].b], [stt.b])
                    pX = nps()
                    MM(pX.ap[:, 0:129], stt.ap, vaug.ap[:, c, 0:129], True, ci == 0, [stt.b, vaug.b], [pX.b])
                    if ci > 0:
                        MM(pX.ap[:, 0:129], qT.ap[:, csl], Cb.ap[:, 0:129], False, True, [qT.b, Cb.b], [pX.b])
                    s4 = sm.get()
                    kb.op("scalar", lambda e, s4=s4, pX=pX: e.activation(out=s4.ap[:, 0:1], in_=pX.ap[:, 128:129], func=AF.Abs), reads=[pX.b], writes=[s4.b])
                    V(lambda e, s4=s4, d=d, c=c: e.tensor_tensor(out=s4.ap[:, 0:1], in0=s4.ap[:, 0:1], in1=emb.ap[:, d, c:c + 1], op=ALU.max), [s4.b, emb.b], [s4.b])
                    V(lambda e, s4=s4: e.reciprocal(out=s4.ap[:, 1:2], in_=s4.ap[:, 0:1]), [s4.b], [s4.b])
                    if d == 0:
                        V(lambda e, s4=s4, pX=pX, c=c: e.tensor_scalar(out=hfw.ap[:, c, :], in0=pX.ap[:, 0:128], scalar1=s4.ap[:, 1:2], scalar2=None, op0=ALU.mult),
                          [pX.b, s4.b], [hfw.b])
                    else:
                        if ci % 8 == 0:
                            o_t = ost.get()
                        hsum = hs.get()
                        V(lambda e, s4=s4, pX=pX, c=c, hsum=hsum: e.scalar_tensor_tensor(out=hsum.ap, in0=pX.ap[:, 0:128], scalar=s4.ap[:, 1:2], in1=hfw.ap[:, c, :],
                                                                                          op0=ALU.mult, op1=ALU.add), [pX.b, s4.b, hfw.b], [hsum.b])
                        kb.op("scalar", lambda e, hsum=hsum, s4=s4: e.activation(out=jk.ap, in_=hsum.ap, func=AF.Square, accum_out=s4.ap[:, 2:3]),
                              reads=[hsum.b], writes=[jk.b, s4.b])
                        V(lambda e, s4=s4: e.tensor_scalar(out=s4.ap[:, 3:4], in0=s4.ap[:, 2:3], scalar1=1.0 / 128, scalar2=EPS, op0=ALU.mult, op1=ALU.add), [s4.b], [s4.b])
                        kb.op("scalar", lambda e, s4=s4: e.activation(out=s4.ap[:, 3:4], in_=s4.ap[:, 3:4], func=AF.Sqrt), reads=[s4.b], writes=[s4.b])
                        V(lambda e, s4=s4: e.reciprocal(out=s4.ap[:, 3:4], in_=s4.ap[:, 3:4]), [s4.b], [s4.b])
                        hbt = hb.get()
                        V(lambda e, hsum=hsum, s4=s4, hbt=hbt: e.scalar_tensor_tensor(out=hbt.ap, in0=hsum.ap, scalar=s4.ap[:, 3:4], in1=nwb.ap, op0=ALU.mult, op1=ALU.mult),
                          [hsum.b, s4.b, nwb.b], [hbt.b])
                        pT = npsb()
                        kb.op("tensor", lambda e, pT=pT, hbt=hbt: e.transpose(pT.ap[:, 0:128], hbt.ap, ident_b.ap), reads=[hbt.b, ident_b.b], writes=[pT.b])
                        copy_op("scalar", o_t.ap[:, (c % 8) * 128:(c % 8) * 128 + 128], pT.ap[:, 0:128], [pT.b], [o_t.b])
                        if c % 8 == 0:
                            dma("sync", YM[n].ap[128 * h:128 * h + 128, c * 128:(c + 8) * 128], o_t.ap, [o_t.b], [YM[n].b])
                    if ci < nb - 1:
                        kg = kgp.get()
                        V(lambda e, kg=kg, d=d, c=c: e.tensor_scalar(out=kg.ap, in0=ktm.ap[:, c, :], scalar1=gam.ap[:, d, c:c + 1], scalar2=None, op0=ALU.mult),
                          [ktm.b, gam.b], [kg.b], eng="gpsimd")
                        pC = nps()
                        MM(pC.ap[:, 0:129], kg.ap, vaug.ap[:, c, 0:129], True, True, [kg.b, vaug.b], [pC.b])
                        if ci == 0:
                            V(lambda e, pC=pC: e.tensor_copy(out=Cst.ap[:, 0:129], in_=pC.ap[:, 0:129]), [pC.b], [Cst.b])
                        else:
                            V(lambda e, pC=pC, d=d, c=c: e.scalar_tensor_tensor(out=Cst.ap[:, 0:129], in0=Cst.ap[:, 0:129], scalar=eg.ap[:, d, c:c + 1], in1=pC.ap[:, 0:129],
                                                                                  op0=ALU.mult, op1=ALU.add), [Cst.b, eg.b, pC.b], [Cst.b])
                        copy_op("scalar", Cb.ap[:, 0:129], Cst.ap[:, 0:129], [Cst.b], [Cb.b])

    PERSIST = g_["PERSIST"]
    aoff = g_["aoff"]

    def MIX(l, si, n, T):
        g_["psmode"][0] = 6
        for fn in (conv, gqa, na, mlstm):
            if fn.__name__ in SKIP_MIX:
                continue
            fn(l, n, T)
            kb.barrier()
            aoff[0] = PERSIST
        g_["psmode"][0] = 6
    MIX.gqa_tables = gqa_tables
    return MIX


SKIP_MIX = set()
DEBUG_YM = False


_W_KEYS = ["rel_bias", "norm_w", "w_ada", "b_ada", "w_in", "b_gate", "mlstm_norm_w", "na_q_norm", "na_k_norm", "na_rpb",
           "swa_q_norm", "swa_k_norm", "swa_sink", "conv_w", "conv_b", "conv_ln_w", "conv_ln_b", "w_branch", "w_out"]


def kernel(**inputs):
    xp = np.ascontiguousarray(np.asarray(inputs["x_prompt"], dtype=np.float32))
    xs = np.ascontiguousarray(np.asarray(inputs["x_sample"], dtype=np.float32))
    cp = np.asarray(inputs["c_prompt"], dtype=np.float32)
    cs = np.asarray(inputs["c_sample"], dtype=np.float32)
    TP, TS = xp.shape[1], xs.shape[1]
    depth = np.asarray(inputs["norm_w"]).shape[0]
    nc = build([("P", TP), ("S", TS)], depth)
    consts = make_consts()
    shared = {k: np.ascontiguousarray(np.asarray(inputs[k], dtype=np.float32)) for k in _W_KEYS}
    shared.update({"k_ident": consts["ident"], "k_tri_fw": consts["tri_fw"], "k_tri_bw": consts["tri_bw"], "k_blk64": consts["blk64"],
                   "k_gqa_m": consts["gqa_m"], "k_na_m": consts["na_m"]})
    in_maps = []
    for i in range(8):
        m = dict(shared)
        m["x_P"] = xp[i // 4]
        m["c_P"] = cp[i // 4][None]
        m["x_S"] = xs[i // 2]
        m["c_S"] = cs[i // 2][None]
        in_maps.append(m)
    res = run_bass_kernel_spmd(nc, in_maps, core_ids=list(range(8)))
    yp = np.empty_like(xp)
    ys = np.empty_like(xs)
    qp, qs = TP // 4, TS // 2
    for i in range(8):
        r = res.results[i]
        a = i % 4
        yp[i // 4, a * qp:(a + 1) * qp] = np.asarray(r["y_P"])[a * qp:(a + 1) * qp]
        b = i % 2
        ys[i // 2, b * qs:(b + 1) * qs] = np.asarray(r["y_S"])[b * qs:(b + 1) * qs]
    return (yp, ys)
NA_W = 3
GQ_W = 2
```

```python
import contextlib
import math
import numpy as np
import concourse.bass as bass
import concourse.mybir as mybir
from concourse.bass_utils import run_bass_kernel_spmd

F32 = mybir.dt.float32
BF16 = mybir.dt.bfloat16
ALU = mybir.AluOpType
AF = mybir.ActivationFunctionType

D = 2048
KC = 16
IN_COLS = 15632
EPS = 1e-6
C_AQ, C_AK, C_AV, C_AO, C_AZ, C_AG = 0, 512, 1024, 1536, 2048, 2560
C_BQ, C_BK, C_BV, C_BZ = 2576, 3088, 3600, 4112
C_CQ, C_CK, C_CV, C_CZ = 4624, 5136, 5264, 5392
C_DA, C_DG, C_DZ, C_MG = 5904, 6416, 6928, 7440
NFM = 7440

ENGS = ("tensor", "vector", "scalar", "gpsimd", "sync")
NDMA = 28
SEM_EPOCH = 30000


class Buf:
    __slots__ = ("w", "r")

    def __init__(self):
        self.w = None
        self.r = []


class KB:
    def __init__(self, nc):
        self.nc = nc
        self.ops = {e: [] for e in ENGS}
        self.cnt = {}
        self.known = {e: {} for e in ENGS}
        self.dma_rr = 0
        self.dma_last = {}
        self.pending = {e: [] for e in ENGS}
        self.tot = {}

    def barrier(self):
        for e in ENGS:
            for k, v in list(self.cnt.items()) + list(self.dma_last.items()):
                self._need(e, (k, v), self.pending[e])

    def _need(self, eng, dep, waits):
        if dep is None:
            return
        k, v = dep
        if eng == "tensor" and k.startswith("tensor#"):
            return
        if self.known[eng].get(k, 0) >= v:
            return
        self.known[eng][k] = v
        waits.append((k, v))

    def op(self, eng, fn, reads=(), writes=(), dma=False):
        waits = self.pending[eng]
        self.pending[eng] = []
        for b in reads:
            self._need(eng, b.w, waits)
        for b in writes:
            self._need(eng, b.w, waits)
            for d in b.r:
                self._need(eng, d, waits)
        if dma:
            k = f"dma{self.dma_rr % NDMA}"
            self.dma_rr += 1
            last = self.dma_last.get(k, 0)
            if last:
                self._need(eng, (k, last), waits)
            v = last + 16
            self.dma_last[k] = v
            inc = 16
        else:
            tot = self.tot.get(eng, 0) + 1
            self.tot[eng] = tot
            k = f"{eng}#{(tot - 1) // SEM_EPOCH}"
            v = (tot - 1) % SEM_EPOCH + 1
            self.cnt[k] = v
            inc = 1
        tag = (k, v)
        for b in reads:
            b.r.append(tag)
            if len(b.r) > 64:
                b.r = b.r[-64:]
        for b in writes:
            b.w = tag
            b.r = []
        self.ops[eng].append((waits, fn, k, inc))

    def emit(self):
        nc = self.nc
        keys = set()
        for e in ENGS:
            for waits, fn, k, inc in self.ops[e]:
                keys.add(k)
        with contextlib.ExitStack() as st:
            sems = {k: st.enter_context(nc.semaphore(f"s_{k}")) for k in sorted(keys)}
            block = st.enter_context(nc.Block())

            def mk(e):
                def body(eng):
                    if e == "sync" and getattr(self, "pre_sync", None) is not None:
                        self.pre_sync(eng)
                    for waits, fn, k, inc in self.ops[e]:
                        for (wk, wv) in waits:
                            eng.wait_ge(sems[wk], wv)
                        fn(eng).then_inc(sems[k], inc)
                    if e == "sync":
                        for k2 in sorted(keys):
                            tot = self.dma_last.get(k2) if k2.startswith("dma") else self.cnt.get(k2)
                            if tot:
                                eng.wait_ge(sems[k2], tot)
                return body

            for e in ENGS:
                if self.ops[e] or e == "sync":
                    getattr(block, e)(mk(e))


class TL:
    def __init__(self, ap):
        self.ap = ap
        self.b = Buf()

    def __getitem__(self, k):
        return self.ap[k]


def t5_bucket_np(rel):
    half, max_exact = 16, 8
    n = np.abs(rel)
    nf = np.maximum(n, 1).astype(np.float32)
    large = max_exact + (np.log(nf / np.float32(max_exact)) / np.float32(math.log(128 / max_exact)) * (half - max_exact)).astype(np.int32)
    large = np.minimum(large, half - 1)
    return np.where(rel > 0, half, 0) + np.where(n < max_exact, n, large)


def make_consts():
    c = {}
    c["ident"] = np.eye(128, dtype=np.float32)
    s = np.arange(128)
    c["tri_fw"] = (s[:, None] <= s[None, :]).astype(np.float32)
    c["tri_bw"] = (s[:, None] >= s[None, :]).astype(np.float32)
    bo = np.zeros((128, 128), np.float32)
    bo[:64, :64] = 1
    bo[64:, 64:] = 1
    c["blk64"] = bo
    k = np.arange(128)[:, None]
    q = np.arange(128)[None, :]
    gm = np.zeros((3, 32, 128, 128), np.float32)
    for o in range(3):
        rel = k + 128 * (o - 1) - q
        bk = t5_bucket_np(rel)
        ok = np.abs(rel) <= 128
        for b in range(32):
            gm[o, b] = ((bk == b) & ok)
    c["gqa_m"] = gm
    kc = np.arange(64)[:, None]
    qc = np.arange(64)[None, :]
    qs = np.clip(qc - 8, 0, 48)
    ok = (kc >= qs) & (kc < qs + 16)
    dc = np.clip(kc - qc + 15, 0, 30)
    nm = np.zeros((31, 128, 64), np.float32)
    for d in range(31):
        m = ((dc == d) & ok).astype(np.float32)
        nm[d, :64] = m
        nm[d, 64:] = m
    c["na_m"] = nm
    return c


def build(seqs, depth, dbg=()):
    nc = bass.Bass("TRN2", target_bir_lowering=False)
    kb = KB(nc)
    st = contextlib.ExitStack()

    def din(name, shape, dt=F32):
        return TL(nc.dram_tensor(name, list(shape), dt, kind="ExternalInput").ap())

    def dscr(name, shape, dt=BF16, kind="Internal"):
        return TL(nc.dram_tensor(name, list(shape), dt, kind=kind).ap())

    AW = 53200
    arena = st.enter_context(nc.sbuf_tensor("arena", [128, AW], F32))
    aoff = [0]

    def sb(name, shape, dt=F32):
        n = 1
        for d_ in shape[1:]:
            n *= d_
        words = (n * (2 if dt == BF16 else 4) + 3) // 4
        assert aoff[0] + words <= AW, (name, aoff[0], words)
        v = arena[0:shape[0], aoff[0]:aoff[0] + words]
        aoff[0] += words
        if dt != F32:
            v = v.bitcast(dt)
        if len(shape) == 3:
            v = v.rearrange("p (a b) -> p a b", a=shape[1])
        elif len(shape) == 4:
            v = v.rearrange("p (a b c) -> p a b c", a=shape[1], b=shape[2])
        return TL(v)

    def ps(name, shape, dt=F32):
        return TL(st.enter_context(nc.psum_tensor(name, list(shape), dt))[:])

    X = {n: din(f"x_{n}", [T, D]) for n, T in seqs}
    Cc = {n: din(f"c_{n}", [1, D]) for n, T in seqs}
    Yout = {n: dscr(f"y_{n}", [T, D], F32, kind="ExternalOutput") for n, T in seqs}
    rel_bias = din("rel_bias", [32, 8])
    norm_w = din("norm_w", [depth, D])
    w_ada = din("w_ada", [depth, D, 3 * D])
    b_ada = din("b_ada", [depth, 3 * D])
    w_in = din("w_in", [depth, D, IN_COLS])
    b_gate = din("b_gate", [depth, 16])
    mlstm_norm_w = din("mlstm_norm_w", [depth, 512])
    na_q_norm = din("na_q_norm", [depth, 128])
    na_k_norm = din("na_k_norm", [depth, 128])
    na_rpb = din("na_rpb", [depth, 4, 15, 31])
    swa_q_norm = din("swa_q_norm", [depth, 64])
    swa_k_norm = din("swa_k_norm", [depth, 64])
    swa_sink = din("swa_sink", [depth, 8])
    conv_w = din("conv_w", [depth, 31, 512])
    conv_b = din("conv_b", [depth, 512])
    conv_ln_w = din("conv_ln_w", [depth, 512])
    conv_ln_b = din("conv_ln_b", [depth, 512])
    w_branch = din("w_branch", [depth, 4, 512, D])
    w_out = din("w_out", [depth, D, D])
    k_ident = din("k_ident", [128, 128])
    k_tri_fw = din("k_tri_fw", [128, 128])
    k_tri_bw = din("k_tri_bw", [128, 128])
    k_blk64 = din("k_blk64", [128, 128])
    k_gqa_m = din("k_gqa_m", [3, 32, 128, 128])
    k_na_m = din("k_na_m", [31, 128, 64])

    FM = {n: dscr(f"fm_{n}", [NFM, T]) for n, T in seqs}
    TMv = {n: dscr(f"tm_{n}", [T, 1664]) for n, T in seqs}
    TMg = {n: dscr(f"tg_{n}", [T, 16], F32) for n, T in seqs}
    HT = {n: dscr(f"ht_{n}", [D, T]) for n, T in seqs}
    YM = {n: dscr(f"ym_{n}", [D, T], BF16, kind=("ExternalOutput" if DEBUG_YM else "Internal")) for n, T in seqs}
    X1 = {n: dscr(f"x1_{n}", [T, D], F32) for n, T in seqs}
    modrow = dscr("modrow", [depth * len(seqs), D], F32)
    WT = {}
    WSRC = {}
    DBG = {}

    ident_f = sb("ident_f", [128, 128])
    ident_b = sb("ident_b", [128, 128], BF16)
    ones_b = sb("ones_b", [128, 128], BF16)
    ones_f = sb("ones_f", [128, 128])
    blk64_b = sb("blk64_b", [128, 128], BF16)
    tri_f = {"fw": sb("tri_fw", [128, 128]), "bw": sb("tri_bw", [128, 128])}
    kb.op("sync", lambda e: e.dma_start(out=ident_f.ap, in_=k_ident.ap), writes=[ident_f.b], dma=True)
    kb.op("vector", lambda e: e.tensor_copy(out=ident_b.ap, in_=ident_f.ap), reads=[ident_f.b], writes=[ident_b.b])
    kb.op("vector", lambda e: e.memset(ones_b.ap, 1.0), writes=[ones_b.b])
    kb.op("vector", lambda e: e.memset(ones_f.ap, 1.0), writes=[ones_f.b])
    kb.op("sync", lambda e: e.dma_start(out=tri_f["fw"].ap, in_=k_tri_fw.ap), writes=[tri_f["fw"].b], dma=True)
    kb.op("sync", lambda e: e.dma_start(out=tri_f["bw"].ap, in_=k_tri_bw.ap), writes=[tri_f["bw"].b], dma=True)
    eps_c = sb("eps_c", [128, 1])
    kb.op("vector", lambda e: e.memset(eps_c.ap, EPS), writes=[eps_c.b])
    tmpc = sb("tmpc", [128, 128])
    kb.op("sync", lambda e: e.dma_start(out=tmpc.ap, in_=k_blk64.ap), writes=[tmpc.b], dma=True)
    kb.op("vector", lambda e: e.tensor_copy(out=blk64_b.ap, in_=tmpc.ap), reads=[tmpc.b], writes=[blk64_b.b])

    PS = [ps(f"ps{i}", [128, 512]) for i in range(6)]
    PSB = [ps(f"psb{i}", [128, 1024], BF16) for i in range(2)]
    psrr = [0]

    psmode = [6]

    def nps():
        psrr[0] += 1
        return PS[psrr[0] % psmode[0]]

    PSH = [TL(PS[4].ap[:, 0:256]), TL(PS[4].ap[:, 256:512]), TL(PS[5].ap[:, 0:256]), TL(PS[5].ap[:, 256:512])]
    pshr = [0]

    def npsh():
        pshr[0] += 1
        return PSH[pshr[0] % 4]

    psbr = [0]

    def npsb():
        psbr[0] += 1
        return PSB[psbr[0] % 2]

    class Pool:
        def __init__(self, name, shape, dt, n):
            self.t = [sb(f"{name}{i}", shape, dt) for i in range(n)]
            self.i = 0

        def get(self):
            self.i += 1
            return self.t[self.i % len(self.t)]

    evac_rr = [0]

    def evac_eng():
        evac_rr[0] += 1
        return "vector" if evac_rr[0] % 2 else "scalar"

    def copy_op(eng, out, in_, reads, writes):
        if eng == "scalar":
            kb.op("scalar", lambda e: e.activation(out=out, in_=in_, func=AF.Copy), reads=reads, writes=writes)
        else:
            kb.op(eng, lambda e: e.tensor_copy(out=out, in_=in_), reads=reads, writes=writes)

    nseq = len(seqs)
    modA = sb("modA", [128, depth, nseq, KC])
    modB = sb("modB", [128, depth, nseq, KC])
    cs = sb("cs", [128, KC, nseq])
    modfm = sb("modfm", [128, 48, nseq])
    badafm = sb("badafm", [128, 48])
    nwfm = sb("nwfm", [128, KC])
    bg_bc = sb("bg_bc", [128, 16])
    EB = sb("EB", [128, 3, 8, 128])
    PERSIST = aoff[0]

    def adaln():
        w32 = Pool("w32", [128, KC, 128], F32, 3)
        for si, (n, T) in enumerate(seqs):
            kb.op("sync", lambda e, si=si, n=n: e.dma_start(out=cs.ap[:, :, si], in_=Cc[n].ap.rearrange("o (k p) -> p (o k)", p=128), allow_slow_non_contiguous=True),
                  writes=[cs.b], dma=True)
        kb.op("scalar", lambda e: e.activation(out=cs.ap, in_=cs.ap, func=AF.Silu), reads=[cs.b], writes=[cs.b])
        for l in range(depth):
            kb.op("sync", lambda e, l=l: e.dma_start(out=badafm.ap, in_=b_ada.ap[l:l + 1, :].rearrange("o (k p) -> p (o k)", p=128), allow_slow_non_contiguous=True),
                  writes=[badafm.b], dma=True)
            kb.op("sync", lambda e, l=l: e.dma_start(out=nwfm.ap, in_=norm_w.ap[l:l + 1, :].rearrange("o (k p) -> p (o k)", p=128), allow_slow_non_contiguous=True),
                  writes=[nwfm.b], dma=True)
            pm = nps()
            for f in range(48):
                wt = w32.get()
                kb.op("sync", lambda e, l=l, f=f, wt=wt: e.dma_start(out=wt.ap, in_=w_ada.ap[l, :, f * 128:(f + 1) * 128].rearrange("(k p) n -> p k n", p=128)),
                      writes=[wt.b], dma=True)
                for k in range(KC):
                    kb.op("tensor", lambda e, k=k, f=f, pm=pm, wt=wt: e.matmul(pm.ap[:, f * nseq:(f + 1) * nseq], lhsT=wt.ap[:, k, :],
                                                                          rhs=cs.ap[:, k, :], start=(k == 0), stop=(k == KC - 1)),
                          reads=[wt.b, cs.b], writes=[pm.b])
            for si in range(nseq):
                kb.op("vector", lambda e, pm=pm, si=si: e.tensor_tensor(out=modfm.ap[:, :, si], in0=pm.ap[:, 0:48 * nseq].rearrange("p (f s) -> p f s", s=nseq)[:, :, si],
                                                                in1=badafm.ap, op=ALU.add),
                      reads=[pm.b, badafm.b], writes=[modfm.b])
            for si, (n, T) in enumerate(seqs):
                kb.op("vector", lambda e, l=l, si=si: e.scalar_tensor_tensor(out=modA.ap[:, l, si, :], in0=modfm.ap[:, 16:32, si], scalar=1.0, in1=nwfm.ap,
                                                                           op0=ALU.add, op1=ALU.mult),
                      reads=[modfm.b, nwfm.b], writes=[modA.b])
                kb.op("vector", lambda e, l=l, si=si: e.tensor_copy(out=modB.ap[:, l, si, :], in_=modfm.ap[:, 0:16, si]), reads=[modfm.b], writes=[modB.b])
                kb.op("sync", lambda e, l=l, si=si: e.dma_start(out=modrow.ap[l * nseq + si:l * nseq + si + 1, :].rearrange("o (k p) -> p (o k)", p=128),
                                                               in_=modfm.ap[:, 32:48, si], allow_slow_non_contiguous=True),
                      reads=[modfm.b], writes=[modrow.b], dma=True)
        kb.barrier()
        aoff[0] = PERSIST

    FM_RANGES = [(0, 1024), (1536, 2560), (2576, 3600), (4112, 5264), (5392, 7440)]
    TM_RANGES = [(512, 1536, 0), (3600, 4112, 1024), (5264, 5392, 1536)]

    def fm_func(col):
        if C_AO <= col < C_AZ:
            return AF.Sigmoid
        if C_AZ <= col < C_AG or C_BZ <= col < C_CQ or C_CZ <= col < C_DA or C_DZ <= col < C_MG:
            return AF.Silu
        return None

    def phase1(l, si, n, T, xsrc):
        xt_pool = Pool("xt", [128, D], F32, 2)
        xn_t = sb("xn", [128, 4, D], BF16)
        junk = sb("junk", [128, D], BF16)
        ssq = sb("ssq", [128, 8])
        hT = sb("hT", [128, KC, 1024], BF16)
        ofm = Pool("ofm", [128, 1024], BF16, 3)
        otm = Pool("otm", [128, 512], BF16, 3)
        otg = Pool("otg", [128, 16], F32, 2)
        wtile = Pool("wtile", [128, KC, 512], BF16, 4)

        wjobs = []
        for g_ in range(T // 1024):
            for (c0_, c1_) in FM_RANGES:
                for w0_ in range(c0_, c1_, 512):
                    wjobs.append((w0_, min(512, c1_ - w0_)))
            for (c0_, c1_, _d) in TM_RANGES:
                for w0_ in range(c0_, c1_, 512):
                    wjobs.append((w0_, min(512, c1_ - w0_)))
            wjobs.append((C_AG, 16))

        def mk_loader(c0, ncols):
            def ld():
                wt = wtile.get()
                wload(wt.ap, wt.b, l, "in", 0, c0, ncols)
                return wt
            return ld
        pf = Prefetch([mk_loader(c0, nco) for (c0, nco) in wjobs], ahead=2)
        wji = [0]

        def load_w(c0, ncols):
            i = wji[0]
            assert wjobs[i] == (c0, ncols), (wjobs[i], c0, ncols)
            wji[0] += 1
            return pf.get(i)

        kb.op("sync", lambda e: e.dma_start(out=bg_bc.ap, in_=b_gate.ap[l:l + 1, :].broadcast_to([128, 16])), writes=[bg_bc.b], dma=True)
        hTs = [hT, sb("hTb", [128, KC, 1024], BF16)]

        def prep_gen(g):
            hT = hTs[g % 2]
            for half in range(2):
                for j in range(4):
                    t0 = g * 1024 + half * 512 + j * 128
                    xt = xt_pool.get()
                    kb.op("sync", lambda e, xt=xt, t0=t0: e.dma_start(out=xt.ap, in_=xsrc.ap[t0:t0 + 128, :]), reads=[xsrc.b], writes=[xt.b], dma=True)
                    kb.op("scalar", lambda e, xt=xt, j=j: e.activation(out=junk.ap, in_=xt.ap, func=AF.Square, accum_out=ssq.ap[:, j:j + 1]),
                          reads=[xt.b], writes=[junk.b, ssq.b])
                    kb.op("vector", lambda e, j=j: e.tensor_scalar(out=ssq.ap[:, 4 + j:5 + j], in0=ssq.ap[:, j:j + 1], scalar1=1.0 / D, scalar2=EPS,
                                                                    op0=ALU.mult, op1=ALU.add), reads=[ssq.b], writes=[ssq.b])
                    kb.op("scalar", lambda e, j=j: e.activation(out=ssq.ap[:, 4 + j:5 + j], in_=ssq.ap[:, 4 + j:5 + j], func=AF.Sqrt), reads=[ssq.b], writes=[ssq.b])
                    kb.op("vector", lambda e, j=j: e.reciprocal(out=ssq.ap[:, 4 + j:5 + j], in_=ssq.ap[:, 4 + j:5 + j]), reads=[ssq.b], writes=[ssq.b])
                    kb.op("vector", lambda e, xt=xt, j=j: e.tensor_scalar(out=xn_t.ap[:, j, :], in0=xt.ap, scalar1=ssq.ap[:, 4 + j:5 + j], scalar2=None,
                                                                           op0=ALU.mult), reads=[xt.b, ssq.b], writes=[xn_t.b])
                    yield
                for k in range(KC):
                    pb = npsb()
                    for j in range(4):
                        kb.op("tensor", lambda e, pb=pb, j=j, k=k: e.transpose(pb.ap[:, j * 128:(j + 1) * 128], xn_t.ap[:, j, k * 128:(k + 1) * 128], ident_b.ap),
                              reads=[xn_t.b, ident_b.b], writes=[pb.b])
                    dst = hT.ap[:, k, half * 512:(half + 1) * 512]
                    if k % 2 == 0:
                        kb.op("scalar", lambda e, pb=pb, k=k, dst=dst: e.activation(out=dst, in_=pb.ap[:, 0:512], func=AF.Identity,
                                                                                     bias=modB.ap[:, l, si, k:k + 1], scale=modA.ap[:, l, si, k:k + 1]),
                              reads=[pb.b, modA.b, modB.b], writes=[hT.b])
                    else:
                        kb.op("vector", lambda e, pb=pb, k=k, dst=dst: e.tensor_scalar(out=dst, in0=pb.ap[:, 0:512], scalar1=modA.ap[:, l, si, k:k + 1],
                                                                                        scalar2=modB.ap[:, l, si, k:k + 1], op0=ALU.mult, op1=ALU.add),
                              reads=[pb.b, modA.b, modB.b], writes=[hT.b])
                    if k % 4 == 3:
                        yield
            kb.op("sync", lambda e, g=g, hT=hT: e.dma_start(out=HT[n].ap[:, g * 1024:(g + 1) * 1024].rearrange("(k p) t -> p k t", p=128), in_=hT.ap),
                  reads=[hT.b], writes=[HT[n].b], dma=True)

        preps = {}

        def pump(g, nsteps=1):
            if g >= T // 1024:
                return
            if g not in preps:
                preps[g] = prep_gen(g)
            for _ in range(nsteps):
                try:
                    next(preps[g])
                except StopIteration:
                    break

        def do_group1(g):
            pump(g, 1000)
            hT = hTs[g % 2]
            for (c0, c1) in FM_RANGES:
                for w0 in range(c0, c1, 512):
                    ncols = min(512, c1 - w0)
                    wt = load_w(w0, ncols)
                    for mc in range(ncols // 128):
                        col = w0 + mc * 128
                        o = ofm.get()
                        fn = fm_func(col)
                        for tt in range(2):
                            p = nps()
                            for k in range(KC):
                                kb.op("tensor", lambda e, p=p, wt=wt, mc=mc, k=k, tt=tt: e.matmul(p.ap, lhsT=wt.ap[:, k, mc * 128:(mc + 1) * 128],
                                                                                            rhs=hT.ap[:, k, tt * 512:(tt + 1) * 512],
                                                                                            start=(k == 0), stop=(k == KC - 1)),
                                      reads=[wt.b, hT.b], writes=[p.b])
                            dst = o.ap[:, tt * 512:(tt + 1) * 512]
                            if fn is None:
                                copy_op(evac_eng(), dst, p.ap, [p.b], [o.b])
                            else:
                                kb.op("scalar", lambda e, p=p, dst=dst, fn=fn: e.activation(out=dst, in_=p.ap, func=fn), reads=[p.b], writes=[o.b])
                        kb.op("sync", lambda e, o=o, col=col, g=g: e.dma_start(out=FM[n].ap[col:col + 128, g * 1024:(g + 1) * 1024], in_=o.ap),
                              reads=[o.b], writes=[FM[n].b], dma=True)
                        pump(g + 1, 1)
            for (c0, c1, dcol) in TM_RANGES:
                for w0 in range(c0, c1, 512):
                    ncols = min(512, c1 - w0)
                    wt = load_w(w0, ncols)
                    for sub in range(8):
                        p = nps()
                        for k in range(KC):
                            kb.op("tensor", lambda e, p=p, wt=wt, k=k, sub=sub, ncols=ncols: e.matmul(p.ap[:, 0:ncols], lhsT=hT.ap[:, k, sub * 128:(sub + 1) * 128],
                                                                                                 rhs=wt.ap[:, k, 0:ncols], start=(k == 0), stop=(k == KC - 1)),
                                  reads=[wt.b, hT.b], writes=[p.b])
                        o = otm.get()
                        copy_op(evac_eng(), o.ap[:, 0:ncols], p.ap[:, 0:ncols], [p.b], [o.b])
                        t0 = g * 1024 + sub * 128
                        dc = dcol + (w0 - c0)
                        kb.op("sync", lambda e, o=o, t0=t0, dc=dc, ncols=ncols: e.dma_start(out=TMv[n].ap[t0:t0 + 128, dc:dc + ncols], in_=o.ap[:, 0:ncols]),
                              reads=[o.b], writes=[TMv[n].b], dma=True)
            wt = load_w(C_AG, 16)
            for sub in range(8):
                p = nps()
                for k in range(KC):
                    kb.op("tensor", lambda e, p=p, wt=wt, k=k, sub=sub: e.matmul(p.ap[:, 0:16], lhsT=hT.ap[:, k, sub * 128:(sub + 1) * 128],
                                                                            rhs=wt.ap[:, k, 0:16], start=(k == 0), stop=(k == KC - 1)),
                          reads=[wt.b, hT.b], writes=[p.b])
                o = otg.get()
                kb.op("vector", lambda e, o=o, p=p: e.tensor_tensor(out=o.ap, in0=p.ap[:, 0:16], in1=bg_bc.ap, op=ALU.add), reads=[p.b, bg_bc.b], writes=[o.b])
                t0 = g * 1024 + sub * 128
                kb.op("sync", lambda e, o=o, t0=t0: e.dma_start(out=TMg[n].ap[t0:t0 + 128, :], in_=o.ap), reads=[o.b], writes=[TMg[n].b], dma=True)
        for g in range(T // 1024):
            do_group1(g)
        kb.barrier()
        aoff[0] = PERSIST

    def phase3(l, si, n, T, xsrc, xdst):
        hT = sb("hT3", [128, KC, 1024], BF16)
        yT_t = sb("yT", [128, KC, 1024], BF16)
        yTb = [Buf() for _ in range(KC)]
        mT_t = sb("mT", [128, KC, 1024], BF16)
        wm_pool = Pool("wm", [128, KC, 256], BF16, 4)
        wbr_pool = Pool("wbr", [128, 4, 256], BF16, 4)

        def mk_merge(mg, i):
            def ld():
                wt = wm_pool.get()
                wload(wt.ap, wt.b, l, "in", 0, C_MG + i * 2048 + mg * 256, 256)
                wb_ = wbr_pool.get()
                wload(wb_.ap, wb_.b, l, "br", i, mg * 256, 256)
                return (wt, wb_)
            return ld

        def mk_out(nn):
            def ld():
                wt = wm_pool.get()
                wload(wt.ap, wt.b, l, "out", 0, nn * 256, 256)
                return wt
            return ld
        loaders3 = []
        for g_ in range(T // 1024):
            for mg_ in range(8):
                for i_ in range(4):
                    loaders3.append(mk_merge(mg_, i_))
            for nn_ in range(8):
                loaders3.append(mk_out(nn_))
        pf3 = Prefetch(loaders3, ahead=2)
        pfi = [0]
        ytmp = Pool("ytmp", [128, 1024], BF16, 3)
        uld = [sb(f"uld{i}", [128, 1024], BF16) for i in range(4)]
        sg_pool = Pool("sg", [128, 512], F32, 2)
        acc_pool = Pool("acc", [128, 512], F32, 8)
        tmp_pool = Pool("tmp3", [128, 512], F32, 2)
        xo_pool = Pool("xo", [128, 512], F32, 2)
        usq = Pool("usq", [128, 512], BF16, 2)
        stat = Pool("stat", [128, 512], F32, 3)
        gsl = sb("gsl", [128, 512])
        lnw = sb("lnw", [128, 4])
        lnb = sb("lnb", [128, 4])
        kb.op("sync", lambda e: e.dma_start(out=lnw.ap, in_=conv_ln_w.ap[l:l + 1, :].rearrange("o (k p) -> p (o k)", p=128), allow_slow_non_contiguous=True), writes=[lnw.b], dma=True)
        kb.op("sync", lambda e: e.dma_start(out=lnb.ap, in_=conv_ln_b.ap[l:l + 1, :].rearrange("o (k p) -> p (o k)", p=128), allow_slow_non_contiguous=True), writes=[lnb.b], dma=True)
        def prep3_gen(g):
            tsl = slice(g * 1024, (g + 1) * 1024)
            kb.op("sync", lambda e: e.dma_start(out=hT.ap, in_=HT[n].ap[:, tsl].rearrange("(k p) t -> p k t", p=128)), reads=[HT[n].b], writes=[hT.b], dma=True)
            for br in range(3):
                zc = (C_AZ, C_BZ, C_CZ)[br]
                for c4 in range(4):
                    a = ytmp.get()
                    kb.op("sync", lambda e, a=a, br=br, c4=c4: e.dma_start(out=a.ap, in_=YM[n].ap[br * 512 + c4 * 128: br * 512 + c4 * 128 + 128, tsl]),
                          reads=[YM[n].b], writes=[a.b], dma=True)
                    z = ytmp.get()
                    kb.op("sync", lambda e, z=z, zc=zc, c4=c4: e.dma_start(out=z.ap, in_=FM[n].ap[zc + c4 * 128: zc + c4 * 128 + 128, tsl]),
                          reads=[FM[n].b], writes=[z.b], dma=True)
                    dst = yT_t.ap[:, br * 4 + c4, :]
                    if br == 0:
                        s_ = ytmp.get()
                        kb.op("sync", lambda e, s_=s_, c4=c4: e.dma_start(out=s_.ap, in_=FM[n].ap[C_AO + c4 * 128: C_AO + c4 * 128 + 128, tsl]),
                              reads=[FM[n].b], writes=[s_.b], dma=True)
                        kb.op("gpsimd", lambda e, z=z, s_=s_: e.tensor_tensor(out=z.ap, in0=z.ap, in1=s_.ap, op=ALU.mult), reads=[z.b, s_.b], writes=[z.b])
                    kb.op("vector", lambda e, a=a, z=z, dst=dst: e.tensor_tensor(out=dst, in0=a.ap, in1=z.ap, op=ALU.mult), reads=[a.b, z.b], writes=[yTb[br * 4 + c4]])
                    yield
            ul = uld
            for c4 in range(4):
                kb.op("sync", lambda e, c4=c4: e.dma_start(out=ul[c4].ap, in_=YM[n].ap[1536 + c4 * 128: 1536 + c4 * 128 + 128, tsl]),
                      reads=[YM[n].b], writes=[ul[c4].b], dma=True)
            for tt in range(2):
                cs_ = slice(tt * 512, (tt + 1) * 512)
                p1 = nps()
                p2 = nps()
                for c4 in range(4):
                    q2 = usq.get()
                    kb.op("gpsimd", lambda e, q2=q2, c4=c4, cs_=cs_: e.tensor_tensor(out=q2.ap, in0=ul[c4].ap[:, cs_], in1=ul[c4].ap[:, cs_], op=ALU.mult),
                          reads=[ul[c4].b], writes=[q2.b])
                    kb.op("tensor", lambda e, p1=p1, c4=c4, cs_=cs_: e.matmul(p1.ap, lhsT=ones_b.ap, rhs=ul[c4].ap[:, cs_], start=(c4 == 0), stop=(c4 == 3)),
                          reads=[ones_b.b, ul[c4].b], writes=[p1.b])
                    kb.op("tensor", lambda e, p2=p2, q2=q2, c4=c4: e.matmul(p2.ap, lhsT=ones_b.ap, rhs=q2.ap, start=(c4 == 0), stop=(c4 == 3)),
                          reads=[ones_b.b, q2.b], writes=[p2.b])
                mean = stat.get()
                rstd = stat.get()
                m2 = stat.get()
                kb.op("scalar", lambda e, mean=mean, p1=p1: e.activation(out=mean.ap, in_=p1.ap, func=AF.Copy, scale=1.0 / 512), reads=[p1.b], writes=[mean.b])
                kb.op("vector", lambda e, mean=mean, m2=m2: e.tensor_tensor(out=m2.ap, in0=mean.ap, in1=mean.ap, op=ALU.mult), reads=[mean.b], writes=[m2.b])
                kb.op("vector", lambda e, rstd=rstd, p2=p2, m2=m2: e.scalar_tensor_tensor(out=rstd.ap, in0=p2.ap, scalar=1.0 / 512, in1=m2.ap, op0=ALU.mult, op1=ALU.subtract),
                      reads=[p2.b, m2.b], writes=[rstd.b])
                kb.op("scalar", lambda e, rstd=rstd: e.activation(out=rstd.ap, in_=rstd.ap, func=AF.Sqrt, bias=eps_c.ap[:, 0:1]), reads=[rstd.b, eps_c.b], writes=[rstd.b])
                kb.op("vector", lambda e, rstd=rstd: e.reciprocal(out=rstd.ap, in_=rstd.ap), reads=[rstd.b], writes=[rstd.b])
                for c4 in range(4):
                    t1 = tmp_pool.get()
                    kb.op("vector", lambda e, t1=t1, c4=c4, mean=mean, cs_=cs_: e.tensor_tensor(out=t1.ap, in0=ul[c4].ap[:, cs_], in1=mean.ap, op=ALU.subtract),
                          reads=[ul[c4].b, mean.b], writes=[t1.b])
                    kb.op("gpsimd", lambda e, t1=t1, rstd=rstd: e.tensor_tensor(out=t1.ap, in0=t1.ap, in1=rstd.ap, op=ALU.mult), reads=[t1.b, rstd.b], writes=[t1.b])
                    kb.op("scalar", lambda e, t1=t1, c4=c4: e.activation(out=t1.ap, in_=t1.ap, func=AF.Silu, bias=lnb.ap[:, c4:c4 + 1], scale=lnw.ap[:, c4:c4 + 1]),
                          reads=[t1.b, lnw.b, lnb.b], writes=[t1.b])
                    z = usq.get()
                    kb.op("sync", lambda e, z=z, c4=c4, tt=tt: e.dma_start(out=z.ap, in_=FM[n].ap[C_DZ + c4 * 128: C_DZ + c4 * 128 + 128, g * 1024 + tt * 512: g * 1024 + tt * 512 + 512]),
                          reads=[FM[n].b], writes=[z.b], dma=True)
                    kb.op("vector", lambda e, t1=t1, z=z, c4=c4, cs_=cs_: e.tensor_tensor(out=yT_t.ap[:, 12 + c4, cs_], in0=t1.ap, in1=z.ap, op=ALU.mult),
                          reads=[t1.b, z.b], writes=[yTb[12 + c4]])
                    yield
            yield

        preps3 = {}

        def pump3(g, nsteps=1):
            if g >= T // 1024:
                return
            if g not in preps3:
                preps3[g] = prep3_gen(g)
            for _ in range(nsteps):
                try:
                    next(preps3[g])
                except StopIteration:
                    break

        def do_group(g):
            pump3(g, 100000)
            for mg in range(8):
                accs = [acc_pool.get() for _ in range(4)]
                for i in range(4):
                    wt, wb_ = pf3.get(pfi[0])
                    pfi[0] += 1
                    for mc in range(2):
                        m = mg * 2 + mc
                        for tt in range(2):
                            cs_ = slice(tt * 512, (tt + 1) * 512)
                            acc = accs[mc * 2 + tt]
                            pa = nps()
                            for k in range(KC):
                                kb.op("tensor", lambda e, pa=pa, k=k, mc=mc, cs_=cs_, wt=wt: e.matmul(pa.ap, lhsT=wt.ap[:, k, mc * 128:(mc + 1) * 128], rhs=hT.ap[:, k, cs_],
                                                                                                 start=(k == 0), stop=(k == KC - 1)),
                                      reads=[wt.b, hT.b], writes=[pa.b])
                            pb_ = nps()
                            for k in range(4):
                                kb.op("tensor", lambda e, pb_=pb_, i=i, k=k, mc=mc, cs_=cs_, wb_=wb_: e.matmul(pb_.ap, lhsT=wb_.ap[:, k, mc * 128:(mc + 1) * 128], rhs=yT_t.ap[:, i * 4 + k, cs_],
                                                                                                          start=(k == 0), stop=(k == 3)),
                                      reads=[wb_.b, yTb[i * 4 + k]], writes=[pb_.b])
                            sg = sg_pool.get()
                            kb.op("scalar", lambda e, sg=sg, pa=pa: e.activation(out=sg.ap, in_=pa.ap, func=AF.Sigmoid), reads=[pa.b], writes=[sg.b])
                            if i == 0:
                                kb.op("vector", lambda e, acc=acc, sg=sg, pb_=pb_: e.tensor_tensor(out=acc.ap, in0=pb_.ap, in1=sg.ap, op=ALU.mult),
                                      reads=[pb_.b, sg.b], writes=[acc.b])
                            else:
                                kb.op("vector", lambda e, sg=sg, pb_=pb_: e.tensor_tensor(out=sg.ap, in0=pb_.ap, in1=sg.ap, op=ALU.mult),
                                      reads=[pb_.b, sg.b], writes=[sg.b])
                                if i < 3:
                                    kb.op("gpsimd", lambda e, acc=acc, sg=sg: e.tensor_tensor(out=acc.ap, in0=acc.ap, in1=sg.ap, op=ALU.add),
                                          reads=[acc.b, sg.b], writes=[acc.b])
                                else:
                                    kb.op("gpsimd", lambda e, acc=acc, sg=sg, m=m, cs_=cs_: e.tensor_tensor(out=mT_t.ap[:, m, cs_], in0=acc.ap, in1=sg.ap, op=ALU.add),
                                          reads=[acc.b, sg.b], writes=[mT_t.b])
            for nn in range(8):
                wt = pf3.get(pfi[0])
                pfi[0] += 1
                kb.op("sync", lambda e, nn=nn: e.dma_start(out=gsl.ap[:, 0:256], in_=modrow.ap[l * nseq + si:l * nseq + si + 1, nn * 256:(nn + 1) * 256].broadcast_to([128, 256])),
                      reads=[modrow.b], writes=[gsl.b], dma=True)
                for sub in range(8):
                    t0 = g * 1024 + sub * 128
                    p = nps()
                    for k in range(KC):
                        kb.op("tensor", lambda e, p=p, wt=wt, k=k, sub=sub: e.matmul(p.ap[:, 0:256], lhsT=mT_t.ap[:, k, sub * 128:(sub + 1) * 128], rhs=wt.ap[:, k, :],
                                                                               start=(k == 0), stop=(k == KC - 1)),
                              reads=[wt.b, mT_t.b], writes=[p.b])
                    xo = xo_pool.get()
                    kb.op("sync", lambda e, xo=xo, t0=t0, nn=nn: e.dma_start(out=xo.ap[:, 0:256], in_=xsrc.ap[t0:t0 + 128, nn * 256:(nn + 1) * 256]), reads=[xsrc.b], writes=[xo.b], dma=True)
                    t1 = tmp_pool.get()
                    kb.op("vector", lambda e, t1=t1, p=p: e.tensor_tensor(out=t1.ap[:, 0:256], in0=p.ap[:, 0:256], in1=gsl.ap[:, 0:256], op=ALU.mult),
                          reads=[p.b, gsl.b], writes=[t1.b])
                    kb.op("gpsimd", lambda e, t1=t1, xo=xo: e.tensor_tensor(out=xo.ap[:, 0:256], in0=xo.ap[:, 0:256], in1=t1.ap[:, 0:256], op=ALU.add), reads=[xo.b, t1.b], writes=[xo.b])
                    kb.op("sync", lambda e, xo=xo, t0=t0, nn=nn: e.dma_start(out=xdst.ap[t0:t0 + 128, nn * 256:(nn + 1) * 256], in_=xo.ap[:, 0:256]), reads=[xo.b], writes=[xdst.b], dma=True)
                    pump3(g + 1, 1)
        for g in range(T // 1024):
            do_group(g)
        kb.barrier()
        aoff[0] = PERSIST

    def wt_get(l, kind, idx, c0, ncols):
        key = (l, kind, idx, c0, ncols)
        if key in WT:
            return WT[key]
        nk = 4 if kind == "br" else KC
        t = dscr(f"wt_{l}_{kind}_{idx}_{c0}_{ncols}", [128, nk * ncols])
        if kind == "in":
            src = w_in.ap[l, :, c0:c0 + ncols]
        elif kind == "br":
            src = w_branch.ap[l, idx, :, c0:c0 + ncols]
        else:
            src = w_out.ap[l, :, c0:c0 + ncols]
        WSRC[key] = src
        WT[key] = (t, nk)
        return WT[key]

    P1_TILES = []
    for (c0_, c1_) in [(0, 1024), (1536, 2560), (2576, 3600), (4112, 5264), (5392, 7440)]:
        for w0_ in range(c0_, c1_, 512):
            P1_TILES.append((w0_, min(512, c1_ - w0_)))
    for (c0_, c1_) in [(512, 1536), (3600, 4112), (5264, 5392)]:
        for w0_ in range(c0_, c1_, 512):
            P1_TILES.append((w0_, min(512, c1_ - w0_)))
    P1_TILES.append((C_AG, 16))

    def convert_tiles(keys):
        st32 = Pool("cv32", [128, KC, 512], F32, 2)
        st16 = Pool("cv16", [128, KC, 512], BF16, 2)
        for ci, key in enumerate(keys):
            (l, kind, idx, c0, ncols) = key
            t, nk = wt_get(l, kind, idx, c0, ncols)
            src = WSRC[key]
            a = st32.get()
            b = st16.get()
            kb.op("sync", lambda e, a=a, src=src, nk=nk, ncols=ncols: e.dma_start(out=a.ap[:, 0:nk, 0:ncols], in_=src.rearrange("(k p) n -> p k n", p=128)),
                  writes=[a.b], dma=True)
            eng = "gpsimd" if ci % 3 != 2 else "vector"
            kb.op(eng, lambda e, a=a, b=b, nk=nk, ncols=ncols: e.tensor_copy(out=b.ap[:, 0:nk, 0:ncols], in_=a.ap[:, 0:nk, 0:ncols]), reads=[a.b], writes=[b.b])
            kb.op("scalar", lambda e, b=b, t=t, nk=nk, ncols=ncols: e.dma_start(out=t.ap.rearrange("p (k n) -> p k n", k=nk), in_=b.ap[:, 0:nk, 0:ncols]),
                  reads=[b.b], writes=[t.b], dma=True)
        kb.barrier()
        aoff[0] = PERSIST

    def p1_keys(l):
        return [(l, "in", 0, c0, nco) for (c0, nco) in P1_TILES]

    def p3_keys(l):
        ks = []
        for mg in range(8):
            for i in range(4):
                ks.append((l, "in", 0, C_MG + i * 2048 + mg * 256, 256))
                ks.append((l, "br", i, mg * 256, 256))
        for nn in range(8):
            ks.append((l, "out", 0, nn * 256, 256))
        return ks

    WQ = "scalar"

    def wload(dst, dst_b, l, kind, idx, c0, ncols):
        t, nk = wt_get(l, kind, idx, c0, ncols)
        kb.op(WQ, lambda e: e.dma_start(out=dst[:, 0:nk, 0:ncols], in_=t.ap.rearrange("p (k n) -> p k n", k=nk)),
              reads=[t.b], writes=[dst_b], dma=True)

    class Prefetch:
        def __init__(self, loaders, ahead=2):
            self.loaders, self.ahead, self.tiles, self.issued = loaders, ahead, {}, 0

        def get(self, i):
            while self.issued <= min(i + self.ahead, len(self.loaders) - 1):
                self.tiles[self.issued] = self.loaders[self.issued]()
                self.issued += 1
            return self.tiles.pop(i)

    env = dict(locals())
    MIX = build_mixers(env)

    convert_tiles(p1_keys(0))
    adaln()
    MIX.gqa_tables()
    kb.barrier()
    aoff[0] = PERSIST
    for l in range(depth):
        for si, (n, T) in enumerate(seqs):
            xsrc = X[n] if l == 0 else X1[n]
            phase1(l, si, n, T, xsrc)
        convert_tiles(p3_keys(l) + (p1_keys(l + 1) if l + 1 < depth else []))
        for si, (n, T) in enumerate(seqs):
            MIX(l, si, n, T)
            kb.barrier()
            aoff[0] = PERSIST
        for si, (n, T) in enumerate(seqs):
            xsrc = X[n] if l == 0 else X1[n]
            xdst = Yout[n] if l == depth - 1 else X1[n]
            phase3(l, si, n, T, xsrc, xdst)
    kb.emit()
    st.close()
    return nc


def build_mixers(env):
    g_ = env
    kb, sb, nps, npsb, Pool, copy_op = g_["kb"], g_["sb"], g_["nps"], g_["npsb"], g_["Pool"], g_["copy_op"]
    FM, TMv, TMg, YM = g_["FM"], g_["TMv"], g_["TMg"], g_["YM"]
    ident_b, ident_f, ones_b, ones_f, blk64_b, tri_f, eps_c = (g_[k] for k in ("ident_b", "ident_f", "ones_b", "ones_f", "blk64_b", "tri_f", "eps_c"))
    consts = make_consts()
    npsh = g_["npsh"]

    def interleave(gens):
        gens = list(gens)
        while gens:
            nxt = []
            for g in gens:
                try:
                    next(g)
                    nxt.append(g)
                except StopIteration:
                    pass
            gens = nxt

    def dma(eng, out, in_, reads, writes, slow=False):
        if slow:
            kb.op(eng, lambda e: e.dma_start(out=out, in_=in_, allow_slow_non_contiguous=True), reads=reads, writes=writes, dma=True)
        else:
            kb.op(eng, lambda e: e.dma_start(out=out, in_=in_), reads=reads, writes=writes, dma=True)

    def V(fn, reads, writes, eng="vector"):
        kb.op(eng, fn, reads=reads, writes=writes)

    def MM(out, lhsT, rhs, start, stop, reads, writes):
        kb.op("tensor", lambda e: e.matmul(out, lhsT=lhsT, rhs=rhs, start=start, stop=stop), reads=reads, writes=writes)

    def rsqrt_tile(dst, src_ps, scale, n, reads):
        V(lambda e: e.tensor_scalar(out=dst.ap[:, 0:n], in0=src_ps, scalar1=scale, scalar2=EPS, op0=ALU.mult, op1=ALU.add), reads, [dst.b])
        kb.op("scalar", lambda e: e.activation(out=dst.ap[:, 0:n], in_=dst.ap[:, 0:n], func=AF.Sqrt), reads=[dst.b], writes=[dst.b])
        V(lambda e: e.reciprocal(out=dst.ap[:, 0:n], in_=dst.ap[:, 0:n]), [dst.b], [dst.b])

    def headnorm_fm(dst, src, T, wcol, lhs_ones, scale_div, extra_scale):
        sq = Pool("hn_sq", [128, 512], BF16, 2)
        rs = Pool("hn_rs", [128, 512], F32, 2)
        for t0 in range(0, T, 512):
            q2 = sq.get()
            V(lambda e, q2=q2, t0=t0: e.tensor_tensor(out=q2.ap, in0=src.ap[:, t0:t0 + 512], in1=src.ap[:, t0:t0 + 512], op=ALU.mult), [src.b], [q2.b], eng="gpsimd")
            p = nps()
            MM(p.ap, lhs_ones.ap, q2.ap, True, True, [lhs_ones.b, q2.b], [p.b])
            r = rs.get()
            rsqrt_tile(r, p.ap, 1.0 / scale_div, 512, [p.b])
            V(lambda e, r=r, t0=t0: e.scalar_tensor_tensor(out=dst.ap[:, t0:t0 + 512], in0=src.ap[:, t0:t0 + 512], scalar=wcol, in1=r.ap, op0=ALU.mult, op1=ALU.mult),
              [src.b, r.b], [dst.b])

    def conv(l, n, T):
        conv_w, conv_b = g_["conv_w"], g_["conv_b"]
        a_t = sb("cv_a", [128, T], BF16)
        g_t = sb("cv_g", [128, T], BF16)
        up = sb("cv_u", [128, T + 32], BF16)
        dg = sb("cv_dg", [128, 31, 128], BF16)
        cw = sb("cv_w", [128, 31])
        cb = sb("cv_b", [128, 1])
        osb = Pool("cv_o", [128, 512], BF16, 3)
        for c4 in range(4):
            dma("sync", cw.ap, conv_w.ap[l, :, c4 * 128:(c4 + 1) * 128].rearrange("w p -> p w"), [], [cw.b], slow=True)
            dma("sync", cb.ap, conv_b.ap[l:l + 1, c4 * 128:(c4 + 1) * 128].rearrange("o p -> p o"), [], [cb.b], slow=True)
            for w in range(31):
                V(lambda e, w=w: e.tensor_scalar(out=dg.ap[:, w, :], in0=ident_f.ap, scalar1=cw.ap[:, w:w + 1], scalar2=None, op0=ALU.mult), [ident_f.b, cw.b], [dg.b],
                  eng=("vector" if w % 2 else "gpsimd"))
            for t0 in range(0, T, 1024):
                dma("sync", a_t.ap[:, t0:t0 + 1024], FM[n].ap[C_DA + c4 * 128:C_DA + c4 * 128 + 128, t0:t0 + 1024], [FM[n].b], [a_t.b])
                dma("sync", g_t.ap[:, t0:t0 + 1024], FM[n].ap[C_DG + c4 * 128:C_DG + c4 * 128 + 128, t0:t0 + 1024], [FM[n].b], [g_t.b])
            V(lambda e: e.memset(up.ap[:, 0:16], 0.0), [], [up.b])
            V(lambda e: e.memset(up.ap[:, T + 15:T + 32], 0.0), [], [up.b])
            kb.op("scalar", lambda e: e.activation(out=g_t.ap, in_=g_t.ap, func=AF.Sigmoid), reads=[g_t.b], writes=[g_t.b])
            V(lambda e: e.tensor_tensor(out=up.ap[:, 15:15 + T], in0=a_t.ap, in1=g_t.ap, op=ALU.mult), [a_t.b, g_t.b], [up.b])
            for t0 in range(0, T, 512):
                p = nps()
                for w in range(31):
                    MM(p.ap, dg.ap[:, w, :], up.ap[:, t0 + w:t0 + w + 512], w == 0, w == 30, [dg.b, up.b], [p.b])
                o = osb.get()
                kb.op("scalar", lambda e, o=o, p=p: e.activation(out=o.ap, in_=p.ap, func=AF.Identity, bias=cb.ap[:, 0:1]), reads=[p.b, cb.b], writes=[o.b])
                dma("sync", YM[n].ap[1536 + c4 * 128:1536 + c4 * 128 + 128, t0:t0 + 512], o.ap, [o.b], [YM[n].b])

    gq_state = {}

    def gqa_tables():
        rel_bias, k_gqa_m = g_["rel_bias"], g_["k_gqa_m"]
        EB = g_["EB"]
        rbb = sb("gq_rbb", [128, 256])
        val = sb("gq_val", [128, 3, 128])
        mk = Pool("gq_mk", [128, 128], F32, 3)
        dma("sync", rbb.ap, rel_bias.ap.rearrange("b h -> (b h)").unsqueeze(0).broadcast_to([128, 256]), [], [rbb.b])
        V(lambda e: e.memset(EB.ap, 0.0), [], [EB.b])
        V(lambda e: e.memset(val.ap, 0.0), [], [val.b])
        gm = consts["gqa_m"]
        for o in range(3):
            for b in range(32):
                if not gm[o, b].any():
                    continue
                m = mk.get()
                dma("sync", m.ap, k_gqa_m.ap[o, b], [], [m.b])
                V(lambda e, m=m, o=o: e.tensor_tensor(out=val.ap[:, o, :], in0=val.ap[:, o, :], in1=m.ap, op=ALU.add), [m.b, val.b], [val.b], eng="gpsimd")
                for h in range(8):
                    V(lambda e, m=m, o=o, b=b, h=h: e.scalar_tensor_tensor(out=EB.ap[:, o, h, :], in0=m.ap, scalar=rbb.ap[:, b * 8 + h:b * 8 + h + 1], in1=EB.ap[:, o, h, :],
                                                                         op0=ALU.mult, op1=ALU.add), [m.b, rbb.b, EB.b], [EB.b])
        kb.op("scalar", lambda e: e.activation(out=EB.ap, in_=EB.ap, func=AF.Exp), reads=[EB.b], writes=[EB.b])
        for o in range(3):
            for h in range(8):
                V(lambda e, o=o, h=h: e.tensor_tensor(out=EB.ap[:, o, h, :], in0=EB.ap[:, o, h, :], in1=val.ap[:, o, :], op=ALU.mult), [EB.b, val.b], [EB.b])

    def gqa(l, n, T):
        swa_q_norm, swa_k_norm, swa_sink = g_["swa_q_norm"], g_["swa_k_norm"], g_["swa_sink"]
        EB = g_["EB"]
        nb = T // 128
        kraw = sb("gq_kraw", [128, T], BF16)
        kn = sb("gq_kn", [128, T], BF16)
        qraw = sb("gq_qraw", [128, T], BF16)
        qn = sb("gq_qn", [128, T], BF16)
        vp = [sb(f"gq_vp{i}", [128, nb, 128], BF16) for i in range(2)]
        vraw = sb("gq_vraw", [128, nb, 64], BF16)
        on = [sb(f"gq_on{i}", [128, 128], BF16) for i in range(2)]
        wq = sb("gq_wq", [128, 1])
        wk = sb("gq_wk", [128, 1])
        sk = sb("gq_sk", [128, 4])
        Pf = Pool("gq_pf", [128, 2, 384], F32, 4)
        Pb = Pool("gq_pb", [128, 2, 384], BF16, 4)
        rc = Pool("gq_rc", [128, 128], F32, 4)
        ost = Pool("gq_ost", [128, 1024], BF16, 2)
        for i in range(2):
            V(lambda e, i=i: e.memset(on[i].ap, 0.0), [], [on[i].b])
            V(lambda e, i=i: e.memset(on[i].ap[:, 64 * i:64 * i + 64], 1.0), [], [on[i].b])
            V(lambda e, i=i: e.memset(vp[i].ap, 0.0), [], [vp[i].b], eng="gpsimd")
        for half in range(2):
            dma("sync", wq.ap[64 * half:64 * half + 64, :], swa_q_norm.ap[l:l + 1, :].rearrange("o p -> p o"), [], [wq.b], slow=True)
            dma("sync", wk.ap[64 * half:64 * half + 64, :], swa_k_norm.ap[l:l + 1, :].rearrange("o p -> p o"), [], [wk.b], slow=True)
        V(lambda e: e.tensor_scalar(out=wq.ap, in0=wq.ap, scalar1=0.125, scalar2=None, op0=ALU.mult), [wq.b], [wq.b])
        for kvh in range(2):
            for half in range(2):
                for t0 in range(0, T, 2048):
                    tw = min(2048, T - t0)
                    dma("sync", kraw.ap[64 * half:64 * half + 64, t0:t0 + tw], FM[n].ap[C_CK + 64 * kvh:C_CK + 64 * kvh + 64, t0:t0 + tw], [FM[n].b], [kraw.b])
            headnorm_fm(kn, kraw, T, wk.ap[:, 0:1], blk64_b, 64.0, 1.0)
            dma("sync", vraw.ap, TMv[n].ap[:, 1536 + 64 * kvh:1536 + 64 * kvh + 64].rearrange("(b p) c -> p b c", p=128), [TMv[n].b], [vraw.b])
            for i in range(2):
                V(lambda e, i=i: e.tensor_copy(out=vp[i].ap[:, :, 64 * i:64 * i + 64], in_=vraw.ap), [vraw.b], [vp[i].b])
            for hp in range(2):
                h0 = kvh * 4 + hp * 2
                for half in range(2):
                    dma("sync", sk.ap[64 * half:64 * half + 64, 0:1], swa_sink.ap[l:l + 1, h0 + half:h0 + half + 1].broadcast_to([64, 1]), [], [sk.b])
                kb.op("scalar", lambda e: e.activation(out=sk.ap[:, 1:2], in_=sk.ap[:, 0:1], func=AF.Exp), reads=[sk.b], writes=[sk.b])
                for t0 in range(0, T, 2048):
                    tw = min(2048, T - t0)
                    dma("sync", qraw.ap[:, t0:t0 + tw], FM[n].ap[C_CQ + 64 * h0:C_CQ + 64 * h0 + 128, t0:t0 + tw], [FM[n].b], [qraw.b])
                headnorm_fm(qn, qraw, T, wq.ap[:, 0:1], blk64_b, 64.0, 1.0)
                ostm = {}

                def qb_gen(qb, h0=h0, kvh=kvh, hp=hp, ostm=ostm):
                    if qb // 8 not in ostm:
                        ostm[qb // 8] = ost.get()
                    o_t = ostm[qb // 8]
                    os_ = [o for o in range(3) if 0 <= qb + o - 1 < nb]
                    o0, o1 = os_[0], os_[-1] + 1
                    pS = [nps(), nps()]
                    for hh in range(2):
                        for o in os_:
                            kbk = qb + o - 1
                            MM(pS[hh].ap[:, o * 128:(o + 1) * 128], kn.ap[64 * hh:64 * hh + 64, kbk * 128:(kbk + 1) * 128], qn.ap[64 * hh:64 * hh + 64, qb * 128:(qb + 1) * 128],
                               True, True, [kn.b, qn.b], [pS[hh].b])
                    yield
                    pf = Pf.get()
                    pb = Pb.get()
                    for hh in range(2):
                        kb.op("scalar", lambda e, hh=hh, pf=pf, pS=pS, o0=o0, o1=o1: e.activation(out=pf.ap[:, hh, o0 * 128:o1 * 128], in_=pS[hh].ap[:, o0 * 128:o1 * 128], func=AF.Exp),
                              reads=[pS[hh].b], writes=[pf.b])
                    yield
                    for hh in range(2):
                        V(lambda e, hh=hh, pf=pf, pb=pb, o0=o0, o1=o1, h0=h0: e.tensor_tensor(out=pb.ap[:, hh, o0 * 128:o1 * 128].rearrange("p (o q) -> p o q", q=128),
                                                                                              in0=pf.ap[:, hh, o0 * 128:o1 * 128].rearrange("p (o q) -> p o q", q=128),
                                                                                              in1=EB.ap[:, o0:o1, h0 + hh, :], op=ALU.mult),
                          [pf.b, EB.b], [pb.b], eng=("vector" if hh else "gpsimd"))
                    yield
                    pOD = nps()
                    tot = 2 * len(os_)
                    cnt = 0
                    for hh in range(2):
                        for o in os_:
                            kbk = qb + o - 1
                            MM(pOD.ap[:, 0:128], vp[hh].ap[:, kbk, :], pb.ap[:, hh, o * 128:(o + 1) * 128], cnt == 0, cnt == tot - 1, [vp[hh].b, pb.b], [pOD.b])
                            cnt += 1
                    cnt = 0
                    for hh in range(2):
                        for o in os_:
                            MM(pOD.ap[:, 128:256], on[hh].ap, pb.ap[:, hh, o * 128:(o + 1) * 128], cnt == 0, cnt == tot - 1, [on[hh].b, pb.b], [pOD.b])
                            cnt += 1
                    yield
                    r = rc.get()
                    V(lambda e, r=r, pOD=pOD: e.tensor_scalar(out=r.ap, in0=pOD.ap[:, 128:256], scalar1=sk.ap[:, 1:2], scalar2=None, op0=ALU.add), [pOD.b, sk.b], [r.b])
                    V(lambda e, r=r: e.reciprocal(out=r.ap, in_=r.ap), [r.b], [r.b])
                    yield
                    V(lambda e, r=r, pOD=pOD, o_t=o_t, qb=qb: e.tensor_tensor(out=o_t.ap[:, (qb % 8) * 128:(qb % 8) * 128 + 128], in0=pOD.ap[:, 0:128], in1=r.ap, op=ALU.mult),
                      [pOD.b, r.b], [o_t.b])
                    if qb % 8 == 7:
                        row = 1024 + 128 * (kvh * 2 + hp)
                        dma("sync", YM[n].ap[row:row + 128, (qb - 7) * 128:(qb + 1) * 128], o_t.ap, [o_t.b], [YM[n].b])

                for q0 in range(0, nb, GQ_W):
                    interleave([qb_gen(q) for q in range(q0, min(q0 + GQ_W, nb))])

    def na(l, n, T):
        na_q_norm, na_k_norm, na_rpb, k_na_m = g_["na_q_norm"], g_["na_k_norm"], g_["na_rpb"], g_["k_na_m"]
        rows = T // 64
        nb = T // 128
        Mc = sb("na_mc", [128, 31, 64])
        RP = sb("na_rp", [128, 14, 31])
        ET = sb("na_et", [128, 14, 64])
        okm = sb("na_ok", [128, 64])
        tmpE = sb("na_te", [128, 14, 64])
        qraw = sb("na_qraw", [128, T], BF16)
        kraw = sb("na_kraw", [128, T], BF16)
        qn = sb("na_qn", [128, T], BF16)
        kn = sb("na_kn", [128, T], BF16)
        vA = sb("na_vA", [128, nb, 128], BF16)
        vB = sb("na_vB", [128, nb, 128], BF16)
        wq = sb("na_wq", [128, 1])
        wk = sb("na_wk", [128, 1])
        Pf = Pool("na_pf", [128, 4, 64], F32, 6)
        Pb = Pool("na_pb", [128, 4, 64], BF16, 6)
        rc = Pool("na_rc", [128, 64], F32, 6)
        ost = Pool("na_ost", [128, 1024], BF16, 2)
        dma("sync", Mc.ap, k_na_m.ap.rearrange("d p q -> p d q"), [], [Mc.b])
        dma("sync", wq.ap, na_q_norm.ap[l:l + 1, :].rearrange("o p -> p o"), [], [wq.b], slow=True)
        dma("sync", wk.ap, na_k_norm.ap[l:l + 1, :].rearrange("o p -> p o"), [], [wk.b], slow=True)
        V(lambda e: e.tensor_scalar(out=wq.ap, in0=wq.ap, scalar1=float(128 ** -0.5), scalar2=None, op0=ALU.mult), [wq.b], [wq.b])
        V(lambda e: e.memset(okm.ap, 0.0), [], [okm.b])
        for dc in range(31):
            V(lambda e, dc=dc: e.tensor_tensor(out=okm.ap, in0=okm.ap, in1=Mc.ap[:, dc, :], op=ALU.add), [Mc.b, okm.b], [okm.b])
        for h in range(4):
            for half in range(2):
                dma("sync", RP.ap[64 * half:64 * half + 64, :, :], na_rpb.ap[l, h:h + 1, half:half + 14, :].broadcast_to([64, 14, 31]), [], [RP.b])
            V(lambda e: e.memset(ET.ap, 0.0), [], [ET.b])
            for dc in range(31):
                V(lambda e, dc=dc: e.tensor_tensor(out=tmpE.ap, in0=Mc.ap[:, dc, :].unsqueeze(1).broadcast_to([128, 14, 64]),
                                                    in1=RP.ap[:, :, dc].unsqueeze(2).broadcast_to([128, 14, 64]), op=ALU.mult), [Mc.b, RP.b], [tmpE.b])
                V(lambda e: e.tensor_tensor(out=ET.ap, in0=ET.ap, in1=tmpE.ap, op=ALU.add), [tmpE.b, ET.b], [ET.b], eng="gpsimd")
            kb.op("scalar", lambda e: e.activation(out=ET.ap, in_=ET.ap, func=AF.Exp), reads=[ET.b], writes=[ET.b])
            V(lambda e: e.tensor_tensor(out=ET.ap, in0=ET.ap, in1=okm.ap.unsqueeze(1).broadcast_to([128, 14, 64]), op=ALU.mult), [ET.b, okm.b], [ET.b])
            for t0 in range(0, T, 2048):
                tw = min(2048, T - t0)
                dma("sync", qraw.ap[:, t0:t0 + tw], FM[n].ap[C_BQ + 128 * h:C_BQ + 128 * h + 128, t0:t0 + tw], [FM[n].b], [qraw.b])
                dma("sync", kraw.ap[:, t0:t0 + tw], FM[n].ap[C_BK + 128 * h:C_BK + 128 * h + 128, t0:t0 + tw], [FM[n].b], [kraw.b])
            headnorm_fm(qn, qraw, T, wq.ap[:, 0:1], ones_b, 128.0, 1.0)
            headnorm_fm(kn, kraw, T, wk.ap[:, 0:1], ones_b, 128.0, 1.0)
            dma("sync", vA.ap, TMv[n].ap[:, 1024 + 128 * h:1024 + 128 * h + 128].rearrange("(b p) c -> p b c", p=128), [TMv[n].b], [vA.b])
            dma("sync", vB.ap[:, 0:nb - 1, :], TMv[n].ap[64:T - 64, 1024 + 128 * h:1024 + 128 * h + 128].rearrange("(b p) c -> p b c", p=128), [TMv[n].b], [vB.b])
            ostm = {}

            def row_gen(r):
                if r // 16 not in ostm:
                    ostm[r // 16] = ost.get()
                o_t = ostm[r // 16]
                start = min(max(r - 4, 0), rows - 8)
                off = start - r + 7
                pS = nps()
                for j2 in range(4):
                    k0 = 64 * (start + 2 * j2)
                    MM(pS.ap[:, j2 * 64:(j2 + 1) * 64], kn.ap[:, k0:k0 + 128], qn.ap[:, 64 * r:64 * r + 64], True, True, [kn.b, qn.b], [pS.b])
                yield
                pf = Pf.get()
                pb = Pb.get()
                kb.op("scalar", lambda e, pf=pf, pS=pS: e.activation(out=pf.ap, in_=pS.ap[:, 0:256].rearrange("p (j q) -> p j q", q=64), func=AF.Exp), reads=[pS.b], writes=[pf.b])
                yield
                V(lambda e, pf=pf, pb=pb, off=off: e.tensor_tensor(out=pb.ap, in0=pf.ap, in1=ET.ap[:, off:off + 7:2, :], op=ALU.mult), [pf.b, ET.b], [pb.b],
                  eng=("vector" if r % 2 else "gpsimd"))
                yield
                pOD = nps()
                for j2 in range(4):
                    rr = start + 2 * j2
                    vt = vA.ap[:, rr // 2, :] if rr % 2 == 0 else vB.ap[:, (rr - 1) // 2, :]
                    vb_ = vA.b if rr % 2 == 0 else vB.b
                    MM(pOD.ap[:, 0:64], vt, pb.ap[:, j2, :], j2 == 0, j2 == 3, [vb_, pb.b], [pOD.b])
                for j2 in range(4):
                    MM(pOD.ap[:, 64:128], ones_b.ap, pb.ap[:, j2, :], j2 == 0, j2 == 3, [ones_b.b, pb.b], [pOD.b])
                yield
                rcp = rc.get()
                V(lambda e, rcp=rcp, pOD=pOD: e.reciprocal(out=rcp.ap, in_=pOD.ap[:, 64:128]), [pOD.b], [rcp.b])
                yield
                V(lambda e, rcp=rcp, pOD=pOD, o_t=o_t, r=r: e.tensor_tensor(out=o_t.ap[:, (r % 16) * 64:(r % 16) * 64 + 64], in0=pOD.ap[:, 0:64], in1=rcp.ap, op=ALU.mult),
                  [pOD.b, rcp.b], [o_t.b])
                if r % 16 == 15:
                    dma("sync", YM[n].ap[512 + 128 * h:512 + 128 * h + 128, (r - 15) * 64:(r + 1) * 64], o_t.ap, [o_t.b], [YM[n].b])

            for r0 in range(0, rows, NA_W):
                interleave([row_gen(r) for r in range(r0, min(r0 + NA_W, rows))])

    def mlstm(l, n, T):
        mlstm_norm_w = g_["mlstm_norm_w"]
        nb = T // 128
        qT = sb("ml_qT", [128, T], BF16)
        kT = sb("ml_kT", [128, T], BF16)
        ktm = sb("ml_ktm", [128, nb, 128], BF16)
        vaug = sb("ml_va", [128, nb, 132], BF16)
        gt = sb("ml_gt", [128, nb, 16])
        hfw = sb("ml_hfw", [128, nb, 128])
        nwb = sb("ml_nw", [128, 128])
        lf = sb("ml_lf", [128, 2, nb])
        bcs = sb("ml_b", [128, 2, nb])
        gb = sb("ml_g", [128, 2, nb])
        beta = sb("ml_beta", [128, 2, nb])
        gam = sb("ml_gam", [128, 2, nb])
        emb = sb("ml_emb", [128, 2, nb])
        eg = sb("ml_eg", [128, 2, nb])
        Cst = sb("ml_C", [128, 132])
        Cb = sb("ml_Cb", [128, 132], BF16)
        STp = Pool("ml_st", [128, 128], BF16, 3)
        kgp = Pool("ml_kg", [128, 128], BF16, 3)
        sm = Pool("ml_sm", [128, 4], F32, 4)
        hs = Pool("ml_hs", [128, 128], F32, 3)
        hb = Pool("ml_hb", [128, 128], BF16, 3)
        jk = sb("ml_jk", [128, 128], BF16)
        ost = Pool("ml_ost", [128, 1024], BF16, 2)
        V(lambda e: e.memset(vaug.ap[:, :, 128:129], 1.0), [], [vaug.b])
        for h in range(4):
            for t0 in range(0, T, 2048):
                tw = min(2048, T - t0)
                dma("sync", qT.ap[:, t0:t0 + tw], FM[n].ap[C_AQ + 128 * h:C_AQ + 128 * h + 128, t0:t0 + tw], [FM[n].b], [qT.b])
                dma("sync", kT.ap[:, t0:t0 + tw], FM[n].ap[C_AK + 128 * h:C_AK + 128 * h + 128, t0:t0 + tw], [FM[n].b], [kT.b])
            dma("sync", ktm.ap, TMv[n].ap[:, 128 * h:128 * h + 128].rearrange("(b p) c -> p b c", p=128), [TMv[n].b], [ktm.b])
            dma("sync", vaug.ap[:, :, 0:128], TMv[n].ap[:, 512 + 128 * h:512 + 128 * h + 128].rearrange("(b p) c -> p b c", p=128), [TMv[n].b], [vaug.b])
            dma("sync", gt.ap, TMg[n].ap.rearrange("(b p) c -> p b c", p=128), [TMg[n].b], [gt.b])
            dma("sync", nwb.ap, mlstm_norm_w.ap[l:l + 1, 128 * h:128 * h + 128].broadcast_to([128, 128]), [], [nwb.b])
            for d, dn in enumerate(("fw", "bw")):
                icol, fcol = 4 * d + h, 8 + 4 * d + h
                kb.op("scalar", lambda e, d=d, fcol=fcol: e.activation(out=lf.ap[:, d, :], in_=gt.ap[:, :, fcol], func=AF.Exp, scale=-1.0), reads=[gt.b], writes=[lf.b])
                V(lambda e, d=d: e.tensor_scalar(out=lf.ap[:, d, :], in0=lf.ap[:, d, :], scalar1=1.0, scalar2=None, op0=ALU.add), [lf.b], [lf.b])
                kb.op("scalar", lambda e, d=d: e.activation(out=lf.ap[:, d, :], in_=lf.ap[:, d, :], func=AF.Ln), reads=[lf.b], writes=[lf.b])
                V(lambda e, d=d: e.tensor_scalar(out=lf.ap[:, d, :], in0=lf.ap[:, d, :], scalar1=-1.0, scalar2=None, op0=ALU.mult), [lf.b], [lf.b])
                p = nps()
                MM(p.ap[:, 0:nb], tri_f[dn].ap, lf.ap[:, d, :], True, True, [tri_f[dn].b, lf.b], [p.b])
                V(lambda e, d=d, p=p: e.tensor_copy(out=bcs.ap[:, d, :], in_=p.ap[:, 0:nb]), [p.b], [bcs.b])
                p2 = nps()
                MM(p2.ap[:, 0:nb], ones_f.ap, lf.ap[:, d, :], True, True, [ones_f.b, lf.b], [p2.b])
                V(lambda e, d=d, p2=p2: e.tensor_copy(out=gb.ap[:, d, :], in_=p2.ap[:, 0:nb]), [p2.b], [gb.b])
                kb.op("scalar", lambda e, d=d: e.activation(out=eg.ap[:, d, :], in_=gb.ap[:, d, :], func=AF.Exp), reads=[gb.b], writes=[eg.b])
                kb.op("scalar", lambda e, d=d: e.activation(out=emb.ap[:, d, :], in_=bcs.ap[:, d, :], func=AF.Exp, scale=-1.0), reads=[bcs.b], writes=[emb.b])
                V(lambda e, d=d, icol=icol: e.tensor_tensor(out=beta.ap[:, d, :], in0=gt.ap[:, :, icol], in1=bcs.ap[:, d, :], op=ALU.subtract), [gt.b, bcs.b], [beta.b])
                kb.op("scalar", lambda e, d=d: e.activation(out=beta.ap[:, d, :], in_=beta.ap[:, d, :], func=AF.Exp), reads=[beta.b], writes=[beta.b])
                V(lambda e, d=d: e.tensor_scalar(out=beta.ap[:, d, :], in0=beta.ap[:, d, :], scalar1=float(128 ** -0.5), scalar2=None, op0=ALU.mult), [beta.b], [beta.b])
                V(lambda e, d=d: e.tensor_tensor(out=gam.ap[:, d, :], in0=beta.ap[:, d, :], in1=eg.ap[:, d, :], op=ALU.mult), [beta.b, eg.b], [gam.b])
            o_t = None
            for d, dn in enumerate(("fw", "bw")):
                order = list(range(nb)) if d == 0 else list(range(nb - 1, -1, -1))
                for ci, c in enumerate(order):
                    csl = slice(c * 128, (c + 1) * 128)
                    pR = nps()
                    MM(pR.ap[:, 0:128], kT.ap[:, csl], qT.ap[:, csl], True, True, [kT.b, qT.b], [pR.b])
                    stt = STp.get()
                    V(lambda e, stt=stt, pR=pR, d=d, c=c, dn=dn: e.scalar_tensor_tensor(out=stt.ap, in0=pR.ap[:, 0:128], scalar=beta.ap[:, d, c:c + 1], in1=tri_f[dn].ap,
                                                                                          op0=ALU.mult, op1=ALU.mult), [pR.b, beta.b, tri_f[dn].b], [stt.b])
                    pX = nps()
                    MM(pX.ap[:, 0:129], stt.ap, vaug.ap[:, c, 0:129], True, ci == 0, [stt.b, vaug.b], [pX.b])
                    if ci > 0:
                        MM(pX.ap[:, 0:129], qT.ap[:, csl], Cb.ap[:, 0:129], False, True, [qT.b, Cb.b], [pX.b])
                    s4 = sm.get()
                    kb.op("scalar", lambda e, s4=s4, pX=pX: e.activation(out=s4.ap[:, 0:1], in_=pX.ap[:, 128:129], func=AF.Abs), reads=[pX.b], writes=[s4.b])
                    V(lambda e, s4=s4, d=d, c=c: e.tensor_tensor(out=s4.ap[:, 0:1], in0=s4.ap[:, 0:1], in1=emb.ap[:, d, c:c + 1], op=ALU.max), [s4.b, emb.b], [s4.b])
                    V(lambda e, s4=s4: e.reciprocal(out=s4.ap[:, 1:2], in_=s4.ap[:, 0:1]), [s4.b], [s4.b])
                    if d == 0:
                        V(lambda e, s4=s4, pX=pX, c=c: e.tensor_scalar(out=hfw.ap[:, c, :], in0=pX.ap[:, 0:128], scalar1=s4.ap[:, 1:2], scalar2=None, op0=ALU.mult),
                          [pX.b, s4.b], [hfw.b])
                    else:
                        if ci % 8 == 0:
                            o_t = ost.get()
                        hsum = hs.get()
                        V(lambda e, s4=s4, pX=pX, c=c, hsum=hsum: e.scalar_tensor_tensor(out=hsum.ap, in0=pX.ap[:, 0:128], scalar=s4.ap[:, 1:2], in1=hfw.ap[:, c, :],
                                                                                          op0=ALU.mult, op1=ALU.add), [pX.b, s4.b, hfw.b], [hsum.b])
                        kb.op("scalar", lambda e, hsum=hsum, s4=s4: e.activation(out=jk.ap, in_=hsum.ap, func=AF.Square, accum_out=s4.ap[:, 2:3]),
                              reads=[hsum.b], writes=[jk.b, s4.b])
                        V(lambda e, s4=s4: e.tensor_scalar(out=s4.ap[:, 3:4], in0=s4.ap[:, 2:3], scalar1=1.0 / 128, scalar2=EPS, op0=ALU.mult, op1=ALU.add), [s4.b], [s4.b])
                        kb.op("scalar", lambda e, s4=s4: e.activation(out=s4.ap[:, 3:4], in_=s4.ap[:, 3:4], func=AF.Sqrt), reads=[s4.b], writes=[s4.b])
                        V(lambda e, s4=s4: e.reciprocal(out=s4.ap[:, 3:4], in_=s4.ap[:, 3:4]), [s4.b], [s4.b])
                        hbt = hb.get()
                        V(lambda e, hsum=hsum, s4=s4, hbt=hbt: e.scalar_tensor_tensor(out=hbt.ap, in0=hsum.ap, scalar=s4.ap[:, 3:4], in1=nwb.ap, op0=ALU.mult, op1=ALU.mult),
                          [hsum.b, s4.b, nwb.b], [hbt.b])
                        pT = npsb()
                        kb.op("tensor", lambda e, pT=pT, hbt=hbt: e.transpose(pT.ap[:, 0:128], hbt.ap, ident_b.ap), reads=[hbt.b, ident_b.b], writes=[pT.b])
                        copy_op("scalar", o_t.ap[:, (c % 8) * 128:(c % 8) * 128 + 128], pT.ap[:, 0:128], [pT.b], [o_t.b])
                        if c % 8 == 0:
                            dma("sync", YM[n].ap[128 * h:128 * h + 128, c * 128:(c + 8) * 128], o_t.ap, [o_t.b], [YM[n].b])
                    if ci < nb - 1:
                        kg = kgp.get()
                        V(lambda e, kg=kg, d=d, c=c: e.tensor_scalar(out=kg.ap, in0=ktm.ap[:, c, :], scalar1=gam.ap[:, d, c:c + 1], scalar2=None, op0=ALU.mult),
                          [ktm.b, gam.b], [kg.b], eng="gpsimd")
                        pC = nps()
                        MM(pC.ap[:, 0:129], kg.ap, vaug.ap[:, c, 0:129], True, True, [kg.b, vaug.b], [pC.b])
                        if ci == 0:
                            V(lambda e, pC=pC: e.tensor_copy(out=Cst.ap[:, 0:129], in_=pC.ap[:, 0:129]), [pC.b], [Cst.b])
                        else:
                            V(lambda e, pC=pC, d=d, c=c: e.scalar_tensor_tensor(out=Cst.ap[:, 0:129], in0=Cst.ap[:, 0:129], scalar=eg.ap[:, d, c:c + 1], in1=pC.ap[:, 0:129],
                                                                                  op0=ALU.mult, op1=ALU.add), [Cst.b, eg.b, pC.b], [Cst.b])
                        copy_op("scalar", Cb.ap[:, 0:129], Cst.ap[:, 0:129], [Cst.b], [Cb.b])

    PERSIST = g_["PERSIST"]
    aoff = g_["aoff"]

    def MIX(l, si, n, T):
        g_["psmode"][0] = 6
        for fn in (conv, gqa, na, mlstm):
            if fn.__name__ in SKIP_MIX:
                continue
            fn(l, n, T)
            kb.barrier()
            aoff[0] = PERSIST
        g_["psmode"][0] = 6
    MIX.gqa_tables = gqa_tables
    return MIX


SKIP_MIX = set()
DEBUG_YM = False


_W_KEYS = ["rel_bias", "norm_w", "w_ada", "b_ada", "w_in", "b_gate", "mlstm_norm_w", "na_q_norm", "na_k_norm", "na_rpb",
           "swa_q_norm", "swa_k_norm", "swa_sink", "conv_w", "conv_b", "conv_ln_w", "conv_ln_b", "w_branch", "w_out"]


def kernel(**inputs):
    xp = np.ascontiguousarray(np.asarray(inputs["x_prompt"], dtype=np.float32))
    xs = np.ascontiguousarray(np.asarray(inputs["x_sample"], dtype=np.float32))
    cp = np.asarray(inputs["c_prompt"], dtype=np.float32)
    cs = np.asarray(inputs["c_sample"], dtype=np.float32)
    TP, TS = xp.shape[1], xs.shape[1]
    depth = np.asarray(inputs["norm_w"]).shape[0]
    nc = build([("P", TP), ("S", TS)], depth)
    consts = make_consts()
    shared = {k: np.ascontiguousarray(np.asarray(inputs[k], dtype=np.float32)) for k in _W_KEYS}
    shared.update({"k_ident": consts["ident"], "k_tri_fw": consts["tri_fw"], "k_tri_bw": consts["tri_bw"], "k_blk64": consts["blk64"],
                   "k_gqa_m": consts["gqa_m"], "k_na_m": consts["na_m"]})
    in_maps = []
    for i in range(8):
        m = dict(shared)
        m["x_P"] = xp[i // 4]
        m["c_P"] = cp[i // 4][None]
        m["x_S"] = xs[i // 2]
        m["c_S"] = cs[i // 2][None]
        in_maps.append(m)
    res = run_bass_kernel_spmd(nc, in_maps, core_ids=list(range(8)))
    yp = np.empty_like(xp)
    ys = np.empty_like(xs)
    qp, qs = TP // 4, TS // 2
    for i in range(8):
        r = res.results[i]
        a = i % 4
        yp[i // 4, a * qp:(a + 1) * qp] = np.asarray(r["y_P"])[a * qp:(a + 1) * qp]
        b = i % 2
        ys[i // 2, b * qs:(b + 1) * qs] = np.asarray(r["y_S"])[b * qs:(b + 1) * qs]
    return (yp, ys)
NA_W = 3
GQ_W = 2
```

```python
import contextlib
import math
import numpy as np
import concourse.bass as bass
import concourse.mybir as mybir
from concourse.bass_utils import run_bass_kernel_spmd

F32 = mybir.dt.float32
BF16 = mybir.dt.bfloat16
ALU = mybir.AluOpType
AF = mybir.ActivationFunctionType

D = 2048
KC = 16
IN_COLS = 15632
EPS = 1e-6
C_AQ, C_AK, C_AV, C_AO, C_AZ, C_AG = 0, 512, 1024, 1536, 2048, 2560
C_BQ, C_BK, C_BV, C_BZ = 2576, 3088, 3600, 4112
C_CQ, C_CK, C_CV, C_CZ = 4624, 5136, 5264, 5392
C_DA, C_DG, C_DZ, C_MG = 5904, 6416, 6928, 7440
NFM = 7440

ENGS = ("tensor", "vector", "scalar", "gpsimd", "sync")
NDMA = 28
SEM_EPOCH = 30000


class Buf:
    __slots__ = ("w", "r")

    def __init__(self):
        self.w = None
        self.r = []


class KB:
    def __init__(self, nc):
        self.nc = nc
        self.ops = {e: [] for e in ENGS}
        self.cnt = {}
        self.known = {e: {} for e in ENGS}
        self.dma_rr = 0
        self.dma_last = {}
        self.pending = {e: [] for e in ENGS}
        self.tot = {}

    def barrier(self):
        for e in ENGS:
            for k, v in list(self.cnt.items()) + list(self.dma_last.items()):
                self._need(e, (k, v), self.pending[e])

    def _need(self, eng, dep, waits):
        if dep is None:
            return
        k, v = dep
        if eng == "tensor" and k.startswith("tensor#"):
            return
        if self.known[eng].get(k, 0) >= v:
            return
        self.known[eng][k] = v
        waits.append((k, v))

    def op(self, eng, fn, reads=(), writes=(), dma=False):
        waits = self.pending[eng]
        self.pending[eng] = []
        for b in reads:
            self._need(eng, b.w, waits)
        for b in writes:
            self._need(eng, b.w, waits)
            for d in b.r:
                self._need(eng, d, waits)
        if dma:
            k = f"dma{self.dma_rr % NDMA}"
            self.dma_rr += 1
            last = self.dma_last.get(k, 0)
            if last:
                self._need(eng, (k, last), waits)
            v = last + 16
            self.dma_last[k] = v
            inc = 16
        else:
            tot = self.tot.get(eng, 0) + 1
            self.tot[eng] = tot
            k = f"{eng}#{(tot - 1) // SEM_EPOCH}"
            v = (tot - 1) % SEM_EPOCH + 1
            self.cnt[k] = v
            inc = 1
        tag = (k, v)
        for b in reads:
            b.r.append(tag)
            if len(b.r) > 64:
                b.r = b.r[-64:]
        for b in writes:
            b.w = tag
            b.r = []
        self.ops[eng].append((waits, fn, k, inc))

    def emit(self):
        nc = self.nc
        keys = set()
        for e in ENGS:
            for waits, fn, k, inc in self.ops[e]:
                keys.add(k)
        with contextlib.ExitStack() as st:
            sems = {k: st.enter_context(nc.semaphore(f"s_{k}")) for k in sorted(keys)}
            block = st.enter_context(nc.Block())

            def mk(e):
                def body(eng):
                    if e == "sync" and getattr(self, "pre_sync", None) is not None:
                        self.pre_sync(eng)
                    for waits, fn, k, inc in self.ops[e]:
                        for (wk, wv) in waits:
                            eng.wait_ge(sems[wk], wv)
                        fn(eng).then_inc(sems[k], inc)
                    if e == "sync":
                        for k2 in sorted(keys):
                            tot = self.dma_last.get(k2) if k2.startswith("dma") else self.cnt.get(k2)
                            if tot:
                                eng.wait_ge(sems[k2], tot)
                return body

            for e in ENGS:
                if self.ops[e] or e == "sync":
                    getattr(block, e)(mk(e))


class TL:
    def __init__(self, ap):
        self.ap = ap
        self.b = Buf()

    def __getitem__(self, k):
        return self.ap[k]


def t5_bucket_np(rel):
    half, max_exact = 16, 8
    n = np.abs(rel)
    nf = np.maximum(n, 1).astype(np.float32)
    large = max_exact + (np.log(nf / np.float32(max_exact)) / np.float32(math.log(128 / max_exact)) * (half - max_exact)).astype(np.int32)
    large = np.minimum(large, half - 1)
    return np.where(rel > 0, half, 0) + np.where(n < max_exact, n, large)


def make_consts():
    c = {}
    c["ident"] = np.eye(128, dtype=np.float32)
    s = np.arange(128)
    c["tri_fw"] = (s[:, None] <= s[None, :]).astype(np.float32)
    c["tri_bw"] = (s[:, None] >= s[None, :]).astype(np.float32)
    bo = np.zeros((128, 128), np.float32)
    bo[:64, :64] = 1
    bo[64:, 64:] = 1
    c["blk64"] = bo
    k = np.arange(128)[:, None]
    q = np.arange(128)[None, :]
    gm = np.zeros((3, 32, 128, 128), np.float32)
    for o in range(3):
        rel = k + 128 * (o - 1) - q
        bk = t5_bucket_np(rel)
        ok = np.abs(rel) <= 128
        for b in range(32):
            gm[o, b] = ((bk == b) & ok)
    c["gqa_m"] = gm
    kc = np.arange(64)[:, None]
    qc = np.arange(64)[None, :]
    qs = np.clip(qc - 8, 0, 48)
    ok = (kc >= qs) & (kc < qs + 16)
    dc = np.clip(kc - qc + 15, 0, 30)
    nm = np.zeros((31, 128, 64), np.float32)
    for d in range(31):
        m = ((dc == d) & ok).astype(np.float32)
        nm[d, :64] = m
        nm[d, 64:] = m
    c["na_m"] = nm
    return c


def build(seqs, depth, dbg=()):
    nc = bass.Bass("TRN2", target_bir_lowering=False)
    kb = KB(nc)
    st = contextlib.ExitStack()

    def din(name, shape, dt=F32):
        return TL(nc.dram_tensor(name, list(shape), dt, kind="ExternalInput").ap())

    def dscr(name, shape, dt=BF16, kind="Internal"):
        return TL(nc.dram_tensor(name, list(shape), dt, kind=kind).ap())

    AW = 53200
    arena = st.enter_context(nc.sbuf_tensor("arena", [128, AW], F32))
    aoff = [0]

    def sb(name, shape, dt=F32):
        n = 1
        for d_ in shape[1:]:
            n *= d_
        words = (n * (2 if dt == BF16 else 4) + 3) // 4
        assert aoff[0] + words <= AW, (name, aoff[0], words)
        v = arena[0:shape[0], aoff[0]:aoff[0] + words]
        aoff[0] += words
        if dt != F32:
            v = v.bitcast(dt)
        if len(shape) == 3:
            v = v.rearrange("p (a b) -> p a b", a=shape[1])
        elif len(shape) == 4:
            v = v.rearrange("p (a b c) -> p a b c", a=shape[1], b=shape[2])
        return TL(v)

    def ps(name, shape, dt=F32):
        return TL(st.enter_context(nc.psum_tensor(name, list(shape), dt))[:])

    X = {n: din(f"x_{n}", [T, D]) for n, T in seqs}
    Cc = {n: din(f"c_{n}", [1, D]) for n, T in seqs}
    Yout = {n: dscr(f"y_{n}", [T, D], F32, kind="ExternalOutput") for n, T in seqs}
    rel_bias = din("rel_bias", [32, 8])
    norm_w = din("norm_w", [depth, D])
    w_ada = din("w_ada", [depth, D, 3 * D])
    b_ada = din("b_ada", [depth, 3 * D])
    w_in = din("w_in", [depth, D, IN_COLS])
    b_gate = din("b_gate", [depth, 16])
    mlstm_norm_w = din("mlstm_norm_w", [depth, 512])
    na_q_norm = din("na_q_norm", [depth, 128])
    na_k_norm = din("na_k_norm", [depth, 128])
    na_rpb = din("na_rpb", [depth, 4, 15, 31])
    swa_q_norm = din("swa_q_norm", [depth, 64])
    swa_k_norm = din("swa_k_norm", [depth, 64])
    swa_sink = din("swa_sink", [depth, 8])
    conv_w = din("conv_w", [depth, 31, 512])
    conv_b = din("conv_b", [depth, 512])
    conv_ln_w = din("conv_ln_w", [depth, 512])
    conv_ln_b = din("conv_ln_b", [depth, 512])
    w_branch = din("w_branch", [depth, 4, 512, D])
    w_out = din("w_out", [depth, D, D])
    k_ident = din("k_ident", [128, 128])
    k_tri_fw = din("k_tri_fw", [128, 128])
    k_tri_bw = din("k_tri_bw", [128, 128])
    k_blk64 = din("k_blk64", [128, 128])
    k_gqa_m = din("k_gqa_m", [3, 32, 128, 128])
    k_na_m = din("k_na_m", [31, 128, 64])

    FM = {n: dscr(f"fm_{n}", [NFM, T]) for n, T in seqs}
    TMv = {n: dscr(f"tm_{n}", [T, 1664]) for n, T in seqs}
    TMg = {n: dscr(f"tg_{n}", [T, 16], F32) for n, T in seqs}
    HT = {n: dscr(f"ht_{n}", [D, T]) for n, T in seqs}
    YM = {n: dscr(f"ym_{n}", [D, T], BF16, kind=("ExternalOutput" if DEBUG_YM else "Internal")) for n, T in seqs}
    X1 = {n: dscr(f"x1_{n}", [T, D], F32) for n, T in seqs}
    modrow = dscr("modrow", [depth * len(seqs), D], F32)
    WT = {}
    WSRC = {}
    DBG = {}

    ident_f = sb("ident_f", [128, 128])
    ident_b = sb("ident_b", [128, 128], BF16)
    ones_b = sb("ones_b", [128, 128], BF16)
    ones_f = sb("ones_f", [128, 128])
    blk64_b = sb("blk64_b", [128, 128], BF16)
    tri_f = {"fw": sb("tri_fw", [128, 128]), "bw": sb("tri_bw", [128, 128])}
    kb.op("sync", lambda e: e.dma_start(out=ident_f.ap, in_=k_ident.ap), writes=[ident_f.b], dma=True)
    kb.op("vector", lambda e: e.tensor_copy(out=ident_b.ap, in_=ident_f.ap), reads=[ident_f.b], writes=[ident_b.b])
    kb.op("vector", lambda e: e.memset(ones_b.ap, 1.0), writes=[ones_b.b])
    kb.op("vector", lambda e: e.memset(ones_f.ap, 1.0), writes=[ones_f.b])
    kb.op("sync", lambda e: e.dma_start(out=tri_f["fw"].ap, in_=k_tri_fw.ap), writes=[tri_f["fw"].b], dma=True)
    kb.op("sync", lambda e: e.dma_start(out=tri_f["bw"].ap, in_=k_tri_bw.ap), writes=[tri_f["bw"].b], dma=True)
    eps_c = sb("eps_c", [128, 1])
    kb.op("vector", lambda e: e.memset(eps_c.ap, EPS), writes=[eps_c.b])
    tmpc = sb("tmpc", [128, 128])
    kb.op("sync", lambda e: e.dma_start(out=tmpc.ap, in_=k_blk64.ap), writes=[tmpc.b], dma=True)
    kb.op("vector", lambda e: e.tensor_copy(out=blk64_b.ap, in_=tmpc.ap), reads=[tmpc.b], writes=[blk64_b.b])

    PS = [ps(f"ps{i}", [128, 512]) for i in range(6)]
    PSB = [ps(f"psb{i}", [128, 1024], BF16) for i in range(2)]
    psrr = [0]

    psmode = [6]

    def nps():
        psrr[0] += 1
        return PS[psrr[0] % psmode[0]]

    PSH = [TL(PS[4].ap[:, 0:256]), TL(PS[4].ap[:, 256:512]), TL(PS[5].ap[:, 0:256]), TL(PS[5].ap[:, 256:512])]
    pshr = [0]

    def npsh():
        pshr[0] += 1
        return PSH[pshr[0] % 4]

    psbr = [0]

    def npsb():
        psbr[0] += 1
        return PSB[psbr[0] % 2]

    class Pool:
        def __init__(self, name, shape, dt, n):
            self.t = [sb(f"{name}{i}", shape, dt) for i in range(n)]
            self.i = 0

        def get(self):
            self.i += 1
            return self.t[self.i % len(self.t)]

    evac_rr = [0]

    def evac_eng():
        evac_rr[0] += 1
        return "vector" if evac_rr[0] % 2 else "scalar"

    def copy_op(eng, out, in_, reads, writes):
        if eng == "scalar":
            kb.op("scalar", lambda e: e.activation(out=out, in_=in_, func=AF.Copy), reads=reads, writes=writes)
        else:
            kb.op(eng, lambda e: e.tensor_copy(out=out, in_=in_), reads=reads, writes=writes)

    nseq = len(seqs)
    modA = sb("modA", [128, depth, nseq, KC])
    modB = sb("modB", [128, depth, nseq, KC])
    cs = sb("cs", [128, KC, nseq])
    modfm = sb("modfm", [128, 48, nseq])
    badafm = sb("badafm", [128, 48])
    nwfm = sb("nwfm", [128, KC])
    bg_bc = sb("bg_bc", [128, 16])
    EB = sb("EB", [128, 3, 8, 128])
    PERSIST = aoff[0]

    def adaln():
        w32 = Pool("w32", [128, KC, 128], F32, 3)
        for si, (n, T) in enumerate(seqs):
            kb.op("sync", lambda e, si=si, n=n: e.dma_start(out=cs.ap[:, :, si], in_=Cc[n].ap.rearrange("o (k p) -> p (o k)", p=128), allow_slow_non_contiguous=True),
                  writes=[cs.b], dma=True)
        kb.op("scalar", lambda e: e.activation(out=cs.ap, in_=cs.ap, func=AF.Silu), reads=[cs.b], writes=[cs.b])
        for l in range(depth):
            kb.op("sync", lambda e, l=l: e.dma_start(out=badafm.ap, in_=b_ada.ap[l:l + 1, :].rearrange("o (k p) -> p (o k)", p=128), allow_slow_non_contiguous=True),
                  writes=[badafm.b], dma=True)
            kb.op("sync", lambda e, l=l: e.dma_start(out=nwfm.ap, in_=norm_w.ap[l:l + 1, :].rearrange("o (k p) -> p (o k)", p=128), allow_slow_non_contiguous=True),
                  writes=[nwfm.b], dma=True)
            pm = nps()
            for f in range(48):
                wt = w32.get()
                kb.op("sync", lambda e, l=l, f=f, wt=wt: e.dma_start(out=wt.ap, in_=w_ada.ap[l, :, f * 128:(f + 1) * 128].rearrange("(k p) n -> p k n", p=128)),
                      writes=[wt.b], dma=True)
                for k in range(KC):
                    kb.op("tensor", lambda e, k=k, f=f, pm=pm, wt=wt: e.matmul(pm.ap[:, f * nseq:(f + 1) * nseq], lhsT=wt.ap[:, k, :],
                                                                          rhs=cs.ap[:, k, :], start=(k == 0), stop=(k == KC - 1)),
                          reads=[wt.b, cs.b], writes=[pm.b])
            for si in range(nseq):
                kb.op("vector", lambda e, pm=pm, si=si: e.tensor_tensor(out=modfm.ap[:, :, si], in0=pm.ap[:, 0:48 * nseq].rearrange("p (f s) -> p f s", s=nseq)[:, :, si],
                                                                in1=badafm.ap, op=ALU.add),
                      reads=[pm.b, badafm.b], writes=[modfm.b])
            for si, (n, T) in enumerate(seqs):
                kb.op("vector", lambda e, l=l, si=si: e.scalar_tensor_tensor(out=modA.ap[:, l, si, :], in0=modfm.ap[:, 16:32, si], scalar=1.0, in1=nwfm.ap,
                                                                           op0=ALU.add, op1=ALU.mult),
                      reads=[modfm.b, nwfm.b], writes=[modA.b])
                kb.op("vector", lambda e, l=l, si=si: e.tensor_copy(out=modB.ap[:, l, si, :], in_=modfm.ap[:, 0:16, si]), reads=[modfm.b], writes=[modB.b])
                kb.op("sync", lambda e, l=l, si=si: e.dma_start(out=modrow.ap[l * nseq + si:l * nseq + si + 1, :].rearrange("o (k p) -> p (o k)", p=128),
                                                               in_=modfm.ap[:, 32:48, si], allow_slow_non_contiguous=True),
                      reads=[modfm.b], writes=[modrow.b], dma=True)
        kb.barrier()
        aoff[0] = PERSIST

    FM_RANGES = [(0, 1024), (1536, 2560), (2576, 3600), (4112, 5264), (5392, 7440)]
    TM_RANGES = [(512, 1536, 0), (3600, 4112, 1024), (5264, 5392, 1536)]

    def fm_func(col):
        if C_AO <= col < C_AZ:
            return AF.Sigmoid
        if C_AZ <= col < C_AG or C_BZ <= col < C_CQ or C_CZ <= col < C_DA or C_DZ <= col < C_MG:
            return AF.Silu
        return None

    def phase1(l, si, n, T, xsrc):
        xt_pool = Pool("xt", [128, D], F32, 2)
        xn_t = sb("xn", [128, 4, D], BF16)
        junk = sb("junk", [128, D], BF16)
        ssq = sb("ssq", [128, 8])
        hT = sb("hT", [128, KC, 1024], BF16)
        ofm = Pool("ofm", [128, 1024], BF16, 3)
        otm = Pool("otm", [128, 512], BF16, 3)
        otg = Pool("otg", [128, 16], F32, 2)
        wtile = Pool("wtile", [128, KC, 512], BF16, 4)

        wjobs = []
        for g_ in range(T // 1024):
            for (c0_, c1_) in FM_RANGES:
                for w0_ in range(c0_, c1_, 512):
                    wjobs.append((w0_, min(512, c1_ - w0_)))
            for (c0_, c1_, _d) in TM_RANGES:
                for w0_ in range(c0_, c1_, 512):
                    wjobs.append((w0_, min(512, c1_ - w0_)))
            wjobs.append((C_AG, 16))

        def mk_loader(c0, ncols):
            def ld():
                wt = wtile.get()
                wload(wt.ap, wt.b, l, "in", 0, c0, ncols)
                return wt
            return ld
        pf = Prefetch([mk_loader(c0, nco) for (c0, nco) in wjobs], ahead=2)
        wji = [0]

        def load_w(c0, ncols):
            i = wji[0]
            assert wjobs[i] == (c0, ncols), (wjobs[i], c0, ncols)
            wji[0] += 1
            return pf.get(i)

        kb.op("sync", lambda e: e.dma_start(out=bg_bc.ap, in_=b_gate.ap[l:l + 1, :].broadcast_to([128, 16])), writes=[bg_bc.b], dma=True)
        hTs = [hT, sb("hTb", [128, KC, 1024], BF16)]

        def prep_gen(g):
            hT = hTs[g % 2]
            for half in range(2):
                for j in range(4):
                    t0 = g * 1024 + half * 512 + j * 128
                    xt = xt_pool.get()
                    kb.op("sync", lambda e, xt=xt, t0=t0: e.dma_start(out=xt.ap, in_=xsrc.ap[t0:t0 + 128, :]), reads=[xsrc.b], writes=[xt.b], dma=True)
                    kb.op("scalar", lambda e, xt=xt, j=j: e.activation(out=junk.ap, in_=xt.ap, func=AF.Square, accum_out=ssq.ap[:, j:j + 1]),
                          reads=[xt.b], writes=[junk.b, ssq.b])
                    kb.op("vector", lambda e, j=j: e.tensor_scalar(out=ssq.ap[:, 4 + j:5 + j], in0=ssq.ap[:, j:j + 1], scalar1=1.0 / D, scalar2=EPS,
                                                                    op0=ALU.mult, op1=ALU.add), reads=[ssq.b], writes=[ssq.b])
                    kb.op("scalar", lambda e, j=j: e.activation(out=ssq.ap[:, 4 + j:5 + j], in_=ssq.ap[:, 4 + j:5 + j], func=AF.Sqrt), reads=[ssq.b], writes=[ssq.b])
                    kb.op("vector", lambda e, j=j: e.reciprocal(out=ssq.ap[:, 4 + j:5 + j], in_=ssq.ap[:, 4 + j:5 + j]), reads=[ssq.b], writes=[ssq.b])
                    kb.op("vector", lambda e, xt=xt, j=j: e.tensor_scalar(out=xn_t.ap[:, j, :], in0=xt.ap, scalar1=ssq.ap[:, 4 + j:5 + j], scalar2=None,
                                                                           op0=ALU.mult), reads=[xt.b, ssq.b], writes=[xn_t.b])
                    yield
                for k in range(KC):
                    pb = npsb()
                    for j in range(4):
                        kb.op("tensor", lambda e, pb=pb, j=j, k=k: e.transpose(pb.ap[:, j * 128:(j + 1) * 128], xn_t.ap[:, j, k * 128:(k + 1) * 128], ident_b.ap),
                              reads=[xn_t.b, ident_b.b], writes=[pb.b])
                    dst = hT.ap[:, k, half * 512:(half + 1) * 512]
                    if k % 2 == 0:
                        kb.op("scalar", lambda e, pb=pb, k=k, dst=dst: e.activation(out=dst, in_=pb.ap[:, 0:512], func=AF.Identity,
                                                                                     bias=modB.ap[:, l, si, k:k + 1], scale=modA.ap[:, l, si, k:k + 1]),
                              reads=[pb.b, modA.b, modB.b], writes=[hT.b])
                    else:
                        kb.op("vector", lambda e, pb=pb, k=k, dst=dst: e.tensor_scalar(out=dst, in0=pb.ap[:, 0:512], scalar1=modA.ap[:, l, si, k:k + 1],
                                                                                        scalar2=modB.ap[:, l, si, k:k + 1], op0=ALU.mult, op1=ALU.add),
                              reads=[pb.b, modA.b, modB.b], writes=[hT.b])
                    if k % 4 == 3:
                        yield
            kb.op("sync", lambda e, g=g, hT=hT: e.dma_start(out=HT[n].ap[:, g * 1024:(g + 1) * 1024].rearrange("(k p) t -> p k t", p=128), in_=hT.ap),
                  reads=[hT.b], writes=[HT[n].b], dma=True)

        preps = {}

        def pump(g, nsteps=1):
            if g >= T // 1024:
                return
            if g not in preps:
                preps[g] = prep_gen(g)
            for _ in range(nsteps):
                try:
                    next(preps[g])
                except StopIteration:
                    break

        def do_group1(g):
            pump(g, 1000)
            hT = hTs[g % 2]
            for (c0, c1) in FM_RANGES:
                for w0 in range(c0, c1, 512):
                    ncols = min(512, c1 - w0)
                    wt = load_w(w0, ncols)
                    for mc in range(ncols // 128):
                        col = w0 + mc * 128
                        o = ofm.get()
                        fn = fm_func(col)
                        for tt in range(2):
                            p = nps()
                            for k in range(KC):
                                kb.op("tensor", lambda e, p=p, wt=wt, mc=mc, k=k, tt=tt: e.matmul(p.ap, lhsT=wt.ap[:, k, mc * 128:(mc + 1) * 128],
                                                                                            rhs=hT.ap[:, k, tt * 512:(tt + 1) * 512],
                                                                                            start=(k == 0), stop=(k == KC - 1)),
                                      reads=[wt.b, hT.b], writes=[p.b])
                            dst = o.ap[:, tt * 512:(tt + 1) * 512]
                            if fn is None:
                                copy_op(evac_eng(), dst, p.ap, [p.b], [o.b])
                            else:
                                kb.op("scalar", lambda e, p=p, dst=dst, fn=fn: e.activation(out=dst, in_=p.ap, func=fn), reads=[p.b], writes=[o.b])
                        kb.op("sync", lambda e, o=o, col=col, g=g: e.dma_start(out=FM[n].ap[col:col + 128, g * 1024:(g + 1) * 1024], in_=o.ap),
                              reads=[o.b], writes=[FM[n].b], dma=True)
                        pump(g + 1, 1)
            for (c0, c1, dcol) in TM_RANGES:
                for w0 in range(c0, c1, 512):
                    ncols = min(512, c1 - w0)
                    wt = load_w(w0, ncols)
                    for sub in range(8):
                        p = nps()
                        for k in range(KC):
                            kb.op("tensor", lambda e, p=p, wt=wt, k=k, sub=sub, ncols=ncols: e.matmul(p.ap[:, 0:ncols], lhsT=hT.ap[:, k, sub * 128:(sub + 1) * 128],
                                                                                                 rhs=wt.ap[:, k, 0:ncols], start=(k == 0), stop=(k == KC - 1)),
                                  reads=[wt.b, hT.b], writes=[p.b])
                        o = otm.get()
                        copy_op(evac_eng(), o.ap[:, 0:ncols], p.ap[:, 0:ncols], [p.b], [o.b])
                        t0 = g * 1024 + sub * 128
                        dc = dcol + (w0 - c0)
                        kb.op("sync", lambda e, o=o, t0=t0, dc=dc, ncols=ncols: e.dma_start(out=TMv[n].ap[t0:t0 + 128, dc:dc + ncols], in_=o.ap[:, 0:ncols]),
                              reads=[o.b], writes=[TMv[n].b], dma=True)
            wt = load_w(C_AG, 16)
            for sub in range(8):
                p = nps()
                for k in range(KC):
                    kb.op("tensor", lambda e, p=p, wt=wt, k=k, sub=sub: e.matmul(p.ap[:, 0:16], lhsT=hT.ap[:, k, sub * 128:(sub + 1) * 128],
                                                                            rhs=wt.ap[:, k, 0:16], start=(k == 0), stop=(k == KC - 1)),
                          reads=[wt.b, hT.b], writes=[p.b])
                o = otg.get()
                kb.op("vector", lambda e, o=o, p=p: e.tensor_tensor(out=o.ap, in0=p.ap[:, 0:16], in1=bg_bc.ap, op=ALU.add), reads=[p.b, bg_bc.b], writes=[o.b])
                t0 = g * 1024 + sub * 128
                kb.op("sync", lambda e, o=o, t0=t0: e.dma_start(out=TMg[n].ap[t0:t0 + 128, :], in_=o.ap), reads=[o.b], writes=[TMg[n].b], dma=True)
        for g in range(T // 1024):
            do_group1(g)
        kb.barrier()
        aoff[0] = PERSIST

    def phase3(l, si, n, T, xsrc, xdst):
        hT = sb("hT3", [128, KC, 1024], BF16)
        yT_t = sb("yT", [128, KC, 1024], BF16)
        yTb = [Buf() for _ in range(KC)]
        mT_t = sb("mT", [128, KC, 1024], BF16)
        wm_pool = Pool("wm", [128, KC, 256], BF16, 4)
        wbr_pool = Pool("wbr", [128, 4, 256], BF16, 4)

        def mk_merge(mg, i):
            def ld():
                wt = wm_pool.get()
                wload(wt.ap, wt.b, l, "in", 0, C_MG + i * 2048 + mg * 256, 256)
                wb_ = wbr_pool.get()
                wload(wb_.ap, wb_.b, l, "br", i, mg * 256, 256)
                return (wt, wb_)
            return ld

        def mk_out(nn):
            def ld():
                wt = wm_pool.get()
                wload(wt.ap, wt.b, l, "out", 0, nn * 256, 256)
                return wt
            return ld
        loaders3 = []
        for g_ in range(T // 1024):
            for mg_ in range(8):
                for i_ in range(4):
                    loaders3.append(mk_merge(mg_, i_))
            for nn_ in range(8):
                loaders3.append(mk_out(nn_))
        pf3 = Prefetch(loaders3, ahead=2)
        pfi = [0]
        ytmp = Pool("ytmp", [128, 1024], BF16, 3)
        uld = [sb(f"uld{i}", [128, 1024], BF16) for i in range(4)]
        sg_pool = Pool("sg", [128, 512], F32, 2)
        acc_pool = Pool("acc", [128, 512], F32, 8)
        tmp_pool = Pool("tmp3", [128, 512], F32, 2)
        xo_pool = Pool("xo", [128, 512], F32, 2)
        usq = Pool("usq", [128, 512], BF16, 2)
        stat = Pool("stat", [128, 512], F32, 3)
        gsl = sb("gsl", [128, 512])
        lnw = sb("lnw", [128, 4])
        lnb = sb("lnb", [128, 4])
        kb.op("sync", lambda e: e.dma_start(out=lnw.ap, in_=conv_ln_w.ap[l:l + 1, :].rearrange("o (k p) -> p (o k)", p=128), allow_slow_non_contiguous=True), writes=[lnw.b], dma=True)
        kb.op("sync", lambda e: e.dma_start(out=lnb.ap, in_=conv_ln_b.ap[l:l + 1, :].rearrange("o (k p) -> p (o k)", p=128), allow_slow_non_contiguous=True), writes=[lnb.b], dma=True)
        def prep3_gen(g):
            tsl = slice(g * 1024, (g + 1) * 1024)
            kb.op("sync", lambda e: e.dma_start(out=hT.ap, in_=HT[n].ap[:, tsl].rearrange("(k p) t -> p k t", p=128)), reads=[HT[n].b], writes=[hT.b], dma=True)
            for br in range(3):
                zc = (C_AZ, C_BZ, C_CZ)[br]
                for c4 in range(4):
                    a = ytmp.get()
                    kb.op("sync", lambda e, a=a, br=br, c4=c4: e.dma_start(out=a.ap, in_=YM[n].ap[br * 512 + c4 * 128: br * 512 + c4 * 128 + 128, tsl]),
                          reads=[YM[n].b], writes=[a.b], dma=True)
                    z = ytmp.get()
                    kb.op("sync", lambda e, z=z, zc=zc, c4=c4: e.dma_start(out=z.ap, in_=FM[n].ap[zc + c4 * 128: zc + c4 * 128 + 128, tsl]),
                          reads=[FM[n].b], writes=[z.b], dma=True)
                    dst = yT_t.ap[:, br * 4 + c4, :]
                    if br == 0:
                        s_ = ytmp.get()
                        kb.op("sync", lambda e, s_=s_, c4=c4: e.dma_start(out=s_.ap, in_=FM[n].ap[C_AO + c4 * 128: C_AO + c4 * 128 + 128, tsl]),
                              reads=[FM[n].b], writes=[s_.b], dma=True)
                        kb.op("gpsimd", lambda e, z=z, s_=s_: e.tensor_tensor(out=z.ap, in0=z.ap, in1=s_.ap, op=ALU.mult), reads=[z.b, s_.b], writes=[z.b])
                    kb.op("vector", lambda e, a=a, z=z, dst=dst: e.tensor_tensor(out=dst, in0=a.ap, in1=z.ap, op=ALU.mult), reads=[a.b, z.b], writes=[yTb[br * 4 + c4]])
                    yield
            ul = uld
            for c4 in range(4):
                kb.op("sync", lambda e, c4=c4: e.dma_start(out=ul[c4].ap, in_=YM[n].ap[1536 + c4 * 128: 1536 + c4 * 128 + 128, tsl]),
                      reads=[YM[n].b], writes=[ul[c4].b], dma=True)
            for tt in range(2):
                cs_ = slice(tt * 512, (tt + 1) * 512)
                p1 = nps()
                p2 = nps()
                for c4 in range(4):
                    q2 = usq.get()
                    kb.op("gpsimd", lambda e, q2=q2, c4=c4, cs_=cs_: e.tensor_tensor(out=q2.ap, in0=ul[c4].ap[:, cs_], in1=ul[c4].ap[:, cs_], op=ALU.mult),
                          reads=[ul[c4].b], writes=[q2.b])
                    kb.op("tensor", lambda e, p1=p1, c4=c4, cs_=cs_: e.matmul(p1.ap, lhsT=ones_b.ap, rhs=ul[c4].ap[:, cs_], start=(c4 == 0), stop=(c4 == 3)),
                          reads=[ones_b.b, ul[c4].b], writes=[p1.b])
                    kb.op("tensor", lambda e, p2=p2, q2=q2, c4=c4: e.matmul(p2.ap, lhsT=ones_b.ap, rhs=q2.ap, start=(c4 == 0), stop=(c4 == 3)),
                          reads=[ones_b.b, q2.b], writes=[p2.b])
                mean = stat.get()
                rstd = stat.get()
                m2 = stat.get()
                kb.op("scalar", lambda e, mean=mean, p1=p1: e.activation(out=mean.ap, in_=p1.ap, func=AF.Copy, scale=1.0 / 512), reads=[p1.b], writes=[mean.b])
                kb.op("vector", lambda e, mean=mean, m2=m2: e.tensor_tensor(out=m2.ap, in0=mean.ap, in1=mean.ap, op=ALU.mult), reads=[mean.b], writes=[m2.b])
                kb.op("vector", lambda e, rstd=rstd, p2=p2, m2=m2: e.scalar_tensor_tensor(out=rstd.ap, in0=p2.ap, scalar=1.0 / 512, in1=m2.ap, op0=ALU.mult, op1=ALU.subtract),
                      reads=[p2.b, m2.b], writes=[rstd.b])
                kb.op("scalar", lambda e, rstd=rstd: e.activation(out=rstd.ap, in_=rstd.ap, func=AF.Sqrt, bias=eps_c.ap[:, 0:1]), reads=[rstd.b, eps_c.b], writes=[rstd.b])
                kb.op("vector", lambda e, rstd=rstd: e.reciprocal(out=rstd.ap, in_=rstd.ap), reads=[rstd.b], writes=[rstd.b])
                for c4 in range(4):
                    t1 = tmp_pool.get()
                    kb.op("vector", lambda e, t1=t1, c4=c4, mean=mean, cs_=cs_: e.tensor_tensor(out=t1.ap, in0=ul[c4].ap[:, cs_], in1=mean.ap, op=ALU.subtract),
                          reads=[ul[c4].b, mean.b], writes=[t1.b])
                    kb.op("gpsimd", lambda e, t1=t1, rstd=rstd: e.tensor_tensor(out=t1.ap, in0=t1.ap, in1=rstd.ap, op=ALU.mult), reads=[t1.b, rstd.b], writes=[t1.b])
                    kb.op("scalar", lambda e, t1=t1, c4=c4: e.activation(out=t1.ap, in_=t1.ap, func=AF.Silu, bias=lnb.ap[:, c4:c4 + 1], scale=lnw.ap[:, c4:c4 + 1]),
                          reads=[t1.b, lnw.b, lnb.b], writes=[t1.b])
                    z = usq.get()
                    kb.op("sync", lambda e, z=z, c4=c4, tt=tt: e.dma_start(out=z.ap, in_=FM[n].ap[C_DZ + c4 * 128: C_DZ + c4 * 128 + 128, g * 1024 + tt * 512: g * 1024 + tt * 512 + 512]),
                          reads=[FM[n].b], writes=[z.b], dma=True)
                    kb.op("vector", lambda e, t1=t1, z=z, c4=c4, cs_=cs_: e.tensor_tensor(out=yT_t.ap[:, 12 + c4, cs_], in0=t1.ap, in1=z.ap, op=ALU.mult),
                          reads=[t1.b, z.b], writes=[yTb[12 + c4]])
                    yield
            yield

        preps3 = {}

        def pump3(g, nsteps=1):
            if g >= T // 1024:
                return
            if g not in preps3:
                preps3[g] = prep3_gen(g)
            for _ in range(nsteps):
                try:
                    next(preps3[g])
                except StopIteration:
                    break

        def do_group(g):
            pump3(g, 100000)
            for mg in range(8):
                accs = [acc_pool.get() for _ in range(4)]
                for i in range(4):
                    wt, wb_ = pf3.get(pfi[0])
                    pfi[0] += 1
                    for mc in range(2):
                        m = mg * 2 + mc
                        for tt in range(2):
                            cs_ = slice(tt * 512, (tt + 1) * 512)
                            acc = accs[mc * 2 + tt]
                            pa = nps()
                            for k in range(KC):
                                kb.op("tensor", lambda e, pa=pa, k=k, mc=mc, cs_=cs_, wt=wt: e.matmul(pa.ap, lhsT=wt.ap[:, k, mc * 128:(mc + 1) * 128], rhs=hT.ap[:, k, cs_],
                                                                                                 start=(k == 0), stop=(k == KC - 1)),
                                      reads=[wt.b, hT.b], writes=[pa.b])
                            pb_ = nps()
                            for k in range(4):
                                kb.op("tensor", lambda e, pb_=pb_, i=i, k=k, mc=mc, cs_=cs_, wb_=wb_: e.matmul(pb_.ap, lhsT=wb_.ap[:, k, mc * 128:(mc + 1) * 128], rhs=yT_t.ap[:, i * 4 + k, cs_],
                                                                                                          start=(k == 0), stop=(k == 3)),
                                      reads=[wb_.b, yTb[i * 4 + k]], writes=[pb_.b])
                            sg = sg_pool.get()
                            kb.op("scalar", lambda e, sg=sg, pa=pa: e.activation(out=sg.ap, in_=pa.ap, func=AF.Sigmoid), reads=[pa.b], writes=[sg.b])
                            if i == 0:
                                kb.op("vector", lambda e, acc=acc, sg=sg, pb_=pb_: e.tensor_tensor(out=acc.ap, in0=pb_.ap, in1=sg.ap, op=ALU.mult),
                                      reads=[pb_.b, sg.b], writes=[acc.b])
                            else:
                                kb.op("vector", lambda e, sg=sg, pb_=pb_: e.tensor_tensor(out=sg.ap, in0=pb_.ap, in1=sg.ap, op=ALU.mult),
                                      reads=[pb_.b, sg.b], writes=[sg.b])
                                if i < 3:
                                    kb.op("gpsimd", lambda e, acc=acc, sg=sg: e.tensor_tensor(out=acc.ap, in0=acc.ap, in1=sg.ap, op=ALU.add),
                                          reads=[acc.b, sg.b], writes=[acc.b])
                                else:
                                    kb.op("gpsimd", lambda e, acc=acc, sg=sg, m=m, cs_=cs_: e.tensor_tensor(out=mT_t.ap[:, m, cs_], in0=acc.ap, in1=sg.ap, op=ALU.add),
                                          reads=[acc.b, sg.b], writes=[mT_t.b])
            for nn in range(8):
                wt = pf3.get(pfi[0])
                pfi[0] += 1
                kb.op("sync", lambda e, nn=nn: e.dma_start(out=gsl.ap[:, 0:256], in_=modrow.ap[l * nseq + si:l * nseq + si + 1, nn * 256:(nn + 1) * 256].broadcast_to([128, 256])),
                      reads=[modrow.b], writes=[gsl.b], dma=True)
                for sub in range(8):
                    t0 = g * 1024 + sub * 128
                    p = nps()
                    for k in range(KC):
                        kb.op("tensor", lambda e, p=p, wt=wt, k=k, sub=sub: e.matmul(p.ap[:, 0:256], lhsT=mT_t.ap[:, k, sub * 128:(sub + 1) * 128], rhs=wt.ap[:, k, :],
                                                                               start=(k == 0), stop=(k == KC - 1)),
                              reads=[wt.b, mT_t.b], writes=[p.b])
                    xo = xo_pool.get()
                    kb.op("sync", lambda e, xo=xo, t0=t0, nn=nn: e.dma_start(out=xo.ap[:, 0:256], in_=xsrc.ap[t0:t0 + 128, nn * 256:(nn + 1) * 256]), reads=[xsrc.b], writes=[xo.b], dma=True)
                    t1 = tmp_pool.get()
                    kb.op("vector", lambda e, t1=t1, p=p: e.tensor_tensor(out=t1.ap[:, 0:256], in0=p.ap[:, 0:256], in1=gsl.ap[:, 0:256], op=ALU.mult),
                          reads=[p.b, gsl.b], writes=[t1.b])
                    kb.op("gpsimd", lambda e, t1=t1, xo=xo: e.tensor_tensor(out=xo.ap[:, 0:256], in0=xo.ap[:, 0:256], in1=t1.ap[:, 0:256], op=ALU.add), reads=[xo.b, t1.b], writes=[xo.b])
                    kb.op("sync", lambda e, xo=xo, t0=t0, nn=nn: e.dma_start(out=xdst.ap[t0:t0 + 128, nn * 256:(nn + 1) * 256], in_=xo.ap[:, 0:256]), reads=[xo.b], writes=[xdst.b], dma=True)
                    pump3(g + 1, 1)
        for g in range(T // 1024):
            do_group(g)
        kb.barrier()
        aoff[0] = PERSIST

    def wt_get(l, kind, idx, c0, ncols):
        key = (l, kind, idx, c0, ncols)
        if key in WT:
            return WT[key]
        nk = 4 if kind == "br" else KC
        t = dscr(f"wt_{l}_{kind}_{idx}_{c0}_{ncols}", [128, nk * ncols])
        if kind == "in":
            src = w_in.ap[l, :, c0:c0 + ncols]
        elif kind == "br":
            src = w_branch.ap[l, idx, :, c0:c0 + ncols]
        else:
            src = w_out.ap[l, :, c0:c0 + ncols]
        WSRC[key] = src
        WT[key] = (t, nk)
        return WT[key]

    P1_TILES = []
    for (c0_, c1_) in [(0, 1024), (1536, 2560), (2576, 3600), (4112, 5264), (5392, 7440)]:
        for w0_ in range(c0_, c1_, 512):
            P1_TILES.append((w0_, min(512, c1_ - w0_)))
    for (c0_, c1_) in [(512, 1536), (3600, 4112), (5264, 5392)]:
        for w0_ in range(c0_, c1_, 512):
            P1_TILES.append((w0_, min(512, c1_ - w0_)))
    P1_TILES.append((C_AG, 16))

    def convert_tiles(keys):
        st32 = Pool("cv32", [128, KC, 512], F32, 2)
        st16 = Pool("cv16", [128, KC, 512], BF16, 2)
        for ci, key in enumerate(keys):
            (l, kind, idx, c0, ncols) = key
            t, nk = wt_get(l, kind, idx, c0, ncols)
            src = WSRC[key]
            a = st32.get()
            b = st16.get()
            kb.op("sync", lambda e, a=a, src=src, nk=nk, ncols=ncols: e.dma_start(out=a.ap[:, 0:nk, 0:ncols], in_=src.rearrange("(k p) n -> p k n", p=128)),
                  writes=[a.b], dma=True)
            eng = "gpsimd" if ci % 3 != 2 else "vector"
            kb.op(eng, lambda e, a=a, b=b, nk=nk, ncols=ncols: e.tensor_copy(out=b.ap[:, 0:nk, 0:ncols], in_=a.ap[:, 0:nk, 0:ncols]), reads=[a.b], writes=[b.b])
            kb.op("scalar", lambda e, b=b, t=t, nk=nk, ncols=ncols: e.dma_start(out=t.ap.rearrange("p (k n) -> p k n", k=nk), in_=b.ap[:, 0:nk, 0:ncols]),
                  reads=[b.b], writes=[t.b], dma=True)
        kb.barrier()
        aoff[0] = PERSIST

    def p1_keys(l):
        return [(l, "in", 0, c0, nco) for (c0, nco) in P1_TILES]

    def p3_keys(l):
        ks = []
        for mg in range(8):
            for i in range(4):
                ks.append((l, "in", 0, C_MG + i * 2048 + mg * 256, 256))
                ks.append((l, "br", i, mg * 256, 256))
        for nn in range(8):
            ks.append((l, "out", 0, nn * 256, 256))
        return ks

    WQ = "scalar"

    def wload(dst, dst_b, l, kind, idx, c0, ncols):
        t, nk = wt_get(l, kind, idx, c0, ncols)
        kb.op(WQ, lambda e: e.dma_start(out=dst[:, 0:nk, 0:ncols], in_=t.ap.rearrange("p (k n) -> p k n", k=nk)),
              reads=[t.b], writes=[dst_b], dma=True)

    class Prefetch:
        def __init__(self, loaders, ahead=2):
            self.loaders, self.ahead, self.tiles, self.issued = loaders, ahead, {}, 0

        def get(self, i):
            while self.issued <= min(i + self.ahead, len(self.loaders) - 1):
                self.tiles[self.issued] = self.loaders[self.issued]()
                self.issued += 1
            return self.tiles.pop(i)

    env = dict(locals())
    MIX = build_mixers(env)

    convert_tiles(p1_keys(0))
    adaln()
    MIX.gqa_tables()
    kb.barrier()
    aoff[0] = PERSIST
    for l in range(depth):
        for si, (n, T) in enumerate(seqs):
            xsrc = X[n] if l == 0 else X1[n]
            phase1(l, si, n, T, xsrc)
        convert_tiles(p3_keys(l) + (p1_keys(l + 1) if l + 1 < depth else []))
        for si, (n, T) in enumerate(seqs):
            MIX(l, si, n, T)
            kb.barrier()
            aoff[0] = PERSIST
        for si, (n, T) in enumerate(seqs):
            xsrc = X[n] if l == 0 else X1[n]
            xdst = Yout[n] if l == depth - 1 else X1[n]
            phase3(l, si, n, T, xsrc, xdst)
    kb.emit()
    st.close()
    return nc


def build_mixers(env):
    g_ = env
    kb, sb, nps, npsb, Pool, copy_op = g_["kb"], g_["sb"], g_["nps"], g_["npsb"], g_["Pool"], g_["copy_op"]
    FM, TMv, TMg, YM = g_["FM"], g_["TMv"], g_["TMg"], g_["YM"]
    ident_b, ident_f, ones_b, ones_f, blk64_b, tri_f, eps_c = (g_[k] for k in ("ident_b", "ident_f", "ones_b", "ones_f", "blk64_b", "tri_f", "eps_c"))
    consts = make_consts()
    npsh = g_["npsh"]

    def interleave(gens):
        gens = list(gens)
        while gens:
            nxt = []
            for g in gens:
                try:
                    next(g)
                    nxt.append(g)
                except StopIteration:
                    pass
            gens = nxt

    def dma(eng, out, in_, reads, writes, slow=False):
        if slow:
            kb.op(eng, lambda e: e.dma_start(out=out, in_=in_, allow_slow_non_contiguous=True), reads=reads, writes=writes, dma=True)
        else:
            kb.op(eng, lambda e: e.dma_start(out=out, in_=in_), reads=reads, writes=writes, dma=True)

    def V(fn, reads, writes, eng="vector"):
        kb.op(eng, fn, reads=reads, writes=writes)

    def MM(out, lhsT, rhs, start, stop, reads, writes):
        kb.op("tensor", lambda e: e.matmul(out, lhsT=lhsT, rhs=rhs, start=start, stop=stop), reads=reads, writes=writes)

    def rsqrt_tile(dst, src_ps, scale, n, reads):
        V(lambda e: e.tensor_scalar(out=dst.ap[:, 0:n], in0=src_ps, scalar1=scale, scalar2=EPS, op0=ALU.mult, op1=ALU.add), reads, [dst.b])
        kb.op("scalar", lambda e: e.activation(out=dst.ap[:, 0:n], in_=dst.ap[:, 0:n], func=AF.Sqrt), reads=[dst.b], writes=[dst.b])
        V(lambda e: e.reciprocal(out=dst.ap[:, 0:n], in_=dst.ap[:, 0:n]), [dst.b], [dst.b])

    def headnorm_fm(dst, src, T, wcol, lhs_ones, scale_div, extra_scale):
        sq = Pool("hn_sq", [128, 512], BF16, 2)
        rs = Pool("hn_rs", [128, 512], F32, 2)
        for t0 in range(0, T, 512):
            q2 = sq.get()
            V(lambda e, q2=q2, t0=t0: e.tensor_tensor(out=q2.ap, in0=src.ap[:, t0:t0 + 512], in1=src.ap[:, t0:t0 + 512], op=ALU.mult), [src.b], [q2.b], eng="gpsimd")
            p = nps()
            MM(p.ap, lhs_ones.ap, q2.ap, True, True, [lhs_ones.b, q2.b], [p.b])
            r = rs.get()
            rsqrt_tile(r, p.ap, 1.0 / scale_div, 512, [p.b])
            V(lambda e, r=r, t0=t0: e.scalar_tensor_tensor(out=dst.ap[:, t0:t0 + 512], in0=src.ap[:, t0:t0 + 512], scalar=wcol, in1=r.ap, op0=ALU.mult, op1=ALU.mult),
              [src.b, r.b], [dst.b])

    def conv(l, n, T):
        conv_w, conv_b = g_["conv_w"], g_["conv_b"]
        a_t = sb("cv_a", [128, T], BF16)
        g_t = sb("cv_g", [128, T], BF16)
        up = sb("cv_u", [128, T + 32], BF16)
        dg = sb("cv_dg", [128, 31, 128], BF16)
        cw = sb("cv_w", [128, 31])
        cb = sb("cv_b", [128, 1])
        osb = Pool("cv_o", [128, 512], BF16, 3)
        for c4 in range(4):
            dma("sync", cw.ap, conv_w.ap[l, :, c4 * 128:(c4 + 1) * 128].rearrange("w p -> p w"), [], [cw.b], slow=True)
            dma("sync", cb.ap, conv_b.ap[l:l + 1, c4 * 128:(c4 + 1) * 128].rearrange("o p -> p o"), [], [cb.b], slow=True)
            for w in range(31):
                V(lambda e, w=w: e.tensor_scalar(out=dg.ap[:, w, :], in0=ident_f.ap, scalar1=cw.ap[:, w:w + 1], scalar2=None, op0=ALU.mult), [ident_f.b, cw.b], [dg.b],
                  eng=("vector" if w % 2 else "gpsimd"))
            for t0 in range(0, T, 1024):
                dma("sync", a_t.ap[:, t0:t0 + 1024], FM[n].ap[C_DA + c4 * 128:C_DA + c4 * 128 + 128, t0:t0 + 1024], [FM[n].b], [a_t.b])
                dma("sync", g_t.ap[:, t0:t0 + 1024], FM[n].ap[C_DG + c4 * 128:C_DG + c4 * 128 + 128, t0:t0 + 1024], [FM[n].b], [g_t.b])
            V(lambda e: e.memset(up.ap[:, 0:16], 0.0), [], [up.b])
            V(lambda e: e.memset(up.ap[:, T + 15:T + 32], 0.0), [], [up.b])
            kb.op("scalar", lambda e: e.activation(out=g_t.ap, in_=g_t.ap, func=AF.Sigmoid), reads=[g_t.b], writes=[g_t.b])
            V(lambda e: e.tensor_tensor(out=up.ap[:, 15:15 + T], in0=a_t.ap, in1=g_t.ap, op=ALU.mult), [a_t.b, g_t.b], [up.b])
            for t0 in range(0, T, 512):
                p = nps()
                for w in range(31):
                    MM(p.ap, dg.ap[:, w, :], up.ap[:, t0 + w:t0 + w + 512], w == 0, w == 30, [dg.b, up.b], [p.b])
                o = osb.get()
                kb.op("scalar", lambda e, o=o, p=p: e.activation(out=o.ap, in_=p.ap, func=AF.Identity, bias=cb.ap[:, 0:1]), reads=[p.b, cb.b], writes=[o.b])
                dma("sync", YM[n].ap[1536 + c4 * 128:1536 + c4 * 128 + 128, t0:t0 + 512], o.ap, [o.b], [YM[n].b])

    gq_state = {}

    def gqa_tables():
        rel_bias, k_gqa_m = g_["rel_bias"], g_["k_gqa_m"]
        EB = g_["EB"]
        rbb = sb("gq_rbb", [128, 256])
        val = sb("gq_val", [128, 3, 128])
        mk = Pool("gq_mk", [128, 128], F32, 3)
        dma("sync", rbb.ap, rel_bias.ap.rearrange("b h -> (b h)").unsqueeze(0).broadcast_to([128, 256]), [], [rbb.b])
        V(lambda e: e.memset(EB.ap, 0.0), [], [EB.b])
        V(lambda e: e.memset(val.ap, 0.0), [], [val.b])
        gm = consts["gqa_m"]
        for o in range(3):
            for b in range(32):
                if not gm[o, b].any():
                    continue
                m = mk.get()
                dma("sync", m.ap, k_gqa_m.ap[o, b], [], [m.b])
                V(lambda e, m=m, o=o: e.tensor_tensor(out=val.ap[:, o, :], in0=val.ap[:, o, :], in1=m.ap, op=ALU.add), [m.b, val.b], [val.b], eng="gpsimd")
                for h in range(8):
                    V(lambda e, m=m, o=o, b=b, h=h: e.scalar_tensor_tensor(out=EB.ap[:, o, h, :], in0=m.ap, scalar=rbb.ap[:, b * 8 + h:b * 8 + h + 1], in1=EB.ap[:, o, h, :],
                                                                         op0=ALU.mult, op1=ALU.add), [m.b, rbb.b, EB.b], [EB.b])
        kb.op("scalar", lambda e: e.activation(out=EB.ap, in_=EB.ap, func=AF.Exp), reads=[EB.b], writes=[EB.b])
        for o in range(3):
            for h in range(8):
                V(lambda e, o=o, h=h: e.tensor_tensor(out=EB.ap[:, o, h, :], in0=EB.ap[:, o, h, :], in1=val.ap[:, o, :], op=ALU.mult), [EB.b, val.b], [EB.b])

    def gqa(l, n, T):
        swa_q_norm, swa_k_norm, swa_sink = g_["swa_q_norm"], g_["swa_k_norm"], g_["swa_sink"]
        EB = g_["EB"]
        nb = T // 128
        kraw = sb("gq_kraw", [128, T], BF16)
        kn = sb("gq_kn", [128, T], BF16)
        qraw = sb("gq_qraw", [128, T], BF16)
        qn = sb("gq_qn", [128, T], BF16)
        vp = [sb(f"gq_vp{i}", [128, nb, 128], BF16) for i in range(2)]
        vraw = sb("gq_vraw", [128, nb, 64], BF16)
        on = [sb(f"gq_on{i}", [128, 128], BF16) for i in range(2)]
        wq = sb("gq_wq", [128, 1])
        wk = sb("gq_wk", [128, 1])
        sk = sb("gq_sk", [128, 4])
        Pf = Pool("gq_pf", [128, 2, 384], F32, 4)
        Pb = Pool("gq_pb", [128, 2, 384], BF16, 4)
        rc = Pool("gq_rc", [128, 128], F32, 4)
        ost = Pool("gq_ost", [128, 1024], BF16, 2)
        for i in range(2):
            V(lambda e, i=i: e.memset(on[i].ap, 0.0), [], [on[i].b])
            V(lambda e, i=i: e.memset(on[i].ap[:, 64 * i:64 * i + 64], 1.0), [], [on[i].b])
            V(lambda e, i=i: e.memset(vp[i].ap, 0.0), [], [vp[i].b], eng="gpsimd")
        for half in range(2):
            dma("sync", wq.ap[64 * half:64 * half + 64, :], swa_q_norm.ap[l:l + 1, :].rearrange("o p -> p o"), [], [wq.b], slow=True)
            dma("sync", wk.ap[64 * half:64 * half + 64, :], swa_k_norm.ap[l:l + 1, :].rearrange("o p -> p o"), [], [wk.b], slow=True)
        V(lambda e: e.tensor_scalar(out=wq.ap, in0=wq.ap, scalar1=0.125, scalar2=None, op0=ALU.mult), [wq.b], [wq.b])
        for kvh in range(2):
            for half in range(2):
                for t0 in range(0, T, 2048):
                    tw = min(2048, T - t0)
                    dma("sync", kraw.ap[64 * half:64 * half + 64, t0:t0 + tw], FM[n].ap[C_CK + 64 * kvh:C_CK + 64 * kvh + 64, t0:t0 + tw], [FM[n].b], [kraw.b])
            headnorm_fm(kn, kraw, T, wk.ap[:, 0:1], blk64_b, 64.0, 1.0)
            dma("sync", vraw.ap, TMv[n].ap[:, 1536 + 64 * kvh:1536 + 64 * kvh + 64].rearrange("(b p) c -> p b c", p=128), [TMv[n].b], [vraw.b])
            for i in range(2):
                V(lambda e, i=i: e.tensor_copy(out=vp[i].ap[:, :, 64 * i:64 * i + 64], in_=vraw.ap), [vraw.b], [vp[i].b])
            for hp in range(2):
                h0 = kvh * 4 + hp * 2
                for half in range(2):
                    dma("sync", sk.ap[64 * half:64 * half + 64, 0:1], swa_sink.ap[l:l + 1, h0 + half:h0 + half + 1].broadcast_to([64, 1]), [], [sk.b])
                kb.op("scalar", lambda e: e.activation(out=sk.ap[:, 1:2], in_=sk.ap[:, 0:1], func=AF.Exp), reads=[sk.b], writes=[sk.b])
                for t0 in range(0, T, 2048):
                    tw = min(2048, T - t0)
                    dma("sync", qraw.ap[:, t0:t0 + tw], FM[n].ap[C_CQ + 64 * h0:C_CQ + 64 * h0 + 128, t0:t0 + tw], [FM[n].b], [qraw.b])
                headnorm_fm(qn, qraw, T, wq.ap[:, 0:1], blk64_b, 64.0, 1.0)
                ostm = {}

                def qb_gen(qb, h0=h0, kvh=kvh, hp=hp, ostm=ostm):
                    if qb // 8 not in ostm:
                        ostm[qb // 8] = ost.get()
                    o_t = ostm[qb // 8]
                    os_ = [o for o in range(3) if 0 <= qb + o - 1 < nb]
                    o0, o1 = os_[0], os_[-1] + 1
                    pS = [nps(), nps()]
                    for hh in range(2):
                        for o in os_:
                            kbk = qb + o - 1
                            MM(pS[hh].ap[:, o * 128:(o + 1) * 128], kn.ap[64 * hh:64 * hh + 64, kbk * 128:(kbk + 1) * 128], qn.ap[64 * hh:64 * hh + 64, qb * 128:(qb + 1) * 128],
                               True, True, [kn.b, qn.b], [pS[hh].b])
                    yield
                    pf = Pf.get()
                    pb = Pb.get()
                    for hh in range(2):
                        kb.op("scalar", lambda e, hh=hh, pf=pf, pS=pS, o0=o0, o1=o1: e.activation(out=pf.ap[:, hh, o0 * 128:o1 * 128], in_=pS[hh].ap[:, o0 * 128:o1 * 128], func=AF.Exp),
                              reads=[pS[hh].b], writes=[pf.b])
                    yield
                    for hh in range(2):
                        V(lambda e, hh=hh, pf=pf, pb=pb, o0=o0, o1=o1, h0=h0: e.tensor_tensor(out=pb.ap[:, hh, o0 * 128:o1 * 128].rearrange("p (o q) -> p o q", q=128),
                                                                                              in0=pf.ap[:, hh, o0 * 128:o1 * 128].rearrange("p (o q) -> p o q", q=128),
                                                                                              in1=EB.ap[:, o0:o1, h0 + hh, :], op=ALU.mult),
                          [pf.b, EB.b], [pb.b], eng=("vector" if hh else "gpsimd"))
                    yield
                    pOD = nps()
                    tot = 2 * len(os_)
                    cnt = 0
                    for hh in range(2):
                        for o in os_:
                            kbk = qb + o - 1
                            MM(pOD.ap[:, 0:128], vp[hh].ap[:, kbk, :], pb.ap[:, hh, o * 128:(o + 1) * 128], cnt == 0, cnt == tot - 1, [vp[hh].b, pb.b], [pOD.b])
                            cnt += 1
                    cnt = 0
                    for hh in range(2):
                        for o in os_:
                            MM(pOD.ap[:, 128:256], on[hh].ap, pb.ap[:, hh, o * 128:(o + 1) * 128], cnt == 0, cnt == tot - 1, [on[hh].b, pb.b], [pOD.b])
                            cnt += 1
                    yield
                    r = rc.get()
                    V(lambda e, r=r, pOD=pOD: e.tensor_scalar(out=r.ap, in0=pOD.ap[:, 128:256], scalar1=sk.ap[:, 1:2], scalar2=None, op0=ALU.add), [pOD.b, sk.b], [r.b])
                    V(lambda e, r=r: e.reciprocal(out=r.ap, in_=r.ap), [r.b], [r.b])
                    yield
                    V(lambda e, r=r, pOD=pOD, o_t=o_t, qb=qb: e.tensor_tensor(out=o_t.ap[:, (qb % 8) * 128:(qb % 8) * 128 + 128], in0=pOD.ap[:, 0:128], in1=r.ap, op=ALU.mult),
                      [pOD.b, r.b], [o_t.b])
                    if qb % 8 == 7:
                        row = 1024 + 128 * (kvh * 2 + hp)
                        dma("sync", YM[n].ap[row:row + 128, (qb - 7) * 128:(qb + 1) * 128], o_t.ap, [o_t.b], [YM[n].b])

                for q0 in range(0, nb, GQ_W):
                    interleave([qb_gen(q) for q in range(q0, min(q0 + GQ_W, nb))])

    def na(l, n, T):
        na_q_norm, na_k_norm, na_rpb, k_na_m = g_["na_q_norm"], g_["na_k_norm"], g_["na_rpb"], g_["k_na_m"]
        rows = T // 64
        nb = T // 128
        Mc = sb("na_mc", [128, 31, 64])
        RP = sb("na_rp", [128, 14, 31])
        ET = sb("na_et", [128, 14, 64])
        okm = sb("na_ok", [128, 64])
        tmpE = sb("na_te", [128, 14, 64])
        qraw = sb("na_qraw", [128, T], BF16)
        kraw = sb("na_kraw", [128, T], BF16)
        qn = sb("na_qn", [128, T], BF16)
        kn = sb("na_kn", [128, T], BF16)
        vA = sb("na_vA", [128, nb, 128], BF16)
        vB = sb("na_vB", [128, nb, 128], BF16)
        wq = sb("na_wq", [128, 1])
        wk = sb("na_wk", [128, 1])
        Pf = Pool("na_pf", [128, 4, 64], F32, 6)
        Pb = Pool("na_pb", [128, 4, 64], BF16, 6)
        rc = Pool("na_rc", [128, 64], F32, 6)
        ost = Pool("na_ost", [128, 1024], BF16, 2)
        dma("sync", Mc.ap, k_na_m.ap.rearrange("d p q -> p d q"), [], [Mc.b])
        dma("sync", wq.ap, na_q_norm.ap[l:l + 1, :].rearrange("o p -> p o"), [], [wq.b], slow=True)
        dma("sync", wk.ap, na_k_norm.ap[l:l + 1, :].rearrange("o p -> p o"), [], [wk.b], slow=True)
        V(lambda e: e.tensor_scalar(out=wq.ap, in0=wq.ap, scalar1=float(128 ** -0.5), scalar2=None, op0=ALU.mult), [wq.b], [wq.b])
        V(lambda e: e.memset(okm.ap, 0.0), [], [okm.b])
        for dc in range(31):
            V(lambda e, dc=dc: e.tensor_tensor(out=okm.ap, in0=okm.ap, in1=Mc.ap[:, dc, :], op=ALU.add), [Mc.b, okm.b], [okm.b])
        for h in range(4):
            for half in range(2):
                dma("sync", RP.ap[64 * half:64 * half + 64, :, :], na_rpb.ap[l, h:h + 1, half:half + 14, :].broadcast_to([64, 14, 31]), [], [RP.b])
            V(lambda e: e.memset(ET.ap, 0.0), [], [ET.b])
            for dc in range(31):
                V(lambda e, dc=dc: e.tensor_tensor(out=tmpE.ap, in0=Mc.ap[:, dc, :].unsqueeze(1).broadcast_to([128, 14, 64]),
                                                    in1=RP.ap[:, :, dc].unsqueeze(2).broadcast_to([128, 14, 64]), op=ALU.mult), [Mc.b, RP.b], [tmpE.b])
                V(lambda e: e.tensor_tensor(out=ET.ap, in0=ET.ap, in1=tmpE.ap, op=ALU.add), [tmpE.b, ET.b], [ET.b], eng="gpsimd")
            kb.op("scalar", lambda e: e.activation(out=ET.ap, in_=ET.ap, func=AF.Exp), reads=[ET.b], writes=[ET.b])
            V(lambda e: e.tensor_tensor(out=ET.ap, in0=ET.ap, in1=okm.ap.unsqueeze(1).broadcast_to([128, 14, 64]), op=ALU.mult), [ET.b, okm.b], [ET.b])
            for t0 in range(0, T, 2048):
                tw = min(2048, T - t0)
                dma("sync", qraw.ap[:, t0:t0 + tw], FM[n].ap[C_BQ + 128 * h:C_BQ + 128 * h + 128, t0:t0 + tw], [FM[n].b], [qraw.b])
                dma("sync", kraw.ap[:, t0:t0 + tw], FM[n].ap[C_BK + 128 * h:C_BK + 128 * h + 128, t0:t0 + tw], [FM[n].b], [kraw.b])
            headnorm_fm(qn, qraw, T, wq.ap[:, 0:1], ones_b, 128.0, 1.0)
            headnorm_fm(kn, kraw, T, wk.ap[:, 0:1], ones_b, 128.0, 1.0)
            dma("sync", vA.ap, TMv[n].ap[:, 1024 + 128 * h:1024 + 128 * h + 128].rearrange("(b p) c -> p b c", p=128), [TMv[n].b], [vA.b])
            dma("sync", vB.ap[:, 0:nb - 1, :], TMv[n].ap[64:T - 64, 1024 + 128 * h:1024 + 128 * h + 128].rearrange("(b p) c -> p b c", p=128), [TMv[n].b], [vB.b])
            ostm = {}

            def row_gen(r):
                if r // 16 not in ostm:
                    ostm[r // 16] = ost.get()
                o_t = ostm[r // 16]
                start = min(max(r - 4, 0), rows - 8)
                off = start - r + 7
                pS = nps()
                for j2 in range(4):
                    k0 = 64 * (start + 2 * j2)
                    MM(pS.ap[:, j2 * 64:(j2 + 1) * 64], kn.ap[:, k0:k0 + 128], qn.ap[:, 64 * r:64 * r + 64], True, True, [kn.b, qn.b], [pS.b])
                yield
                pf = Pf.get()
                pb = Pb.get()
                kb.op("scalar", lambda e, pf=pf, pS=pS: e.activation(out=pf.ap, in_=pS.ap[:, 0:256].rearrange("p (j q) -> p j q", q=64), func=AF.Exp), reads=[pS.b], writes=[pf.b])
                yield
                V(lambda e, pf=pf, pb=pb, off=off: e.tensor_tensor(out=pb.ap, in0=pf.ap, in1=ET.ap[:, off:off + 7:2, :], op=ALU.mult), [pf.b, ET.b], [pb.b],
                  eng=("vector" if r % 2 else "gpsimd"))
                yield
                pOD = nps()
                for j2 in range(4):
                    rr = start + 2 * j2
                    vt = vA.ap[:, rr // 2, :] if rr % 2 == 0 else vB.ap[:, (rr - 1) // 2, :]
                    vb_ = vA.b if rr % 2 == 0 else vB.b
                    MM(pOD.ap[:, 0:64], vt, pb.ap[:, j2, :], j2 == 0, j2 == 3, [vb_, pb.b], [pOD.b])
                for j2 in range(4):
                    MM(pOD.ap[:, 64:128], ones_b.ap, pb.ap[:, j2, :], j2 == 0, j2 == 3, [ones_b.b, pb.b], [pOD.b])
                yield
                rcp = rc.get()
                V(lambda e, rcp=rcp, pOD=pOD: e.reciprocal(out=rcp.ap, in_=pOD.ap[:, 64:128]), [pOD.b], [rcp.b])
                yield
                V(lambda e, rcp=rcp, pOD=pOD, o_t=o_t, r=r: e.tensor_tensor(out=o_t.ap[:, (r % 16) * 64:(r % 16) * 64 + 64], in0=pOD.ap[:, 0:64], in1=rcp.ap, op=ALU.mult),
                  [pOD.b, rcp.b], [o_t.b])
                if r % 16 == 15:
                    dma("sync", YM[n].ap[512 + 128 * h:512 + 128 * h + 128, (r - 15) * 64:(r + 1) * 64], o_t.ap, [o_t.b], [YM[n].b])

            for r0 in range(0, rows, NA_W):
                interleave([row_gen(r) for r in range(r0, min(r0 + NA_W, rows))])

    def mlstm(l, n, T):
        mlstm_norm_w = g_["mlstm_norm_w"]
        nb = T // 128
        qT = sb("ml_qT", [128, T], BF16)
        kT = sb("ml_kT", [128, T], BF16)
        ktm = sb("ml_ktm", [128, nb, 128], BF16)
        vaug = sb("ml_va", [128, nb, 132], BF16)
        gt = sb("ml_gt", [128, nb, 16])
        hfw = sb("ml_hfw", [128, nb, 128])
        nwb = sb("ml_nw", [128, 128])
        lf = sb("ml_lf", [128, 2, nb])
        bcs = sb("ml_b", [128, 2, nb])
        gb = sb("ml_g", [128, 2, nb])
        beta = sb("ml_beta", [128, 2, nb])
        gam = sb("ml_gam", [128, 2, nb])
        emb = sb("ml_emb", [128, 2, nb])
        eg = sb("ml_eg", [128, 2, nb])
        hbw = sb("ml_hbw", [128, nb, 128])
        Cs = [sb(f"ml_C{i}", [128, 132]) for i in range(2)]
        Cbs = [sb(f"ml_Cb{i}", [128, 132], BF16) for i in range(2)]
        hs8 = Pool("ml_hs8", [128, 8, 128], F32, 2)
        sq8 = Pool("ml_sq8", [128, 8, 128], F32, 2)
        hb8 = Pool("ml_hb8", [128, 8, 128], BF16, 2)
        sm8 = Pool("ml_sm8", [128, 16], F32, 2)
        STp = Pool("ml_st", [128, 128], BF16, 4)
        kgp = Pool("ml_kg", [128, 128], BF16, 4)
        sm = Pool("ml_sm", [128, 4], F32, 6)
        ost = Pool("ml_ost", [128, 1024], BF16, 2)
        V(lambda e: e.memset(vaug.ap[:, :, 128:129], 1.0), [], [vaug.b])
        for h in range(4):
            for t0 in range(0, T, 2048):
                tw = min(2048, T - t0)
                dma("sync", qT.ap[:, t0:t0 + tw], FM[n].ap[C_AQ + 128 * h:C_AQ + 128 * h + 128, t0:t0 + tw], [FM[n].b], [qT.b])
                dma("sync", kT.ap[:, t0:t0 + tw], FM[n].ap[C_AK + 128 * h:C_AK + 128 * h + 128, t0:t0 + tw], [FM[n].b], [kT.b])
            dma("sync", ktm.ap, TMv[n].ap[:, 128 * h:128 * h + 128].rearrange("(b p) c -> p b c", p=128), [TMv[n].b], [ktm.b])
            dma("sync", vaug.ap[:, :, 0:128], TMv[n].ap[:, 512 + 128 * h:512 + 128 * h + 128].rearrange("(b p) c -> p b c", p=128), [TMv[n].b], [vaug.b])
            dma("sync", gt.ap, TMg[n].ap.rearrange("(b p) c -> p b c", p=128), [TMg[n].b], [gt.b])
            dma("sync", nwb.ap, mlstm_norm_w.ap[l:l + 1, 128 * h:128 * h + 128].broadcast_to([128, 128]), [], [nwb.b])
            for d, dn in enumerate(("fw", "bw")):
                icol, fcol = 4 * d + h, 8 + 4 * d + h
                kb.op("scalar", lambda e, d=d, fcol=fcol: e.activation(out=lf.ap[:, d, :], in_=gt.ap[:, :, fcol], func=AF.Exp, scale=-1.0), reads=[gt.b], writes=[lf.b])
                V(lambda e, d=d: e.tensor_scalar(out=lf.ap[:, d, :], in0=lf.ap[:, d, :], scalar1=1.0, scalar2=None, op0=ALU.add), [lf.b], [lf.b])
                kb.op("scalar", lambda e, d=d: e.activation(out=lf.ap[:, d, :], in_=lf.ap[:, d, :], func=AF.Ln), reads=[lf.b], writes=[lf.b])
                V(lambda e, d=d: e.tensor_scalar(out=lf.ap[:, d, :], in0=lf.ap[:, d, :], scalar1=-1.0, scalar2=None, op0=ALU.mult), [lf.b], [lf.b])
                p = nps()
                MM(p.ap[:, 0:nb], tri_f[dn].ap, lf.ap[:, d, :], True, True, [tri_f[dn].b, lf.b], [p.b])
                V(lambda e, d=d, p=p: e.tensor_copy(out=bcs.ap[:, d, :], in_=p.ap[:, 0:nb]), [p.b], [bcs.b])
                p2 = nps()
                MM(p2.ap[:, 0:nb], ones_f.ap, lf.ap[:, d, :], True, True, [ones_f.b, lf.b], [p2.b])
                V(lambda e, d=d, p2=p2: e.tensor_copy(out=gb.ap[:, d, :], in_=p2.ap[:, 0:nb]), [p2.b], [gb.b])
                kb.op("scalar", lambda e, d=d: e.activation(out=eg.ap[:, d, :], in_=gb.ap[:, d, :], func=AF.Exp), reads=[gb.b], writes=[eg.b])
                kb.op("scalar", lambda e, d=d: e.activation(out=emb.ap[:, d, :], in_=bcs.ap[:, d, :], func=AF.Exp, scale=-1.0), reads=[bcs.b], writes=[emb.b])
                V(lambda e, d=d, icol=icol: e.tensor_tensor(out=beta.ap[:, d, :], in0=gt.ap[:, :, icol], in1=bcs.ap[:, d, :], op=ALU.subtract), [gt.b, bcs.b], [beta.b])
                kb.op("scalar", lambda e, d=d: e.activation(out=beta.ap[:, d, :], in_=beta.ap[:, d, :], func=AF.Exp), reads=[beta.b], writes=[beta.b])
                V(lambda e, d=d: e.tensor_scalar(out=beta.ap[:, d, :], in0=beta.ap[:, d, :], scalar1=float(128 ** -0.5), scalar2=None, op0=ALU.mult), [beta.b], [beta.b])
                V(lambda e, d=d: e.tensor_tensor(out=gam.ap[:, d, :], in0=beta.ap[:, d, :], in1=eg.ap[:, d, :], op=ALU.mult), [beta.b, eg.b], [gam.b])
            def dir_gen(d, dn):
                order = list(range(nb)) if d == 0 else list(range(nb - 1, -1, -1))
                hdir = hfw if d == 0 else hbw
                Cst, Cb = Cs[d], Cbs[d]
                for ci, c in enumerate(order):
                    csl = slice(c * 128, (c + 1) * 128)
                    last = (ci == nb - 1)
                    if not last:
                        kg = kgp.get()
                        V(lambda e, kg=kg, d=d, c=c: e.tensor_scalar(out=kg.ap, in0=ktm.ap[:, c, :], scalar1=gam.ap[:, d, c:c + 1], scalar2=None, op0=ALU.mult),
                          [ktm.b, gam.b], [kg.b], eng="gpsimd")
                    pR = nps()
                    MM(pR.ap[:, 0:128], kT.ap[:, csl], qT.ap[:, csl], True, True, [kT.b, qT.b], [pR.b])
                    yield
                    stt = STp.get()
                    V(lambda e, stt=stt, pR=pR, d=d, c=c, dn=dn: e.scalar_tensor_tensor(out=stt.ap, in0=pR.ap[:, 0:128], scalar=beta.ap[:, d, c:c + 1], in1=tri_f[dn].ap,
                                                                                          op0=ALU.mult, op1=ALU.mult), [pR.b, beta.b, tri_f[dn].b], [stt.b])
                    if not last:
                        pC = nps()
                        MM(pC.ap[:, 0:129], kg.ap, vaug.ap[:, c, 0:129], True, True, [kg.b, vaug.b], [pC.b])
                    yield
                    pX = nps()
                    MM(pX.ap[:, 0:129], stt.ap, vaug.ap[:, c, 0:129], True, ci == 0, [stt.b, vaug.b], [pX.b])
                    if ci > 0:
                        MM(pX.ap[:, 0:129], qT.ap[:, csl], Cb.ap[:, 0:129], False, True, [qT.b, Cb.b], [pX.b])
                    if not last:
                        if ci == 0:
                            V(lambda e, pC=pC, Cst=Cst: e.tensor_copy(out=Cst.ap[:, 0:129], in_=pC.ap[:, 0:129]), [pC.b], [Cst.b])
                        else:
                            V(lambda e, pC=pC, d=d, c=c, Cst=Cst: e.scalar_tensor_tensor(out=Cst.ap[:, 0:129], in0=Cst.ap[:, 0:129], scalar=eg.ap[:, d, c:c + 1], in1=pC.ap[:, 0:129],
                                                                                          op0=ALU.mult, op1=ALU.add), [Cst.b, eg.b, pC.b], [Cst.b])
                    yield
                    s4 = sm.get()
                    kb.op("scalar", lambda e, s4=s4, pX=pX: e.activation(out=s4.ap[:, 0:1], in_=pX.ap[:, 128:129], func=AF.Abs), reads=[pX.b], writes=[s4.b])
                    if not last:
                        copy_op("scalar", Cb.ap[:, 0:129], Cst.ap[:, 0:129], [Cst.b], [Cb.b])
                    yield
                    V(lambda e, s4=s4, d=d, c=c: e.tensor_tensor(out=s4.ap[:, 0:1], in0=s4.ap[:, 0:1], in1=emb.ap[:, d, c:c + 1], op=ALU.max), [s4.b, emb.b], [s4.b])
                    V(lambda e, s4=s4: e.reciprocal(out=s4.ap[:, 1:2], in_=s4.ap[:, 0:1]), [s4.b], [s4.b])
                    yield
                    V(lambda e, s4=s4, pX=pX, c=c, hdir=hdir: e.tensor_scalar(out=hdir.ap[:, c, :], in0=pX.ap[:, 0:128], scalar1=s4.ap[:, 1:2], scalar2=None, op0=ALU.mult),
                      [pX.b, s4.b], [hdir.b])
                    yield

            interleave([dir_gen(0, "fw"), dir_gen(1, "bw")])
            for c0 in range(0, nb, 8):
                hsum = hs8.get()
                V(lambda e, hsum=hsum, c0=c0: e.tensor_tensor(out=hsum.ap, in0=hfw.ap[:, c0:c0 + 8, :], in1=hbw.ap[:, c0:c0 + 8, :], op=ALU.add), [hfw.b, hbw.b], [hsum.b])
                sq = sq8.get()
                V(lambda e, hsum=hsum, sq=sq: e.tensor_tensor(out=sq.ap, in0=hsum.ap, in1=hsum.ap, op=ALU.mult), [hsum.b], [sq.b], eng="gpsimd")
                s8 = sm8.get()
                V(lambda e, sq=sq, s8=s8: e.reduce_sum(out=s8.ap[:, 0:8], in_=sq.ap, axis=mybir.AxisListType.X), [sq.b], [s8.b])
                V(lambda e, s8=s8: e.tensor_scalar(out=s8.ap[:, 8:16], in0=s8.ap[:, 0:8], scalar1=1.0 / 128, scalar2=EPS, op0=ALU.mult, op1=ALU.add), [s8.b], [s8.b])
                kb.op("scalar", lambda e, s8=s8: e.activation(out=s8.ap[:, 8:16], in_=s8.ap[:, 8:16], func=AF.Sqrt), reads=[s8.b], writes=[s8.b])
                V(lambda e, s8=s8: e.reciprocal(out=s8.ap[:, 8:16], in_=s8.ap[:, 8:16]), [s8.b], [s8.b])
                V(lambda e, hsum=hsum, s8=s8: e.tensor_tensor(out=hsum.ap, in0=hsum.ap, in1=s8.ap[:, 8:16].unsqueeze(2).broadcast_to([128, 8, 128]), op=ALU.mult), [hsum.b, s8.b], [hsum.b])
                hbt = hb8.get()
                V(lambda e, hsum=hsum, hbt=hbt: e.tensor_tensor(out=hbt.ap, in0=hsum.ap, in1=nwb.ap.unsqueeze(1).broadcast_to([128, 8, 128]), op=ALU.mult), [hsum.b, nwb.b], [hbt.b], eng="gpsimd")
                pT = npsb()
                for j in range(8):
                    kb.op("tensor", lambda e, pT=pT, hbt=hbt, j=j: e.transpose(pT.ap[:, j * 128:(j + 1) * 128], hbt.ap[:, j, :], ident_b.ap), reads=[hbt.b, ident_b.b], writes=[pT.b])
                o_t = ost.get()
                copy_op("scalar", o_t.ap, pT.ap, [pT.b], [o_t.b])
                dma("sync", YM[n].ap[128 * h:128 * h + 128, c0 * 128:(c0 + 8) * 128], o_t.ap, [o_t.b], [YM[n].b])

    PERSIST = g_["PERSIST"]
    aoff = g_["aoff"]

    def MIX(l, si, n, T):
        g_["psmode"][0] = 6
        for fn in (conv, gqa, na, mlstm):
            if fn.__name__ in SKIP_MIX:
                continue
            fn(l, n, T)
            kb.barrier()
            aoff[0] = PERSIST
        g_["psmode"][0] = 6
    MIX.gqa_tables = gqa_tables
    return MIX


SKIP_MIX = set()
DEBUG_YM = False


_W_KEYS = ["rel_bias", "norm_w", "w_ada", "b_ada", "w_in", "b_gate", "mlstm_norm_w", "na_q_norm", "na_k_norm", "na_rpb",
           "swa_q_norm", "swa_k_norm", "swa_sink", "conv_w", "conv_b", "conv_ln_w", "conv_ln_b", "w_branch", "w_out"]


def kernel(**inputs):
    xp = np.ascontiguousarray(np.asarray(inputs["x_prompt"], dtype=np.float32))
    xs = np.ascontiguousarray(np.asarray(inputs["x_sample"], dtype=np.float32))
    cp = np.asarray(inputs["c_prompt"], dtype=np.float32)
    cs = np.asarray(inputs["c_sample"], dtype=np.float32)
    TP, TS = xp.shape[1], xs.shape[1]
    depth = np.asarray(inputs["norm_w"]).shape[0]
    nc = build([("P", TP), ("S", TS)], depth)
    consts = make_consts()
    shared = {k: np.ascontiguousarray(np.asarray(inputs[k], dtype=np.float32)) for k in _W_KEYS}
    shared.update({"k_ident": consts["ident"], "k_tri_fw": consts["tri_fw"], "k_tri_bw": consts["tri_bw"], "k_blk64": consts["blk64"],
                   "k_gqa_m": consts["gqa_m"], "k_na_m": consts["na_m"]})
    in_maps = []
    for i in range(8):
        m = dict(shared)
        m["x_P"] = xp[i // 4]
        m["c_P"] = cp[i // 4][None]
        m["x_S"] = xs[i // 2]
        m["c_S"] = cs[i // 2][None]
        in_maps.append(m)
    res = run_bass_kernel_spmd(nc, in_maps, core_ids=list(range(8)))
    yp = np.empty_like(xp)
    ys = np.empty_like(xs)
    qp, qs = TP // 4, TS // 2
    for i in range(8):
        r = res.results[i]
        a = i % 4
        yp[i // 4, a * qp:(a + 1) * qp] = np.asarray(r["y_P"])[a * qp:(a + 1) * qp]
        b = i % 2
        ys[i // 2, b * qs:(b + 1) * qs] = np.asarray(r["y_S"])[b * qs:(b + 1) * qs]
    return (yp, ys)
NA_W = 3
GQ_W = 2
```

```python
import contextlib
import math
import numpy as np
import concourse.bass as bass
import concourse.mybir as mybir
from concourse.bass_utils import run_bass_kernel_spmd

F32 = mybir.dt.float32
BF16 = mybir.dt.bfloat16
ALU = mybir.AluOpType
AF = mybir.ActivationFunctionType

D = 2048
KC = 16
IN_COLS = 15632
EPS = 1e-6
C_AQ, C_AK, C_AV, C_AO, C_AZ, C_AG = 0, 512, 1024, 1536, 2048, 2560
C_BQ, C_BK, C_BV, C_BZ = 2576, 3088, 3600, 4112
C_CQ, C_CK, C_CV, C_CZ = 4624, 5136, 5264, 5392
C_DA, C_DG, C_DZ, C_MG = 5904, 6416, 6928, 7440
NFM = 7440

ENGS = ("tensor", "vector", "scalar", "gpsimd", "sync")
NDMA = 28
SEM_EPOCH = 30000


class Buf:
    __slots__ = ("w", "r")

    def __init__(self):
        self.w = None
        self.r = []


class KB:
    def __init__(self, nc):
        self.nc = nc
        self.ops = {e: [] for e in ENGS}
        self.cnt = {}
        self.known = {e: {} for e in ENGS}
        self.dma_rr = 0
        self.dma_last = {}
        self.pending = {e: [] for e in ENGS}
        self.tot = {}

    def barrier(self):
        for e in ENGS:
            for k, v in list(self.cnt.items()) + list(self.dma_last.items()):
                self._need(e, (k, v), self.pending[e])

    def _need(self, eng, dep, waits):
        if dep is None:
            return
        k, v = dep
        if eng == "tensor" and k.startswith("tensor#"):
            return
        if self.known[eng].get(k, 0) >= v:
            return
        self.known[eng][k] = v
        waits.append((k, v))

    def op(self, eng, fn, reads=(), writes=(), dma=False):
        waits = self.pending[eng]
        self.pending[eng] = []
        for b in reads:
            self._need(eng, b.w, waits)
        for b in writes:
            self._need(eng, b.w, waits)
            for d in b.r:
                self._need(eng, d, waits)
        if dma:
            k = f"dma{self.dma_rr % NDMA}"
            self.dma_rr += 1
            last = self.dma_last.get(k, 0)
            if last:
                self._need(eng, (k, last), waits)
            v = last + 16
            self.dma_last[k] = v
            inc = 16
        else:
            tot = self.tot.get(eng, 0) + 1
            self.tot[eng] = tot
            k = f"{eng}#{(tot - 1) // SEM_EPOCH}"
            v = (tot - 1) % SEM_EPOCH + 1
            self.cnt[k] = v
            inc = 1
        tag = (k, v)
        for b in reads:
            b.r.append(tag)
            if len(b.r) > 64:
                b.r = b.r[-64:]
        for b in writes:
            b.w = tag
            b.r = []
        self.ops[eng].append((waits, fn, k, inc))

    def emit(self):
        nc = self.nc
        keys = set()
        for e in ENGS:
            for waits, fn, k, inc in self.ops[e]:
                keys.add(k)
        with contextlib.ExitStack() as st:
            sems = {k: st.enter_context(nc.semaphore(f"s_{k}")) for k in sorted(keys)}
            block = st.enter_context(nc.Block())

            def mk(e):
                def body(eng):
                    if e == "sync" and getattr(self, "pre_sync", None) is not None:
                        self.pre_sync(eng)
                    for waits, fn, k, inc in self.ops[e]:
                        for (wk, wv) in waits:
                            eng.wait_ge(sems[wk], wv)
                        fn(eng).then_inc(sems[k], inc)
                    if e == "sync":
                        for k2 in sorted(keys):
                            tot = self.dma_last.get(k2) if k2.startswith("dma") else self.cnt.get(k2)
                            if tot:
                                eng.wait_ge(sems[k2], tot)
                return body

            for e in ENGS:
                if self.ops[e] or e == "sync":
                    getattr(block, e)(mk(e))


class TL:
    def __init__(self, ap):
        self.ap = ap
        self.b = Buf()

    def __getitem__(self, k):
        return self.ap[k]


def t5_bucket_np(rel):
    half, max_exact = 16, 8
    n = np.abs(rel)
    nf = np.maximum(n, 1).astype(np.float32)
    large = max_exact + (np.log(nf / np.float32(max_exact)) / np.float32(math.log(128 / max_exact)) * (half - max_exact)).astype(np.int32)
    large = np.minimum(large, half - 1)
    return np.where(rel > 0, half, 0) + np.where(n < max_exact, n, large)


def make_consts():
    c = {}
    c["ident"] = np.eye(128, dtype=np.float32)
    s = np.arange(128)
    c["tri_fw"] = (s[:, None] <= s[None, :]).astype(np.float32)
    c["tri_bw"] = (s[:, None] >= s[None, :]).astype(np.float32)
    bo = np.zeros((128, 128), np.float32)
    bo[:64, :64] = 1
    bo[64:, 64:] = 1
    c["blk64"] = bo
    k = np.arange(128)[:, None]
    q = np.arange(128)[None, :]
    gm = np.zeros((3, 32, 128, 128), np.float32)
    for o in range(3):
        rel = k + 128 * (o - 1) - q
        bk = t5_bucket_np(rel)
        ok = np.abs(rel) <= 128
        for b in range(32):
            gm[o, b] = ((bk == b) & ok)
    c["gqa_m"] = gm
    kc = np.arange(64)[:, None]
    qc = np.arange(64)[None, :]
    qs = np.clip(qc - 8, 0, 48)
    ok = (kc >= qs) & (kc < qs + 16)
    dc = np.clip(kc - qc + 15, 0, 30)
    nm = np.zeros((31, 128, 64), np.float32)
    for d in range(31):
        m = ((dc == d) & ok).astype(np.float32)
        nm[d, :64] = m
        nm[d, 64:] = m
    c["na_m"] = nm
    return c


def build(seqs, depth, dbg=()):
    nc = bass.Bass("TRN2", target_bir_lowering=False)
    kb = KB(nc)
    st = contextlib.ExitStack()

    def din(name, shape, dt=F32):
        return TL(nc.dram_tensor(name, list(shape), dt, kind="ExternalInput").ap())

    def dscr(name, shape, dt=BF16, kind="Internal"):
        return TL(nc.dram_tensor(name, list(shape), dt, kind=kind).ap())

    AW = 53200
    arena = st.enter_context(nc.sbuf_tensor("arena", [128, AW], F32))
    aoff = [0]

    def sb(name, shape, dt=F32):
        n = 1
        for d_ in shape[1:]:
            n *= d_
        words = (n * (2 if dt == BF16 else 4) + 3) // 4
        assert aoff[0] + words <= AW, (name, aoff[0], words)
        v = arena[0:shape[0], aoff[0]:aoff[0] + words]
        aoff[0] += words
        if dt != F32:
            v = v.bitcast(dt)
        if len(shape) == 3:
            v = v.rearrange("p (a b) -> p a b", a=shape[1])
        elif len(shape) == 4:
            v = v.rearrange("p (a b c) -> p a b c", a=shape[1], b=shape[2])
        return TL(v)

    def ps(name, shape, dt=F32):
        return TL(st.enter_context(nc.psum_tensor(name, list(shape), dt))[:])

    X = {n: din(f"x_{n}", [T, D]) for n, T in seqs}
    Cc = {n: din(f"c_{n}", [1, D]) for n, T in seqs}
    Yout = {n: dscr(f"y_{n}", [T, D], F32, kind="ExternalOutput") for n, T in seqs}
    rel_bias = din("rel_bias", [32, 8])
    norm_w = din("norm_w", [depth, D])
    w_ada = din("w_ada", [depth, D, 3 * D])
    b_ada = din("b_ada", [depth, 3 * D])
    w_in = din("w_in", [depth, D, IN_COLS])
    b_gate = din("b_gate", [depth, 16])
    mlstm_norm_w = din("mlstm_norm_w", [depth, 512])
    na_q_norm = din("na_q_norm", [depth, 128])
    na_k_norm = din("na_k_norm", [depth, 128])
    na_rpb = din("na_rpb", [depth, 4, 15, 31])
    swa_q_norm = din("swa_q_norm", [depth, 64])
    swa_k_norm = din("swa_k_norm", [depth, 64])
    swa_sink = din("swa_sink", [depth, 8])
    conv_w = din("conv_w", [depth, 31, 512])
    conv_b = din("conv_b", [depth, 512])
    conv_ln_w = din("conv_ln_w", [depth, 512])
    conv_ln_b = din("conv_ln_b", [depth, 512])
    w_branch = din("w_branch", [depth, 4, 512, D])
    w_out = din("w_out", [depth, D, D])
    k_ident = din("k_ident", [128, 128])
    k_tri_fw = din("k_tri_fw", [128, 128])
    k_tri_bw = din("k_tri_bw", [128, 128])
    k_blk64 = din("k_blk64", [128, 128])
    k_gqa_m = din("k_gqa_m", [3, 32, 128, 128])
    k_na_m = din("k_na_m", [31, 128, 64])

    FM = {n: dscr(f"fm_{n}", [NFM, T]) for n, T in seqs}
    TMv = {n: dscr(f"tm_{n}", [T, 1664]) for n, T in seqs}
    TMg = {n: dscr(f"tg_{n}", [T, 16], F32) for n, T in seqs}
    HT = {n: dscr(f"ht_{n}", [D, T]) for n, T in seqs}
    YM = {n: dscr(f"ym_{n}", [D, T], BF16, kind=("ExternalOutput" if DEBUG_YM else "Internal")) for n, T in seqs}
    X1 = {n: dscr(f"x1_{n}", [T, D], F32) for n, T in seqs}
    modrow = dscr("modrow", [depth * len(seqs), D], F32)
    WT = {}
    WSRC = {}
    DBG = {}

    ident_f = sb("ident_f", [128, 128])
    ident_b = sb("ident_b", [128, 128], BF16)
    ones_b = sb("ones_b", [128, 128], BF16)
    ones_f = sb("ones_f", [128, 128])
    blk64_b = sb("blk64_b", [128, 128], BF16)
    tri_f = {"fw": sb("tri_fw", [128, 128]), "bw": sb("tri_bw", [128, 128])}
    kb.op("sync", lambda e: e.dma_start(out=ident_f.ap, in_=k_ident.ap), writes=[ident_f.b], dma=True)
    kb.op("vector", lambda e: e.tensor_copy(out=ident_b.ap, in_=ident_f.ap), reads=[ident_f.b], writes=[ident_b.b])
    kb.op("vector", lambda e: e.memset(ones_b.ap, 1.0), writes=[ones_b.b])
    kb.op("vector", lambda e: e.memset(ones_f.ap, 1.0), writes=[ones_f.b])
    kb.op("sync", lambda e: e.dma_start(out=tri_f["fw"].ap, in_=k_tri_fw.ap), writes=[tri_f["fw"].b], dma=True)
    kb.op("sync", lambda e: e.dma_start(out=tri_f["bw"].ap, in_=k_tri_bw.ap), writes=[tri_f["bw"].b], dma=True)
    eps_c = sb("eps_c", [128, 1])
    kb.op("vector", lambda e: e.memset(eps_c.ap, EPS), writes=[eps_c.b])
    tmpc = sb("tmpc", [128, 128])
    kb.op("sync", lambda e: e.dma_start(out=tmpc.ap, in_=k_blk64.ap), writes=[tmpc.b], dma=True)
    kb.op("vector", lambda e: e.tensor_copy(out=blk64_b.ap, in_=tmpc.ap), reads=[tmpc.b], writes=[blk64_b.b])

    PS = [ps(f"ps{i}", [128, 512]) for i in range(6)]
    PSB = [ps(f"psb{i}", [128, 1024], BF16) for i in range(2)]
    psrr = [0]

    psmode = [6]

    def nps():
        psrr[0] += 1
        return PS[psrr[0] % psmode[0]]

    PSH = [TL(PS[4].ap[:, 0:256]), TL(PS[4].ap[:, 256:512]), TL(PS[5].ap[:, 0:256]), TL(PS[5].ap[:, 256:512])]
    pshr = [0]

    def npsh():
        pshr[0] += 1
        return PSH[pshr[0] % 4]

    psbr = [0]

    def npsb():
        psbr[0] += 1
        return PSB[psbr[0] % 2]

    class Pool:
        def __init__(self, name, shape, dt, n):
            self.t = [sb(f"{name}{i}", shape, dt) for i in range(n)]
            self.i = 0

        def get(self):
            self.i += 1
            return self.t[self.i % len(self.t)]

    evac_rr = [0]

    def evac_eng():
        evac_rr[0] += 1
        return "vector" if evac_rr[0] % 2 else "scalar"

    def copy_op(eng, out, in_, reads, writes):
        if eng == "scalar":
            kb.op("scalar", lambda e: e.activation(out=out, in_=in_, func=AF.Copy), reads=reads, writes=writes)
        else:
            kb.op(eng, lambda e: e.tensor_copy(out=out, in_=in_), reads=reads, writes=writes)

    nseq = len(seqs)
    modA = sb("modA", [128, depth, nseq, KC])
    modB = sb("modB", [128, depth, nseq, KC])
    cs = sb("cs", [128, KC, nseq])
    modfm = sb("modfm", [128, 48, nseq])
    badafm = sb("badafm", [128, 48])
    nwfm = sb("nwfm", [128, KC])
    bg_bc = sb("bg_bc", [128, 16])
    EB = sb("EB", [128, 3, 8, 128])
    PERSIST = aoff[0]

    def adaln():
        w32 = Pool("w32", [128, KC, 128], F32, 3)
        for si, (n, T) in enumerate(seqs):
            kb.op("sync", lambda e, si=si, n=n: e.dma_start(out=cs.ap[:, :, si], in_=Cc[n].ap.rearrange("o (k p) -> p (o k)", p=128), allow_slow_non_contiguous=True),
                  writes=[cs.b], dma=True)
        kb.op("scalar", lambda e: e.activation(out=cs.ap, in_=cs.ap, func=AF.Silu), reads=[cs.b], writes=[cs.b])
        for l in range(depth):
            kb.op("sync", lambda e, l=l: e.dma_start(out=badafm.ap, in_=b_ada.ap[l:l + 1, :].rearrange("o (k p) -> p (o k)", p=128), allow_slow_non_contiguous=True),
                  writes=[badafm.b], dma=True)
            kb.op("sync", lambda e, l=l: e.dma_start(out=nwfm.ap, in_=norm_w.ap[l:l + 1, :].rearrange("o (k p) -> p (o k)", p=128), allow_slow_non_contiguous=True),
                  writes=[nwfm.b], dma=True)
            pm = nps()
            for f in range(48):
                wt = w32.get()
                kb.op("sync", lambda e, l=l, f=f, wt=wt: e.dma_start(out=wt.ap, in_=w_ada.ap[l, :, f * 128:(f + 1) * 128].rearrange("(k p) n -> p k n", p=128)),
                      writes=[wt.b], dma=True)
                for k in range(KC):
                    kb.op("tensor", lambda e, k=k, f=f, pm=pm, wt=wt: e.matmul(pm.ap[:, f * nseq:(f + 1) * nseq], lhsT=wt.ap[:, k, :],
                                                                          rhs=cs.ap[:, k, :], start=(k == 0), stop=(k == KC - 1)),
                          reads=[wt.b, cs.b], writes=[pm.b])
            for si in range(nseq):
                kb.op("vector", lambda e, pm=pm, si=si: e.tensor_tensor(out=modfm.ap[:, :, si], in0=pm.ap[:, 0:48 * nseq].rearrange("p (f s) -> p f s", s=nseq)[:, :, si],
                                                                in1=badafm.ap, op=ALU.add),
                      reads=[pm.b, badafm.b], writes=[modfm.b])
            for si, (n, T) in enumerate(seqs):
                kb.op("vector", lambda e, l=l, si=si: e.scalar_tensor_tensor(out=modA.ap[:, l, si, :], in0=modfm.ap[:, 16:32, si], scalar=1.0, in1=nwfm.ap,
                                                                           op0=ALU.add, op1=ALU.mult),
                      reads=[modfm.b, nwfm.b], writes=[modA.b])
                kb.op("vector", lambda e, l=l, si=si: e.tensor_copy(out=modB.ap[:, l, si, :], in_=modfm.ap[:, 0:16, si]), reads=[modfm.b], writes=[modB.b])
                kb.op("sync", lambda e, l=l, si=si: e.dma_start(out=modrow.ap[l * nseq + si:l * nseq + si + 1, :].rearrange("o (k p) -> p (o k)", p=128),
                                                               in_=modfm.ap[:, 32:48, si], allow_slow_non_contiguous=True),
                      reads=[modfm.b], writes=[modrow.b], dma=True)
        kb.barrier()
        aoff[0] = PERSIST

    FM_RANGES = [(0, 1024), (1536, 2560), (2576, 3600), (4112, 5264), (5392, 7440)]
    TM_RANGES = [(512, 1536, 0), (3600, 4112, 1024), (5264, 5392, 1536)]

    def fm_func(col):
        if C_AO <= col < C_AZ:
            return AF.Sigmoid
        if C_AZ <= col < C_AG or C_BZ <= col < C_CQ or C_CZ <= col < C_DA or C_DZ <= col < C_MG:
            return AF.Silu
        return None

    def phase1(l, si, n, T, xsrc):
        xt_pool = Pool("xt", [128, D], F32, 2)
        xn_t = sb("xn", [128, 4, D], BF16)
        junk = sb("junk", [128, D], BF16)
        ssq = sb("ssq", [128, 8])
        hT = sb("hT", [128, KC, 1024], BF16)
        ofm = Pool("ofm", [128, 1024], BF16, 3)
        otm = Pool("otm", [128, 512], BF16, 3)
        otg = Pool("otg", [128, 16], F32, 2)
        wtile = Pool("wtile", [128, KC, 512], BF16, 4)

        wjobs = []
        for g_ in range(T // 1024):
            for (c0_, c1_) in FM_RANGES:
                for w0_ in range(c0_, c1_, 512):
                    wjobs.append((w0_, min(512, c1_ - w0_)))
            for (c0_, c1_, _d) in TM_RANGES:
                for w0_ in range(c0_, c1_, 512):
                    wjobs.append((w0_, min(512, c1_ - w0_)))
            wjobs.append((C_AG, 16))

        def mk_loader(c0, ncols):
            def ld():
                wt = wtile.get()
                wload(wt.ap, wt.b, l, "in", 0, c0, ncols)
                return wt
            return ld
        pf = Prefetch([mk_loader(c0, nco) for (c0, nco) in wjobs], ahead=2)
        wji = [0]

        def load_w(c0, ncols):
            i = wji[0]
            assert wjobs[i] == (c0, ncols), (wjobs[i], c0, ncols)
            wji[0] += 1
            return pf.get(i)

        kb.op("sync", lambda e: e.dma_start(out=bg_bc.ap, in_=b_gate.ap[l:l + 1, :].broadcast_to([128, 16])), writes=[bg_bc.b], dma=True)
        hTs = [hT, sb("hTb", [128, KC, 1024], BF16)]

        def prep_gen(g):
            hT = hTs[g % 2]
            for half in range(2):
                for j in range(4):
                    t0 = g * 1024 + half * 512 + j * 128
                    xt = xt_pool.get()
                    kb.op("sync", lambda e, xt=xt, t0=t0: e.dma_start(out=xt.ap, in_=xsrc.ap[t0:t0 + 128, :]), reads=[xsrc.b], writes=[xt.b], dma=True)
                    kb.op("scalar", lambda e, xt=xt, j=j: e.activation(out=junk.ap, in_=xt.ap, func=AF.Square, accum_out=ssq.ap[:, j:j + 1]),
                          reads=[xt.b], writes=[junk.b, ssq.b])
                    kb.op("vector", lambda e, j=j: e.tensor_scalar(out=ssq.ap[:, 4 + j:5 + j], in0=ssq.ap[:, j:j + 1], scalar1=1.0 / D, scalar2=EPS,
                                                                    op0=ALU.mult, op1=ALU.add), reads=[ssq.b], writes=[ssq.b])
                    kb.op("scalar", lambda e, j=j: e.activation(out=ssq.ap[:, 4 + j:5 + j], in_=ssq.ap[:, 4 + j:5 + j], func=AF.Sqrt), reads=[ssq.b], writes=[ssq.b])
                    kb.op("vector", lambda e, j=j: e.reciprocal(out=ssq.ap[:, 4 + j:5 + j], in_=ssq.ap[:, 4 + j:5 + j]), reads=[ssq.b], writes=[ssq.b])
                    kb.op("vector", lambda e, xt=xt, j=j: e.tensor_scalar(out=xn_t.ap[:, j, :], in0=xt.ap, scalar1=ssq.ap[:, 4 + j:5 + j], scalar2=None,
                                                                           op0=ALU.mult), reads=[xt.b, ssq.b], writes=[xn_t.b])
                    yield
                for k in range(KC):
                    pb = npsb()
                    for j in range(4):
                        kb.op("tensor", lambda e, pb=pb, j=j, k=k: e.transpose(pb.ap[:, j * 128:(j + 1) * 128], xn_t.ap[:, j, k * 128:(k + 1) * 128], ident_b.ap),
                              reads=[xn_t.b, ident_b.b], writes=[pb.b])
                    dst = hT.ap[:, k, half * 512:(half + 1) * 512]
                    if k % 2 == 0:
                        kb.op("scalar", lambda e, pb=pb, k=k, dst=dst: e.activation(out=dst, in_=pb.ap[:, 0:512], func=AF.Identity,
                                                                                     bias=modB.ap[:, l, si, k:k + 1], scale=modA.ap[:, l, si, k:k + 1]),
                              reads=[pb.b, modA.b, modB.b], writes=[hT.b])
                    else:
                        kb.op("vector", lambda e, pb=pb, k=k, dst=dst: e.tensor_scalar(out=dst, in0=pb.ap[:, 0:512], scalar1=modA.ap[:, l, si, k:k + 1],
                                                                                        scalar2=modB.ap[:, l, si, k:k + 1], op0=ALU.mult, op1=ALU.add),
                              reads=[pb.b, modA.b, modB.b], writes=[hT.b])
                    if k % 4 == 3:
                        yield
            kb.op("sync", lambda e, g=g, hT=hT: e.dma_start(out=HT[n].ap[:, g * 1024:(g + 1) * 1024].rearrange("(k p) t -> p k t", p=128), in_=hT.ap),
                  reads=[hT.b], writes=[HT[n].b], dma=True)

        preps = {}

        def pump(g, nsteps=1):
            if g >= T // 1024:
                return
            if g not in preps:
                preps[g] = prep_gen(g)
            for _ in range(nsteps):
                try:
                    next(preps[g])
                except StopIteration:
                    break

        def do_group1(g):
            pump(g, 1000)
            hT = hTs[g % 2]
            for (c0, c1) in FM_RANGES:
                for w0 in range(c0, c1, 512):
                    ncols = min(512, c1 - w0)
                    wt = load_w(w0, ncols)
                    for mc in range(ncols // 128):
                        col = w0 + mc * 128
                        o = ofm.get()
                        fn = fm_func(col)
                        for tt in range(2):
                            p = nps()
                            for k in range(KC):
                                kb.op("tensor", lambda e, p=p, wt=wt, mc=mc, k=k, tt=tt: e.matmul(p.ap, lhsT=wt.ap[:, k, mc * 128:(mc + 1) * 128],
                                                                                            rhs=hT.ap[:, k, tt * 512:(tt + 1) * 512],
                                                                                            start=(k == 0), stop=(k == KC - 1)),
                                      reads=[wt.b, hT.b], writes=[p.b])
                            dst = o.ap[:, tt * 512:(tt + 1) * 512]
                            if fn is None:
                                copy_op(evac_eng(), dst, p.ap, [p.b], [o.b])
                            else:
                                kb.op("scalar", lambda e, p=p, dst=dst, fn=fn: e.activation(out=dst, in_=p.ap, func=fn), reads=[p.b], writes=[o.b])
                        kb.op("sync", lambda e, o=o, col=col, g=g: e.dma_start(out=FM[n].ap[col:col + 128, g * 1024:(g + 1) * 1024], in_=o.ap),
                              reads=[o.b], writes=[FM[n].b], dma=True)
                        pump(g + 1, 1)
            for (c0, c1, dcol) in TM_RANGES:
                for w0 in range(c0, c1, 512):
                    ncols = min(512, c1 - w0)
                    wt = load_w(w0, ncols)
                    for sub in range(8):
                        p = nps()
                        for k in range(KC):
                            kb.op("tensor", lambda e, p=p, wt=wt, k=k, sub=sub, ncols=ncols: e.matmul(p.ap[:, 0:ncols], lhsT=hT.ap[:, k, sub * 128:(sub + 1) * 128],
                                                                                                 rhs=wt.ap[:, k, 0:ncols], start=(k == 0), stop=(k == KC - 1)),
                                  reads=[wt.b, hT.b], writes=[p.b])
                        o = otm.get()
                        copy_op(evac_eng(), o.ap[:, 0:ncols], p.ap[:, 0:ncols], [p.b], [o.b])
                        t0 = g * 1024 + sub * 128
                        dc = dcol + (w0 - c0)
                        kb.op("sync", lambda e, o=o, t0=t0, dc=dc, ncols=ncols: e.dma_start(out=TMv[n].ap[t0:t0 + 128, dc:dc + ncols], in_=o.ap[:, 0:ncols]),
                              reads=[o.b], writes=[TMv[n].b], dma=True)
            wt = load_w(C_AG, 16)
            for sub in range(8):
                p = nps()
                for k in range(KC):
                    kb.op("tensor", lambda e, p=p, wt=wt, k=k, sub=sub: e.matmul(p.ap[:, 0:16], lhsT=hT.ap[:, k, sub * 128:(sub + 1) * 128],
                                                                            rhs=wt.ap[:, k, 0:16], start=(k == 0), stop=(k == KC - 1)),
                          reads=[wt.b, hT.b], writes=[p.b])
                o = otg.get()
                kb.op("vector", lambda e, o=o, p=p: e.tensor_tensor(out=o.ap, in0=p.ap[:, 0:16], in1=bg_bc.ap, op=ALU.add), reads=[p.b, bg_bc.b], writes=[o.b])
                t0 = g * 1024 + sub * 128
                kb.op("sync", lambda e, o=o, t0=t0: e.dma_start(out=TMg[n].ap[t0:t0 + 128, :], in_=o.ap), reads=[o.b], writes=[TMg[n].b], dma=True)
        for g in range(T // 1024):
            do_group1(g)
        kb.barrier()
        aoff[0] = PERSIST

    def phase3(l, si, n, T, xsrc, xdst):
        hT = sb("hT3", [128, KC, 1024], BF16)
        yT_t = sb("yT", [128, KC, 1024], BF16)
        yTb = [Buf() for _ in range(KC)]
        mT_t = sb("mT", [128, KC, 1024], BF16)
        wm_pool = Pool("wm", [128, KC, 512], BF16, 3)
        wbr_pool = Pool("wbr", [128, 4, 256], BF16, 2)

        def mk_merge(mg, i):
            def ld():
                wt = wm_pool.get()
                wload(wt.ap, wt.b, l, "in", 0, C_MG + i * 2048 + mg * 256, 256)
                wb_ = wbr_pool.get()
                wload(wb_.ap, wb_.b, l, "br", i, mg * 256, 256)
                return (wt, wb_)
            return ld

        def mk_out(nn):
            def ld():
                wt = wm_pool.get()
                wload(wt.ap, wt.b, l, "out", 0, nn * 512, 512)
                return wt
            return ld
        loaders3 = []
        for g_ in range(T // 1024):
            for mg_ in range(8):
                for i_ in range(4):
                    loaders3.append(mk_merge(mg_, i_))
            for nn_ in range(4):
                loaders3.append(mk_out(nn_))
        pf3 = Prefetch(loaders3, ahead=1)
        pfi = [0]
        ytmp = Pool("ytmp", [128, 1024], BF16, 3)
        uld = [sb(f"uld{i}", [128, 1024], BF16) for i in range(4)]
        sg_pool = Pool("sg", [128, 512], F32, 2)
        acc_pool = Pool("acc", [128, 512], F32, 4)
        tmp_pool = Pool("tmp3", [128, 512], F32, 2)
        xo_pool = Pool("xo", [128, 512], F32, 2)
        usq = Pool("usq", [128, 512], BF16, 2)
        stat = Pool("stat", [128, 512], F32, 2)
        gsl = sb("gsl", [128, 512])
        lnw = sb("lnw", [128, 4])
        lnb = sb("lnb", [128, 4])
        kb.op("sync", lambda e: e.dma_start(out=lnw.ap, in_=conv_ln_w.ap[l:l + 1, :].rearrange("o (k p) -> p (o k)", p=128), allow_slow_non_contiguous=True), writes=[lnw.b], dma=True)
        kb.op("sync", lambda e: e.dma_start(out=lnb.ap, in_=conv_ln_b.ap[l:l + 1, :].rearrange("o (k p) -> p (o k)", p=128), allow_slow_non_contiguous=True), writes=[lnb.b], dma=True)
        def prep3_gen(g):
            tsl = slice(g * 1024, (g + 1) * 1024)
            kb.op("sync", lambda e: e.dma_start(out=hT.ap, in_=HT[n].ap[:, tsl].rearrange("(k p) t -> p k t", p=128)), reads=[HT[n].b], writes=[hT.b], dma=True)
            for br in range(3):
                zc = (C_AZ, C_BZ, C_CZ)[br]
                for c4 in range(4):
                    a = ytmp.get()
                    kb.op("sync", lambda e, a=a, br=br, c4=c4: e.dma_start(out=a.ap, in_=YM[n].ap[br * 512 + c4 * 128: br * 512 + c4 * 128 + 128, tsl]),
                          reads=[YM[n].b], writes=[a.b], dma=True)
                    z = ytmp.get()
                    kb.op("sync", lambda e, z=z, zc=zc, c4=c4: e.dma_start(out=z.ap, in_=FM[n].ap[zc + c4 * 128: zc + c4 * 128 + 128, tsl]),
                          reads=[FM[n].b], writes=[z.b], dma=True)
                    dst = yT_t.ap[:, br * 4 + c4, :]
                    if br == 0:
                        s_ = ytmp.get()
                        kb.op("sync", lambda e, s_=s_, c4=c4: e.dma_start(out=s_.ap, in_=FM[n].ap[C_AO + c4 * 128: C_AO + c4 * 128 + 128, tsl]),
                              reads=[FM[n].b], writes=[s_.b], dma=True)
                        kb.op("gpsimd", lambda e, z=z, s_=s_: e.tensor_tensor(out=z.ap, in0=z.ap, in1=s_.ap, op=ALU.mult), reads=[z.b, s_.b], writes=[z.b])
                    kb.op("vector", lambda e, a=a, z=z, dst=dst: e.tensor_tensor(out=dst, in0=a.ap, in1=z.ap, op=ALU.mult), reads=[a.b, z.b], writes=[yTb[br * 4 + c4]])
                    yield
            ul = uld
            for c4 in range(4):
                kb.op("sync", lambda e, c4=c4: e.dma_start(out=ul[c4].ap, in_=YM[n].ap[1536 + c4 * 128: 1536 + c4 * 128 + 128, tsl]),
                      reads=[YM[n].b], writes=[ul[c4].b], dma=True)
            for tt in range(2):
                cs_ = slice(tt * 512, (tt + 1) * 512)
                p1 = nps()
                p2 = nps()
                for c4 in range(4):
                    q2 = usq.get()
                    kb.op("gpsimd", lambda e, q2=q2, c4=c4, cs_=cs_: e.tensor_tensor(out=q2.ap, in0=ul[c4].ap[:, cs_], in1=ul[c4].ap[:, cs_], op=ALU.mult),
                          reads=[ul[c4].b], writes=[q2.b])
                    kb.op("tensor", lambda e, p1=p1, c4=c4, cs_=cs_: e.matmul(p1.ap, lhsT=ones_b.ap, rhs=ul[c4].ap[:, cs_], start=(c4 == 0), stop=(c4 == 3)),
                          reads=[ones_b.b, ul[c4].b], writes=[p1.b])
                    kb.op("tensor", lambda e, p2=p2, q2=q2, c4=c4: e.matmul(p2.ap, lhsT=ones_b.ap, rhs=q2.ap, start=(c4 == 0), stop=(c4 == 3)),
                          reads=[ones_b.b, q2.b], writes=[p2.b])
                mean = stat.get()
                rstd = stat.get()
                m2 = rstd
                kb.op("scalar", lambda e, mean=mean, p1=p1: e.activation(out=mean.ap, in_=p1.ap, func=AF.Copy, scale=1.0 / 512), reads=[p1.b], writes=[mean.b])
                kb.op("vector", lambda e, mean=mean, m2=m2: e.tensor_tensor(out=m2.ap, in0=mean.ap, in1=mean.ap, op=ALU.mult), reads=[mean.b], writes=[m2.b])
                kb.op("vector", lambda e, rstd=rstd, p2=p2, m2=m2: e.scalar_tensor_tensor(out=rstd.ap, in0=p2.ap, scalar=1.0 / 512, in1=m2.ap, op0=ALU.mult, op1=ALU.subtract),
                      reads=[p2.b, m2.b], writes=[rstd.b])
                kb.op("scalar", lambda e, rstd=rstd: e.activation(out=rstd.ap, in_=rstd.ap, func=AF.Sqrt, bias=eps_c.ap[:, 0:1]), reads=[rstd.b, eps_c.b], writes=[rstd.b])
                kb.op("vector", lambda e, rstd=rstd: e.reciprocal(out=rstd.ap, in_=rstd.ap), reads=[rstd.b], writes=[rstd.b])
                for c4 in range(4):
                    t1 = tmp_pool.get()
                    kb.op("vector", lambda e, t1=t1, c4=c4, mean=mean, cs_=cs_: e.tensor_tensor(out=t1.ap, in0=ul[c4].ap[:, cs_], in1=mean.ap, op=ALU.subtract),
                          reads=[ul[c4].b, mean.b], writes=[t1.b])
                    kb.op("gpsimd", lambda e, t1=t1, rstd=rstd: e.tensor_tensor(out=t1.ap, in0=t1.ap, in1=rstd.ap, op=ALU.mult), reads=[t1.b, rstd.b], writes=[t1.b])
                    kb.op("scalar", lambda e, t1=t1, c4=c4: e.activation(out=t1.ap, in_=t1.ap, func=AF.Silu, bias=lnb.ap[:, c4:c4 + 1], scale=lnw.ap[:, c4:c4 + 1]),
                          reads=[t1.b, lnw.b, lnb.b], writes=[t1.b])
                    z = usq.get()
                    kb.op("sync", lambda e, z=z, c4=c4, tt=tt: e.dma_start(out=z.ap, in_=FM[n].ap[C_DZ + c4 * 128: C_DZ + c4 * 128 + 128, g * 1024 + tt * 512: g * 1024 + tt * 512 + 512]),
                          reads=[FM[n].b], writes=[z.b], dma=True)
                    kb.op("vector", lambda e, t1=t1, z=z, c4=c4, cs_=cs_: e.tensor_tensor(out=yT_t.ap[:, 12 + c4, cs_], in0=t1.ap, in1=z.ap, op=ALU.mult),
                          reads=[t1.b, z.b], writes=[yTb[12 + c4]])
                    yield
            yield

        preps3 = {}

        def pump3(g, nsteps=1):
            if g >= T // 1024:
                return
            if g not in preps3:
                preps3[g] = prep3_gen(g)
            for _ in range(nsteps):
                try:
                    next(preps3[g])
                except StopIteration:
                    break

        def do_group(g):
            pump3(g, 100000)
            for mg in range(8):
                accs = [acc_pool.get() for _ in range(4)]
                for i in range(4):
                    wt, wb_ = pf3.get(pfi[0])
                    pfi[0] += 1
                    for mc in range(2):
                        m = mg * 2 + mc
                        for tt in range(2):
                            cs_ = slice(tt * 512, (tt + 1) * 512)
                            acc = accs[mc * 2 + tt]
                            pa = nps()
                            for k in range(KC):
                                kb.op("tensor", lambda e, pa=pa, k=k, mc=mc, cs_=cs_, wt=wt: e.matmul(pa.ap, lhsT=wt.ap[:, k, mc * 128:(mc + 1) * 128], rhs=hT.ap[:, k, cs_],
                                                                                                 start=(k == 0), stop=(k == KC - 1)),
                                      reads=[wt.b, hT.b], writes=[pa.b])
                            pb_ = nps()
                            for k in range(4):
                                kb.op("tensor", lambda e, pb_=pb_, i=i, k=k, mc=mc, cs_=cs_, wb_=wb_: e.matmul(pb_.ap, lhsT=wb_.ap[:, k, mc * 128:(mc + 1) * 128], rhs=yT_t.ap[:, i * 4 + k, cs_],
                                                                                                          start=(k == 0), stop=(k == 3)),
                                      reads=[wb_.b, yTb[i * 4 + k]], writes=[pb_.b])
                            sg = sg_pool.get()
                            kb.op("scalar", lambda e, sg=sg, pa=pa: e.activation(out=sg.ap, in_=pa.ap, func=AF.Sigmoid), reads=[pa.b], writes=[sg.b])
                            if i == 0:
                                kb.op("vector", lambda e, acc=acc, sg=sg, pb_=pb_: e.tensor_tensor(out=acc.ap, in0=pb_.ap, in1=sg.ap, op=ALU.mult),
                                      reads=[pb_.b, sg.b], writes=[acc.b])
                            else:
                                kb.op("vector", lambda e, sg=sg, pb_=pb_: e.tensor_tensor(out=sg.ap, in0=pb_.ap, in1=sg.ap, op=ALU.mult),
                                      reads=[pb_.b, sg.b], writes=[sg.b])
                                if i < 3:
                                    kb.op("gpsimd", lambda e, acc=acc, sg=sg: e.tensor_tensor(out=acc.ap, in0=acc.ap, in1=sg.ap, op=ALU.add),
                                          reads=[acc.b, sg.b], writes=[acc.b])
                                else:
                                    kb.op("gpsimd", lambda e, acc=acc, sg=sg, m=m, cs_=cs_: e.tensor_tensor(out=mT_t.ap[:, m, cs_], in0=acc.ap, in1=sg.ap, op=ALU.add),
                                          reads=[acc.b, sg.b], writes=[mT_t.b])
            for nn in range(4):
                wt = pf3.get(pfi[0])
                pfi[0] += 1
                kb.op("sync", lambda e, nn=nn: e.dma_start(out=gsl.ap, in_=modrow.ap[l * nseq + si:l * nseq + si + 1, nn * 512:(nn + 1) * 512].broadcast_to([128, 512])),
                      reads=[modrow.b], writes=[gsl.b], dma=True)
                for sub in range(8):
                    t0 = g * 1024 + sub * 128
                    p = nps()
                    for k in range(KC):
                        kb.op("tensor", lambda e, p=p, wt=wt, k=k, sub=sub: e.matmul(p.ap, lhsT=mT_t.ap[:, k, sub * 128:(sub + 1) * 128], rhs=wt.ap[:, k, :],
                                                                               start=(k == 0), stop=(k == KC - 1)),
                              reads=[wt.b, mT_t.b], writes=[p.b])
                    xo = xo_pool.get()
                    kb.op("sync", lambda e, xo=xo, t0=t0, nn=nn: e.dma_start(out=xo.ap, in_=xsrc.ap[t0:t0 + 128, nn * 512:(nn + 1) * 512]), reads=[xsrc.b], writes=[xo.b], dma=True)
                    t1 = tmp_pool.get()
                    kb.op("vector", lambda e, t1=t1, p=p: e.tensor_tensor(out=t1.ap, in0=p.ap, in1=gsl.ap, op=ALU.mult),
                          reads=[p.b, gsl.b], writes=[t1.b])
                    kb.op("gpsimd", lambda e, t1=t1, xo=xo: e.tensor_tensor(out=xo.ap, in0=xo.ap, in1=t1.ap, op=ALU.add), reads=[xo.b, t1.b], writes=[xo.b])
                    kb.op("sync", lambda e, xo=xo, t0=t0, nn=nn: e.dma_start(out=xdst.ap[t0:t0 + 128, nn * 512:(nn + 1) * 512], in_=xo.ap), reads=[xo.b], writes=[xdst.b], dma=True)
                    pump3(g + 1, 1)
        for g in range(T // 1024):
            do_group(g)
        kb.barrier()
        aoff[0] = PERSIST

    def wt_get(l, kind, idx, c0, ncols):
        key = (l, kind, idx, c0, ncols)
        if key in WT:
            return WT[key]
        nk = 4 if kind == "br" else KC
        t = dscr(f"wt_{l}_{kind}_{idx}_{c0}_{ncols}", [128, nk * ncols])
        if kind == "in":
            src = w_in.ap[l, :, c0:c0 + ncols]
        elif kind == "br":
            src = w_branch.ap[l, idx, :, c0:c0 + ncols]
        else:
            src = w_out.ap[l, :, c0:c0 + ncols]
        WSRC[key] = src
        WT[key] = (t, nk)
        return WT[key]

    P1_TILES = []
    for (c0_, c1_) in [(0, 1024), (1536, 2560), (2576, 3600), (4112, 5264), (5392, 7440)]:
        for w0_ in range(c0_, c1_, 512):
            P1_TILES.append((w0_, min(512, c1_ - w0_)))
    for (c0_, c1_) in [(512, 1536), (3600, 4112), (5264, 5392)]:
        for w0_ in range(c0_, c1_, 512):
            P1_TILES.append((w0_, min(512, c1_ - w0_)))
    P1_TILES.append((C_AG, 16))

    def convert_tiles(keys):
        st32 = Pool("cv32", [128, KC, 512], F32, 2)
        st16 = Pool("cv16", [128, KC, 512], BF16, 2)
        for ci, key in enumerate(keys):
            (l, kind, idx, c0, ncols) = key
            t, nk = wt_get(l, kind, idx, c0, ncols)
            src = WSRC[key]
            a = st32.get()
            b = st16.get()
            kb.op("sync", lambda e, a=a, src=src, nk=nk, ncols=ncols: e.dma_start(out=a.ap[:, 0:nk, 0:ncols], in_=src.rearrange("(k p) n -> p k n", p=128)),
                  writes=[a.b], dma=True)
            eng = "gpsimd" if ci % 3 != 2 else "vector"
            kb.op(eng, lambda e, a=a, b=b, nk=nk, ncols=ncols: e.tensor_copy(out=b.ap[:, 0:nk, 0:ncols], in_=a.ap[:, 0:nk, 0:ncols]), reads=[a.b], writes=[b.b])
            kb.op("scalar", lambda e, b=b, t=t, nk=nk, ncols=ncols: e.dma_start(out=t.ap.rearrange("p (k n) -> p k n", k=nk), in_=b.ap[:, 0:nk, 0:ncols]),
                  reads=[b.b], writes=[t.b], dma=True)
        kb.barrier()
        aoff[0] = PERSIST

    def p1_keys(l):
        return [(l, "in", 0, c0, nco) for (c0, nco) in P1_TILES]

    def p3_keys(l):
        ks = []
        for mg in range(8):
            for i in range(4):
                ks.append((l, "in", 0, C_MG + i * 2048 + mg * 256, 256))
                ks.append((l, "br", i, mg * 256, 256))
        for nn in range(4):
            ks.append((l, "out", 0, nn * 512, 512))
        return ks

    WQ = "scalar"

    def wload(dst, dst_b, l, kind, idx, c0, ncols):
        t, nk = wt_get(l, kind, idx, c0, ncols)
        kb.op(WQ, lambda e: e.dma_start(out=dst[:, 0:nk, 0:ncols], in_=t.ap.rearrange("p (k n) -> p k n", k=nk)),
              reads=[t.b], writes=[dst_b], dma=True)

    class Prefetch:
        def __init__(self, loaders, ahead=2):
            self.loaders, self.ahead, self.tiles, self.issued = loaders, ahead, {}, 0

        def get(self, i):
            while self.issued <= min(i + self.ahead, len(self.loaders) - 1):
                self.tiles[self.issued] = self.loaders[self.issued]()
                self.issued += 1
            return self.tiles.pop(i)

    env = dict(locals())
    MIX = build_mixers(env)

    convert_tiles(p1_keys(0))
    adaln()
    MIX.gqa_tables()
    kb.barrier()
    aoff[0] = PERSIST
    for l in range(depth):
        for si, (n, T) in enumerate(seqs):
            xsrc = X[n] if l == 0 else X1[n]
            phase1(l, si, n, T, xsrc)
        convert_tiles(p3_keys(l) + (p1_keys(l + 1) if l + 1 < depth else []))
        for si, (n, T) in enumerate(seqs):
            MIX(l, si, n, T)
            kb.barrier()
            aoff[0] = PERSIST
        for si, (n, T) in enumerate(seqs):
            xsrc = X[n] if l == 0 else X1[n]
            xdst = Yout[n] if l == depth - 1 else X1[n]
            phase3(l, si, n, T, xsrc, xdst)
    kb.emit()
    st.close()
    return nc


def build_mixers(env):
    g_ = env
    kb, sb, nps, npsb, Pool, copy_op = g_["kb"], g_["sb"], g_["nps"], g_["npsb"], g_["Pool"], g_["copy_op"]
    FM, TMv, TMg, YM = g_["FM"], g_["TMv"], g_["TMg"], g_["YM"]
    ident_b, ident_f, ones_b, ones_f, blk64_b, tri_f, eps_c = (g_[k] for k in ("ident_b", "ident_f", "ones_b", "ones_f", "blk64_b", "tri_f", "eps_c"))
    consts = make_consts()
    npsh = g_["npsh"]

    def interleave(gens):
        gens = list(gens)
        while gens:
            nxt = []
            for g in gens:
                try:
                    next(g)
                    nxt.append(g)
                except StopIteration:
                    pass
            gens = nxt

    def dma(eng, out, in_, reads, writes, slow=False):
        if slow:
            kb.op(eng, lambda e: e.dma_start(out=out, in_=in_, allow_slow_non_contiguous=True), reads=reads, writes=writes, dma=True)
        else:
            kb.op(eng, lambda e: e.dma_start(out=out, in_=in_), reads=reads, writes=writes, dma=True)

    def V(fn, reads, writes, eng="vector"):
        kb.op(eng, fn, reads=reads, writes=writes)

    def MM(out, lhsT, rhs, start, stop, reads, writes):
        kb.op("tensor", lambda e: e.matmul(out, lhsT=lhsT, rhs=rhs, start=start, stop=stop), reads=reads, writes=writes)

    def rsqrt_tile(dst, src_ps, scale, n, reads):
        V(lambda e: e.tensor_scalar(out=dst.ap[:, 0:n], in0=src_ps, scalar1=scale, scalar2=EPS, op0=ALU.mult, op1=ALU.add), reads, [dst.b])
        kb.op("scalar", lambda e: e.activation(out=dst.ap[:, 0:n], in_=dst.ap[:, 0:n], func=AF.Sqrt), reads=[dst.b], writes=[dst.b])
        V(lambda e: e.reciprocal(out=dst.ap[:, 0:n], in_=dst.ap[:, 0:n]), [dst.b], [dst.b])

    def headnorm_fm(dst, src, T, wcol, lhs_ones, scale_div, extra_scale):
        sq = Pool("hn_sq", [128, 512], BF16, 2)
        rs = Pool("hn_rs", [128, 512], F32, 2)
        for t0 in range(0, T, 512):
            q2 = sq.get()
            V(lambda e, q2=q2, t0=t0: e.tensor_tensor(out=q2.ap, in0=src.ap[:, t0:t0 + 512], in1=src.ap[:, t0:t0 + 512], op=ALU.mult), [src.b], [q2.b], eng="gpsimd")
            p = nps()
            MM(p.ap, lhs_ones.ap, q2.ap, True, True, [lhs_ones.b, q2.b], [p.b])
            r = rs.get()
            rsqrt_tile(r, p.ap, 1.0 / scale_div, 512, [p.b])
            V(lambda e, r=r, t0=t0: e.scalar_tensor_tensor(out=dst.ap[:, t0:t0 + 512], in0=src.ap[:, t0:t0 + 512], scalar=wcol, in1=r.ap, op0=ALU.mult, op1=ALU.mult),
              [src.b, r.b], [dst.b])

    def conv(l, n, T):
        conv_w, conv_b = g_["conv_w"], g_["conv_b"]
        a_t = sb("cv_a", [128, T], BF16)
        g_t = sb("cv_g", [128, T], BF16)
        up = sb("cv_u", [128, T + 32], BF16)
        dg = sb("cv_dg", [128, 31, 128], BF16)
        cw = sb("cv_w", [128, 31])
        cb = sb("cv_b", [128, 1])
        osb = Pool("cv_o", [128, 512], BF16, 3)
        for c4 in range(4):
            dma("sync", cw.ap, conv_w.ap[l, :, c4 * 128:(c4 + 1) * 128].rearrange("w p -> p w"), [], [cw.b], slow=True)
            dma("sync", cb.ap, conv_b.ap[l:l + 1, c4 * 128:(c4 + 1) * 128].rearrange("o p -> p o"), [], [cb.b], slow=True)
            for w in range(31):
                V(lambda e, w=w: e.tensor_scalar(out=dg.ap[:, w, :], in0=ident_f.ap, scalar1=cw.ap[:, w:w + 1], scalar2=None, op0=ALU.mult), [ident_f.b, cw.b], [dg.b],
                  eng=("vector" if w % 2 else "gpsimd"))
            for t0 in range(0, T, 1024):
                dma("sync", a_t.ap[:, t0:t0 + 1024], FM[n].ap[C_DA + c4 * 128:C_DA + c4 * 128 + 128, t0:t0 + 1024], [FM[n].b], [a_t.b])
                dma("sync", g_t.ap[:, t0:t0 + 1024], FM[n].ap[C_DG + c4 * 128:C_DG + c4 * 128 + 128, t0:t0 + 1024], [FM[n].b], [g_t.b])
            V(lambda e: e.memset(up.ap[:, 0:16], 0.0), [], [up.b])
            V(lambda e: e.memset(up.ap[:, T + 15:T + 32], 0.0), [], [up.b])
            kb.op("scalar", lambda e: e.activation(out=g_t.ap, in_=g_t.ap, func=AF.Sigmoid), reads=[g_t.b], writes=[g_t.b])
            V(lambda e: e.tensor_tensor(out=up.ap[:, 15:15 + T], in0=a_t.ap, in1=g_t.ap, op=ALU.mult), [a_t.b, g_t.b], [up.b])
            for t0 in range(0, T, 512):
                p = nps()
                for w in range(31):
                    MM(p.ap, dg.ap[:, w, :], up.ap[:, t0 + w:t0 + w + 512], w == 0, w == 30, [dg.b, up.b], [p.b])
                o = osb.get()
                kb.op("scalar", lambda e, o=o, p=p: e.activation(out=o.ap, in_=p.ap, func=AF.Identity, bias=cb.ap[:, 0:1]), reads=[p.b, cb.b], writes=[o.b])
                dma("sync", YM[n].ap[1536 + c4 * 128:1536 + c4 * 128 + 128, t0:t0 + 512], o.ap, [o.b], [YM[n].b])

    gq_state = {}

    def gqa_tables():
        rel_bias, k_gqa_m = g_["rel_bias"], g_["k_gqa_m"]
        EB = g_["EB"]
        rbb = sb("gq_rbb", [128, 256])
        val = sb("gq_val", [128, 3, 128])
        mk = Pool("gq_mk", [128, 128], F32, 3)
        dma("sync", rbb.ap, rel_bias.ap.rearrange("b h -> (b h)").unsqueeze(0).broadcast_to([128, 256]), [], [rbb.b])
        V(lambda e: e.memset(EB.ap, 0.0), [], [EB.b])
        V(lambda e: e.memset(val.ap, 0.0), [], [val.b])
        gm = consts["gqa_m"]
        for o in range(3):
            for b in range(32):
                if not gm[o, b].any():
                    continue
                m = mk.get()
                dma("sync", m.ap, k_gqa_m.ap[o, b], [], [m.b])
                V(lambda e, m=m, o=o: e.tensor_tensor(out=val.ap[:, o, :], in0=val.ap[:, o, :], in1=m.ap, op=ALU.add), [m.b, val.b], [val.b], eng="gpsimd")
                for h in range(8):
                    V(lambda e, m=m, o=o, b=b, h=h: e.scalar_tensor_tensor(out=EB.ap[:, o, h, :], in0=m.ap, scalar=rbb.ap[:, b * 8 + h:b * 8 + h + 1], in1=EB.ap[:, o, h, :],
                                                                         op0=ALU.mult, op1=ALU.add), [m.b, rbb.b, EB.b], [EB.b])
        kb.op("scalar", lambda e: e.activation(out=EB.ap, in_=EB.ap, func=AF.Exp), reads=[EB.b], writes=[EB.b])
        for o in range(3):
            for h in range(8):
                V(lambda e, o=o, h=h: e.tensor_tensor(out=EB.ap[:, o, h, :], in0=EB.ap[:, o, h, :], in1=val.ap[:, o, :], op=ALU.mult), [EB.b, val.b], [EB.b])

    def gqa(l, n, T):
        swa_q_norm, swa_k_norm, swa_sink = g_["swa_q_norm"], g_["swa_k_norm"], g_["swa_sink"]
        EB = g_["EB"]
        nb = T // 128
        kraw = sb("gq_kraw", [128, T], BF16)
        kn = sb("gq_kn", [128, T], BF16)
        qraw = sb("gq_qraw", [128, T], BF16)
        qn = sb("gq_qn", [128, T], BF16)
        vp = [sb(f"gq_vp{i}", [128, nb, 128], BF16) for i in range(2)]
        vraw = sb("gq_vraw", [128, nb, 64], BF16)
        on = [sb(f"gq_on{i}", [128, 128], BF16) for i in range(2)]
        wq = sb("gq_wq", [128, 1])
        wk = sb("gq_wk", [128, 1])
        sk = sb("gq_sk", [128, 4])
        Pf = Pool("gq_pf", [128, 2, 384], F32, 4)
        Pb = Pool("gq_pb", [128, 2, 384], BF16, 4)
        rc = Pool("gq_rc", [128, 128], F32, 4)
        ost = Pool("gq_ost", [128, 1024], BF16, 2)
        for i in range(2):
            V(lambda e, i=i: e.memset(on[i].ap, 0.0), [], [on[i].b])
            V(lambda e, i=i: e.memset(on[i].ap[:, 64 * i:64 * i + 64], 1.0), [], [on[i].b])
            V(lambda e, i=i: e.memset(vp[i].ap, 0.0), [], [vp[i].b], eng="gpsimd")
        for half in range(2):
            dma("sync", wq.ap[64 * half:64 * half + 64, :], swa_q_norm.ap[l:l + 1, :].rearrange("o p -> p o"), [], [wq.b], slow=True)
            dma("sync", wk.ap[64 * half:64 * half + 64, :], swa_k_norm.ap[l:l + 1, :].rearrange("o p -> p o"), [], [wk.b], slow=True)
        V(lambda e: e.tensor_scalar(out=wq.ap, in0=wq.ap, scalar1=0.125, scalar2=None, op0=ALU.mult), [wq.b], [wq.b])
        for kvh in range(2):
            for half in range(2):
                for t0 in range(0, T, 2048):
                    tw = min(2048, T - t0)
                    dma("sync", kraw.ap[64 * half:64 * half + 64, t0:t0 + tw], FM[n].ap[C_CK + 64 * kvh:C_CK + 64 * kvh + 64, t0:t0 + tw], [FM[n].b], [kraw.b])
            headnorm_fm(kn, kraw, T, wk.ap[:, 0:1], blk64_b, 64.0, 1.0)
            dma("sync", vraw.ap, TMv[n].ap[:, 1536 + 64 * kvh:1536 + 64 * kvh + 64].rearrange("(b p) c -> p b c", p=128), [TMv[n].b], [vraw.b])
            for i in range(2):
                V(lambda e, i=i: e.tensor_copy(out=vp[i].ap[:, :, 64 * i:64 * i + 64], in_=vraw.ap), [vraw.b], [vp[i].b])
            for hp in range(2):
                h0 = kvh * 4 + hp * 2
                for half in range(2):
                    dma("sync", sk.ap[64 * half:64 * half + 64, 0:1], swa_sink.ap[l:l + 1, h0 + half:h0 + half + 1].broadcast_to([64, 1]), [], [sk.b])
                kb.op("scalar", lambda e: e.activation(out=sk.ap[:, 1:2], in_=sk.ap[:, 0:1], func=AF.Exp), reads=[sk.b], writes=[sk.b])
                for t0 in range(0, T, 2048):
                    tw = min(2048, T - t0)
                    dma("sync", qraw.ap[:, t0:t0 + tw], FM[n].ap[C_CQ + 64 * h0:C_CQ + 64 * h0 + 128, t0:t0 + tw], [FM[n].b], [qraw.b])
                headnorm_fm(qn, qraw, T, wq.ap[:, 0:1], blk64_b, 64.0, 1.0)
                ostm = {}

                def qb_gen(qb, h0=h0, kvh=kvh, hp=hp, ostm=ostm):
                    if qb // 8 not in ostm:
                        ostm[qb // 8] = ost.get()
                    o_t = ostm[qb // 8]
                    os_ = [o for o in range(3) if 0 <= qb + o - 1 < nb]
                    o0, o1 = os_[0], os_[-1] + 1
                    pS = [nps(), nps()]
                    for hh in range(2):
                        for o in os_:
                            kbk = qb + o - 1
                            MM(pS[hh].ap[:, o * 128:(o + 1) * 128], kn.ap[64 * hh:64 * hh + 64, kbk * 128:(kbk + 1) * 128], qn.ap[64 * hh:64 * hh + 64, qb * 128:(qb + 1) * 128],
                               True, True, [kn.b, qn.b], [pS[hh].b])
                    yield
                    pf = Pf.get()
                    pb = Pb.get()
                    for hh in range(2):
                        kb.op("scalar", lambda e, hh=hh, pf=pf, pS=pS, o0=o0, o1=o1: e.activation(out=pf.ap[:, hh, o0 * 128:o1 * 128], in_=pS[hh].ap[:, o0 * 128:o1 * 128], func=AF.Exp),
                              reads=[pS[hh].b], writes=[pf.b])
                    yield
                    for hh in range(2):
                        V(lambda e, hh=hh, pf=pf, pb=pb, o0=o0, o1=o1, h0=h0: e.tensor_tensor(out=pb.ap[:, hh, o0 * 128:o1 * 128].rearrange("p (o q) -> p o q", q=128),
                                                                                              in0=pf.ap[:, hh, o0 * 128:o1 * 128].rearrange("p (o q) -> p o q", q=128),
                                                                                              in1=EB.ap[:, o0:o1, h0 + hh, :], op=ALU.mult),
                          [pf.b, EB.b], [pb.b], eng=("vector" if hh else "gpsimd"))
                    yield
                    pOD = nps()
                    tot = 2 * len(os_)
                    cnt = 0
                    for hh in range(2):
                        for o in os_:
                            kbk = qb + o - 1
                            MM(pOD.ap[:, 0:128], vp[hh].ap[:, kbk, :], pb.ap[:, hh, o * 128:(o + 1) * 128], cnt == 0, cnt == tot - 1, [vp[hh].b, pb.b], [pOD.b])
                            cnt += 1
                    cnt = 0
                    for hh in range(2):
                        for o in os_:
                            MM(pOD.ap[:, 128:256], on[hh].ap, pb.ap[:, hh, o * 128:(o + 1) * 128], cnt == 0, cnt == tot - 1, [on[hh].b, pb.b], [pOD.b])
                            cnt += 1
                    yield
                    r = rc.get()
                    V(lambda e, r=r, pOD=pOD: e.tensor_scalar(out=r.ap, in0=pOD.ap[:, 128:256], scalar1=sk.ap[:, 1:2], scalar2=None, op0=ALU.add), [pOD.b, sk.b], [r.b])
                    V(lambda e, r=r: e.reciprocal(out=r.ap, in_=r.ap), [r.b], [r.b])
                    yield
                    V(lambda e, r=r, pOD=pOD, o_t=o_t, qb=qb: e.tensor_tensor(out=o_t.ap[:, (qb % 8) * 128:(qb % 8) * 128 + 128], in0=pOD.ap[:, 0:128], in1=r.ap, op=ALU.mult),
                      [pOD.b, r.b], [o_t.b])
                    if qb % 8 == 7:
                        row = 1024 + 128 * (kvh * 2 + hp)
                        dma("sync", YM[n].ap[row:row + 128, (qb - 7) * 128:(qb + 1) * 128], o_t.ap, [o_t.b], [YM[n].b])

                for q0 in range(0, nb, GQ_W):
                    interleave([qb_gen(q) for q in range(q0, min(q0 + GQ_W, nb))])

    def na(l, n, T):
        na_q_norm, na_k_norm, na_rpb, k_na_m = g_["na_q_norm"], g_["na_k_norm"], g_["na_rpb"], g_["k_na_m"]
        rows = T // 64
        nb = T // 128
        Mc = sb("na_mc", [128, 31, 64])
        RP = sb("na_rp", [128, 14, 31])
        ET = sb("na_et", [128, 14, 64])
        okm = sb("na_ok", [128, 64])
        tmpE = sb("na_te", [128, 14, 64])
        qraw = sb("na_qraw", [128, T], BF16)
        kraw = sb("na_kraw", [128, T], BF16)
        qn = sb("na_qn", [128, T], BF16)
        kn = sb("na_kn", [128, T], BF16)
        vA = sb("na_vA", [128, nb, 128], BF16)
        vB = sb("na_vB", [128, nb, 128], BF16)
        wq = sb("na_wq", [128, 1])
        wk = sb("na_wk", [128, 1])
        Pf = Pool("na_pf", [128, 4, 64], F32, 6)
        Pb = Pool("na_pb", [128, 4, 64], BF16, 6)
        rc = Pool("na_rc", [128, 64], F32, 6)
        ost = Pool("na_ost", [128, 1024], BF16, 2)
        dma("sync", Mc.ap, k_na_m.ap.rearrange("d p q -> p d q"), [], [Mc.b])
        dma("sync", wq.ap, na_q_norm.ap[l:l + 1, :].rearrange("o p -> p o"), [], [wq.b], slow=True)
        dma("sync", wk.ap, na_k_norm.ap[l:l + 1, :].rearrange("o p -> p o"), [], [wk.b], slow=True)
        V(lambda e: e.tensor_scalar(out=wq.ap, in0=wq.ap, scalar1=float(128 ** -0.5), scalar2=None, op0=ALU.mult), [wq.b], [wq.b])
        V(lambda e: e.memset(okm.ap, 0.0), [], [okm.b])
        for dc in range(31):
            V(lambda e, dc=dc: e.tensor_tensor(out=okm.ap, in0=okm.ap, in1=Mc.ap[:, dc, :], op=ALU.add), [Mc.b, okm.b], [okm.b])
        for h in range(4):
            for half in range(2):
                dma("sync", RP.ap[64 * half:64 * half + 64, :, :], na_rpb.ap[l, h:h + 1, half:half + 14, :].broadcast_to([64, 14, 31]), [], [RP.b])
            V(lambda e: e.memset(ET.ap, 0.0), [], [ET.b])
            for dc in range(31):
                V(lambda e, dc=dc: e.tensor_tensor(out=tmpE.ap, in0=Mc.ap[:, dc, :].unsqueeze(1).broadcast_to([128, 14, 64]),
                                                    in1=RP.ap[:, :, dc].unsqueeze(2).broadcast_to([128, 14, 64]), op=ALU.mult), [Mc.b, RP.b], [tmpE.b])
                V(lambda e: e.tensor_tensor(out=ET.ap, in0=ET.ap, in1=tmpE.ap, op=ALU.add), [tmpE.b, ET.b], [ET.b], eng="gpsimd")
            kb.op("scalar", lambda e: e.activation(out=ET.ap, in_=ET.ap, func=AF.Exp), reads=[ET.b], writes=[ET.b])
            V(lambda e: e.tensor_tensor(out=ET.ap, in0=ET.ap, in1=okm.ap.unsqueeze(1).broadcast_to([128, 14, 64]), op=ALU.mult), [ET.b, okm.b], [ET.b])
            for t0 in range(0, T, 2048):
                tw = min(2048, T - t0)
                dma("sync", qraw.ap[:, t0:t0 + tw], FM[n].ap[C_BQ + 128 * h:C_BQ + 128 * h + 128, t0:t0 + tw], [FM[n].b], [qraw.b])
                dma("sync", kraw.ap[:, t0:t0 + tw], FM[n].ap[C_BK + 128 * h:C_BK + 128 * h + 128, t0:t0 + tw], [FM[n].b], [kraw.b])
            headnorm_fm(qn, qraw, T, wq.ap[:, 0:1], ones_b, 128.0, 1.0)
            headnorm_fm(kn, kraw, T, wk.ap[:, 0:1], ones_b, 128.0, 1.0)
            dma("sync", vA.ap, TMv[n].ap[:, 1024 + 128 * h:1024 + 128 * h + 128].rearrange("(b p) c -> p b c", p=128), [TMv[n].b], [vA.b])
            dma("sync", vB.ap[:, 0:nb - 1, :], TMv[n].ap[64:T - 64, 1024 + 128 * h:1024 + 128 * h + 128].rearrange("(b p) c -> p b c", p=128), [TMv[n].b], [vB.b])
            ostm = {}

            def row_gen(r):
                if r // 16 not in ostm:
                    ostm[r // 16] = ost.get()
                o_t = ostm[r // 16]
                start = min(max(r - 4, 0), rows - 8)
                off = start - r + 7
                pS = nps()
                for j2 in range(4):
                    k0 = 64 * (start + 2 * j2)
                    MM(pS.ap[:, j2 * 64:(j2 + 1) * 64], kn.ap[:, k0:k0 + 128], qn.ap[:, 64 * r:64 * r + 64], True, True, [kn.b, qn.b], [pS.b])
                yield
                pf = Pf.get()
                pb = Pb.get()
                kb.op("scalar", lambda e, pf=pf, pS=pS: e.activation(out=pf.ap, in_=pS.ap[:, 0:256].rearrange("p (j q) -> p j q", q=64), func=AF.Exp), reads=[pS.b], writes=[pf.b])
                yield
                V(lambda e, pf=pf, pb=pb, off=off: e.tensor_tensor(out=pb.ap, in0=pf.ap, in1=ET.ap[:, off:off + 7:2, :], op=ALU.mult), [pf.b, ET.b], [pb.b],
                  eng=("vector" if r % 2 else "gpsimd"))
                yield
                pOD = nps()
                for j2 in range(4):
                    rr = start + 2 * j2
                    vt = vA.ap[:, rr // 2, :] if rr % 2 == 0 else vB.ap[:, (rr - 1) // 2, :]
                    vb_ = vA.b if rr % 2 == 0 else vB.b
                    MM(pOD.ap[:, 0:64], vt, pb.ap[:, j2, :], j2 == 0, j2 == 3, [vb_, pb.b], [pOD.b])
                for j2 in range(4):
                    MM(pOD.ap[:, 64:128], ones_b.ap, pb.ap[:, j2, :], j2 == 0, j2 == 3, [ones_b.b, pb.b], [pOD.b])
                yield
                rcp = rc.get()
                V(lambda e, rcp=rcp, pOD=pOD: e.reciprocal(out=rcp.ap, in_=pOD.ap[:, 64:128]), [pOD.b], [rcp.b])
                yield
                V(lambda e, rcp=rcp, pOD=pOD, o_t=o_t, r=r: e.tensor_tensor(out=o_t.ap[:, (r % 16) * 64:(r % 16) * 64 + 64], in0=pOD.ap[:, 0:64], in1=rcp.ap, op=ALU.mult),
                  [pOD.b, rcp.b], [o_t.b])
                if r % 16 == 15:
                    dma("sync", YM[n].ap[512 + 128 * h:512 + 128 * h + 128, (r - 15) * 64:(r + 1) * 64], o_t.ap, [o_t.b], [YM[n].b])

            for r0 in range(0, rows, NA_W):
                interleave([row_gen(r) for r in range(r0, min(r0 + NA_W, rows))])

    def mlstm(l, n, T):
        mlstm_norm_w = g_["mlstm_norm_w"]
        nb = T // 128
        qT = sb("ml_qT", [128, T], BF16)
        kT = sb("ml_kT", [128, T], BF16)
        ktm = sb("ml_ktm", [128, nb, 128], BF16)
        vaug = sb("ml_va", [128, nb, 132], BF16)
        gt = sb("ml_gt", [128, nb, 16])
        hfw = sb("ml_hfw", [128, nb, 128])
        nwb = sb("ml_nw", [128, 128])
        lf = sb("ml_lf", [128, 2, nb])
        bcs = sb("ml_b", [128, 2, nb])
        gb = sb("ml_g", [128, 2, nb])
        beta = sb("ml_beta", [128, 2, nb])
        gam = sb("ml_gam", [128, 2, nb])
        emb = sb("ml_emb", [128, 2, nb])
        eg = sb("ml_eg", [128, 2, nb])
        hbw = sb("ml_hbw", [128, nb, 128])
        Cs = [sb(f"ml_C{i}", [128, 132]) for i in range(2)]
        Cbs = [sb(f"ml_Cb{i}", [128, 132], BF16) for i in range(2)]
        hs8 = Pool("ml_hs8", [128, 8, 128], F32, 2)
        sq8 = Pool("ml_sq8", [128, 8, 128], F32, 2)
        hb8 = Pool("ml_hb8", [128, 8, 128], BF16, 2)
        sm8 = Pool("ml_sm8", [128, 16], F32, 2)
        STp = Pool("ml_st", [128, 128], BF16, 4)
        kgp = Pool("ml_kg", [128, 128], BF16, 4)
        sm = Pool("ml_sm", [128, 4], F32, 6)
        ost = Pool("ml_ost", [128, 1024], BF16, 2)
        V(lambda e: e.memset(vaug.ap[:, :, 128:129], 1.0), [], [vaug.b])
        for h in range(4):
            for t0 in range(0, T, 2048):
                tw = min(2048, T - t0)
                dma("sync", qT.ap[:, t0:t0 + tw], FM[n].ap[C_AQ + 128 * h:C_AQ + 128 * h + 128, t0:t0 + tw], [FM[n].b], [qT.b])
                dma("sync", kT.ap[:, t0:t0 + tw], FM[n].ap[C_AK + 128 * h:C_AK + 128 * h + 128, t0:t0 + tw], [FM[n].b], [kT.b])
            dma("sync", ktm.ap, TMv[n].ap[:, 128 * h:128 * h + 128].rearrange("(b p) c -> p b c", p=128), [TMv[n].b], [ktm.b])
            dma("sync", vaug.ap[:, :, 0:128], TMv[n].ap[:, 512 + 128 * h:512 + 128 * h + 128].rearrange("(b p) c -> p b c", p=128), [TMv[n].b], [vaug.b])
            dma("sync", gt.ap, TMg[n].ap.rearrange("(b p) c -> p b c", p=128), [TMg[n].b], [gt.b])
            dma("sync", nwb.ap, mlstm_norm_w.ap[l:l + 1, 128 * h:128 * h + 128].broadcast_to([128, 128]), [], [nwb.b])
            for d, dn in enumerate(("fw", "bw")):
                icol, fcol = 4 * d + h, 8 + 4 * d + h
                kb.op("scalar", lambda e, d=d, fcol=fcol: e.activation(out=lf.ap[:, d, :], in_=gt.ap[:, :, fcol], func=AF.Exp, scale=-1.0), reads=[gt.b], writes=[lf.b])
                V(lambda e, d=d: e.tensor_scalar(out=lf.ap[:, d, :], in0=lf.ap[:, d, :], scalar1=1.0, scalar2=None, op0=ALU.add), [lf.b], [lf.b])
                kb.op("scalar", lambda e, d=d: e.activation(out=lf.ap[:, d, :], in_=lf.ap[:, d, :], func=AF.Ln), reads=[lf.b], writes=[lf.b])
                V(lambda e, d=d: e.tensor_scalar(out=lf.ap[:, d, :], in0=lf.ap[:, d, :], scalar1=-1.0, scalar2=None, op0=ALU.mult), [lf.b], [lf.b])
                p = nps()
                MM(p.ap[:, 0:nb], tri_f[dn].ap, lf.ap[:, d, :], True, True, [tri_f[dn].b, lf.b], [p.b])
                V(lambda e, d=d, p=p: e.tensor_copy(out=bcs.ap[:, d, :], in_=p.ap[:, 0:nb]), [p.b], [bcs.b])
                p2 = nps()
                MM(p2.ap[:, 0:nb], ones_f.ap, lf.ap[:, d, :], True, True, [ones_f.b, lf.b], [p2.b])
                V(lambda e, d=d, p2=p2: e.tensor_copy(out=gb.ap[:, d, :], in_=p2.ap[:, 0:nb]), [p2.b], [gb.b])
                kb.op("scalar", lambda e, d=d: e.activation(out=eg.ap[:, d, :], in_=gb.ap[:, d, :], func=AF.Exp), reads=[gb.b], writes=[eg.b])
                kb.op("scalar", lambda e, d=d: e.activation(out=emb.ap[:, d, :], in_=bcs.ap[:, d, :], func=AF.Exp, scale=-1.0), reads=[bcs.b], writes=[emb.b])
                V(lambda e, d=d, icol=icol: e.tensor_tensor(out=beta.ap[:, d, :], in0=gt.ap[:, :, icol], in1=bcs.ap[:, d, :], op=ALU.subtract), [gt.b, bcs.b], [beta.b])
                kb.op("scalar", lambda e, d=d: e.activation(out=beta.ap[:, d, :], in_=beta.ap[:, d, :], func=AF.Exp), reads=[beta.b], writes=[beta.b])
                V(lambda e, d=d: e.tensor_scalar(out=beta.ap[:, d, :], in0=beta.ap[:, d, :], scalar1=float(128 ** -0.5), scalar2=None, op0=ALU.mult), [beta.b], [beta.b])
                V(lambda e, d=d: e.tensor_tensor(out=gam.ap[:, d, :], in0=beta.ap[:, d, :], in1=eg.ap[:, d, :], op=ALU.mult), [beta.b, eg.b], [gam.b])
            def dir_gen(d, dn):
                order = list(range(nb)) if d == 0 else list(range(nb - 1, -1, -1))
                hdir = hfw if d == 0 else hbw
                Cst, Cb = Cs[d], Cbs[d]
                for ci, c in enumerate(order):
                    csl = slice(c * 128, (c + 1) * 128)
                    last = (ci == nb - 1)
                    if not last:
                        kg = kgp.get()
                        V(lambda e, kg=kg, d=d, c=c: e.tensor_scalar(out=kg.ap, in0=ktm.ap[:, c, :], scalar1=gam.ap[:, d, c:c + 1], scalar2=None, op0=ALU.mult),
                          [ktm.b, gam.b], [kg.b], eng="gpsimd")
                    pR = nps()
                    MM(pR.ap[:, 0:128], kT.ap[:, csl], qT.ap[:, csl], True, True, [kT.b, qT.b], [pR.b])
                    yield
                    stt = STp.get()
                    V(lambda e, stt=stt, pR=pR, d=d, c=c, dn=dn: e.scalar_tensor_tensor(out=stt.ap, in0=pR.ap[:, 0:128], scalar=beta.ap[:, d, c:c + 1], in1=tri_f[dn].ap,
                                                                                          op0=ALU.mult, op1=ALU.mult), [pR.b, beta.b, tri_f[dn].b], [stt.b])
                    if not last:
                        pC = nps()
                        MM(pC.ap[:, 0:129], kg.ap, vaug.ap[:, c, 0:129], True, True, [kg.b, vaug.b], [pC.b])
                    yield
                    pX = nps()
                    MM(pX.ap[:, 0:129], stt.ap, vaug.ap[:, c, 0:129], True, ci == 0, [stt.b, vaug.b], [pX.b])
                    if ci > 0:
                        MM(pX.ap[:, 0:129], qT.ap[:, csl], Cb.ap[:, 0:129], False, True, [qT.b, Cb.b], [pX.b])
                    if not last:
                        if ci == 0:
                            V(lambda e, pC=pC, Cst=Cst: e.tensor_copy(out=Cst.ap[:, 0:129], in_=pC.ap[:, 0:129]), [pC.b], [Cst.b])
                        else:
                            V(lambda e, pC=pC, d=d, c=c, Cst=Cst: e.scalar_tensor_tensor(out=Cst.ap[:, 0:129], in0=Cst.ap[:, 0:129], scalar=eg.ap[:, d, c:c + 1], in1=pC.ap[:, 0:129],
                                                                                          op0=ALU.mult, op1=ALU.add), [Cst.b, eg.b, pC.b], [Cst.b])
                    yield
                    s4 = sm.get()
                    kb.op("scalar", lambda e, s4=s4, pX=pX: e.activation(out=s4.ap[:, 0:1], in_=pX.ap[:, 128:129], func=AF.Abs), reads=[pX.b], writes=[s4.b])
                    if not last:
                        copy_op("scalar", Cb.ap[:, 0:129], Cst.ap[:, 0:129], [Cst.b], [Cb.b])
                    yield
                    V(lambda e, s4=s4, d=d, c=c: e.tensor_tensor(out=s4.ap[:, 0:1], in0=s4.ap[:, 0:1], in1=emb.ap[:, d, c:c + 1], op=ALU.max), [s4.b, emb.b], [s4.b])
                    V(lambda e, s4=s4: e.reciprocal(out=s4.ap[:, 1:2], in_=s4.ap[:, 0:1]), [s4.b], [s4.b])
                    yield
                    V(lambda e, s4=s4, pX=pX, c=c, hdir=hdir: e.tensor_scalar(out=hdir.ap[:, c, :], in0=pX.ap[:, 0:128], scalar1=s4.ap[:, 1:2], scalar2=None, op0=ALU.mult),
                      [pX.b, s4.b], [hdir.b])
                    yield

            interleave([dir_gen(0, "fw"), dir_gen(1, "bw")])
            for c0 in range(0, nb, 8):
                hsum = hs8.get()
                V(lambda e, hsum=hsum, c0=c0: e.tensor_tensor(out=hsum.ap, in0=hfw.ap[:, c0:c0 + 8, :], in1=hbw.ap[:, c0:c0 + 8, :], op=ALU.add), [hfw.b, hbw.b], [hsum.b])
                sq = sq8.get()
                V(lambda e, hsum=hsum, sq=sq: e.tensor_tensor(out=sq.ap, in0=hsum.ap, in1=hsum.ap, op=ALU.mult), [hsum.b], [sq.b], eng="gpsimd")
                s8 = sm8.get()
                V(lambda e, sq=sq, s8=s8: e.reduce_sum(out=s8.ap[:, 0:8], in_=sq.ap, axis=mybir.AxisListType.X), [sq.b], [s8.b])
                V(lambda e, s8=s8: e.tensor_scalar(out=s8.ap[:, 8:16], in0=s8.ap[:, 0:8], scalar1=1.0 / 128, scalar2=EPS, op0=ALU.mult, op1=ALU.add), [s8.b], [s8.b])
                kb.op("scalar", lambda e, s8=s8: e.activation(out=s8.ap[:, 8:16], in_=s8.ap[:, 8:16], func=AF.Sqrt), reads=[s8.b], writes=[s8.b])
                V(lambda e, s8=s8: e.reciprocal(out=s8.ap[:, 8:16], in_=s8.ap[:, 8:16]), [s8.b], [s8.b])
                V(lambda e, hsum=hsum, s8=s8: e.tensor_tensor(out=hsum.ap, in0=hsum.ap, in1=s8.ap[:, 8:16].unsqueeze(2).broadcast_to([128, 8, 128]), op=ALU.mult), [hsum.b, s8.b], [hsum.b])
                hbt = hb8.get()
                V(lambda e, hsum=hsum, hbt=hbt: e.tensor_tensor(out=hbt.ap, in0=hsum.ap, in1=nwb.ap.unsqueeze(1).broadcast_to([128, 8, 128]), op=ALU.mult), [hsum.b, nwb.b], [hbt.b], eng="gpsimd")
                pT = npsb()
                for j in range(8):
                    kb.op("tensor", lambda e, pT=pT, hbt=hbt, j=j: e.transpose(pT.ap[:, j * 128:(j + 1) * 128], hbt.ap[:, j, :], ident_b.ap), reads=[hbt.b, ident_b.b], writes=[pT.b])
                o_t = ost.get()
                copy_op("scalar", o_t.ap, pT.ap, [pT.b], [o_t.b])
                dma("sync", YM[n].ap[128 * h:128 * h + 128, c0 * 128:(c0 + 8) * 128], o_t.ap, [o_t.b], [YM[n].b])

    PERSIST = g_["PERSIST"]
    aoff = g_["aoff"]

    def MIX(l, si, n, T):
        g_["psmode"][0] = 6
        for fn in (conv, gqa, na, mlstm):
            if fn.__name__ in SKIP_MIX:
                continue
            fn(l, n, T)
            kb.barrier()
            aoff[0] = PERSIST
        g_["psmode"][0] = 6
    MIX.gqa_tables = gqa_tables
    return MIX


SKIP_MIX = set()
DEBUG_YM = False


_W_KEYS = ["rel_bias", "norm_w", "w_ada", "b_ada", "w_in", "b_gate", "mlstm_norm_w", "na_q_norm", "na_k_norm", "na_rpb",
           "swa_q_norm", "swa_k_norm", "swa_sink", "conv_w", "conv_b", "conv_ln_w", "conv_ln_b", "w_branch", "w_out"]


def kernel(**inputs):
    xp = np.ascontiguousarray(np.asarray(inputs["x_prompt"], dtype=np.float32))
    xs = np.ascontiguousarray(np.asarray(inputs["x_sample"], dtype=np.float32))
    cp = np.asarray(inputs["c_prompt"], dtype=np.float32)
    cs = np.asarray(inputs["c_sample"], dtype=np.float32)
    TP, TS = xp.shape[1], xs.shape[1]
    depth = np.asarray(inputs["norm_w"]).shape[0]
    nc = build([("P", TP), ("S", TS)], depth)
    consts = make_consts()
    shared = {k: np.ascontiguousarray(np.asarray(inputs[k], dtype=np.float32)) for k in _W_KEYS}
    shared.update({"k_ident": consts["ident"], "k_tri_fw": consts["tri_fw"], "k_tri_bw": consts["tri_bw"], "k_blk64": consts["blk64"],
                   "k_gqa_m": consts["gqa_m"], "k_na_m": consts["na_m"]})
    in_maps = []
    for i in range(8):
        m = dict(shared)
        m["x_P"] = xp[i // 4]
        m["c_P"] = cp[i // 4][None]
        m["x_S"] = xs[i // 2]
        m["c_S"] = cs[i // 2][None]
        in_maps.append(m)
    res = run_bass_kernel_spmd(nc, in_maps, core_ids=list(range(8)))
    yp = np.empty_like(xp)
    ys = np.empty_like(xs)
    qp, qs = TP // 4, TS // 2
    for i in range(8):
        r = res.results[i]
        a = i % 4
        yp[i // 4, a * qp:(a + 1) * qp] = np.asarray(r["y_P"])[a * qp:(a + 1) * qp]
        b = i % 2
        ys[i // 2, b * qs:(b + 1) * qs] = np.asarray(r["y_S"])[b * qs:(b + 1) * qs]
    return (yp, ys)
NA_W = 3
GQ_W = 2
```
